# Optimizing a Trainium2 kernel written in Bass

```python
import math
import jax, jax.numpy as jnp
from jax import lax
import numpy as np

D_MODEL = 1024
BATCH = 8
SEQ = 2048
DEPTH = 2

HEAD_DIM = 64
ROPE_THETA = 10000.0
NORM_EPS = 1e-6
NEG_INF = -1e30
Q_BLOCK = 128

A_HEADS = 8
A_TOPK_MAX = 256
IDX_HEADS = 8
IDX_DIM = HEAD_DIM
B_HEADS = 4
B_VDIM = 2 * HEAD_DIM
SUBLN_EPS = 1e-5
C_HEADS = D_MODEL // HEAD_DIM
C_PATTERNS = ((128, 1), (512, 4), (2048, 16))
D_FF = ((8 * D_MODEL // 3 + 127) // 128) * 128
PLE_DIM = 256

N_EVEN = (DEPTH + 1) // 2
N_ODD = DEPTH // 2

EVEN_SPLITS = (
    A_HEADS * HEAD_DIM,
    HEAD_DIM,
    HEAD_DIM,
    IDX_HEADS * IDX_DIM,
    IDX_DIM,
    IDX_HEADS,
    2 * B_HEADS * HEAD_DIM,
    2 * B_HEADS * HEAD_DIM,
    B_HEADS * B_VDIM,
)
EVEN_IN = sum(EVEN_SPLITS)
EVEN_MIX_WIDTH = A_HEADS * HEAD_DIM + B_HEADS * B_VDIM

kernel_name = "hybrid_dsa_diff_dilated_macaron"

f32 = jnp.float32


def rms_norm(x, g, eps=NORM_EPS):
    x32 = x.astype(f32)
    y = x32 * lax.rsqrt(jnp.mean(x32 * x32, axis=-1, keepdims=True) + eps)
    return (y * g.astype(f32)).astype(x.dtype)


def rope_tables(T, dim):
    inv = 1.0 / (ROPE_THETA ** (jnp.arange(0, dim, 2, dtype=f32) / dim))
    ang = jnp.arange(T, dtype=f32)[:, None] * inv[None, :]
    return jnp.cos(ang), jnp.sin(ang)


def apply_rope(x, cos, sin):
    shape = (x.shape[1],) + (1,) * (x.ndim - 3) + (cos.shape[-1],)
    c, s = cos.reshape(shape), sin.reshape(shape)
    x1, x2 = jnp.split(x.astype(f32), 2, axis=-1)
    return jnp.concatenate([x1 * c - x2 * s, x2 * c + x1 * s], axis=-1).astype(x.dtype)


def swiglu(h, wg, wu, wd):
    return (jax.nn.silu(h @ wg) * (h @ wu)) @ wd


def to_query_blocks(a):
    B, T = a.shape[:2]
    a = a.reshape((B, T // Q_BLOCK, Q_BLOCK) + a.shape[2:])
    return jnp.moveaxis(a, 1, 0)


def from_query_blocks(a):
    a = jnp.moveaxis(a, 0, 1)
    return a.reshape((a.shape[0], a.shape[1] * a.shape[2]) + a.shape[3:])


def dsa_attention(q, k, v, q_idx, k_idx, w_idx, top_k):
    T = q.shape[1]
    scale = HEAD_DIM ** -0.5
    idx_scale = (IDX_DIM ** -0.5) * (IDX_HEADS ** -0.5)
    k_idx32 = k_idx.astype(f32)
    key_pos = jnp.arange(T)
    gather = jax.vmap(lambda a, i: a[i])

    def block(args):
        qb, qib, wb, start = args
        qpos = start + jnp.arange(Q_BLOCK)
        causal = key_pos[None, :] <= qpos[:, None]
        rel = jax.nn.relu(jnp.einsum('bqhd,bsd->bqhs', qib.astype(f32), k_idx32))
        iscore = jnp.einsum('bqhs,bqh->bqs', rel, wb.astype(f32)) * idx_scale
        iscore = jnp.where(causal[None], iscore, NEG_INF)
        _, sel = lax.top_k(iscore, top_k)
        ks = gather(k, sel)
        vs = gather(v, sel)
        s = jnp.einsum('bqhd,bqkd->bqhk', qb, ks).astype(f32) * scale
        valid = (sel <= qpos[None, :, None])[:, :, None, :]
        pr = jax.nn.softmax(jnp.where(valid, s, NEG_INF), axis=-1)
        return jnp.einsum('bqhk,bqkd->bqhd', pr.astype(vs.dtype), vs)

    starts = jnp.arange(T // Q_BLOCK) * Q_BLOCK
    out = lax.map(block, (to_query_blocks(q), to_query_blocks(q_idx), to_query_blocks(w_idx), starts))
    return from_query_blocks(out)


def diff_attention(q1, q2, k1, k2, v, lam):
    T = q1.shape[1]
    scale = HEAD_DIM ** -0.5
    key_pos = jnp.arange(T)

    def block(args):
        q1b, q2b, start = args
        qpos = start + jnp.arange(Q_BLOCK)
        causal = (key_pos[None, :] <= qpos[:, None])[None, None]

        def probs(qb, kk):
            s = jnp.einsum('bqhd,bshd->bhqs', qb, kk).astype(f32) * scale
            return jax.nn.softmax(jnp.where(causal, s, NEG_INF), axis=-1)

        a = probs(q1b, k1) - lam * probs(q2b, k2)
        return jnp.einsum('bhqs,bshe->bqhe', a.astype(v.dtype), v)

    starts = jnp.arange(T // Q_BLOCK) * Q_BLOCK
    out = lax.map(block, (to_query_blocks(q1), to_query_blocks(q2), starts))
    return from_query_blocks(out)


def dilated_branch(q, k, v, window, dilation):
    B, T, H, D = q.shape
    span = window // dilation
    n = T // dilation
    nb = -(-n // span)
    n_pad = nb * span
    Z = B * dilation

    def to_sub(a):
        E = a.shape[-1]
        a = a.reshape(B, n, dilation, H, E).transpose(0, 2, 1, 3, 4).reshape(Z, n, H, E)
        return jnp.pad(a, ((0, 0), (0, n_pad - n), (0, 0), (0, 0)))

    def banded(a):
        E = a.shape[-1]
        a = jnp.pad(a, ((0, 0), (span, 0), (0, 0), (0, 0))).reshape(Z, nb + 1, span, H, E)
        return jnp.concatenate([a[:, :-1], a[:, 1:]], axis=2)

    qb = to_sub(q).reshape(Z, nb, span, H, D)
    kb = banded(to_sub(k))
    vb = banded(to_sub(v))
    dist = jnp.arange(span)[:, None] + span - jnp.arange(2 * span)[None, :]
    m_key = jnp.arange(nb)[:, None] * span + jnp.arange(2 * span)[None, :] - span
    mask = ((dist >= 0) & (dist <= span))[None] & ((m_key >= 0) & (m_key < n))[:, None, :]
    s = jnp.einsum('znqhd,znkhd->znhqk', qb, kb).astype(f32) * (HEAD_DIM ** -0.5)
    s = jnp.where(mask[None, :, None], s, NEG_INF)
    lse = jax.nn.logsumexp(s, axis=-1)
    pr = jnp.exp(s - lse[..., None])
    o = jnp.einsum('znhqk,znkhd->znqhd', pr.astype(v.dtype), vb)

    def from_sub(a):
        a = a.reshape((B, dilation, n_pad) + a.shape[3:])[:, :, :n]
        a = jnp.moveaxis(a, 1, 2)
        return a.reshape((B, T) + a.shape[3:])

    return from_sub(o), from_sub(jnp.moveaxis(lse, 2, 3))


def dilated_mixture(q, k, v):
    outs, lses = [], []
    for window, dilation in C_PATTERNS:
        o, l = dilated_branch(q, k, v, window, dilation)
        outs.append(o)
        lses.append(l)
    alpha = jax.nn.softmax(jnp.stack(lses, 0), axis=0)
    return jnp.sum(alpha[..., None].astype(q.dtype) * jnp.stack(outs, 0), axis=0)


def even_mixer(h, w_in, w_out, lq1, lk1, lq2, lk2, subln, lambda_init, cos, sin, top_k):
    B, T, _ = h.shape
    cuts = [int(c) for c in np.cumsum(EVEN_SPLITS)[:-1]]
    qa, ka, va, qi, ki, wi, qb, kb, vb = jnp.split(h @ w_in, cuts, axis=-1)
    qa = apply_rope(qa.reshape(B, T, A_HEADS, HEAD_DIM), cos, sin)
    ka = apply_rope(ka, cos, sin)
    qi = apply_rope(qi.reshape(B, T, IDX_HEADS, IDX_DIM), cos, sin)
    ki = apply_rope(ki, cos, sin)
    out_a = dsa_attention(qa, ka, va, qi, ki, wi, top_k).reshape(B, T, A_HEADS * HEAD_DIM)
    qb = apply_rope(qb.reshape(B, T, 2 * B_HEADS, HEAD_DIM), cos, sin).reshape(B, T, B_HEADS, 2, HEAD_DIM)
    kb = apply_rope(kb.reshape(B, T, 2 * B_HEADS, HEAD_DIM), cos, sin).reshape(B, T, B_HEADS, 2, HEAD_DIM)
    vb = vb.reshape(B, T, B_HEADS, B_VDIM)
    lam = (jnp.exp(jnp.sum(lq1.astype(f32) * lk1.astype(f32)))
           - jnp.exp(jnp.sum(lq2.astype(f32) * lk2.astype(f32))) + lambda_init)
    ob = diff_attention(qb[..., 0, :], qb[..., 1, :], kb[..., 0, :], kb[..., 1, :], vb, lam)
    ob = rms_norm(ob, subln, SUBLN_EPS) * (1.0 - lambda_init)
    merged = jnp.concatenate([out_a, ob.reshape(B, T, B_HEADS * B_VDIM)], axis=-1)
    return merged @ w_out


def odd_mixer(h, w_in, w_out, cos, sin):
    B, T, _ = h.shape
    q, k, v = jnp.split(h @ w_in, 3, axis=-1)
    q = apply_rope(q.reshape(B, T, C_HEADS, HEAD_DIM), cos, sin)
    k = apply_rope(k.reshape(B, T, C_HEADS, HEAD_DIM), cos, sin)
    v = v.reshape(B, T, C_HEADS, HEAD_DIM)
    return dilated_mixture(q, k, v).reshape(B, T, C_HEADS * HEAD_DIM) @ w_out


def setup_inputs(seed: int = 0) -> dict:
    key = jax.random.key(seed)
    ks = iter(jax.random.split(key, 32))

    def w(shape, fan_in):
        return jax.random.normal(next(ks), shape, f32) * (fan_in ** -0.5)

    def gain(shape):
        return 1.0 + 0.02 * jax.random.normal(next(ks), shape, f32)

    return {
        "x": jax.random.normal(next(ks), (BATCH, SEQ, D_MODEL), f32),
        "p": jax.random.normal(next(ks), (DEPTH, BATCH, SEQ, PLE_DIM), f32),
        "norm_ffn_a": gain((DEPTH, D_MODEL)),
        "ffn_a_wg": w((DEPTH, D_MODEL, D_FF), D_MODEL),
        "ffn_a_wu": w((DEPTH, D_MODEL, D_FF), D_MODEL),
        "ffn_a_wd": w((DEPTH, D_FF, D_MODEL), D_FF),
        "norm_mix": gain((DEPTH, D_MODEL)),
        "norm_ffn_b": gain((DEPTH, D_MODEL)),
        "ffn_b_wg": w((DEPTH, D_MODEL, D_FF), D_MODEL),
        "ffn_b_wu": w((DEPTH, D_MODEL, D_FF), D_MODEL),
        "ffn_b_wd": w((DEPTH, D_FF, D_MODEL), D_FF),
        "norm_ple": gain((DEPTH, D_MODEL)),
        "ple_gate": w((DEPTH, D_MODEL, D_MODEL), D_MODEL),
        "ple_proj": w((DEPTH, PLE_DIM, D_MODEL), PLE_DIM),
        "even_w_in": w((N_EVEN, D_MODEL, EVEN_IN), D_MODEL),
        "even_w_out": w((N_EVEN, EVEN_MIX_WIDTH, D_MODEL), EVEN_MIX_WIDTH),
        "diff_lambda_q1": 0.1 * jax.random.normal(next(ks), (N_EVEN, HEAD_DIM), f32),
        "diff_lambda_k1": 0.1 * jax.random.normal(next(ks), (N_EVEN, HEAD_DIM), f32),
        "diff_lambda_q2": 0.1 * jax.random.normal(next(ks), (N_EVEN, HEAD_DIM), f32),
        "diff_lambda_k2": 0.1 * jax.random.normal(next(ks), (N_EVEN, HEAD_DIM), f32),
        "diff_subln": gain((N_EVEN, B_VDIM)),
        "odd_w_in": w((N_ODD, D_MODEL, 3 * C_HEADS * HEAD_DIM), D_MODEL),
        "odd_w_out": w((N_ODD, C_HEADS * HEAD_DIM, D_MODEL), C_HEADS * HEAD_DIM),
        "final_norm": gain((D_MODEL,)),
    }


def reference(x, p, norm_ffn_a, ffn_a_wg, ffn_a_wu, ffn_a_wd, norm_mix, norm_ffn_b,
              ffn_b_wg, ffn_b_wu, ffn_b_wd, norm_ple, ple_gate, ple_proj, even_w_in,
              even_w_out, diff_lambda_q1, diff_lambda_k1, diff_lambda_q2, diff_lambda_k2,
              diff_subln, odd_w_in, odd_w_out, final_norm):
    T = x.shape[1]
    top_k = min(A_TOPK_MAX, T // 4)
    cos, sin = rope_tables(T, HEAD_DIM)
    h = x
    for i in range(DEPTH):
        h = h + 0.5 * swiglu(rms_norm(h, norm_ffn_a[i]), ffn_a_wg[i], ffn_a_wu[i], ffn_a_wd[i])
        hn = rms_norm(h, norm_mix[i])
        if i % 2 == 0:
            e = i // 2
            lambda_init = 0.8 - 0.6 * math.exp(-0.3 * i)
            mix = even_mixer(hn, even_w_in[e], even_w_out[e], diff_lambda_q1[e], diff_lambda_k1[e],
                             diff_lambda_q2[e], diff_lambda_k2[e], diff_subln[e], lambda_init,
                             cos, sin, top_k)
        else:
            o = i // 2
            mix = odd_mixer(hn, odd_w_in[o], odd_w_out[o], cos, sin)
        h = h + mix
        h = h + 0.5 * swiglu(rms_norm(h, norm_ffn_b[i]), ffn_b_wg[i], ffn_b_wu[i], ffn_b_wd[i])
        gate = jax.nn.sigmoid(rms_norm(h, norm_ple[i]) @ ple_gate[i])
        h = h + gate * (p[i] @ ple_proj[i])
    return rms_norm(h, final_norm)
```

```python
import numpy as np
import ml_dtypes
import concourse.bass as bass
import concourse.mybir as mybir
from concourse.bass_utils import run_bass_kernel_spmd

F32 = mybir.dt.float32
BF16 = mybir.dt.bfloat16
ALU = mybir.AluOpType
AF = mybir.ActivationFunctionType
AX = mybir.AxisListType

T = 2048
D = 1024
DFF = 2816
NKC = D // 128
NFC = DFF // 128
NTB = T // 128
NTC = T // 512
PLE = 256
EVEN_IN = 2760
NEG = -1.0e30
NEG2 = -2.0e30


class Op:
    __slots__ = ("eng", "fn", "waits", "signal", "sem", "val", "is_dma", "idx", "is_bar")

    def __init__(self, eng, fn, is_dma=False):
        self.eng = eng
        self.fn = fn
        self.waits = []
        self.signal = False
        self.sem = None
        self.val = 0
        self.is_dma = is_dma
        self.idx = -1
        self.is_bar = False


ENGS = ("pe", "act", "dve", "pool", "sp")
SEM_LIM = 30000


class Prog:
    def __init__(self, nc):
        self.nc = nc
        self.ops = {e: [] for e in ENGS}
        self.last_w = {}
        self.readers = {}
        self.seen = {e: {s: -1 for s in ENGS} for e in ENGS}
        self.seen_dma = {e: set() for e in ENGS}
        self.pending_dma = []
        self.dma_list = []
        self.nops = 0

    NDMA = 24

    def add(self, eng, fn, reads=(), writes=(), dma=False):
        op = Op(eng, fn, is_dma=dma)
        deps = []
        if dma:
            k = len(self.dma_list)
            if k >= self.NDMA:
                deps.append((self.dma_list[k - self.NDMA], True))
            self.dma_list.append(op)
        for r in reads:
            w = self.last_w.get(r)
            if w is not None:
                deps.append((w, True))
            if r == "pT" or (isinstance(r, tuple) and r[0] == "ps"):
                for rd in self.readers.get(r, ()):
                    if rd.eng != eng:
                        deps.append((rd, True))
        for r in writes:
            w = self.last_w.get(r)
            if w is not None:
                deps.append((w, True))
            for rd in self.readers.get(r, ()):
                deps.append((rd, False))
        best = {}
        for d, is_raw in deps:
            if d.is_dma:
                self._dep(op, d, is_raw)
                continue
            if d.eng == eng and eng in ("pe", "sp"):
                continue
            cur = best.get(d.eng)
            if cur is None or d.idx > cur.idx:
                best[d.eng] = d
        for d in best.values():
            self._dep(op, d, True)
        op.idx = len(self.ops[eng])
        self.ops[eng].append(op)
        for r in reads:
            self.readers.setdefault(r, []).append(op)
        for r in writes:
            self.last_w[r] = op
            self.readers[r] = []
        if dma:
            self.pending_dma.append(op)
        self.nops += 1
        return op

    def _dep(self, op, d, is_raw):
        e = op.eng
        if d.is_dma:
            if d in self.seen_dma[e]:
                return
            self.seen_dma[e].add(d)
            d.signal = True
            op.waits.append(d)
            return
        if d.eng == e:
            if e == "pe" or e == "sp":
                return
        if self.seen[e][d.eng] >= d.idx:
            return
        self.seen[e][d.eng] = d.idx
        d.signal = True
        op.waits.append(d)

    def barrier(self):
        bar = Op("sp", None)
        bar.is_bar = True
        for e in ENGS:
            if e == "sp":
                continue
            if self.ops[e]:
                self._dep(bar, self.ops[e][-1], True)
        for d in self.pending_dma:
            self._dep(bar, d, True)
        self.pending_dma = []
        bar.idx = len(self.ops["sp"])
        self.ops["sp"].append(bar)
        bar.signal = True
        for e in ENGS:
            if e == "sp":
                continue
            w = Op(e, None)
            w.is_bar = True
            w.waits.append(bar)
            self.seen[e]["sp"] = bar.idx
            w.idx = len(self.ops[e])
            self.ops[e].append(w)
        self.last_w = {}
        self.readers = {}

    def emit(self, stack):
        nc = self.nc
        ndma = self.NDMA
        dsems = [stack.enter_context(nc.semaphore(f"s_dma_{i}")) for i in range(ndma)]
        dcnt = [0] * ndma
        for k, op in enumerate(self.dma_list):
            di = k % ndma
            if dcnt[di] + 16 > SEM_LIM:
                dsems[di] = stack.enter_context(nc.semaphore(f"s_dma_{di}_{k}"))
                dcnt[di] = 0
            dcnt[di] += 16
            op.sem = dsems[di]
            op.val = dcnt[di]
        eng_sems = {}
        for e in ENGS:
            cur = None
            cnt = 0
            for op in self.ops[e]:
                if not op.signal or op.is_dma:
                    continue
                if cur is None or cnt >= SEM_LIM:
                    cur = stack.enter_context(nc.semaphore(f"s_{e}_{len(eng_sems)}"))
                    eng_sems[(e, len(eng_sems))] = cur
                    cnt = 0
                cnt += 1
                op.sem = cur
                op.val = cnt
        block = stack.enter_context(nc.Block())

        def run(e):
            def body(eng):
                for op in self.ops[e]:
                    for d in op.waits:
                        eng.wait_ge(d.sem, d.val)
                    if op.fn is None:
                        if op.signal:
                            eng.sem_inc(op.sem, 1)
                        continue
                    ins = op.fn(eng)
                    if op.is_dma:
                        ins.then_inc(op.sem, 16)
                    elif op.signal:
                        ins.then_inc(op.sem, 1)
            return body

        block.tensor(run("pe"))
        block.scalar(run("act"))
        block.vector(run("dve"))
        block.gpsimd(run("pool"))
        block.sync(run("sp"))


STRIP_W = 2432


def _host_consts():
    f32 = np.float32
    inv = (1.0 / (f32(10000.0) ** (np.arange(0, 64, 2, dtype=f32) / f32(64)))).astype(f32)
    ang = (np.arange(T, dtype=f32)[:, None] * inv[None, :]).astype(f32)
    cos = np.cos(ang).astype(f32)
    sin = np.sin(ang).astype(f32)
    p = np.arange(128)
    d = p % 64
    j = d % 32
    C2 = cos[:, j].T.copy()
    S2 = (sin[:, j].T * np.where(d < 32, -1.0, 1.0)[:, None]).astype(f32)
    rope = np.ascontiguousarray(np.stack([C2, S2], 0)).astype(f32)
    x = np.arange(STRIP_W)[None, :] - 384 - p[:, None]
    causal = (x >= 0).astype(f32)
    mult = (((x >= 0) & (x <= 128)).astype(f32)
            + ((x >= 0) & (x % 4 == 0) & (x <= 512)).astype(f32)
            + ((x >= 0) & (x % 16 == 0) & (x <= 2048)).astype(f32))
    strips = np.stack([causal, mult], 0).astype(ml_dtypes.bfloat16)
    ident = np.eye(128, dtype=f32)
    perm = np.zeros((128, 128), f32)
    for m in range(128):
        perm[m ^ 32, m] = 1.0
    ones = np.ones((128, 128), f32)
    mats = np.stack([ident, perm, ones], 0).astype(ml_dtypes.bfloat16)
    tt = np.arange(128)
    negmask = np.where(tt[None, :] > tt[:, None], np.float32(NEG), np.float32(0.0)).astype(f32)
    return rope, strips, mats, negmask


def _col(v):
    v = np.asarray(v, np.float32)
    return np.ascontiguousarray(v.reshape(-1, 128).T)


CFG = {"layers": 2, "mix_even": True, "mix_odd": True, "dsa": True, "diff": True, "ffn": True, "ple": True, "proj": True, "attn": True}


class Rot:
    def __init__(self, items):
        self.items = list(items)
        self.i = 0

    def next(self):
        v = self.items[self.i % len(self.items)]
        self.i += 1
        return v


def build_program(cfg=None):
    from contextlib import ExitStack
    cfg = dict(CFG if cfg is None else cfg)
    nc = bass.Bass("TRN2", target_bir_lowering=False)

    declared = []

    def din(name, shape, dt=F32, need=True):
        if not need:
            return None
        declared.append(name)
        return nc.dram_tensor(name, list(shape), dt, kind="ExternalInput").ap()

    xT = din("xT", [D, T])
    pTd = din("pT", [2, PLE, T])
    gains_d = din("gains", [128, 72])
    ffn_w = {}
    for nm in ("ffn_a_wg", "ffn_a_wu", "ffn_b_wg", "ffn_b_wu"):
        ffn_w[nm] = din(nm, [2, D, DFF], need=cfg["ffn"])
    for nm in ("ffn_a_wd", "ffn_b_wd"):
        ffn_w[nm] = din(nm, [2, DFF, D], need=cfg["ffn"])
    ple_gate = din("ple_gate", [2, D, D], need=cfg["ple"])
    ple_proj = din("ple_proj", [2, PLE, D], need=cfg["ple"])
    ew_fm = din("ew_fm", [D, 2304])
    ew_va = din("ew_va", [D, 64])
    ew_wi = din("ew_wi", [D, 8])
    ew_vb = din("ew_vb", [D, 512])
    even_w_out = din("even_w_out", [D, D])
    lamv = din("lamv", [4, 64])
    subln_d = din("subln", [128, 1])
    odd_w_in = din("odd_w_in", [D, 3072])
    odd_w_out = din("odd_w_out", [D, D])
    rope_d = din("rope", [2, 128, T])
    strips_d = din("strips", [2, 128, STRIP_W], BF16)
    mats_d = din("mats", [3, 128, 128], BF16)
    negm_d = din("negmask", [128, 128])
    outT = nc.dram_tensor("outT", [D, T], F32, kind="ExternalOutput").ap()

    uid = [0]

    with ExitStack() as st:
        def mk_sb(stack):
            def f(name, shape, dt):
                uid[0] += 1
                return stack.enter_context(nc.sbuf_tensor(f"{name}_{uid[0]}", list(shape), dt))
            return f

        sb = mk_sb(st)
        h = sb("h", [128, NKC, T], F32)
        gsb = sb("gsb", [128, 72], F32)
        mats = sb("mats", [128, 3, 128], BF16)
        negm = sb("negm", [128, 128], F32)
        epsb = sb("epsb", [128, 2], F32)
        lv = sb("lv", [128, 4, 64], F32)
        lsm = sb("lsm", [128, 8], F32)
        sqb = [sb(f"sqb{i}", [128, 512], BF16) for i in range(2)]
        sdt = sb("sdt", [128, 512], F32)
        rstd = sb("rstd", [128, 512], F32)
        ps = [st.enter_context(nc.psum_tensor(f"ps{i}", [128, 512], F32)) for i in range(7)]
        pTp = st.enter_context(nc.psum_tensor("pTp", [128, 1024], BF16))
        ident = mats[:, 0, :]
        perm = mats[:, 1, :]
        ones = mats[:, 2, :]

        P = Prog(nc)

        def A(eng, fn, r=(), w=()):
            return P.add(eng, fn, reads=r, writes=w)

        def DMA(out, in_, r=(), w=()):
            return P.add("sp", lambda e: e.dma_start(out=out, in_=in_), reads=r, writes=w, dma=True)

        for k in range(NKC):
            DMA(h[:, k, :], xT[k * 128:(k + 1) * 128, :], w=[("h", k, c) for c in range(NTC)])
        DMA(gsb[:], gains_d, w=["gsb"])
        DMA(mats[:], mats_d.rearrange("a p n -> p a n"), w=["mats"])
        DMA(negm[:], negm_d, w=["negm"])
        DMA(lv[:], lamv.partition_broadcast(128), w=["lv"])
        DMA(lsm[:, 7:8], subln_d, w=["lsm7"])
        A("pool", lambda e: e.memset(epsb[:, 0:1], 1e-6), w=["epsb"])
        A("pool", lambda e: e.memset(epsb[:, 1:2], 1e-5), w=["epsb"])

        sqrot = Rot([0, 1])

        class WPool:
            def __init__(self, stack, elems, nst, nbf, tag):
                f = mk_sb(stack)
                self.stg = [f(f"wst{tag}{i}", [128, elems], F32) for i in range(nst)]
                self.bf = [f(f"wbf{tag}{i}", [128, elems], BF16) for i in range(nbf)]
                self.i = 0
                uid[0] += 1
                self.tag = f"{tag}{uid[0]}"

            def load(self, desc):
                src, a, b = desc
                i = self.i
                self.i += 1
                si, bi = i % len(self.stg), i % len(self.bf)
                sv = self.stg[si][:, 0:a * b].rearrange("p (a b) -> p a b", a=a)
                bv = self.bf[bi][:, 0:a * b].rearrange("p (a b) -> p a b", a=a)
                sk, bk = (self.tag, "st", si), (self.tag, "bf", bi)
                DMA(sv, src, w=[sk])
                A("pool", lambda e: e.tensor_copy(out=bv, in_=sv), r=[sk], w=[bk])
                return bv, bk

            def load_into(self, desc, dst, dkey):
                src, a, b = desc
                i = self.i
                self.i += 1
                si = i % len(self.stg)
                sv = self.stg[si][:, 0:a * b].rearrange("p (a b) -> p a b", a=a)
                sk = (self.tag, "st", si)
                DMA(sv, src, w=[sk])
                A("pool", lambda e: e.tensor_copy(out=dst, in_=sv), r=[sk], w=[dkey])

        def stream(wp, descs, depth=2):
            q = []
            n = len(descs)
            nxt = 0
            for i in range(n):
                while nxt < n and nxt <= i + depth:
                    q.append(wp.load(descs[nxt]))
                    nxt += 1
                yield q[i]

        def wdesc(w2d, r0, nr, c0, ncol):
            return (w2d[r0:r0 + nr, c0:c0 + ncol].rearrange("(a p) n -> p a n", p=128), nr // 128, ncol)

        def norm_chunk(gi, c, dst_fn, dkey_fn, srot, after=None, engs=("dve",)):
            b = srot.next()
            for k in range(NKC):
                si = sqrot.next()
                A("act", lambda e, k=k, si=si: e.activation(out=sqb[si][:], in_=h[:, k, c * 512:(c + 1) * 512],
                                                            func=AF.Square, scale=1.0 / 32.0),
                  r=[("h", k, c)], w=[("sqb", si)])
                A("pe", lambda e, k=k, si=si: e.matmul(ps[b][:], lhsT=ones, rhs=sqb[si][:], start=(k == 0), stop=(k == NKC - 1)),
                  r=[("sqb", si), "mats"], w=[("ps", b)])
            A("act", lambda e: e.activation(out=sdt[:], in_=ps[b][:], func=AF.Sqrt, bias=epsb[:, 0:1]),
              r=[("ps", b), "epsb"], w=["sdt"])
            A("dve", lambda e: e.reciprocal(out=rstd[:], in_=sdt[:]), r=["sdt"], w=["rstd"])
            for k in range(NKC):
                eng = engs[k % len(engs)]
                dst = dst_fn(k)
                A(eng, lambda e, k=k, dst=dst: e.scalar_tensor_tensor(out=dst, in0=h[:, k, c * 512:(c + 1) * 512],
                                                                      scalar=gsb[:, gi * 8 + k:gi * 8 + k + 1], in1=rstd[:],
                                                                      op0=ALU.mult, op1=ALU.mult),
                  r=[("h", k, c), "gsb", "rstd"], w=[dkey_fn(k)])
                if after is not None:
                    after(k)

        def ffn(l, wg, wu, wd, gi):
            for tp in range(2):
                ffn_pass(l, wg, wu, wd, gi, tp)
                P.barrier()

        def ffn_pass(l, wg, wu, wd, gi, tp):
            if True:
                with ExitStack() as s2:
                    f2sb = mk_sb(s2)
                    hn = f2sb("hn", [128, NKC, 1024], BF16)
                    act = f2sb("act", [128, NFC, 1024], BF16)
                    sg = [f2sb(f"sg{i}", [128, 512], F32) for i in range(2)]
                    wp = WPool(s2, 2816, 3, 4, "f")
                    srot = Rot(range(7))
                    sgrot = Rot([0, 1])
                    for sub in range(2):
                        norm_chunk(gi, tp * 2 + sub, lambda k, sub=sub: hn[:, k, sub * 512:(sub + 1) * 512],
                                   lambda k, sub=sub: ("hn", k, sub), srot)
                    descs = []
                    for f2 in range(NFC // 2):
                        descs.append(wdesc(wg[l], 0, D, f2 * 256, 256))
                        descs.append(wdesc(wu[l], 0, D, f2 * 256, 256))
                    it = stream(wp, descs)
                    for f2 in range(NFC // 2):
                        gt, gk = next(it)
                        ut, uk = next(it)
                        for fi in range(2):
                            f = 2 * f2 + fi
                            for sub in range(2):
                                bg = srot.next()
                                bu = srot.next()
                                for k in range(NKC):
                                    A("pe", lambda e, k=k, bg=bg, gt=gt, fi=fi, sub=sub: e.matmul(
                                        ps[bg][:], lhsT=gt[:, k, fi * 128:(fi + 1) * 128], rhs=hn[:, k, sub * 512:(sub + 1) * 512],
                                        start=(k == 0), stop=(k == NKC - 1)), r=[gk, ("hn", k, sub)], w=[("ps", bg)])
                                for k in range(NKC):
                                    A("pe", lambda e, k=k, bu=bu, ut=ut, fi=fi, sub=sub: e.matmul(
                                        ps[bu][:], lhsT=ut[:, k, fi * 128:(fi + 1) * 128], rhs=hn[:, k, sub * 512:(sub + 1) * 512],
                                        start=(k == 0), stop=(k == NKC - 1)), r=[uk, ("hn", k, sub)], w=[("ps", bu)])
                                si = sgrot.next()
                                A("act", lambda e, si=si, bg=bg: e.activation(out=sg[si][:], in_=ps[bg][:], func=AF.Silu),
                                  r=[("ps", bg)], w=[("sg", si)])
                                A("dve", lambda e, si=si, bu=bu, f=f, sub=sub: e.tensor_tensor(
                                    out=act[:, f, sub * 512:(sub + 1) * 512], in0=sg[si][:], in1=ps[bu][:], op=ALU.mult),
                                  r=[("sg", si), ("ps", bu)], w=[("act", f, sub)])
                    descs = []
                    for d2 in range(4):
                        for half in range(2):
                            descs.append(wdesc(wd[l], half * 1408, 1408, d2 * 256, 256))
                    it = stream(wp, descs)
                    for d2 in range(4):
                        tl0, tk0 = next(it)
                        tl1, tk1 = next(it)
                        for di in range(2):
                            d = 2 * d2 + di
                            for sub in range(2):
                                b = srot.next()
                                c = tp * 2 + sub
                                for fc in range(NFC):
                                    tl, tk = (tl0, tk0) if fc < 11 else (tl1, tk1)
                                    A("pe", lambda e, fc=fc, tl=tl, di=di, sub=sub, b=b: e.matmul(
                                        ps[b][:], lhsT=tl[:, fc % 11, di * 128:(di + 1) * 128], rhs=act[:, fc, sub * 512:(sub + 1) * 512],
                                        start=(fc == 0), stop=(fc == NFC - 1)), r=[tk, ("act", fc, sub)], w=[("ps", b)])
                                A("dve", lambda e, d=d, c=c, b=b: e.scalar_tensor_tensor(
                                    out=h[:, d, c * 512:(c + 1) * 512], in0=ps[b][:], scalar=0.5, in1=h[:, d, c * 512:(c + 1) * 512],
                                    op0=ALU.mult, op1=ALU.add), r=[("ps", b), ("h", d, c)], w=[("h", d, c)])

        def ple(l, gi):
            with ExitStack() as s2:
                f2sb = mk_sb(s2)
                hn = f2sb("hnp", [128, NKC, T], BF16)
                ptb = f2sb("ptb", [128, 2, T], BF16)
                sig = [f2sb(f"sig{i}", [128, 512], F32) for i in range(2)]
                tmp = [f2sb(f"ptmp{i}", [128, 512], F32) for i in range(2)]
                wp = WPool(s2, 2048, 3, 4, "p")
                srot = Rot(range(7))
                r2 = Rot([0, 1])
                for c in range(NTC):
                    norm_chunk(gi, c, lambda k, c=c: hn[:, k, c * 512:(c + 1) * 512], lambda k, c=c: ("hnp", k, c), srot)
                    wp.load_into((pTd[l][:, c * 512:(c + 1) * 512].rearrange("(a p) t -> p a t", p=128), 2, 512),
                                 ptb[:, :, c * 512:(c + 1) * 512], ("ptb", c))
                descs = []
                for d2 in range(4):
                    descs.append(wdesc(ple_gate[l], 0, D, d2 * 256, 256))
                    descs.append(wdesc(ple_proj[l], 0, PLE, d2 * 256, 256))
                it = stream(wp, descs)
                for d2 in range(4):
                    gt, gk = next(it)
                    pt, pk = next(it)
                    for di in range(2):
                        d = 2 * d2 + di
                        for c in range(NTC):
                            bg = srot.next()
                            bp = srot.next()
                            for k in range(NKC):
                                A("pe", lambda e, k=k, bg=bg, gt=gt, di=di, c=c: e.matmul(
                                    ps[bg][:], lhsT=gt[:, k, di * 128:(di + 1) * 128], rhs=hn[:, k, c * 512:(c + 1) * 512],
                                    start=(k == 0), stop=(k == NKC - 1)), r=[gk, ("hnp", k, c)], w=[("ps", bg)])
                            for k in range(2):
                                A("pe", lambda e, k=k, bp=bp, pt=pt, di=di, c=c: e.matmul(
                                    ps[bp][:], lhsT=pt[:, k, di * 128:(di + 1) * 128], rhs=ptb[:, k, c * 512:(c + 1) * 512],
                                    start=(k == 0), stop=(k == 1)), r=[pk, ("ptb", c)], w=[("ps", bp)])
                            si = r2.next()
                            A("act", lambda e, si=si, bg=bg: e.activation(out=sig[si][:], in_=ps[bg][:], func=AF.Sigmoid),
                              r=[("ps", bg)], w=[("sig", si)])
                            A("dve", lambda e, si=si, bp=bp: e.tensor_tensor(out=tmp[si][:], in0=sig[si][:], in1=ps[bp][:], op=ALU.mult),
                              r=[("sig", si), ("ps", bp)], w=[("ptmp", si)])
                            A("pool", lambda e, si=si, d=d, c=c: e.tensor_tensor(
                                out=h[:, d, c * 512:(c + 1) * 512], in0=h[:, d, c * 512:(c + 1) * 512], in1=tmp[si][:], op=ALU.add),
                              r=[("ptmp", si), ("h", d, c)], w=[("h", d, c)])
            P.barrier()

        def final_norm():
            with ExitStack() as s2:
                f2sb = mk_sb(s2)
                ot = [f2sb(f"ot{i}", [128, 512], F32) for i in range(4)]
                srot = Rot(range(7))
                for c in range(NTC):
                    def after(k, c=c):
                        i = (c * 8 + k) % 4
                        DMA(outT[k * 128:(k + 1) * 128, c * 512:(c + 1) * 512], ot[i][:], r=[("ot", i)])
                    norm_chunk(8, c, lambda k, c=c: ot[(c * 8 + k) % 4][:], lambda k, c=c: ("ot", (c * 8 + k) % 4), srot,
                               after=after, engs=("dve",))
            P.barrier()

        def mixer_proj(gi, fm_items, tm_items):
            import os as _os
            _nfm = int(_os.environ.get("PROJ_FM", "99"))
            _ntm = int(_os.environ.get("PROJ_TM", "99"))
            fm_items = fm_items[:_nfm]
            tm_items = tm_items[:_ntm]
            with ExitStack() as s2:
                f2sb = mk_sb(s2)
                hnc = f2sb("hnc", [128, NKC, 512], BF16)
                ropc = [f2sb(f"ropc{i}", [128, 2, 512], F32) for i in range(2)]
                xb = [f2sb(f"xb{i}", [128, 512], BF16) for i in range(2)]
                t1 = [f2sb(f"t1{i}", [128, 512], F32) for i in range(2)]
                t2 = [f2sb(f"t2{i}", [128, 512], F32) for i in range(2)]
                wp = WPool(s2, 1024, 2, 3, "m")
                srot = Rot(range(7))
                r2 = Rot([0, 1])
                for c in range(NTC):
                    norm_chunk(gi, c, lambda k: hnc[:, k, :], lambda k: ("hnc", k), srot)
                    rc = c % 2
                    DMA(ropc[rc][:], rope_d[:, :, c * 512:(c + 1) * 512].rearrange("a p t -> p a t"), w=[("ropc", rc)])
                    descs = [(w_ap.rearrange("(a p) n -> p a n", p=128), 8, 128) for (w_ap, _) in fm_items]
                    descs += [(w_ap.rearrange("(a p) n -> p a n", p=128), 8, n) for (w_ap, n, _, _) in tm_items]
                    it = stream(wp, descs)
                    for (_, dst_fn) in fm_items:
                        wt, wk = next(it)
                        b = srot.next()
                        for k in range(NKC):
                            A("pe", lambda e, k=k, b=b, wt=wt: e.matmul(ps[b][:], lhsT=wt[:, k, :], rhs=hnc[:, k, :],
                                                                        start=(k == 0), stop=(k == NKC - 1)),
                              r=[wk, ("hnc", k)], w=[("ps", b)])
                        i = r2.next()
                        A("act", lambda e, i=i, b=b: e.activation(out=xb[i][:], in_=ps[b][:], func=AF.Copy),
                          r=[("ps", b)], w=[("xb", i)])
                        A("dve", lambda e, i=i, b=b, rc=rc: e.tensor_tensor(out=t1[i][:], in0=ps[b][:], in1=ropc[rc][:, 0, :], op=ALU.mult),
                          r=[("ps", b), ("ropc", rc), ("xb", i)], w=[("t1", i)])
                        b2 = srot.next()
                        A("pe", lambda e, i=i, b2=b2: e.matmul(ps[b2][:], lhsT=perm, rhs=xb[i][:], start=True, stop=True),
                          r=[("xb", i), "mats"], w=[("ps", b2)])
                        A("dve", lambda e, i=i, b2=b2, rc=rc: e.tensor_tensor(out=t2[i][:], in0=ps[b2][:], in1=ropc[rc][:, 1, :], op=ALU.mult),
                          r=[("ps", b2), ("ropc", rc)], w=[("t2", i)])
                        dst, dkey = dst_fn(c)
                        A("pool", lambda e, i=i, dst=dst: e.tensor_tensor(out=dst, in0=t1[i][:], in1=t2[i][:], op=ALU.add),
                          r=[("t1", i), ("t2", i)], w=[dkey])
                    for (_, n, dst_fn, _) in tm_items:
                        wt, wk = next(it)
                        for tb in range(4):
                            b = srot.next()
                            for k in range(NKC):
                                A("pe", lambda e, k=k, b=b, wt=wt, tb=tb, n=n: e.matmul(
                                    ps[b][:, 0:n], lhsT=hnc[:, k, tb * 128:(tb + 1) * 128], rhs=wt[:, k, :],
                                    start=(k == 0), stop=(k == NKC - 1)), r=[wk, ("hnc", k)], w=[("ps", b)])
                            dst, dkey = dst_fn(c * 4 + tb)
                            A("act", lambda e, b=b, dst=dst, n=n: e.activation(out=dst, in_=ps[b][:, 0:n], func=AF.Copy),
                              r=[("ps", b)], w=[dkey])
            P.barrier()

        def attn_pairs_chunk(c, Q, qc, qkey, K, kc, kkey, vfn, vkey, mfn, mkey, dst, dkey, accs, srot, tiles):
            ebs, pbs, rds, erot, prot, rrot, mengs = tiles
            nb, db = accs
            last = 4 * c + 3
            for j in range(last + 1):
                t0 = max(0, j * 128 - c * 512)
                n = 512 - t0
                for e_ in range(2):
                    lo, hi = e_ * 64, (e_ + 1) * 64
                    b = srot.next()
                    A("pe", lambda e, b=b, lo=lo, hi=hi, j=j, t0=t0, n=n: e.matmul(
                        ps[b][:, 0:n], lhsT=K[lo:hi, kc, j * 128:(j + 1) * 128], rhs=Q[lo:hi, qc, c * 512 + t0:(c + 1) * 512],
                        start=True, stop=True), r=[qkey, kkey], w=[("ps", b)])
                    ei = erot.next()
                    A("act", lambda e, b=b, ei=ei, n=n: e.activation(out=ebs[ei][:, 0:n], in_=ps[b][:, 0:n], func=AF.Exp, scale=0.125),
                      r=[("ps", b)], w=[("eb", ei)])
                    m = mfn(j, t0, n)
                    if m is not None:
                        pi = prot.next()
                        A(mengs.next(), lambda e, ei=ei, pi=pi, n=n, m=m: e.tensor_tensor(out=pbs[pi][:, 0:n], in0=ebs[ei][:, 0:n], in1=m, op=ALU.mult),
                          r=[("eb", ei), mkey], w=[("pb", pi)])
                        src, skey = pbs[pi], ("pb", pi)
                    else:
                        src, skey = ebs[ei], ("eb", ei)
                    v = vfn(j, e_)
                    A("pe", lambda e, lo=lo, hi=hi, t0=t0, n=n, v=v, src=src, j=j: e.matmul(
                        ps[nb][lo:hi, t0:512], lhsT=v, rhs=src[:, 0:n], start=(j == 0), stop=(j == last), tile_position=(0, lo)),
                      r=[skey, vkey], w=[("ps", nb)])
                    A("pe", lambda e, lo=lo, hi=hi, t0=t0, n=n, src=src, j=j: e.matmul(
                        ps[db][lo:hi, t0:512], lhsT=ones[:, 0:64], rhs=src[:, 0:n], start=(j == 0), stop=(j == last), tile_position=(0, lo)),
                      r=[skey, "mats"], w=[("ps", db)])
            ri = rrot.next()
            A("dve", lambda e, ri=ri: e.reciprocal(out=rds[ri][:], in_=ps[db][:]), r=[("ps", db)], w=[("rd", ri)])
            A("dve", lambda e, ri=ri: e.tensor_tensor(out=dst, in0=ps[nb][:], in1=rds[ri][:], op=ALU.mult),
              r=[("ps", nb), ("rd", ri)], w=[dkey])

        def attn_tiles(f2sb, mengs=("dve", "pool")):
            ebs = [f2sb(f"eb{i}", [128, 512], BF16) for i in range(3)]
            pbs = [f2sb(f"pb{i}", [128, 512], BF16) for i in range(3)]
            rds = [f2sb(f"rd{i}", [128, 512], F32) for i in range(2)]
            return (ebs, pbs, rds, Rot(range(3)), Rot(range(3)), Rot(range(2)), Rot(mengs))

        def apply_wout(w2d, merged):
            with ExitStack() as s2:
                wp = WPool(s2, 2048, 3, 4, "o")
                srot = Rot(range(7))
                descs = [wdesc(w2d, 0, D, d2 * 256, 256) for d2 in range(4)]
                it = stream(wp, descs)
                for d2 in range(4):
                    wt, wk = next(it)
                    for di in range(2):
                        d = 2 * d2 + di
                        for c in range(NTC):
                            b = srot.next()
                            for k in range(NKC):
                                A("pe", lambda e, k=k, b=b, wt=wt, di=di, c=c: e.matmul(
                                    ps[b][:], lhsT=wt[:, k, di * 128:(di + 1) * 128], rhs=merged[:, k, c * 512:(c + 1) * 512],
                                    start=(k == 0), stop=(k == NKC - 1)), r=[wk, ("mg", k, c)], w=[("ps", b)])
                            A("dve", lambda e, b=b, d=d, c=c: e.tensor_tensor(
                                out=h[:, d, c * 512:(c + 1) * 512], in0=ps[b][:], in1=h[:, d, c * 512:(c + 1) * 512], op=ALU.add),
                              r=[("ps", b), ("h", d, c)], w=[("h", d, c)])
            P.barrier()

        def even_mixer(gi):
            with ExitStack() as sm:
                msb = mk_sb(sm)
                merged = msb("merged", [128, NKC, T], BF16)
                A("dve", lambda e: e.tensor_tensor(out=lv[:, 0, :], in0=lv[:, 0, :], in1=lv[:, 1, :], op=ALU.mult), r=["lv"], w=["lv"])
                A("dve", lambda e: e.tensor_tensor(out=lv[:, 2, :], in0=lv[:, 2, :], in1=lv[:, 3, :], op=ALU.mult), r=["lv"], w=["lv"])
                A("dve", lambda e: e.reduce_sum(out=lsm[:, 0:1], in_=lv[:, 0, :], axis=AX.X), r=["lv"], w=["lsm"])
                A("dve", lambda e: e.reduce_sum(out=lsm[:, 1:2], in_=lv[:, 2, :], axis=AX.X), r=["lv"], w=["lsm"])
                A("act", lambda e: e.activation(out=lsm[:, 2:4], in_=lsm[:, 0:2], func=AF.Exp), r=["lsm"], w=["lsm"])
                A("dve", lambda e: e.tensor_tensor(out=lsm[:, 4:5], in0=lsm[:, 3:4], in1=lsm[:, 2:3], op=ALU.subtract), r=["lsm"], w=["lsm"])
                A("dve", lambda e: e.tensor_scalar(out=lsm[:, 5:6], in0=lsm[:, 4:5], scalar1=-0.2, scalar2=None, op0=ALU.add), r=["lsm"], w=["lsm"])
                A("dve", lambda e: e.tensor_scalar(out=lsm[:, 6:7], in0=lsm[:, 7:8], scalar1=0.8, scalar2=None, op0=ALU.mult), r=["lsm", "lsm7"], w=["lsm"])
                neglam = lsm[:, 5:6]
                gsub = lsm[:, 6:7]

                for kk in range(NKC):
                    if (kk < 4 and not cfg["dsa"]) or (kk >= 4 and not cfg["diff"]):
                        A("pool", lambda e, kk=kk: e.memset(merged[:, kk, :], 0.0), w=[("mg", kk, c) for c in range(NTC)])
                if cfg["dsa"]:
                    with ExitStack() as sa:
                        asb = mk_sb(sa)
                        QA = asb("QA", [128, 4, T], BF16)
                        KA = asb("KA", [128, 1, T], BF16)
                        QI = asb("QI", [128, 4, T], BF16)
                        KI = asb("KI", [128, 1, T], BF16)
                        VA = asb("VA", [128, NTB, 64], BF16)
                        WI = asb("WI", [128, NTB, 8], F32)
                        fm = []
                        bufs = [(QA, n, "QA") for n in range(4)] + [(KA, 0, "KA")] + [(QI, n, "QI") for n in range(4)] + [(KI, 0, "KI")]
                        for ci, (buf, n, nm) in enumerate(bufs):
                            fm.append((ew_fm[:, ci * 128:(ci + 1) * 128],
                                       lambda c, buf=buf, n=n, nm=nm: (buf[:, n, c * 512:(c + 1) * 512], (nm, n, c))))
                        tm = [(ew_va, 64, lambda blk: (VA[:, blk, :], ("VA", blk)), None),
                              (ew_wi, 8, lambda blk: (WI[:, blk, :], ("WI", blk)), None)]
                        mixer_proj(gi, fm, tm)
                        with ExitStack() as s2:
                            f2sb = mk_sb(s2)
                            Ib = [f2sb(f"Ib{i}", [128, T], F32) for i in range(2)]
                            rb = [f2sb(f"rb{i}", [128, 512], F32) for i in range(3)]
                            mk = f2sb("mk", [128, T], BF16)
                            maskT = f2sb("maskT", [128, NTB, 512], BF16)
                            t8 = f2sb("t8", [128, 8], F32)
                            tiles = attn_tiles(f2sb, mengs=("dve", "pool"))
                            srot = Rot([0, 1, 2])
                            rrot = Rot(range(3))
                            accrot = Rot([(3, 4), (5, 6)])
                            grp = [0]
                            for c in range(NTC):
                                for ti in range(4):
                                    i = 4 * c + ti
                                    cols = (i + 1) * 128
                                    I = Ib[i % 2]
                                    ik = ("I", i % 2)
                                    for sc in range((cols + 511) // 512):
                                        w_ = min(512, cols - sc * 512)
                                        for hd in range(8):
                                            lo, hi = (hd % 2) * 64, (hd % 2 + 1) * 64
                                            b = srot.next()
                                            A("pe", lambda e, b=b, lo=lo, hi=hi, hd=hd, i=i, sc=sc, w_=w_: e.matmul(
                                                ps[b][:, 0:w_], lhsT=QI[lo:hi, hd // 2, i * 128:(i + 1) * 128],
                                                rhs=KI[lo:hi, 0, sc * 512:sc * 512 + w_], start=True, stop=True),
                                              r=[("QI", hd // 2, c), ("KI", 0, sc)], w=[("ps", b)])
                                            ri = rrot.next()
                                            A("act", lambda e, b=b, ri=ri, w_=w_: e.activation(out=rb[ri][:, 0:w_], in_=ps[b][:, 0:w_], func=AF.Relu),
                                              r=[("ps", b)], w=[("rb", ri)])
                                            if hd == 0:
                                                A("pool", lambda e, ri=ri, I=I, sc=sc, w_=w_, i=i: e.tensor_scalar(
                                                    out=I[:, sc * 512:sc * 512 + w_], in0=rb[ri][:, 0:w_], scalar1=WI[:, i, 0:1], scalar2=None, op0=ALU.mult),
                                                  r=[("rb", ri), ("WI", i)], w=[ik])
                                            else:
                                                A("pool", lambda e, ri=ri, w_=w_, i=i, hd=hd: e.tensor_scalar(
                                                    out=rb[ri][:, 0:w_], in0=rb[ri][:, 0:w_], scalar1=WI[:, i, hd:hd + 1], scalar2=None, op0=ALU.mult),
                                                  r=[("rb", ri), ("WI", i)], w=[("rb", ri)])
                                                A("pool", lambda e, ri=ri, I=I, sc=sc, w_=w_: e.tensor_tensor(
                                                    out=I[:, sc * 512:sc * 512 + w_], in0=I[:, sc * 512:sc * 512 + w_], in1=rb[ri][:, 0:w_], op=ALU.add),
                                                  r=[("rb", ri), ik], w=[ik])
                                    A("pool", lambda e, I=I, i=i: e.tensor_tensor(out=I[:, i * 128:(i + 1) * 128], in0=I[:, i * 128:(i + 1) * 128],
                                                                                  in1=negm[:], op=ALU.add), r=[ik, "negm"], w=[ik])
                                    if i >= 2:
                                        for rnd in range(32):
                                            A("dve", lambda e, I=I, cols=cols: e.max(out=t8[:], in_=I[:, 0:cols]), r=[ik], w=["t8"])
                                            A("dve", lambda e, I=I, cols=cols: e.match_replace(out=I[:, 0:cols], in_to_replace=t8[:],
                                                                                               in_values=I[:, 0:cols], imm_value=NEG2),
                                              r=[ik, "t8"], w=[ik])
                                        A("dve", lambda e, I=I, cols=cols: e.tensor_single_scalar(out=mk[:, 0:cols], in_=I[:, 0:cols],
                                                                                                  scalar=-1.5e30, op=ALU.is_lt), r=[ik], w=["mk"])
                                    else:
                                        A("dve", lambda e, I=I, cols=cols: e.tensor_single_scalar(out=mk[:, 0:cols], in_=I[:, 0:cols],
                                                                                                  scalar=-1.0e29, op=ALU.is_ge), r=[ik], w=["mk"])
                                    for j0 in range(0, i + 1, 8):
                                        nn = min(8, i + 1 - j0)
                                        for jj in range(nn):
                                            A("pe", lambda e, jj=jj, j0=j0: e.transpose(
                                                out=pTp[:, jj * 128:(jj + 1) * 128],
                                                in_=mk[:, (j0 + jj) * 128:(j0 + jj + 1) * 128], identity=ident),
                                              r=["mk", "mats"], w=["pT"])
                                        A("act", lambda e, nn=nn, j0=j0, ti=ti: e.activation(
                                            out=maskT[:, j0:j0 + nn, ti * 128:(ti + 1) * 128],
                                            in_=pTp[:, 0:nn * 128].rearrange("p (a b) -> p a b", a=nn), func=AF.Copy),
                                          r=["pT"], w=["maskT"])
                                for hp in range(4):
                                    attn_pairs_chunk(
                                        c, QA, hp, ("QA", hp, c), KA, 0, ("KA", 0, c),
                                        lambda j, e_: VA[:, j, :], ("VA", 0),
                                        lambda j, t0, n: maskT[:, j, t0:512], "maskT",
                                        merged[:, hp, c * 512:(c + 1) * 512], ("mg", hp, c), accrot.next(), srot, tiles)
                        P.barrier()

                if cfg["diff"]:
                    with ExitStack() as sa:
                        asb = mk_sb(sa)
                        QB = asb("QB", [128, 4, T], BF16)
                        KB = asb("KB", [128, 4, T], BF16)
                        VB = asb("VB", [128, NTB, 512], BF16)
                        fm = []
                        bufs = [(QB, n, "QB") for n in range(4)] + [(KB, n, "KB") for n in range(4)]
                        for ci, (buf, n, nm) in enumerate(bufs):
                            fm.append((ew_fm[:, (10 + ci) * 128:(11 + ci) * 128],
                                       lambda c, buf=buf, n=n, nm=nm: (buf[:, n, c * 512:(c + 1) * 512], (nm, n, c))))
                        tm = [(ew_vb[:, n * 128:(n + 1) * 128], 128,
                               lambda blk, n=n: (VB[:, blk, n * 128:(n + 1) * 128], ("VB", blk, n)), None) for n in range(4)]
                        mixer_proj(gi, fm, tm)
                        with ExitStack() as s2:
                            f2sb = mk_sb(s2)
                            strip = f2sb("cstrip", [128, STRIP_W], BF16)
                            DMA(strip[:], strips_d[0], w=["strip"])
                            ebs = [f2sb(f"eb{i}", [128, 512], BF16) for i in range(3)]
                            pbs = [f2sb(f"pb{i}", [128, 512], BF16) for i in range(3)]
                            fa = [f2sb(f"fa{i}", [128, 512], F32) for i in range(4)]
                            ob = f2sb("ob", [128, 512], F32)
                            sq2 = f2sb("sq2", [128, 512], BF16)
                            erot, prot, mrot = Rot(range(3)), Rot(range(3)), Rot(["dve", "pool"])
                            srot = Rot([0, 1, 2])
                            for hd in range(4):
                                for c in range(NTC):
                                    last = 4 * c + 3
                                    for j in range(last + 1):
                                        t0 = max(0, j * 128 - c * 512)
                                        n = 512 - t0
                                        for e_ in range(2):
                                            lo, hi = e_ * 64, (e_ + 1) * 64
                                            b = srot.next()
                                            A("pe", lambda e, b=b, lo=lo, hi=hi, j=j, t0=t0, n=n, hd=hd, c=c: e.matmul(
                                                ps[b][:, 0:n], lhsT=KB[lo:hi, hd, j * 128:(j + 1) * 128],
                                                rhs=QB[lo:hi, hd, c * 512 + t0:(c + 1) * 512], start=True, stop=True),
                                              r=[("QB", hd, c), ("KB", hd, j // 4)], w=[("ps", b)])
                                            ei = erot.next()
                                            A("act", lambda e, b=b, ei=ei, n=n: e.activation(out=ebs[ei][:, 0:n], in_=ps[b][:, 0:n],
                                                                                             func=AF.Exp, scale=0.125),
                                              r=[("ps", b)], w=[("eb", ei)])
                                            if j >= 4 * c:
                                                off = c * 512 - j * 128 + 384 + t0
                                                pi = prot.next()
                                                A(mrot.next(), lambda e, ei=ei, pi=pi, n=n, off=off: e.tensor_tensor(
                                                    out=pbs[pi][:, 0:n], in0=ebs[ei][:, 0:n], in1=strip[:, off:off + n], op=ALU.mult),
                                                  r=[("eb", ei), "strip"], w=[("pb", pi)])
                                                src, skey = pbs[pi], ("pb", pi)
                                            else:
                                                src, skey = ebs[ei], ("eb", ei)
                                            nb, db = 3 + 2 * e_, 4 + 2 * e_
                                            A("pe", lambda e, nb=nb, t0=t0, n=n, j=j, hd=hd, src=src, last=last: e.matmul(
                                                ps[nb][:, t0:512], lhsT=VB[:, j, hd * 128:(hd + 1) * 128], rhs=src[:, 0:n],
                                                start=(j == 0), stop=(j == last)),
                                              r=[skey] + [("VB", j, hd)], w=[("ps", nb)])
                                            A("pe", lambda e, db=db, t0=t0, n=n, j=j, src=src, last=last: e.matmul(
                                                ps[db][:, t0:512], lhsT=ones, rhs=src[:, 0:n], start=(j == 0), stop=(j == last)),
                                              r=[skey, "mats"], w=[("ps", db)])
                                    A("dve", lambda e: e.reciprocal(out=fa[0][:], in_=ps[4][:]), r=[("ps", 4)], w=[("fa", 0)])
                                    A("dve", lambda e: e.tensor_tensor(out=fa[1][:], in0=ps[3][:], in1=fa[0][:], op=ALU.mult),
                                      r=[("ps", 3), ("fa", 0)], w=[("fa", 1)])
                                    A("dve", lambda e: e.reciprocal(out=fa[2][:], in_=ps[6][:]), r=[("ps", 6)], w=[("fa", 2)])
                                    A("dve", lambda e: e.tensor_tensor(out=fa[3][:], in0=ps[5][:], in1=fa[2][:], op=ALU.mult),
                                      r=[("ps", 5), ("fa", 2)], w=[("fa", 3)])
                                    A("dve", lambda e: e.scalar_tensor_tensor(out=ob[:], in0=fa[3][:], scalar=neglam, in1=fa[1][:],
                                                                              op0=ALU.mult, op1=ALU.add),
                                      r=[("fa", 3), ("fa", 1), "lsm"], w=["ob"])
                                    A("act", lambda e: e.activation(out=sq2[:], in_=ob[:], func=AF.Square, scale=float(128.0 ** -0.5)),
                                      r=["ob"], w=["sq2"])
                                    b = srot.next()
                                    A("pe", lambda e, b=b: e.matmul(ps[b][:], lhsT=ones, rhs=sq2[:], start=True, stop=True),
                                      r=["sq2", "mats"], w=[("ps", b)])
                                    A("act", lambda e, b=b: e.activation(out=fa[0][:], in_=ps[b][:], func=AF.Sqrt, bias=epsb[:, 1:2]),
                                      r=[("ps", b), "epsb"], w=[("fa", 0)])
                                    A("dve", lambda e: e.reciprocal(out=fa[2][:], in_=fa[0][:]), r=[("fa", 0)], w=[("fa", 2)])
                                    A("dve", lambda e, hd=hd, c=c: e.scalar_tensor_tensor(
                                        out=merged[:, 4 + hd, c * 512:(c + 1) * 512], in0=ob[:], scalar=gsub, in1=fa[2][:],
                                        op0=ALU.mult, op1=ALU.mult), r=["ob", ("fa", 2), "lsm"], w=[("mg", 4 + hd, c)])
                        P.barrier()
                apply_wout(even_w_out, merged)

        def odd_mixer(gi):
            with ExitStack() as sm:
                msb = mk_sb(sm)
                merged = msb("mergedo", [128, NKC, T], BF16)
                for g in range(2):
                    odd_group(g, gi, merged)
                apply_wout(odd_w_out, merged)

        def odd_group(g, gi, merged):
            if True:
                if True:
                    with ExitStack() as sa:
                        asb = mk_sb(sa)
                        QG = asb("QG", [128, 4, T], BF16)
                        KG = asb("KG", [128, 4, T], BF16)
                        VG = asb("VG", [128, NTB, 512], BF16)
                        fm = []
                        for n in range(4):
                            fm.append((odd_w_in[:, (4 * g + n) * 128:(4 * g + n + 1) * 128],
                                       lambda c, n=n: (QG[:, n, c * 512:(c + 1) * 512], ("QG", n, c))))
                        for n in range(4):
                            fm.append((odd_w_in[:, 1024 + (4 * g + n) * 128:1024 + (4 * g + n + 1) * 128],
                                       lambda c, n=n: (KG[:, n, c * 512:(c + 1) * 512], ("KG", n, c))))
                        tm = [(odd_w_in[:, 2048 + g * 512 + n * 128:2048 + g * 512 + (n + 1) * 128], 128,
                               lambda blk, n=n: (VG[:, blk, n * 128:(n + 1) * 128], ("VG", blk, n)), None) for n in range(4)]
                        if cfg["proj"]:
                            mixer_proj(gi, fm, tm)
                        if not cfg["attn"]:
                            for hp in range(4):
                                A("pool", lambda e, hp=hp: e.memset(merged[:, 4 * g + hp, :], 0.0), w=[("mg", 4 * g + hp, c) for c in range(NTC)])
                            return
                        with ExitStack() as s2:
                            f2sb = mk_sb(s2)
                            strip = f2sb("mstrip", [128, STRIP_W], BF16)
                            DMA(strip[:], strips_d[1], w=["strip"])
                            tiles = attn_tiles(f2sb, mengs=("dve", "pool"))
                            srot = Rot([0, 1, 2])
                            accrot = Rot([(3, 4), (5, 6)])
                            for hp in range(4):
                                for c in range(NTC):
                                    attn_pairs_chunk(
                                        c, QG, hp, ("QG", hp, c), KG, hp, ("KG", hp, c),
                                        lambda j, e_, hp=hp: VG[:, j, (2 * hp + e_) * 64:(2 * hp + e_ + 1) * 64], ("VG", 0, 0),
                                        lambda j, t0, n, c=c: strip[:, c * 512 - j * 128 + 384 + t0:c * 512 - j * 128 + 384 + t0 + n], "strip",
                                        merged[:, 4 * g + hp, c * 512:(c + 1) * 512], ("mg", 4 * g + hp, c), accrot.next(), srot, tiles)
                        P.barrier()

        for l in range(cfg["layers"]):
            if cfg["ffn"]:
                ffn(l, ffn_w["ffn_a_wg"], ffn_w["ffn_a_wu"], ffn_w["ffn_a_wd"], l * 4 + 0)
            if l % 2 == 0:
                if cfg["mix_even"]:
                    even_mixer(l * 4 + 1)
            else:
                if cfg["mix_odd"]:
                    odd_mixer(l * 4 + 1)
            if cfg["ffn"]:
                ffn(l, ffn_w["ffn_b_wg"], ffn_w["ffn_b_wu"], ffn_w["ffn_b_wd"], l * 4 + 2)
            if cfg["ple"]:
                ple(l, l * 4 + 3)
        final_norm()
        P.emit(st)
    nc._declared_inputs = declared
    return nc


def prep_inputs(inputs, b):
    f = np.float32
    g = lambda k: np.asarray(inputs[k], f)
    rope, strips, mats, negmask = _host_consts()
    gains = []
    for l in range(2):
        for nm in ("norm_ffn_a", "norm_mix", "norm_ffn_b", "norm_ple"):
            gains.append(_col(g(nm)[l]))
    gains.append(_col(g("final_norm")))
    gains = np.ascontiguousarray(np.concatenate(gains, axis=1))
    ewi = g("even_w_in")[0]
    qa, ka, va = ewi[:, 0:512], ewi[:, 512:576], ewi[:, 576:640]
    qi, ki, wi = ewi[:, 640:1152], ewi[:, 1152:1216], ewi[:, 1216:1224]
    qb, kb, vb = ewi[:, 1224:1736], ewi[:, 1736:2248], ewi[:, 2248:2760]
    ew_fm = np.ascontiguousarray(np.concatenate([qa, ka, ka, qi, ki, ki, qb, kb], axis=1))
    m = {
        "xT": np.ascontiguousarray(g("x")[b].T),
        "pT": np.ascontiguousarray(np.transpose(g("p")[:, b], (0, 2, 1))),
        "gains": gains,
        "ple_gate": g("ple_gate"), "ple_proj": g("ple_proj"),
        "ew_fm": ew_fm, "ew_va": np.ascontiguousarray(va), "ew_wi": np.ascontiguousarray(wi),
        "ew_vb": np.ascontiguousarray(vb),
        "even_w_out": g("even_w_out")[0],
        "lamv": np.ascontiguousarray(np.stack([g("diff_lambda_q1")[0], g("diff_lambda_k1")[0],
                                               g("diff_lambda_q2")[0], g("diff_lambda_k2")[0]], 0)),
        "subln": np.ascontiguousarray(g("diff_subln")[0].reshape(128, 1)),
        "odd_w_in": g("odd_w_in")[0], "odd_w_out": g("odd_w_out")[0],
        "rope": rope, "strips": strips, "mats": mats, "negmask": negmask,
    }
    for nm in ("ffn_a_wg", "ffn_a_wu", "ffn_a_wd", "ffn_b_wg", "ffn_b_wu", "ffn_b_wd"):
        m[nm] = g(nm)
    return m


_NC_CACHE = {}


def kernel(**inputs):
    import time as _time
    _t0 = _time.time()
    key = tuple(sorted(CFG.items()))
    if key not in _NC_CACHE:
        _NC_CACHE[key] = build_program(CFG)
    nc = _NC_CACHE[key]
    print(f"[kernel] build {_time.time() - _t0:.1f}s", flush=True)
    n = 8
    shared = prep_inputs(inputs, 0)
    in_maps = []
    for b in range(n):
        m = dict(shared)
        m["xT"] = np.ascontiguousarray(np.asarray(inputs["x"], np.float32)[b].T)
        m["pT"] = np.ascontiguousarray(np.transpose(np.asarray(inputs["p"], np.float32)[:, b], (0, 2, 1)))
        in_maps.append({k: v for k, v in m.items() if k in nc._declared_inputs})
    print(f"[kernel] prep {_time.time() - _t0:.1f}s", flush=True)
    import os as _os
    _nd = int(_os.environ.get("DBG_CORES", "8"))
    if _nd != 8:
        res = run_bass_kernel_spmd(nc, in_maps[:_nd], core_ids=list(range(_nd)))
        res.results.extend([res.results[0]] * (8 - _nd))
    else:
        res = run_bass_kernel_spmd(nc, in_maps, core_ids=list(range(n)))
    print(f"[kernel] ran {_time.time() - _t0:.1f}s", flush=True)
    out = np.stack([np.asarray(res.results[b]["outT"], np.float32).T for b in range(n)], 0)
    return np.ascontiguousarray(out)
```

```python
import numpy as np
import ml_dtypes
import concourse.bass as bass
import concourse.mybir as mybir
from concourse.bass_utils import run_bass_kernel_spmd

F32 = mybir.dt.float32
BF16 = mybir.dt.bfloat16
ALU = mybir.AluOpType
AF = mybir.ActivationFunctionType
AX = mybir.AxisListType

T = 2048
D = 1024
DFF = 2816
NKC = D // 128
NFC = DFF // 128
NTB = T // 128
NTC = T // 512
PLE = 256
EVEN_IN = 2760
NEG = -1.0e30
NEG2 = -2.0e30


class Op:
    __slots__ = ("eng", "fn", "waits", "signal", "sem", "val", "is_dma", "idx", "is_bar")

    def __init__(self, eng, fn, is_dma=False):
        self.eng = eng
        self.fn = fn
        self.waits = []
        self.signal = False
        self.sem = None
        self.val = 0
        self.is_dma = is_dma
        self.idx = -1
        self.is_bar = False


ENGS = ("pe", "act", "dve", "pool", "sp")
SEM_LIM = 30000


class Prog:
    def __init__(self, nc):
        self.nc = nc
        self.ops = {e: [] for e in ENGS}
        self.last_w = {}
        self.readers = {}
        self.seen = {e: {s: -1 for s in ENGS} for e in ENGS}
        self.seen_dma = {e: set() for e in ENGS}
        self.pending_dma = []
        self.dma_list = []
        self.nops = 0

    NDMA = 24

    def add(self, eng, fn, reads=(), writes=(), dma=False):
        op = Op(eng, fn, is_dma=dma)
        deps = []
        if dma:
            k = len(self.dma_list)
            if k >= self.NDMA:
                deps.append((self.dma_list[k - self.NDMA], True))
            self.dma_list.append(op)
        for r in reads:
            w = self.last_w.get(r)
            if w is not None:
                deps.append((w, True))
            if r == "pT" or (isinstance(r, tuple) and r[0] == "ps"):
                for rd in self.readers.get(r, ()):
                    if rd.eng != eng:
                        deps.append((rd, True))
        for r in writes:
            w = self.last_w.get(r)
            if w is not None:
                deps.append((w, True))
            for rd in self.readers.get(r, ()):
                deps.append((rd, False))
        best = {}
        for d, is_raw in deps:
            if d.is_dma:
                self._dep(op, d, is_raw)
                continue
            if d.eng == eng and eng in ("pe", "sp"):
                continue
            cur = best.get(d.eng)
            if cur is None or d.idx > cur.idx:
                best[d.eng] = d
        for d in best.values():
            self._dep(op, d, True)
        op.idx = len(self.ops[eng])
        self.ops[eng].append(op)
        for r in reads:
            self.readers.setdefault(r, []).append(op)
        for r in writes:
            self.last_w[r] = op
            self.readers[r] = []
        if dma:
            self.pending_dma.append(op)
        self.nops += 1
        return op

    def _dep(self, op, d, is_raw):
        e = op.eng
        if d.is_dma:
            if d in self.seen_dma[e]:
                return
            self.seen_dma[e].add(d)
            d.signal = True
            op.waits.append(d)
            return
        if d.eng == e:
            if e == "pe" or e == "sp":
                return
        if self.seen[e][d.eng] >= d.idx:
            return
        self.seen[e][d.eng] = d.idx
        d.signal = True
        op.waits.append(d)

    def barrier(self):
        bar = Op("sp", None)
        bar.is_bar = True
        for e in ENGS:
            if e == "sp":
                continue
            if self.ops[e]:
                self._dep(bar, self.ops[e][-1], True)
        for d in self.pending_dma:
            self._dep(bar, d, True)
        self.pending_dma = []
        bar.idx = len(self.ops["sp"])
        self.ops["sp"].append(bar)
        bar.signal = True
        for e in ENGS:
            if e == "sp":
                continue
            w = Op(e, None)
            w.is_bar = True
            w.waits.append(bar)
            self.seen[e]["sp"] = bar.idx
            w.idx = len(self.ops[e])
            self.ops[e].append(w)
        self.last_w = {}
        self.readers = {}

    def emit(self, stack):
        nc = self.nc
        ndma = self.NDMA
        dsems = [stack.enter_context(nc.semaphore(f"s_dma_{i}")) for i in range(ndma)]
        dcnt = [0] * ndma
        for k, op in enumerate(self.dma_list):
            di = k % ndma
            if dcnt[di] + 16 > SEM_LIM:
                dsems[di] = stack.enter_context(nc.semaphore(f"s_dma_{di}_{k}"))
                dcnt[di] = 0
            dcnt[di] += 16
            op.sem = dsems[di]
            op.val = dcnt[di]
        eng_sems = {}
        for e in ENGS:
            cur = None
            cnt = 0
            for op in self.ops[e]:
                if not op.signal or op.is_dma:
                    continue
                if cur is None or cnt >= SEM_LIM:
                    cur = stack.enter_context(nc.semaphore(f"s_{e}_{len(eng_sems)}"))
                    eng_sems[(e, len(eng_sems))] = cur
                    cnt = 0
                cnt += 1
                op.sem = cur
                op.val = cnt
        block = stack.enter_context(nc.Block())

        def run(e):
            def body(eng):
                for op in self.ops[e]:
                    for d in op.waits:
                        eng.wait_ge(d.sem, d.val)
                    if op.fn is None:
                        if op.signal:
                            eng.sem_inc(op.sem, 1)
                        continue
                    ins = op.fn(eng)
                    if op.is_dma:
                        ins.then_inc(op.sem, 16)
                    elif op.signal:
                        ins.then_inc(op.sem, 1)
            return body

        block.tensor(run("pe"))
        block.scalar(run("act"))
        block.vector(run("dve"))
        block.gpsimd(run("pool"))
        block.sync(run("sp"))


STRIP_W = 2432


def _host_consts():
    f32 = np.float32
    inv = (1.0 / (f32(10000.0) ** (np.arange(0, 64, 2, dtype=f32) / f32(64)))).astype(f32)
    ang = (np.arange(T, dtype=f32)[:, None] * inv[None, :]).astype(f32)
    cos = np.cos(ang).astype(f32)
    sin = np.sin(ang).astype(f32)
    p = np.arange(128)
    d = p % 64
    j = d % 32
    C2 = cos[:, j].T.copy()
    S2 = (sin[:, j].T * np.where(d < 32, -1.0, 1.0)[:, None]).astype(f32)
    rope = np.ascontiguousarray(np.stack([C2, S2], 0)).astype(f32)
    x = np.arange(STRIP_W)[None, :] - 384 - p[:, None]
    causal = (x >= 0).astype(f32)
    mult = (((x >= 0) & (x <= 128)).astype(f32)
            + ((x >= 0) & (x % 4 == 0) & (x <= 512)).astype(f32)
            + ((x >= 0) & (x % 16 == 0) & (x <= 2048)).astype(f32))
    strips = np.stack([causal, mult], 0).astype(ml_dtypes.bfloat16)
    ident = np.eye(128, dtype=f32)
    perm = np.zeros((128, 128), f32)
    for m in range(128):
        perm[m ^ 32, m] = 1.0
    ones = np.ones((128, 128), f32)
    mats = np.stack([ident, perm, ones], 0).astype(ml_dtypes.bfloat16)
    tt = np.arange(128)
    negmask = np.where(tt[None, :] > tt[:, None], np.float32(NEG), np.float32(0.0)).astype(f32)
    return rope, strips, mats, negmask


def _col(v):
    v = np.asarray(v, np.float32)
    return np.ascontiguousarray(v.reshape(-1, 128).T)


CFG = {"layers": 2, "mix_even": True, "mix_odd": True, "dsa": True, "diff": True, "ffn": True, "ple": True, "proj": True, "attn": True}


class Rot:
    def __init__(self, items):
        self.items = list(items)
        self.i = 0

    def next(self):
        v = self.items[self.i % len(self.items)]
        self.i += 1
        return v


def build_program(cfg=None):
    from contextlib import ExitStack
    cfg = dict(CFG if cfg is None else cfg)
    nc = bass.Bass("TRN2", target_bir_lowering=False)

    declared = []

    def din(name, shape, dt=F32, need=True):
        if not need:
            return None
        declared.append(name)
        return nc.dram_tensor(name, list(shape), dt, kind="ExternalInput").ap()

    xT = din("xT", [D, T])
    pTd = din("pT", [2, PLE, T])
    gains_d = din("gains", [128, 72])
    ffn_w = {}
    for nm in ("ffn_a_wg", "ffn_a_wu", "ffn_b_wg", "ffn_b_wu"):
        ffn_w[nm] = din(nm, [2, D, DFF], need=cfg["ffn"])
    for nm in ("ffn_a_wd", "ffn_b_wd"):
        ffn_w[nm] = din(nm, [2, DFF, D], need=cfg["ffn"])
    ple_gate = din("ple_gate", [2, D, D], need=cfg["ple"])
    ple_proj = din("ple_proj", [2, PLE, D], need=cfg["ple"])
    ew_fm = din("ew_fm", [D, 2304])
    ew_va = din("ew_va", [D, 64])
    ew_wi = din("ew_wi", [D, 8])
    ew_vb = din("ew_vb", [D, 512])
    even_w_out = din("even_w_out", [D, D])
    lamv = din("lamv", [4, 64])
    subln_d = din("subln", [128, 1])
    odd_w_in = din("odd_w_in", [D, 3072])
    odd_w_out = din("odd_w_out", [D, D])
    rope_d = din("rope", [2, 128, T])
    strips_d = din("strips", [2, 128, STRIP_W], BF16)
    mats_d = din("mats", [3, 128, 128], BF16)
    negm_d = din("negmask", [128, 128])
    outT = nc.dram_tensor("outT", [D, T], F32, kind="ExternalOutput").ap()

    uid = [0]

    with ExitStack() as st:
        def mk_sb(stack):
            def f(name, shape, dt):
                uid[0] += 1
                return stack.enter_context(nc.sbuf_tensor(f"{name}_{uid[0]}", list(shape), dt))
            return f

        sb = mk_sb(st)
        h = sb("h", [128, NKC, T], F32)
        gsb = sb("gsb", [128, 72], F32)
        mats = sb("mats", [128, 3, 128], BF16)
        negm = sb("negm", [128, 128], F32)
        epsb = sb("epsb", [128, 2], F32)
        lv = sb("lv", [128, 4, 64], F32)
        lsm = sb("lsm", [128, 8], F32)
        sqb = [sb(f"sqb{i}", [128, 512], BF16) for i in range(2)]
        sdt = sb("sdt", [128, 512], F32)
        rstd = sb("rstd", [128, 512], F32)
        ps = [st.enter_context(nc.psum_tensor(f"ps{i}", [128, 512], F32)) for i in range(7)]
        pTp = st.enter_context(nc.psum_tensor("pTp", [128, 1024], BF16))
        ident = mats[:, 0, :]
        perm = mats[:, 1, :]
        ones = mats[:, 2, :]

        P = Prog(nc)

        def A(eng, fn, r=(), w=()):
            return P.add(eng, fn, reads=r, writes=w)

        def DMA(out, in_, r=(), w=()):
            return P.add("sp", lambda e: e.dma_start(out=out, in_=in_), reads=r, writes=w, dma=True)

        for k in range(NKC):
            DMA(h[:, k, :], xT[k * 128:(k + 1) * 128, :], w=[("h", k, c) for c in range(NTC)])
        DMA(gsb[:], gains_d, w=["gsb"])
        DMA(mats[:], mats_d.rearrange("a p n -> p a n"), w=["mats"])
        DMA(negm[:], negm_d, w=["negm"])
        DMA(lv[:], lamv.partition_broadcast(128), w=["lv"])
        DMA(lsm[:, 7:8], subln_d, w=["lsm7"])
        A("pool", lambda e: e.memset(epsb[:, 0:1], 1e-6), w=["epsb"])
        A("pool", lambda e: e.memset(epsb[:, 1:2], 1e-5), w=["epsb"])

        sqrot = Rot([0, 1])
        cvt_rot = Rot(["act", "dve"])

        class WPool:
            def __init__(self, stack, elems, nst, nbf, tag):
                f = mk_sb(stack)
                self.stg = [f(f"wst{tag}{i}", [128, elems], F32) for i in range(nst)]
                self.bf = [f(f"wbf{tag}{i}", [128, elems], BF16) for i in range(nbf)]
                self.i = 0
                uid[0] += 1
                self.tag = f"{tag}{uid[0]}"

            def load(self, desc):
                src, a, b = desc
                i = self.i
                self.i += 1
                si, bi = i % len(self.stg), i % len(self.bf)
                sv = self.stg[si][:, 0:a * b].rearrange("p (a b) -> p a b", a=a)
                bv = self.bf[bi][:, 0:a * b].rearrange("p (a b) -> p a b", a=a)
                sk, bk = (self.tag, "st", si), (self.tag, "bf", bi)
                DMA(sv, src, w=[sk])
                self._cvt(bv, sv, sk, bk)
                return bv, bk

            def _cvt(self, dst, src, sk, dk):
                eng = cvt_rot.next()
                if eng == "act":
                    A("act", lambda e: e.activation(out=dst, in_=src, func=AF.Copy), r=[sk], w=[dk])
                else:
                    A(eng, lambda e: e.tensor_copy(out=dst, in_=src), r=[sk], w=[dk])

            def load_into(self, desc, dst, dkey):
                src, a, b = desc
                i = self.i
                self.i += 1
                si = i % len(self.stg)
                sv = self.stg[si][:, 0:a * b].rearrange("p (a b) -> p a b", a=a)
                sk = (self.tag, "st", si)
                DMA(sv, src, w=[sk])
                self._cvt(dst, sv, sk, dkey)

        def stream(wp, descs, depth=2):
            q = []
            n = len(descs)
            nxt = 0
            for i in range(n):
                while nxt < n and nxt <= i + depth:
                    q.append(wp.load(descs[nxt]))
                    nxt += 1
                yield q[i]

        def wdesc(w2d, r0, nr, c0, ncol):
            return (w2d[r0:r0 + nr, c0:c0 + ncol].rearrange("(a p) n -> p a n", p=128), nr // 128, ncol)

        def norm_chunk(gi, c, dst_fn, dkey_fn, srot, after=None, engs=("dve",)):
            b = srot.next()
            for k in range(NKC):
                si = sqrot.next()
                A("act", lambda e, k=k, si=si: e.activation(out=sqb[si][:], in_=h[:, k, c * 512:(c + 1) * 512],
                                                            func=AF.Square, scale=1.0 / 32.0),
                  r=[("h", k, c)], w=[("sqb", si)])
                A("pe", lambda e, k=k, si=si: e.matmul(ps[b][:], lhsT=ones, rhs=sqb[si][:], start=(k == 0), stop=(k == NKC - 1)),
                  r=[("sqb", si), "mats"], w=[("ps", b)])
            A("act", lambda e: e.activation(out=sdt[:], in_=ps[b][:], func=AF.Sqrt, bias=epsb[:, 0:1]),
              r=[("ps", b), "epsb"], w=["sdt"])
            A("dve", lambda e: e.reciprocal(out=rstd[:], in_=sdt[:]), r=["sdt"], w=["rstd"])
            for k in range(NKC):
                eng = engs[k % len(engs)]
                dst = dst_fn(k)
                A(eng, lambda e, k=k, dst=dst: e.scalar_tensor_tensor(out=dst, in0=h[:, k, c * 512:(c + 1) * 512],
                                                                      scalar=gsb[:, gi * 8 + k:gi * 8 + k + 1], in1=rstd[:],
                                                                      op0=ALU.mult, op1=ALU.mult),
                  r=[("h", k, c), "gsb", "rstd"], w=[dkey_fn(k)])
                if after is not None:
                    after(k)

        def ffn(l, wg, wu, wd, gi):
            for tp in range(2):
                ffn_pass(l, wg, wu, wd, gi, tp)
                P.barrier()

        def ffn_pass(l, wg, wu, wd, gi, tp):
            if True:
                with ExitStack() as s2:
                    f2sb = mk_sb(s2)
                    hn = f2sb("hn", [128, NKC, 1024], BF16)
                    act = f2sb("act", [128, NFC, 1024], BF16)
                    sg = [f2sb(f"sg{i}", [128, 512], F32) for i in range(2)]
                    wp = WPool(s2, 2816, 3, 4, "f")
                    srot = Rot(range(7))
                    sgrot = Rot([0, 1])
                    for sub in range(2):
                        norm_chunk(gi, tp * 2 + sub, lambda k, sub=sub: hn[:, k, sub * 512:(sub + 1) * 512],
                                   lambda k, sub=sub: ("hn", k, sub), srot)
                    descs = []
                    for f2 in range(NFC // 2):
                        descs.append(wdesc(wg[l], 0, D, f2 * 256, 256))
                        descs.append(wdesc(wu[l], 0, D, f2 * 256, 256))
                    it = stream(wp, descs)
                    for f2 in range(NFC // 2):
                        gt, gk = next(it)
                        ut, uk = next(it)
                        for fi in range(2):
                            f = 2 * f2 + fi
                            for sub in range(2):
                                bg = srot.next()
                                bu = srot.next()
                                for k in range(NKC):
                                    A("pe", lambda e, k=k, bg=bg, gt=gt, fi=fi, sub=sub: e.matmul(
                                        ps[bg][:], lhsT=gt[:, k, fi * 128:(fi + 1) * 128], rhs=hn[:, k, sub * 512:(sub + 1) * 512],
                                        start=(k == 0), stop=(k == NKC - 1)), r=[gk, ("hn", k, sub)], w=[("ps", bg)])
                                for k in range(NKC):
                                    A("pe", lambda e, k=k, bu=bu, ut=ut, fi=fi, sub=sub: e.matmul(
                                        ps[bu][:], lhsT=ut[:, k, fi * 128:(fi + 1) * 128], rhs=hn[:, k, sub * 512:(sub + 1) * 512],
                                        start=(k == 0), stop=(k == NKC - 1)), r=[uk, ("hn", k, sub)], w=[("ps", bu)])
                                si = sgrot.next()
                                A("act", lambda e, si=si, bg=bg: e.activation(out=sg[si][:], in_=ps[bg][:], func=AF.Silu),
                                  r=[("ps", bg)], w=[("sg", si)])
                                A("dve", lambda e, si=si, bu=bu, f=f, sub=sub: e.tensor_tensor(
                                    out=act[:, f, sub * 512:(sub + 1) * 512], in0=sg[si][:], in1=ps[bu][:], op=ALU.mult),
                                  r=[("sg", si), ("ps", bu)], w=[("act", f, sub)])
                    descs = []
                    for d2 in range(4):
                        for half in range(2):
                            descs.append(wdesc(wd[l], half * 1408, 1408, d2 * 256, 256))
                    it = stream(wp, descs)
                    for d2 in range(4):
                        tl0, tk0 = next(it)
                        tl1, tk1 = next(it)
                        for di in range(2):
                            d = 2 * d2 + di
                            for sub in range(2):
                                b = srot.next()
                                c = tp * 2 + sub
                                for fc in range(NFC):
                                    tl, tk = (tl0, tk0) if fc < 11 else (tl1, tk1)
                                    A("pe", lambda e, fc=fc, tl=tl, di=di, sub=sub, b=b: e.matmul(
                                        ps[b][:], lhsT=tl[:, fc % 11, di * 128:(di + 1) * 128], rhs=act[:, fc, sub * 512:(sub + 1) * 512],
                                        start=(fc == 0), stop=(fc == NFC - 1)), r=[tk, ("act", fc, sub)], w=[("ps", b)])
                                A("dve", lambda e, d=d, c=c, b=b: e.scalar_tensor_tensor(
                                    out=h[:, d, c * 512:(c + 1) * 512], in0=ps[b][:], scalar=0.5, in1=h[:, d, c * 512:(c + 1) * 512],
                                    op0=ALU.mult, op1=ALU.add), r=[("ps", b), ("h", d, c)], w=[("h", d, c)])

        def ple(l, gi):
            with ExitStack() as s2:
                f2sb = mk_sb(s2)
                hn = f2sb("hnp", [128, NKC, T], BF16)
                ptb = f2sb("ptb", [128, 2, T], BF16)
                sig = [f2sb(f"sig{i}", [128, 512], F32) for i in range(2)]
                tmp = [f2sb(f"ptmp{i}", [128, 512], F32) for i in range(2)]
                wp = WPool(s2, 2048, 3, 4, "p")
                srot = Rot(range(7))
                r2 = Rot([0, 1])
                for c in range(NTC):
                    norm_chunk(gi, c, lambda k, c=c: hn[:, k, c * 512:(c + 1) * 512], lambda k, c=c: ("hnp", k, c), srot)
                    wp.load_into((pTd[l][:, c * 512:(c + 1) * 512].rearrange("(a p) t -> p a t", p=128), 2, 512),
                                 ptb[:, :, c * 512:(c + 1) * 512], ("ptb", c))
                descs = []
                for d2 in range(4):
                    descs.append(wdesc(ple_gate[l], 0, D, d2 * 256, 256))
                    descs.append(wdesc(ple_proj[l], 0, PLE, d2 * 256, 256))
                it = stream(wp, descs)
                for d2 in range(4):
                    gt, gk = next(it)
                    pt, pk = next(it)
                    for di in range(2):
                        d = 2 * d2 + di
                        for c in range(NTC):
                            bg = srot.next()
                            bp = srot.next()
                            for k in range(NKC):
                                A("pe", lambda e, k=k, bg=bg, gt=gt, di=di, c=c: e.matmul(
                                    ps[bg][:], lhsT=gt[:, k, di * 128:(di + 1) * 128], rhs=hn[:, k, c * 512:(c + 1) * 512],
                                    start=(k == 0), stop=(k == NKC - 1)), r=[gk, ("hnp", k, c)], w=[("ps", bg)])
                            for k in range(2):
                                A("pe", lambda e, k=k, bp=bp, pt=pt, di=di, c=c: e.matmul(
                                    ps[bp][:], lhsT=pt[:, k, di * 128:(di + 1) * 128], rhs=ptb[:, k, c * 512:(c + 1) * 512],
                                    start=(k == 0), stop=(k == 1)), r=[pk, ("ptb", c)], w=[("ps", bp)])
                            si = r2.next()
                            A("act", lambda e, si=si, bg=bg: e.activation(out=sig[si][:], in_=ps[bg][:], func=AF.Sigmoid),
                              r=[("ps", bg)], w=[("sig", si)])
                            A("dve", lambda e, si=si, bp=bp: e.tensor_tensor(out=tmp[si][:], in0=sig[si][:], in1=ps[bp][:], op=ALU.mult),
                              r=[("sig", si), ("ps", bp)], w=[("ptmp", si)])
                            A("pool", lambda e, si=si, d=d, c=c: e.tensor_tensor(
                                out=h[:, d, c * 512:(c + 1) * 512], in0=h[:, d, c * 512:(c + 1) * 512], in1=tmp[si][:], op=ALU.add),
                              r=[("ptmp", si), ("h", d, c)], w=[("h", d, c)])
            P.barrier()

        def final_norm():
            with ExitStack() as s2:
                f2sb = mk_sb(s2)
                ot = [f2sb(f"ot{i}", [128, 512], F32) for i in range(4)]
                srot = Rot(range(7))
                for c in range(NTC):
                    def after(k, c=c):
                        i = (c * 8 + k) % 4
                        DMA(outT[k * 128:(k + 1) * 128, c * 512:(c + 1) * 512], ot[i][:], r=[("ot", i)])
                    norm_chunk(8, c, lambda k, c=c: ot[(c * 8 + k) % 4][:], lambda k, c=c: ("ot", (c * 8 + k) % 4), srot,
                               after=after, engs=("dve",))
            P.barrier()

        def mixer_proj(gi, fm_items, tm_items):
            import os as _os
            _nfm = int(_os.environ.get("PROJ_FM", "99"))
            _ntm = int(_os.environ.get("PROJ_TM", "99"))
            fm_items = fm_items[:_nfm]
            tm_items = tm_items[:_ntm]
            with ExitStack() as s2:
                f2sb = mk_sb(s2)
                hnc = f2sb("hnc", [128, NKC, 512], BF16)
                ropc = [f2sb(f"ropc{i}", [128, 2, 512], F32) for i in range(2)]
                xb = [f2sb(f"xb{i}", [128, 512], BF16) for i in range(2)]
                t1 = [f2sb(f"t1{i}", [128, 512], F32) for i in range(2)]
                t2 = [f2sb(f"t2{i}", [128, 512], F32) for i in range(2)]
                wp = WPool(s2, 1024, 2, 3, "m")
                srot = Rot(range(7))
                r2 = Rot([0, 1])
                for c in range(NTC):
                    norm_chunk(gi, c, lambda k: hnc[:, k, :], lambda k: ("hnc", k), srot)
                    rc = c % 2
                    DMA(ropc[rc][:], rope_d[:, :, c * 512:(c + 1) * 512].rearrange("a p t -> p a t"), w=[("ropc", rc)])
                    descs = [(w_ap.rearrange("(a p) n -> p a n", p=128), 8, 128) for (w_ap, _) in fm_items]
                    descs += [(w_ap.rearrange("(a p) n -> p a n", p=128), 8, n) for (w_ap, n, _, _) in tm_items]
                    it = stream(wp, descs)
                    for (_, dst_fn) in fm_items:
                        wt, wk = next(it)
                        b = srot.next()
                        for k in range(NKC):
                            A("pe", lambda e, k=k, b=b, wt=wt: e.matmul(ps[b][:], lhsT=wt[:, k, :], rhs=hnc[:, k, :],
                                                                        start=(k == 0), stop=(k == NKC - 1)),
                              r=[wk, ("hnc", k)], w=[("ps", b)])
                        i = r2.next()
                        A("act", lambda e, i=i, b=b: e.activation(out=xb[i][:], in_=ps[b][:], func=AF.Copy),
                          r=[("ps", b)], w=[("xb", i)])
                        A("dve", lambda e, i=i, b=b, rc=rc: e.tensor_tensor(out=t1[i][:], in0=ps[b][:], in1=ropc[rc][:, 0, :], op=ALU.mult),
                          r=[("ps", b), ("ropc", rc), ("xb", i)], w=[("t1", i)])
                        b2 = srot.next()
                        A("pe", lambda e, i=i, b2=b2: e.matmul(ps[b2][:], lhsT=perm, rhs=xb[i][:], start=True, stop=True),
                          r=[("xb", i), "mats"], w=[("ps", b2)])
                        A("dve", lambda e, i=i, b2=b2, rc=rc: e.tensor_tensor(out=t2[i][:], in0=ps[b2][:], in1=ropc[rc][:, 1, :], op=ALU.mult),
                          r=[("ps", b2), ("ropc", rc)], w=[("t2", i)])
                        dst, dkey = dst_fn(c)
                        A("pool", lambda e, i=i, dst=dst: e.tensor_tensor(out=dst, in0=t1[i][:], in1=t2[i][:], op=ALU.add),
                          r=[("t1", i), ("t2", i)], w=[dkey])
                    for (_, n, dst_fn, _) in tm_items:
                        wt, wk = next(it)
                        for tb in range(4):
                            b = srot.next()
                            for k in range(NKC):
                                A("pe", lambda e, k=k, b=b, wt=wt, tb=tb, n=n: e.matmul(
                                    ps[b][:, 0:n], lhsT=hnc[:, k, tb * 128:(tb + 1) * 128], rhs=wt[:, k, :],
                                    start=(k == 0), stop=(k == NKC - 1)), r=[wk, ("hnc", k)], w=[("ps", b)])
                            dst, dkey = dst_fn(c * 4 + tb)
                            A("act", lambda e, b=b, dst=dst, n=n: e.activation(out=dst, in_=ps[b][:, 0:n], func=AF.Copy),
                              r=[("ps", b)], w=[dkey])
            P.barrier()

        def attn_pairs_chunk(c, Q, qc, qkey, K, kc, kkey, vfn, vkey, mfn, mkey, dst, dkey, accs, srot, tiles):
            ebs, pbs, rds, erot, prot, rrot, mengs = tiles
            nb, db = accs
            last = 4 * c + 3
            for j in range(last + 1):
                t0 = max(0, j * 128 - c * 512)
                n = 512 - t0
                for e_ in range(2):
                    lo, hi = e_ * 64, (e_ + 1) * 64
                    b = srot.next()
                    A("pe", lambda e, b=b, lo=lo, hi=hi, j=j, t0=t0, n=n: e.matmul(
                        ps[b][:, 0:n], lhsT=K[lo:hi, kc, j * 128:(j + 1) * 128], rhs=Q[lo:hi, qc, c * 512 + t0:(c + 1) * 512],
                        start=True, stop=True), r=[qkey, kkey], w=[("ps", b)])
                    ei = erot.next()
                    A("act", lambda e, b=b, ei=ei, n=n: e.activation(out=ebs[ei][:, 0:n], in_=ps[b][:, 0:n], func=AF.Exp, scale=0.125),
                      r=[("ps", b)], w=[("eb", ei)])
                    m = mfn(j, t0, n)
                    if m is not None:
                        pi = prot.next()
                        A(mengs.next(), lambda e, ei=ei, pi=pi, n=n, m=m: e.tensor_tensor(out=pbs[pi][:, 0:n], in0=ebs[ei][:, 0:n], in1=m, op=ALU.mult),
                          r=[("eb", ei), mkey], w=[("pb", pi)])
                        src, skey = pbs[pi], ("pb", pi)
                    else:
                        src, skey = ebs[ei], ("eb", ei)
                    v = vfn(j, e_)
                    A("pe", lambda e, lo=lo, hi=hi, t0=t0, n=n, v=v, src=src, j=j: e.matmul(
                        ps[nb][lo:hi, t0:512], lhsT=v, rhs=src[:, 0:n], start=(j == 0), stop=(j == last), tile_position=(0, lo)),
                      r=[skey, vkey], w=[("ps", nb)])
                    A("pe", lambda e, lo=lo, hi=hi, t0=t0, n=n, src=src, j=j: e.matmul(
                        ps[db][lo:hi, t0:512], lhsT=ones[:, 0:64], rhs=src[:, 0:n], start=(j == 0), stop=(j == last), tile_position=(0, lo)),
                      r=[skey, "mats"], w=[("ps", db)])
            ri = rrot.next()
            A("dve", lambda e, ri=ri: e.reciprocal(out=rds[ri][:], in_=ps[db][:]), r=[("ps", db)], w=[("rd", ri)])
            A("dve", lambda e, ri=ri: e.tensor_tensor(out=dst, in0=ps[nb][:], in1=rds[ri][:], op=ALU.mult),
              r=[("ps", nb), ("rd", ri)], w=[dkey])

        def attn_tiles(f2sb, mengs=("dve", "pool")):
            ebs = [f2sb(f"eb{i}", [128, 512], BF16) for i in range(3)]
            pbs = [f2sb(f"pb{i}", [128, 512], BF16) for i in range(3)]
            rds = [f2sb(f"rd{i}", [128, 512], F32) for i in range(2)]
            return (ebs, pbs, rds, Rot(range(3)), Rot(range(3)), Rot(range(2)), Rot(mengs))

        def apply_wout(w2d, merged):
            with ExitStack() as s2:
                wp = WPool(s2, 2048, 3, 4, "o")
                srot = Rot(range(7))
                descs = [wdesc(w2d, 0, D, d2 * 256, 256) for d2 in range(4)]
                it = stream(wp, descs)
                for d2 in range(4):
                    wt, wk = next(it)
                    for di in range(2):
                        d = 2 * d2 + di
                        for c in range(NTC):
                            b = srot.next()
                            for k in range(NKC):
                                A("pe", lambda e, k=k, b=b, wt=wt, di=di, c=c: e.matmul(
                                    ps[b][:], lhsT=wt[:, k, di * 128:(di + 1) * 128], rhs=merged[:, k, c * 512:(c + 1) * 512],
                                    start=(k == 0), stop=(k == NKC - 1)), r=[wk, ("mg", k, c)], w=[("ps", b)])
                            A("dve", lambda e, b=b, d=d, c=c: e.tensor_tensor(
                                out=h[:, d, c * 512:(c + 1) * 512], in0=ps[b][:], in1=h[:, d, c * 512:(c + 1) * 512], op=ALU.add),
                              r=[("ps", b), ("h", d, c)], w=[("h", d, c)])
            P.barrier()

        def even_mixer(gi):
            with ExitStack() as sm:
                msb = mk_sb(sm)
                merged = msb("merged", [128, NKC, T], BF16)
                A("dve", lambda e: e.tensor_tensor(out=lv[:, 0, :], in0=lv[:, 0, :], in1=lv[:, 1, :], op=ALU.mult), r=["lv"], w=["lv"])
                A("dve", lambda e: e.tensor_tensor(out=lv[:, 2, :], in0=lv[:, 2, :], in1=lv[:, 3, :], op=ALU.mult), r=["lv"], w=["lv"])
                A("dve", lambda e: e.reduce_sum(out=lsm[:, 0:1], in_=lv[:, 0, :], axis=AX.X), r=["lv"], w=["lsm"])
                A("dve", lambda e: e.reduce_sum(out=lsm[:, 1:2], in_=lv[:, 2, :], axis=AX.X), r=["lv"], w=["lsm"])
                A("act", lambda e: e.activation(out=lsm[:, 2:4], in_=lsm[:, 0:2], func=AF.Exp), r=["lsm"], w=["lsm"])
                A("dve", lambda e: e.tensor_tensor(out=lsm[:, 4:5], in0=lsm[:, 3:4], in1=lsm[:, 2:3], op=ALU.subtract), r=["lsm"], w=["lsm"])
                A("dve", lambda e: e.tensor_scalar(out=lsm[:, 5:6], in0=lsm[:, 4:5], scalar1=-0.2, scalar2=None, op0=ALU.add), r=["lsm"], w=["lsm"])
                A("dve", lambda e: e.tensor_scalar(out=lsm[:, 6:7], in0=lsm[:, 7:8], scalar1=0.8, scalar2=None, op0=ALU.mult), r=["lsm", "lsm7"], w=["lsm"])
                neglam = lsm[:, 5:6]
                gsub = lsm[:, 6:7]

                for kk in range(NKC):
                    if (kk < 4 and not cfg["dsa"]) or (kk >= 4 and not cfg["diff"]):
                        A("pool", lambda e, kk=kk: e.memset(merged[:, kk, :], 0.0), w=[("mg", kk, c) for c in range(NTC)])
                if cfg["dsa"]:
                    with ExitStack() as sa:
                        asb = mk_sb(sa)
                        QA = asb("QA", [128, 4, T], BF16)
                        KA = asb("KA", [128, 1, T], BF16)
                        QI = asb("QI", [128, 4, T], BF16)
                        KI = asb("KI", [128, 1, T], BF16)
                        VA = asb("VA", [128, NTB, 64], BF16)
                        WI = asb("WI", [128, NTB, 8], F32)
                        fm = []
                        bufs = [(QA, n, "QA") for n in range(4)] + [(KA, 0, "KA")] + [(QI, n, "QI") for n in range(4)] + [(KI, 0, "KI")]
                        for ci, (buf, n, nm) in enumerate(bufs):
                            fm.append((ew_fm[:, ci * 128:(ci + 1) * 128],
                                       lambda c, buf=buf, n=n, nm=nm: (buf[:, n, c * 512:(c + 1) * 512], (nm, n, c))))
                        tm = [(ew_va, 64, lambda blk: (VA[:, blk, :], ("VA", blk)), None),
                              (ew_wi, 8, lambda blk: (WI[:, blk, :], ("WI", blk)), None)]
                        mixer_proj(gi, fm, tm)
                        with ExitStack() as s2:
                            f2sb = mk_sb(s2)
                            Ib = [f2sb(f"Ib{i}", [128, T], F32) for i in range(2)]
                            rb = [f2sb(f"rb{i}", [128, 512], F32) for i in range(3)]
                            mk = f2sb("mk", [128, T], BF16)
                            junk = f2sb("junk", [128, T], BF16)
                            maskT = f2sb("maskT", [128, NTB, 512], BF16)
                            bnd = [f2sb(f"bnd{i}", [128, 8], F32) for i in range(2)]
                            gem = [f2sb(f"gem{i}", [128, 2], mybir.dt.uint32) for i in range(2)]
                            tiles = attn_tiles(f2sb, mengs=("dve",))
                            srot = Rot([0, 1, 2])
                            rrot = Rot(range(3))
                            accrot = Rot([(3, 4), (5, 6)])
                            NIT = 22

                            def s1(i):
                                c = i // 4
                                cols = (i + 1) * 128
                                I = Ib[i % 2]
                                ik = ("I", i % 2)
                                bd = bnd[i % 2]
                                bk = ("bnd", i % 2)
                                for sc in range((cols + 511) // 512):
                                    w_ = min(512, cols - sc * 512)
                                    for hd in range(8):
                                        lo, hi = (hd % 2) * 64, (hd % 2 + 1) * 64
                                        b = srot.next()
                                        A("pe", lambda e, b=b, lo=lo, hi=hi, hd=hd, sc=sc, w_=w_: e.matmul(
                                            ps[b][:, 0:w_], lhsT=QI[lo:hi, hd // 2, i * 128:(i + 1) * 128],
                                            rhs=KI[lo:hi, 0, sc * 512:sc * 512 + w_], start=True, stop=True),
                                          r=[("QI", hd // 2, c), ("KI", 0, sc)], w=[("ps", b)])
                                        ri = rrot.next()
                                        A("act", lambda e, b=b, ri=ri, w_=w_: e.activation(out=rb[ri][:, 0:w_], in_=ps[b][:, 0:w_], func=AF.Relu),
                                          r=[("ps", b)], w=[("rb", ri)])
                                        if hd == 0:
                                            A("dve", lambda e, ri=ri, sc=sc, w_=w_: e.tensor_scalar(
                                                out=I[:, sc * 512:sc * 512 + w_], in0=rb[ri][:, 0:w_], scalar1=WI[:, i, 0:1], scalar2=None, op0=ALU.mult),
                                              r=[("rb", ri), ("WI", i)], w=[ik])
                                        else:
                                            A("dve", lambda e, ri=ri, sc=sc, w_=w_, hd=hd: e.scalar_tensor_tensor(
                                                out=I[:, sc * 512:sc * 512 + w_], in0=rb[ri][:, 0:w_], scalar=WI[:, i, hd:hd + 1],
                                                in1=I[:, sc * 512:sc * 512 + w_], op0=ALU.mult, op1=ALU.add),
                                              r=[("rb", ri), ("WI", i), ik], w=[ik])
                                if i >= 2:
                                    A("dve", lambda e: e.tensor_reduce(out=bd[:, 0:1], in_=I[:, 0:cols], axis=AX.X, op=ALU.min), r=[ik], w=[bk])
                                    A("dve", lambda e: e.tensor_reduce(out=bd[:, 1:2], in_=I[:, 0:cols], axis=AX.X, op=ALU.max), r=[ik], w=[bk])
                                    A("dve", lambda e: e.tensor_tensor(out=bd[:, 2:3], in0=bd[:, 1:2], in1=bd[:, 0:1], op=ALU.subtract), r=[bk], w=[bk])
                                A("pool", lambda e: e.tensor_tensor(out=I[:, i * 128:(i + 1) * 128], in0=I[:, i * 128:(i + 1) * 128],
                                                                    in1=negm[:], op=ALU.add), r=[ik, "negm"], w=[ik])

                            def s2(i):
                                ti = i % 4
                                cols = (i + 1) * 128
                                I = Ib[i % 2]
                                ik = ("I", i % 2)
                                bd = bnd[i % 2]
                                bk = ("bnd", i % 2)
                                gm = gem[i % 2]
                                gk = ("gem", i % 2)
                                if i >= 2:
                                    for it in range(NIT):
                                        A("dve", lambda e, it=it: e.scalar_tensor_tensor(out=bd[:, 3:4], in0=bd[:, 2:3], scalar=float(2.0 ** -(it + 1)),
                                                                                        in1=bd[:, 0:1], op0=ALU.mult, op1=ALU.add), r=[bk], w=[bk])
                                        A("dve", lambda e: e.tensor_scalar(out=junk[:, 0:cols], in0=I[:, 0:cols], scalar1=bd[:, 3:4], scalar2=None,
                                                                           op0=ALU.is_ge, op1=ALU.add, accum_out=bd[:, 4:5]), r=[ik, bk], w=[bk, "junk"])
                                        A("dve", lambda e: e.tensor_single_scalar(out=gm[:, 0:1], in_=bd[:, 4:5], scalar=255.5, op=ALU.is_ge), r=[bk], w=[gk])
                                        A("dve", lambda e: e.copy_predicated(out=bd[:, 0:1], mask=gm[:, 0:1], data=bd[:, 3:4]), r=[bk, gk], w=[bk])
                                    A("dve", lambda e: e.tensor_scalar(out=mk[:, 0:cols], in0=I[:, 0:cols], scalar1=bd[:, 0:1], scalar2=None, op0=ALU.is_ge),
                                      r=[ik, bk], w=["mk"])
                                else:
                                    A("dve", lambda e: e.tensor_single_scalar(out=mk[:, 0:cols], in_=I[:, 0:cols], scalar=-1.0e29, op=ALU.is_ge), r=[ik], w=["mk"])
                                for j0 in range(0, i + 1, 8):
                                    nn = min(8, i + 1 - j0)
                                    for jj in range(nn):
                                        A("pe", lambda e, jj=jj, j0=j0: e.transpose(
                                            out=pTp[:, jj * 128:(jj + 1) * 128],
                                            in_=mk[:, (j0 + jj) * 128:(j0 + jj + 1) * 128], identity=ident),
                                          r=["mk", "mats"], w=["pT"])
                                    A("act", lambda e, nn=nn, j0=j0: e.activation(
                                        out=maskT[:, j0:j0 + nn, ti * 128:(ti + 1) * 128],
                                        in_=pTp[:, 0:nn * 128].rearrange("p (a b) -> p a b", a=nn), func=AF.Copy),
                                      r=["pT"], w=["maskT"])

                            s1(0)
                            for i in range(NTB):
                                if i + 1 < NTB:
                                    s1(i + 1)
                                s2(i)
                                if i % 4 == 3:
                                    c = i // 4
                                    for hp in range(4):
                                        attn_pairs_chunk(
                                            c, QA, hp, ("QA", hp, c), KA, 0, ("KA", 0, c),
                                            lambda j, e_: VA[:, j, :], ("VA", 0),
                                            lambda j, t0, n: maskT[:, j, t0:512], "maskT",
                                            merged[:, hp, c * 512:(c + 1) * 512], ("mg", hp, c), accrot.next(), srot, tiles)
                        P.barrier()

                if cfg["diff"]:
                    with ExitStack() as sa:
                        asb = mk_sb(sa)
                        QB = asb("QB", [128, 4, T], BF16)
                        KB = asb("KB", [128, 4, T], BF16)
                        VB = asb("VB", [128, NTB, 512], BF16)
                        fm = []
                        bufs = [(QB, n, "QB") for n in range(4)] + [(KB, n, "KB") for n in range(4)]
                        for ci, (buf, n, nm) in enumerate(bufs):
                            fm.append((ew_fm[:, (10 + ci) * 128:(11 + ci) * 128],
                                       lambda c, buf=buf, n=n, nm=nm: (buf[:, n, c * 512:(c + 1) * 512], (nm, n, c))))
                        tm = [(ew_vb[:, n * 128:(n + 1) * 128], 128,
                               lambda blk, n=n: (VB[:, blk, n * 128:(n + 1) * 128], ("VB", blk, n)), None) for n in range(4)]
                        mixer_proj(gi, fm, tm)
                        with ExitStack() as s2:
                            f2sb = mk_sb(s2)
                            strip = f2sb("cstrip", [128, STRIP_W], BF16)
                            DMA(strip[:], strips_d[0], w=["strip"])
                            ebs = [f2sb(f"eb{i}", [128, 512], BF16) for i in range(3)]
                            pbs = [f2sb(f"pb{i}", [128, 512], BF16) for i in range(3)]
                            fa = [f2sb(f"fa{i}", [128, 512], F32) for i in range(4)]
                            ob = f2sb("ob", [128, 512], F32)
                            sq2 = f2sb("sq2", [128, 512], BF16)
                            erot, prot, mrot = Rot(range(3)), Rot(range(3)), Rot(["dve"])
                            srot = Rot([0, 1, 2])
                            for hd in range(4):
                                for c in range(NTC):
                                    last = 4 * c + 3
                                    for j in range(last + 1):
                                        t0 = max(0, j * 128 - c * 512)
                                        n = 512 - t0
                                        for e_ in range(2):
                                            lo, hi = e_ * 64, (e_ + 1) * 64
                                            b = srot.next()
                                            A("pe", lambda e, b=b, lo=lo, hi=hi, j=j, t0=t0, n=n, hd=hd, c=c: e.matmul(
                                                ps[b][:, 0:n], lhsT=KB[lo:hi, hd, j * 128:(j + 1) * 128],
                                                rhs=QB[lo:hi, hd, c * 512 + t0:(c + 1) * 512], start=True, stop=True),
                                              r=[("QB", hd, c), ("KB", hd, j // 4)], w=[("ps", b)])
                                            ei = erot.next()
                                            A("act", lambda e, b=b, ei=ei, n=n: e.activation(out=ebs[ei][:, 0:n], in_=ps[b][:, 0:n],
                                                                                             func=AF.Exp, scale=0.125),
                                              r=[("ps", b)], w=[("eb", ei)])
                                            if j >= 4 * c:
                                                off = c * 512 - j * 128 + 384 + t0
                                                pi = prot.next()
                                                A(mrot.next(), lambda e, ei=ei, pi=pi, n=n, off=off: e.tensor_tensor(
                                                    out=pbs[pi][:, 0:n], in0=ebs[ei][:, 0:n], in1=strip[:, off:off + n], op=ALU.mult),
                                                  r=[("eb", ei), "strip"], w=[("pb", pi)])
                                                src, skey = pbs[pi], ("pb", pi)
                                            else:
                                                src, skey = ebs[ei], ("eb", ei)
                                            nb, db = 3 + 2 * e_, 4 + 2 * e_
                                            A("pe", lambda e, nb=nb, t0=t0, n=n, j=j, hd=hd, src=src, last=last: e.matmul(
                                                ps[nb][:, t0:512], lhsT=VB[:, j, hd * 128:(hd + 1) * 128], rhs=src[:, 0:n],
                                                start=(j == 0), stop=(j == last)),
                                              r=[skey] + [("VB", j, hd)], w=[("ps", nb)])
                                            A("pe", lambda e, db=db, t0=t0, n=n, j=j, src=src, last=last: e.matmul(
                                                ps[db][:, t0:512], lhsT=ones, rhs=src[:, 0:n], start=(j == 0), stop=(j == last)),
                                              r=[skey, "mats"], w=[("ps", db)])
                                    A("dve", lambda e: e.reciprocal(out=fa[0][:], in_=ps[4][:]), r=[("ps", 4)], w=[("fa", 0)])
                                    A("dve", lambda e: e.tensor_tensor(out=fa[1][:], in0=ps[3][:], in1=fa[0][:], op=ALU.mult),
                                      r=[("ps", 3), ("fa", 0)], w=[("fa", 1)])
                                    A("dve", lambda e: e.reciprocal(out=fa[2][:], in_=ps[6][:]), r=[("ps", 6)], w=[("fa", 2)])
                                    A("dve", lambda e: e.tensor_tensor(out=fa[3][:], in0=ps[5][:], in1=fa[2][:], op=ALU.mult),
                                      r=[("ps", 5), ("fa", 2)], w=[("fa", 3)])
                                    A("dve", lambda e: e.scalar_tensor_tensor(out=ob[:], in0=fa[3][:], scalar=neglam, in1=fa[1][:],
                                                                              op0=ALU.mult, op1=ALU.add),
                                      r=[("fa", 3), ("fa", 1), "lsm"], w=["ob"])
                                    A("act", lambda e: e.activation(out=sq2[:], in_=ob[:], func=AF.Square, scale=float(128.0 ** -0.5)),
                                      r=["ob"], w=["sq2"])
                                    b = srot.next()
                                    A("pe", lambda e, b=b: e.matmul(ps[b][:], lhsT=ones, rhs=sq2[:], start=True, stop=True),
                                      r=["sq2", "mats"], w=[("ps", b)])
                                    A("act", lambda e, b=b: e.activation(out=fa[0][:], in_=ps[b][:], func=AF.Sqrt, bias=epsb[:, 1:2]),
                                      r=[("ps", b), "epsb"], w=[("fa", 0)])
                                    A("dve", lambda e: e.reciprocal(out=fa[2][:], in_=fa[0][:]), r=[("fa", 0)], w=[("fa", 2)])
                                    A("dve", lambda e, hd=hd, c=c: e.scalar_tensor_tensor(
                                        out=merged[:, 4 + hd, c * 512:(c + 1) * 512], in0=ob[:], scalar=gsub, in1=fa[2][:],
                                        op0=ALU.mult, op1=ALU.mult), r=["ob", ("fa", 2), "lsm"], w=[("mg", 4 + hd, c)])
                        P.barrier()
                apply_wout(even_w_out, merged)

        def odd_mixer(gi):
            with ExitStack() as sm:
                msb = mk_sb(sm)
                merged = msb("mergedo", [128, NKC, T], BF16)
                for g in range(2):
                    odd_group(g, gi, merged)
                apply_wout(odd_w_out, merged)

        def odd_group(g, gi, merged):
            if True:
                if True:
                    with ExitStack() as sa:
                        asb = mk_sb(sa)
                        QG = asb("QG", [128, 4, T], BF16)
                        KG = asb("KG", [128, 4, T], BF16)
                        VG = asb("VG", [128, NTB, 512], BF16)
                        fm = []
                        for n in range(4):
                            fm.append((odd_w_in[:, (4 * g + n) * 128:(4 * g + n + 1) * 128],
                                       lambda c, n=n: (QG[:, n, c * 512:(c + 1) * 512], ("QG", n, c))))
                        for n in range(4):
                            fm.append((odd_w_in[:, 1024 + (4 * g + n) * 128:1024 + (4 * g + n + 1) * 128],
                                       lambda c, n=n: (KG[:, n, c * 512:(c + 1) * 512], ("KG", n, c))))
                        tm = [(odd_w_in[:, 2048 + g * 512 + n * 128:2048 + g * 512 + (n + 1) * 128], 128,
                               lambda blk, n=n: (VG[:, blk, n * 128:(n + 1) * 128], ("VG", blk, n)), None) for n in range(4)]
                        if cfg["proj"]:
                            mixer_proj(gi, fm, tm)
                        if not cfg["attn"]:
                            for hp in range(4):
                                A("pool", lambda e, hp=hp: e.memset(merged[:, 4 * g + hp, :], 0.0), w=[("mg", 4 * g + hp, c) for c in range(NTC)])
                            return
                        with ExitStack() as s2:
                            f2sb = mk_sb(s2)
                            strip = f2sb("mstrip", [128, STRIP_W], BF16)
                            DMA(strip[:], strips_d[1], w=["strip"])
                            tiles = attn_tiles(f2sb, mengs=("dve",))
                            srot = Rot([0, 1, 2])
                            accrot = Rot([(3, 4), (5, 6)])
                            for hp in range(4):
                                for c in range(NTC):
                                    attn_pairs_chunk(
                                        c, QG, hp, ("QG", hp, c), KG, hp, ("KG", hp, c),
                                        lambda j, e_, hp=hp: VG[:, j, (2 * hp + e_) * 64:(2 * hp + e_ + 1) * 64], ("VG", 0, 0),
                                        lambda j, t0, n, c=c: strip[:, c * 512 - j * 128 + 384 + t0:c * 512 - j * 128 + 384 + t0 + n], "strip",
                                        merged[:, 4 * g + hp, c * 512:(c + 1) * 512], ("mg", 4 * g + hp, c), accrot.next(), srot, tiles)
                        P.barrier()

        for l in range(cfg["layers"]):
            if cfg["ffn"]:
                ffn(l, ffn_w["ffn_a_wg"], ffn_w["ffn_a_wu"], ffn_w["ffn_a_wd"], l * 4 + 0)
            if l % 2 == 0:
                if cfg["mix_even"]:
                    even_mixer(l * 4 + 1)
            else:
                if cfg["mix_odd"]:
                    odd_mixer(l * 4 + 1)
            if cfg["ffn"]:
                ffn(l, ffn_w["ffn_b_wg"], ffn_w["ffn_b_wu"], ffn_w["ffn_b_wd"], l * 4 + 2)
            if cfg["ple"]:
                ple(l, l * 4 + 3)
        final_norm()
        P.emit(st)
    nc._declared_inputs = declared
    return nc


def prep_inputs(inputs, b):
    f = np.float32
    g = lambda k: np.asarray(inputs[k], f)
    rope, strips, mats, negmask = _host_consts()
    gains = []
    for l in range(2):
        for nm in ("norm_ffn_a", "norm_mix", "norm_ffn_b", "norm_ple"):
            gains.append(_col(g(nm)[l]))
    gains.append(_col(g("final_norm")))
    gains = np.ascontiguousarray(np.concatenate(gains, axis=1))
    ewi = g("even_w_in")[0]
    qa, ka, va = ewi[:, 0:512], ewi[:, 512:576], ewi[:, 576:640]
    qi, ki, wi = ewi[:, 640:1152], ewi[:, 1152:1216], ewi[:, 1216:1224]
    qb, kb, vb = ewi[:, 1224:1736], ewi[:, 1736:2248], ewi[:, 2248:2760]
    ew_fm = np.ascontiguousarray(np.concatenate([qa, ka, ka, qi, ki, ki, qb, kb], axis=1))
    m = {
        "xT": np.ascontiguousarray(g("x")[b].T),
        "pT": np.ascontiguousarray(np.transpose(g("p")[:, b], (0, 2, 1))),
        "gains": gains,
        "ple_gate": g("ple_gate"), "ple_proj": g("ple_proj"),
        "ew_fm": ew_fm, "ew_va": np.ascontiguousarray(va), "ew_wi": np.ascontiguousarray(wi),
        "ew_vb": np.ascontiguousarray(vb),
        "even_w_out": g("even_w_out")[0],
        "lamv": np.ascontiguousarray(np.stack([g("diff_lambda_q1")[0], g("diff_lambda_k1")[0],
                                               g("diff_lambda_q2")[0], g("diff_lambda_k2")[0]], 0)),
        "subln": np.ascontiguousarray(g("diff_subln")[0].reshape(128, 1)),
        "odd_w_in": g("odd_w_in")[0], "odd_w_out": g("odd_w_out")[0],
        "rope": rope, "strips": strips, "mats": mats, "negmask": negmask,
    }
    for nm in ("ffn_a_wg", "ffn_a_wu", "ffn_a_wd", "ffn_b_wg", "ffn_b_wu", "ffn_b_wd"):
        m[nm] = g(nm)
    return m


_NC_CACHE = {}


def kernel(**inputs):
    import time as _time
    _t0 = _time.time()
    key = tuple(sorted(CFG.items()))
    if key not in _NC_CACHE:
        _NC_CACHE[key] = build_program(CFG)
    nc = _NC_CACHE[key]
    print(f"[kernel] build {_time.time() - _t0:.1f}s", flush=True)
    n = 8
    shared = prep_inputs(inputs, 0)
    in_maps = []
    for b in range(n):
        m = dict(shared)
        m["xT"] = np.ascontiguousarray(np.asarray(inputs["x"], np.float32)[b].T)
        m["pT"] = np.ascontiguousarray(np.transpose(np.asarray(inputs["p"], np.float32)[:, b], (0, 2, 1)))
        in_maps.append({k: v for k, v in m.items() if k in nc._declared_inputs})
    print(f"[kernel] prep {_time.time() - _t0:.1f}s", flush=True)
    import os as _os
    _nd = int(_os.environ.get("DBG_CORES", "8"))
    if _nd != 8:
        res = run_bass_kernel_spmd(nc, in_maps[:_nd], core_ids=list(range(_nd)))
        res.results.extend([res.results[0]] * (8 - _nd))
    else:
        res = run_bass_kernel_spmd(nc, in_maps, core_ids=list(range(n)))
    print(f"[kernel] ran {_time.time() - _t0:.1f}s", flush=True)
    out = np.stack([np.asarray(res.results[b]["outT"], np.float32).T for b in range(n)], 0)
    return np.ascontiguousarray(out)
```

```python
import numpy as np
import ml_dtypes
import concourse.bass as bass
import concourse.mybir as mybir
from concourse.bass_utils import run_bass_kernel_spmd

F32 = mybir.dt.float32
BF16 = mybir.dt.bfloat16
ALU = mybir.AluOpType
AF = mybir.ActivationFunctionType
AX = mybir.AxisListType

T = 2048
D = 1024
DFF = 2816
NKC = D // 128
NFC = DFF // 128
NTB = T // 128
NTC = T // 512
PLE = 256
EVEN_IN = 2760
NEG = -1.0e30
NEG2 = -2.0e30


class Op:
    __slots__ = ("eng", "fn", "waits", "signal", "sem", "val", "is_dma", "idx", "is_bar")

    def __init__(self, eng, fn, is_dma=False):
        self.eng = eng
        self.fn = fn
        self.waits = []
        self.signal = False
        self.sem = None
        self.val = 0
        self.is_dma = is_dma
        self.idx = -1
        self.is_bar = False


ENGS = ("pe", "act", "dve", "pool", "sp")
SEM_LIM = 30000


class Prog:
    def __init__(self, nc):
        self.nc = nc
        self.ops = {e: [] for e in ENGS}
        self.last_w = {}
        self.readers = {}
        self.seen = {e: {s: -1 for s in ENGS} for e in ENGS}
        self.seen_dma = {e: set() for e in ENGS}
        self.pending_dma = []
        self.dma_list = []
        self.nops = 0

    NDMA = 24

    def add(self, eng, fn, reads=(), writes=(), dma=False):
        op = Op(eng, fn, is_dma=dma)
        deps = []
        if dma:
            k = len(self.dma_list)
            if k >= self.NDMA:
                deps.append((self.dma_list[k - self.NDMA], True))
            self.dma_list.append(op)
        for r in reads:
            w = self.last_w.get(r)
            if w is not None:
                deps.append((w, True))
            if r == "pT" or (isinstance(r, tuple) and r[0] == "ps"):
                for rd in self.readers.get(r, ()):
                    if rd.eng != eng:
                        deps.append((rd, True))
        for r in writes:
            w = self.last_w.get(r)
            if w is not None:
                deps.append((w, True))
            for rd in self.readers.get(r, ()):
                deps.append((rd, False))
        best = {}
        for d, is_raw in deps:
            if d.is_dma:
                self._dep(op, d, is_raw)
                continue
            if d.eng == eng and eng in ("pe", "sp"):
                continue
            cur = best.get(d.eng)
            if cur is None or d.idx > cur.idx:
                best[d.eng] = d
        for d in best.values():
            self._dep(op, d, True)
        op.idx = len(self.ops[eng])
        self.ops[eng].append(op)
        for r in reads:
            self.readers.setdefault(r, []).append(op)
        for r in writes:
            self.last_w[r] = op
            self.readers[r] = []
        if dma:
            self.pending_dma.append(op)
        self.nops += 1
        return op

    def _dep(self, op, d, is_raw):
        e = op.eng
        if d.is_dma:
            if d in self.seen_dma[e]:
                return
            self.seen_dma[e].add(d)
            d.signal = True
            op.waits.append(d)
            return
        if d.eng == e:
            if e == "pe" or e == "sp":
                return
        if self.seen[e][d.eng] >= d.idx:
            return
        self.seen[e][d.eng] = d.idx
        d.signal = True
        op.waits.append(d)

    def barrier(self):
        bar = Op("sp", None)
        bar.is_bar = True
        for e in ENGS:
            if e == "sp":
                continue
            if self.ops[e]:
                self._dep(bar, self.ops[e][-1], True)
        for d in self.pending_dma:
            self._dep(bar, d, True)
        self.pending_dma = []
        bar.idx = len(self.ops["sp"])
        self.ops["sp"].append(bar)
        bar.signal = True
        for e in ENGS:
            if e == "sp":
                continue
            w = Op(e, None)
            w.is_bar = True
            w.waits.append(bar)
            self.seen[e]["sp"] = bar.idx
            w.idx = len(self.ops[e])
            self.ops[e].append(w)
        self.last_w = {}
        self.readers = {}

    def emit(self, stack):
        nc = self.nc
        ndma = self.NDMA
        dsems = [stack.enter_context(nc.semaphore(f"s_dma_{i}")) for i in range(ndma)]
        dcnt = [0] * ndma
        for k, op in enumerate(self.dma_list):
            di = k % ndma
            if dcnt[di] + 16 > SEM_LIM:
                dsems[di] = stack.enter_context(nc.semaphore(f"s_dma_{di}_{k}"))
                dcnt[di] = 0
            dcnt[di] += 16
            op.sem = dsems[di]
            op.val = dcnt[di]
        eng_sems = {}
        for e in ENGS:
            cur = None
            cnt = 0
            for op in self.ops[e]:
                if not op.signal or op.is_dma:
                    continue
                if cur is None or cnt >= SEM_LIM:
                    cur = stack.enter_context(nc.semaphore(f"s_{e}_{len(eng_sems)}"))
                    eng_sems[(e, len(eng_sems))] = cur
                    cnt = 0
                cnt += 1
                op.sem = cur
                op.val = cnt
        block = stack.enter_context(nc.Block())

        def run(e):
            def body(eng):
                for op in self.ops[e]:
                    for d in op.waits:
                        eng.wait_ge(d.sem, d.val)
                    if op.fn is None:
                        if op.signal:
                            eng.sem_inc(op.sem, 1)
                        continue
                    ins = op.fn(eng)
                    if op.is_dma:
                        ins.then_inc(op.sem, 16)
                    elif op.signal:
                        ins.then_inc(op.sem, 1)
            return body

        block.tensor(run("pe"))
        block.scalar(run("act"))
        block.vector(run("dve"))
        block.gpsimd(run("pool"))
        block.sync(run("sp"))


STRIP_W = 2432


def _host_consts():
    f32 = np.float32
    inv = (1.0 / (f32(10000.0) ** (np.arange(0, 64, 2, dtype=f32) / f32(64)))).astype(f32)
    ang = (np.arange(T, dtype=f32)[:, None] * inv[None, :]).astype(f32)
    cos = np.cos(ang).astype(f32)
    sin = np.sin(ang).astype(f32)
    p = np.arange(128)
    d = p % 64
    j = d % 32
    C2 = cos[:, j].T.copy()
    S2 = (sin[:, j].T * np.where(d < 32, -1.0, 1.0)[:, None]).astype(f32)
    rope = np.ascontiguousarray(np.stack([C2, S2], 0)).astype(f32)
    x = np.arange(STRIP_W)[None, :] - 384 - p[:, None]
    causal = (x >= 0).astype(f32)
    mult = (((x >= 0) & (x <= 128)).astype(f32)
            + ((x >= 0) & (x % 4 == 0) & (x <= 512)).astype(f32)
            + ((x >= 0) & (x % 16 == 0) & (x <= 2048)).astype(f32))
    strips = np.stack([causal, mult], 0).astype(ml_dtypes.bfloat16)
    ident = np.eye(128, dtype=f32)
    perm = np.zeros((128, 128), f32)
    for m in range(128):
        perm[m ^ 32, m] = 1.0
    ones = np.ones((128, 128), f32)
    mats = np.stack([ident, perm, ones], 0).astype(ml_dtypes.bfloat16)
    tt = np.arange(128)
    negmask = np.where(tt[None, :] > tt[:, None], np.float32(NEG), np.float32(0.0)).astype(f32)
    return rope, strips, mats, negmask


def _col(v):
    v = np.asarray(v, np.float32)
    return np.ascontiguousarray(v.reshape(-1, 128).T)


CFG = {"layers": 2, "mix_even": True, "mix_odd": True, "dsa": True, "diff": True, "ffn": True, "ple": True, "proj": True, "attn": True}


class Rot:
    def __init__(self, items):
        self.items = list(items)
        self.i = 0

    def next(self):
        v = self.items[self.i % len(self.items)]
        self.i += 1
        return v


def build_program(cfg=None):
    from contextlib import ExitStack
    cfg = dict(CFG if cfg is None else cfg)
    nc = bass.Bass("TRN2", target_bir_lowering=False)

    declared = []

    def din(name, shape, dt=F32, need=True):
        if not need:
            return None
        declared.append(name)
        return nc.dram_tensor(name, list(shape), dt, kind="ExternalInput").ap()

    xT = din("xT", [D, T])
    pTd = din("pT", [2, PLE, T])
    gains_d = din("gains", [128, 72])
    ffn_w = {}
    for nm in ("ffn_a_wg", "ffn_a_wu", "ffn_b_wg", "ffn_b_wu"):
        ffn_w[nm] = din(nm, [2, 11, 128, 8, 256], need=cfg["ffn"])
    for nm in ("ffn_a_wd", "ffn_b_wd"):
        ffn_w[nm] = din(nm, [2, 4, 2, 128, 11, 256], need=cfg["ffn"])
    ple_gate = din("ple_gate", [2, 4, 128, 8, 256], need=cfg["ple"])
    ple_proj = din("ple_proj", [2, 4, 128, 2, 256], need=cfg["ple"])
    ew_fm = din("ew_fm", [18, 128, 8, 128])
    ew_va = din("ew_va", [1, 128, 8, 64])
    ew_wi = din("ew_wi", [1, 128, 8, 8])
    ew_vb = din("ew_vb", [4, 128, 8, 128])
    even_w_out = din("even_w_out", [4, 128, 8, 256])
    lamv = din("lamv", [4, 64])
    subln_d = din("subln", [128, 1])
    odd_w_in = din("odd_w_in", [24, 128, 8, 128])
    odd_w_out = din("odd_w_out", [4, 128, 8, 256])
    rope_d = din("rope", [2, 128, T])
    strips_d = din("strips", [2, 128, STRIP_W], BF16)
    mats_d = din("mats", [3, 128, 128], BF16)
    negm_d = din("negmask", [128, 128])
    outT = nc.dram_tensor("outT", [D, T], F32, kind="ExternalOutput").ap()

    uid = [0]

    with ExitStack() as st:
        def mk_sb(stack):
            def f(name, shape, dt):
                uid[0] += 1
                return stack.enter_context(nc.sbuf_tensor(f"{name}_{uid[0]}", list(shape), dt))
            return f

        sb = mk_sb(st)
        h = sb("h", [128, NKC, T], F32)
        gsb = sb("gsb", [128, 72], F32)
        mats = sb("mats", [128, 3, 128], BF16)
        negm = sb("negm", [128, 128], F32)
        epsb = sb("epsb", [128, 2], F32)
        lv = sb("lv", [128, 4, 64], F32)
        lsm = sb("lsm", [128, 8], F32)
        sqb = [sb(f"sqb{i}", [128, 512], BF16) for i in range(2)]
        sdt = sb("sdt", [128, 512], F32)
        rstd = sb("rstd", [128, 512], F32)
        ps = [st.enter_context(nc.psum_tensor(f"ps{i}", [128, 512], F32)) for i in range(7)]
        pTp = st.enter_context(nc.psum_tensor("pTp", [128, 1024], BF16))
        ident = mats[:, 0, :]
        perm = mats[:, 1, :]
        ones = mats[:, 2, :]

        P = Prog(nc)

        def A(eng, fn, r=(), w=()):
            return P.add(eng, fn, reads=r, writes=w)

        def DMA(out, in_, r=(), w=()):
            return P.add("sp", lambda e: e.dma_start(out=out, in_=in_), reads=r, writes=w, dma=True)

        for k in range(NKC):
            DMA(h[:, k, :], xT[k * 128:(k + 1) * 128, :], w=[("h", k, c) for c in range(NTC)])
        DMA(gsb[:], gains_d, w=["gsb"])
        DMA(mats[:], mats_d.rearrange("a p n -> p a n"), w=["mats"])
        DMA(negm[:], negm_d, w=["negm"])
        DMA(lv[:], lamv.partition_broadcast(128), w=["lv"])
        DMA(lsm[:, 7:8], subln_d, w=["lsm7"])
        A("pool", lambda e: e.memset(epsb[:, 0:1], 1e-6), w=["epsb"])
        A("pool", lambda e: e.memset(epsb[:, 1:2], 1e-5), w=["epsb"])

        sqrot = Rot([0, 1])
        cvt_rot = Rot(["act", "dve"])

        class WPool:
            def __init__(self, stack, elems, nst, nbf, tag):
                f = mk_sb(stack)
                self.stg = [f(f"wst{tag}{i}", [128, elems], F32) for i in range(nst)]
                self.bf = [f(f"wbf{tag}{i}", [128, elems], BF16) for i in range(nbf)]
                self.i = 0
                uid[0] += 1
                self.tag = f"{tag}{uid[0]}"

            def load(self, desc):
                src, a, b = desc
                i = self.i
                self.i += 1
                si, bi = i % len(self.stg), i % len(self.bf)
                sv = self.stg[si][:, 0:a * b].rearrange("p (a b) -> p a b", a=a)
                bv = self.bf[bi][:, 0:a * b].rearrange("p (a b) -> p a b", a=a)
                sk, bk = (self.tag, "st", si), (self.tag, "bf", bi)
                DMA(sv, src, w=[sk])
                self._cvt(bv, sv, sk, bk)
                return bv, bk

            def _cvt(self, dst, src, sk, dk):
                eng = cvt_rot.next()
                if eng == "act":
                    A("act", lambda e: e.activation(out=dst, in_=src, func=AF.Copy), r=[sk], w=[dk])
                else:
                    A(eng, lambda e: e.tensor_copy(out=dst, in_=src), r=[sk], w=[dk])

            def load_into(self, desc, dst, dkey):
                src, a, b = desc
                i = self.i
                self.i += 1
                si = i % len(self.stg)
                sv = self.stg[si][:, 0:a * b].rearrange("p (a b) -> p a b", a=a)
                sk = (self.tag, "st", si)
                DMA(sv, src, w=[sk])
                self._cvt(dst, sv, sk, dkey)

        def stream(wp, descs, depth=2):
            q = []
            n = len(descs)
            nxt = 0
            for i in range(n):
                while nxt < n and nxt <= i + depth:
                    q.append(wp.load(descs[nxt]))
                    nxt += 1
                yield q[i]

        def wdesc(w2d, r0, nr, c0, ncol):
            return (w2d[r0:r0 + nr, c0:c0 + ncol].rearrange("(a p) n -> p a n", p=128), nr // 128, ncol)

        def norm_chunk(gi, c, dst_fn, dkey_fn, srot, after=None, engs=("dve",)):
            b = srot.next()
            for k in range(NKC):
                si = sqrot.next()
                A("act", lambda e, k=k, si=si: e.activation(out=sqb[si][:], in_=h[:, k, c * 512:(c + 1) * 512],
                                                            func=AF.Square, scale=1.0 / 32.0),
                  r=[("h", k, c)], w=[("sqb", si)])
                A("pe", lambda e, k=k, si=si: e.matmul(ps[b][:], lhsT=ones, rhs=sqb[si][:], start=(k == 0), stop=(k == NKC - 1)),
                  r=[("sqb", si), "mats"], w=[("ps", b)])
            A("act", lambda e: e.activation(out=sdt[:], in_=ps[b][:], func=AF.Sqrt, bias=epsb[:, 0:1]),
              r=[("ps", b), "epsb"], w=["sdt"])
            A("dve", lambda e: e.reciprocal(out=rstd[:], in_=sdt[:]), r=["sdt"], w=["rstd"])
            for k in range(NKC):
                eng = engs[k % len(engs)]
                dst = dst_fn(k)
                A(eng, lambda e, k=k, dst=dst: e.scalar_tensor_tensor(out=dst, in0=h[:, k, c * 512:(c + 1) * 512],
                                                                      scalar=gsb[:, gi * 8 + k:gi * 8 + k + 1], in1=rstd[:],
                                                                      op0=ALU.mult, op1=ALU.mult),
                  r=[("h", k, c), "gsb", "rstd"], w=[dkey_fn(k)])
                if after is not None:
                    after(k)

        def ffn(l, wg, wu, wd, gi):
            for tp in range(2):
                ffn_pass(l, wg, wu, wd, gi, tp)
                P.barrier()

        def ffn_pass(l, wg, wu, wd, gi, tp):
            if True:
                with ExitStack() as s2:
                    f2sb = mk_sb(s2)
                    hn = f2sb("hn", [128, NKC, 1024], BF16)
                    act = f2sb("act", [128, NFC, 1024], BF16)
                    sg = [f2sb(f"sg{i}", [128, 512], F32) for i in range(2)]
                    wp = WPool(s2, 2816, 3, 4, "f")
                    srot = Rot(range(7))
                    sgrot = Rot([0, 1])
                    for sub in range(2):
                        norm_chunk(gi, tp * 2 + sub, lambda k, sub=sub: hn[:, k, sub * 512:(sub + 1) * 512],
                                   lambda k, sub=sub: ("hn", k, sub), srot)
                    descs = []
                    for f2 in range(NFC // 2):
                        descs.append((wg[l, f2], 8, 256))
                        descs.append((wu[l, f2], 8, 256))
                    it = stream(wp, descs)
                    for f2 in range(NFC // 2):
                        gt, gk = next(it)
                        ut, uk = next(it)
                        for fi in range(2):
                            f = 2 * f2 + fi
                            for sub in range(2):
                                bg = srot.next()
                                bu = srot.next()
                                for k in range(NKC):
                                    A("pe", lambda e, k=k, bg=bg, gt=gt, fi=fi, sub=sub: e.matmul(
                                        ps[bg][:], lhsT=gt[:, k, fi * 128:(fi + 1) * 128], rhs=hn[:, k, sub * 512:(sub + 1) * 512],
                                        start=(k == 0), stop=(k == NKC - 1)), r=[gk, ("hn", k, sub)], w=[("ps", bg)])
                                for k in range(NKC):
                                    A("pe", lambda e, k=k, bu=bu, ut=ut, fi=fi, sub=sub: e.matmul(
                                        ps[bu][:], lhsT=ut[:, k, fi * 128:(fi + 1) * 128], rhs=hn[:, k, sub * 512:(sub + 1) * 512],
                                        start=(k == 0), stop=(k == NKC - 1)), r=[uk, ("hn", k, sub)], w=[("ps", bu)])
                                si = sgrot.next()
                                A("act", lambda e, si=si, bg=bg: e.activation(out=sg[si][:], in_=ps[bg][:], func=AF.Silu),
                                  r=[("ps", bg)], w=[("sg", si)])
                                A("dve", lambda e, si=si, bu=bu, f=f, sub=sub: e.tensor_tensor(
                                    out=act[:, f, sub * 512:(sub + 1) * 512], in0=sg[si][:], in1=ps[bu][:], op=ALU.mult),
                                  r=[("sg", si), ("ps", bu)], w=[("act", f, sub)])
                    descs = []
                    for d2 in range(4):
                        for half in range(2):
                            descs.append((wd[l, d2, half], 11, 256))
                    it = stream(wp, descs)
                    for d2 in range(4):
                        tl0, tk0 = next(it)
                        tl1, tk1 = next(it)
                        for di in range(2):
                            d = 2 * d2 + di
                            for sub in range(2):
                                b = srot.next()
                                c = tp * 2 + sub
                                for fc in range(NFC):
                                    tl, tk = (tl0, tk0) if fc < 11 else (tl1, tk1)
                                    A("pe", lambda e, fc=fc, tl=tl, di=di, sub=sub, b=b: e.matmul(
                                        ps[b][:], lhsT=tl[:, fc % 11, di * 128:(di + 1) * 128], rhs=act[:, fc, sub * 512:(sub + 1) * 512],
                                        start=(fc == 0), stop=(fc == NFC - 1)), r=[tk, ("act", fc, sub)], w=[("ps", b)])
                                A("dve", lambda e, d=d, c=c, b=b: e.scalar_tensor_tensor(
                                    out=h[:, d, c * 512:(c + 1) * 512], in0=ps[b][:], scalar=0.5, in1=h[:, d, c * 512:(c + 1) * 512],
                                    op0=ALU.mult, op1=ALU.add), r=[("ps", b), ("h", d, c)], w=[("h", d, c)])

        def ple(l, gi):
            with ExitStack() as s2:
                f2sb = mk_sb(s2)
                hn = f2sb("hnp", [128, NKC, T], BF16)
                ptb = f2sb("ptb", [128, 2, T], BF16)
                sig = [f2sb(f"sig{i}", [128, 512], F32) for i in range(2)]
                tmp = [f2sb(f"ptmp{i}", [128, 512], F32) for i in range(2)]
                wp = WPool(s2, 2048, 3, 4, "p")
                srot = Rot(range(7))
                r2 = Rot([0, 1])
                for c in range(NTC):
                    norm_chunk(gi, c, lambda k, c=c: hn[:, k, c * 512:(c + 1) * 512], lambda k, c=c: ("hnp", k, c), srot)
                    wp.load_into((pTd[l][:, c * 512:(c + 1) * 512].rearrange("(a p) t -> p a t", p=128), 2, 512),
                                 ptb[:, :, c * 512:(c + 1) * 512], ("ptb", c))
                descs = []
                for d2 in range(4):
                    descs.append((ple_gate[l, d2], 8, 256))
                    descs.append((ple_proj[l, d2], 2, 256))
                it = stream(wp, descs)
                for d2 in range(4):
                    gt, gk = next(it)
                    pt, pk = next(it)
                    for di in range(2):
                        d = 2 * d2 + di
                        for c in range(NTC):
                            bg = srot.next()
                            bp = srot.next()
                            for k in range(NKC):
                                A("pe", lambda e, k=k, bg=bg, gt=gt, di=di, c=c: e.matmul(
                                    ps[bg][:], lhsT=gt[:, k, di * 128:(di + 1) * 128], rhs=hn[:, k, c * 512:(c + 1) * 512],
                                    start=(k == 0), stop=(k == NKC - 1)), r=[gk, ("hnp", k, c)], w=[("ps", bg)])
                            for k in range(2):
                                A("pe", lambda e, k=k, bp=bp, pt=pt, di=di, c=c: e.matmul(
                                    ps[bp][:], lhsT=pt[:, k, di * 128:(di + 1) * 128], rhs=ptb[:, k, c * 512:(c + 1) * 512],
                                    start=(k == 0), stop=(k == 1)), r=[pk, ("ptb", c)], w=[("ps", bp)])
                            si = r2.next()
                            A("act", lambda e, si=si, bg=bg: e.activation(out=sig[si][:], in_=ps[bg][:], func=AF.Sigmoid),
                              r=[("ps", bg)], w=[("sig", si)])
                            A("dve", lambda e, si=si, bp=bp: e.tensor_tensor(out=tmp[si][:], in0=sig[si][:], in1=ps[bp][:], op=ALU.mult),
                              r=[("sig", si), ("ps", bp)], w=[("ptmp", si)])
                            A("pool", lambda e, si=si, d=d, c=c: e.tensor_tensor(
                                out=h[:, d, c * 512:(c + 1) * 512], in0=h[:, d, c * 512:(c + 1) * 512], in1=tmp[si][:], op=ALU.add),
                              r=[("ptmp", si), ("h", d, c)], w=[("h", d, c)])
            P.barrier()

        def final_norm():
            with ExitStack() as s2:
                f2sb = mk_sb(s2)
                ot = [f2sb(f"ot{i}", [128, 512], F32) for i in range(4)]
                srot = Rot(range(7))
                for c in range(NTC):
                    def after(k, c=c):
                        i = (c * 8 + k) % 4
                        DMA(outT[k * 128:(k + 1) * 128, c * 512:(c + 1) * 512], ot[i][:], r=[("ot", i)])
                    norm_chunk(8, c, lambda k, c=c: ot[(c * 8 + k) % 4][:], lambda k, c=c: ("ot", (c * 8 + k) % 4), srot,
                               after=after, engs=("dve",))
            P.barrier()

        def mixer_proj(gi, fm_items, tm_items):
            import os as _os
            _nfm = int(_os.environ.get("PROJ_FM", "99"))
            _ntm = int(_os.environ.get("PROJ_TM", "99"))
            fm_items = fm_items[:_nfm]
            tm_items = tm_items[:_ntm]
            with ExitStack() as s2:
                f2sb = mk_sb(s2)
                hnc = f2sb("hnc", [128, NKC, 512], BF16)
                ropc = [f2sb(f"ropc{i}", [128, 2, 512], F32) for i in range(2)]
                xb = [f2sb(f"xb{i}", [128, 512], BF16) for i in range(2)]
                t1 = [f2sb(f"t1{i}", [128, 512], F32) for i in range(2)]
                t2 = [f2sb(f"t2{i}", [128, 512], F32) for i in range(2)]
                wp = WPool(s2, 1024, 2, 3, "m")
                srot = Rot(range(7))
                r2 = Rot([0, 1])
                for c in range(NTC):
                    norm_chunk(gi, c, lambda k: hnc[:, k, :], lambda k: ("hnc", k), srot)
                    rc = c % 2
                    DMA(ropc[rc][:], rope_d[:, :, c * 512:(c + 1) * 512].rearrange("a p t -> p a t"), w=[("ropc", rc)])
                    descs = [(w_ap, 8, 128) for (w_ap, _) in fm_items]
                    descs += [(w_ap, 8, n) for (w_ap, n, _, _) in tm_items]
                    it = stream(wp, descs)
                    for (_, dst_fn) in fm_items:
                        wt, wk = next(it)
                        b = srot.next()
                        for k in range(NKC):
                            A("pe", lambda e, k=k, b=b, wt=wt: e.matmul(ps[b][:], lhsT=wt[:, k, :], rhs=hnc[:, k, :],
                                                                        start=(k == 0), stop=(k == NKC - 1)),
                              r=[wk, ("hnc", k)], w=[("ps", b)])
                        i = r2.next()
                        A("act", lambda e, i=i, b=b: e.activation(out=xb[i][:], in_=ps[b][:], func=AF.Copy),
                          r=[("ps", b)], w=[("xb", i)])
                        A("dve", lambda e, i=i, b=b, rc=rc: e.tensor_tensor(out=t1[i][:], in0=ps[b][:], in1=ropc[rc][:, 0, :], op=ALU.mult),
                          r=[("ps", b), ("ropc", rc), ("xb", i)], w=[("t1", i)])
                        b2 = srot.next()
                        A("pe", lambda e, i=i, b2=b2: e.matmul(ps[b2][:], lhsT=perm, rhs=xb[i][:], start=True, stop=True),
                          r=[("xb", i), "mats"], w=[("ps", b2)])
                        A("dve", lambda e, i=i, b2=b2, rc=rc: e.tensor_tensor(out=t2[i][:], in0=ps[b2][:], in1=ropc[rc][:, 1, :], op=ALU.mult),
                          r=[("ps", b2), ("ropc", rc)], w=[("t2", i)])
                        dst, dkey = dst_fn(c)
                        A("pool", lambda e, i=i, dst=dst: e.tensor_tensor(out=dst, in0=t1[i][:], in1=t2[i][:], op=ALU.add),
                          r=[("t1", i), ("t2", i)], w=[dkey])
                    for (_, n, dst_fn, _) in tm_items:
                        wt, wk = next(it)
                        for tb in range(4):
                            b = srot.next()
                            for k in range(NKC):
                                A("pe", lambda e, k=k, b=b, wt=wt, tb=tb, n=n: e.matmul(
                                    ps[b][:, 0:n], lhsT=hnc[:, k, tb * 128:(tb + 1) * 128], rhs=wt[:, k, :],
                                    start=(k == 0), stop=(k == NKC - 1)), r=[wk, ("hnc", k)], w=[("ps", b)])
                            dst, dkey = dst_fn(c * 4 + tb)
                            A("act", lambda e, b=b, dst=dst, n=n: e.activation(out=dst, in_=ps[b][:, 0:n], func=AF.Copy),
                              r=[("ps", b)], w=[dkey])
            P.barrier()

        def attn_pairs_chunk(c, Q, qc, qkey, K, kc, kkey, vfn, vkey, mfn, mkey, dst, dkey, accs, srot, tiles):
            ebs, pbs, rds, erot, prot, rrot, mengs = tiles
            nb, db = accs
            last = 4 * c + 3
            for j in range(last + 1):
                t0 = max(0, j * 128 - c * 512)
                n = 512 - t0
                for e_ in range(2):
                    lo, hi = e_ * 64, (e_ + 1) * 64
                    b = srot.next()
                    A("pe", lambda e, b=b, lo=lo, hi=hi, j=j, t0=t0, n=n: e.matmul(
                        ps[b][:, 0:n], lhsT=K[lo:hi, kc, j * 128:(j + 1) * 128], rhs=Q[lo:hi, qc, c * 512 + t0:(c + 1) * 512],
                        start=True, stop=True), r=[qkey, kkey], w=[("ps", b)])
                    ei = erot.next()
                    A("act", lambda e, b=b, ei=ei, n=n: e.activation(out=ebs[ei][:, 0:n], in_=ps[b][:, 0:n], func=AF.Exp, scale=0.125),
                      r=[("ps", b)], w=[("eb", ei)])
                    m = mfn(j, t0, n)
                    if m is not None:
                        pi = prot.next()
                        A(mengs.next(), lambda e, ei=ei, pi=pi, n=n, m=m: e.tensor_tensor(out=pbs[pi][:, 0:n], in0=ebs[ei][:, 0:n], in1=m, op=ALU.mult),
                          r=[("eb", ei), mkey], w=[("pb", pi)])
                        src, skey = pbs[pi], ("pb", pi)
                    else:
                        src, skey = ebs[ei], ("eb", ei)
                    v = vfn(j, e_)
                    A("pe", lambda e, lo=lo, hi=hi, t0=t0, n=n, v=v, src=src, j=j: e.matmul(
                        ps[nb][lo:hi, t0:512], lhsT=v, rhs=src[:, 0:n], start=(j == 0), stop=(j == last), tile_position=(0, lo)),
                      r=[skey, vkey], w=[("ps", nb)])
                    A("pe", lambda e, lo=lo, hi=hi, t0=t0, n=n, src=src, j=j: e.matmul(
                        ps[db][lo:hi, t0:512], lhsT=ones[:, 0:64], rhs=src[:, 0:n], start=(j == 0), stop=(j == last), tile_position=(0, lo)),
                      r=[skey, "mats"], w=[("ps", db)])
            ri = rrot.next()
            A("dve", lambda e, ri=ri: e.reciprocal(out=rds[ri][:], in_=ps[db][:]), r=[("ps", db)], w=[("rd", ri)])
            A("dve", lambda e, ri=ri: e.tensor_tensor(out=dst, in0=ps[nb][:], in1=rds[ri][:], op=ALU.mult),
              r=[("ps", nb), ("rd", ri)], w=[dkey])

        def attn_tiles(f2sb, mengs=("dve", "pool")):
            ebs = [f2sb(f"eb{i}", [128, 512], BF16) for i in range(3)]
            pbs = [f2sb(f"pb{i}", [128, 512], BF16) for i in range(3)]
            rds = [f2sb(f"rd{i}", [128, 512], F32) for i in range(2)]
            return (ebs, pbs, rds, Rot(range(3)), Rot(range(3)), Rot(range(2)), Rot(mengs))

        def apply_wout(w2d, merged):
            with ExitStack() as s2:
                wp = WPool(s2, 2048, 3, 4, "o")
                srot = Rot(range(7))
                descs = [(w2d[d2], 8, 256) for d2 in range(4)]
                it = stream(wp, descs)
                for d2 in range(4):
                    wt, wk = next(it)
                    for di in range(2):
                        d = 2 * d2 + di
                        for c in range(NTC):
                            b = srot.next()
                            for k in range(NKC):
                                A("pe", lambda e, k=k, b=b, wt=wt, di=di, c=c: e.matmul(
                                    ps[b][:], lhsT=wt[:, k, di * 128:(di + 1) * 128], rhs=merged[:, k, c * 512:(c + 1) * 512],
                                    start=(k == 0), stop=(k == NKC - 1)), r=[wk, ("mg", k, c)], w=[("ps", b)])
                            A("dve", lambda e, b=b, d=d, c=c: e.tensor_tensor(
                                out=h[:, d, c * 512:(c + 1) * 512], in0=ps[b][:], in1=h[:, d, c * 512:(c + 1) * 512], op=ALU.add),
                              r=[("ps", b), ("h", d, c)], w=[("h", d, c)])
            P.barrier()

        def even_mixer(gi):
            with ExitStack() as sm:
                msb = mk_sb(sm)
                merged = msb("merged", [128, NKC, T], BF16)
                A("dve", lambda e: e.tensor_tensor(out=lv[:, 0, :], in0=lv[:, 0, :], in1=lv[:, 1, :], op=ALU.mult), r=["lv"], w=["lv"])
                A("dve", lambda e: e.tensor_tensor(out=lv[:, 2, :], in0=lv[:, 2, :], in1=lv[:, 3, :], op=ALU.mult), r=["lv"], w=["lv"])
                A("dve", lambda e: e.reduce_sum(out=lsm[:, 0:1], in_=lv[:, 0, :], axis=AX.X), r=["lv"], w=["lsm"])
                A("dve", lambda e: e.reduce_sum(out=lsm[:, 1:2], in_=lv[:, 2, :], axis=AX.X), r=["lv"], w=["lsm"])
                A("act", lambda e: e.activation(out=lsm[:, 2:4], in_=lsm[:, 0:2], func=AF.Exp), r=["lsm"], w=["lsm"])
                A("dve", lambda e: e.tensor_tensor(out=lsm[:, 4:5], in0=lsm[:, 3:4], in1=lsm[:, 2:3], op=ALU.subtract), r=["lsm"], w=["lsm"])
                A("dve", lambda e: e.tensor_scalar(out=lsm[:, 5:6], in0=lsm[:, 4:5], scalar1=-0.2, scalar2=None, op0=ALU.add), r=["lsm"], w=["lsm"])
                A("dve", lambda e: e.tensor_scalar(out=lsm[:, 6:7], in0=lsm[:, 7:8], scalar1=0.8, scalar2=None, op0=ALU.mult), r=["lsm", "lsm7"], w=["lsm"])
                neglam = lsm[:, 5:6]
                gsub = lsm[:, 6:7]

                for kk in range(NKC):
                    if (kk < 4 and not cfg["dsa"]) or (kk >= 4 and not cfg["diff"]):
                        A("pool", lambda e, kk=kk: e.memset(merged[:, kk, :], 0.0), w=[("mg", kk, c) for c in range(NTC)])
                if cfg["dsa"]:
                    with ExitStack() as sa:
                        asb = mk_sb(sa)
                        QA = asb("QA", [128, 4, T], BF16)
                        KA = asb("KA", [128, 1, T], BF16)
                        QI = asb("QI", [128, 4, T], BF16)
                        KI = asb("KI", [128, 1, T], BF16)
                        VA = asb("VA", [128, NTB, 64], BF16)
                        WI = asb("WI", [128, NTB, 8], F32)
                        fm = []
                        bufs = [(QA, n, "QA") for n in range(4)] + [(KA, 0, "KA")] + [(QI, n, "QI") for n in range(4)] + [(KI, 0, "KI")]
                        for ci, (buf, n, nm) in enumerate(bufs):
                            fm.append((ew_fm[ci],
                                       lambda c, buf=buf, n=n, nm=nm: (buf[:, n, c * 512:(c + 1) * 512], (nm, n, c))))
                        tm = [(ew_va[0], 64, lambda blk: (VA[:, blk, :], ("VA", blk)), None),
                              (ew_wi[0], 8, lambda blk: (WI[:, blk, :], ("WI", blk)), None)]
                        mixer_proj(gi, fm, tm)
                        with ExitStack() as s2:
                            f2sb = mk_sb(s2)
                            Ib = [f2sb(f"Ib{i}", [128, T], F32) for i in range(2)]
                            rb = [f2sb(f"rb{i}", [128, 512], F32) for i in range(3)]
                            mk = f2sb("mk", [128, T], BF16)
                            junk = f2sb("junk", [128, T], BF16)
                            maskT = f2sb("maskT", [128, NTB, 512], BF16)
                            bnd = [f2sb(f"bnd{i}", [128, 8], F32) for i in range(2)]
                            gem = [f2sb(f"gem{i}", [128, 2], mybir.dt.uint32) for i in range(2)]
                            tiles = attn_tiles(f2sb, mengs=("dve",))
                            srot = Rot([0, 1, 2])
                            rrot = Rot(range(3))
                            accrot = Rot([(3, 4), (5, 6)])
                            NIT = 18

                            def s1(i):
                                c = i // 4
                                cols = (i + 1) * 128
                                I = Ib[i % 2]
                                ik = ("I", i % 2)
                                bd = bnd[i % 2]
                                bk = ("bnd", i % 2)
                                for sc in range((cols + 511) // 512):
                                    w_ = min(512, cols - sc * 512)
                                    for hd in range(8):
                                        lo, hi = (hd % 2) * 64, (hd % 2 + 1) * 64
                                        b = srot.next()
                                        A("pe", lambda e, b=b, lo=lo, hi=hi, hd=hd, sc=sc, w_=w_: e.matmul(
                                            ps[b][:, 0:w_], lhsT=QI[lo:hi, hd // 2, i * 128:(i + 1) * 128],
                                            rhs=KI[lo:hi, 0, sc * 512:sc * 512 + w_], start=True, stop=True),
                                          r=[("QI", hd // 2, c), ("KI", 0, sc)], w=[("ps", b)])
                                        ri = rrot.next()
                                        A("act", lambda e, b=b, ri=ri, w_=w_: e.activation(out=rb[ri][:, 0:w_], in_=ps[b][:, 0:w_], func=AF.Relu),
                                          r=[("ps", b)], w=[("rb", ri)])
                                        if hd == 0:
                                            A("dve", lambda e, ri=ri, sc=sc, w_=w_: e.tensor_scalar(
                                                out=I[:, sc * 512:sc * 512 + w_], in0=rb[ri][:, 0:w_], scalar1=WI[:, i, 0:1], scalar2=None, op0=ALU.mult),
                                              r=[("rb", ri), ("WI", i)], w=[ik])
                                        else:
                                            A("dve", lambda e, ri=ri, sc=sc, w_=w_, hd=hd: e.scalar_tensor_tensor(
                                                out=I[:, sc * 512:sc * 512 + w_], in0=rb[ri][:, 0:w_], scalar=WI[:, i, hd:hd + 1],
                                                in1=I[:, sc * 512:sc * 512 + w_], op0=ALU.mult, op1=ALU.add),
                                              r=[("rb", ri), ("WI", i), ik], w=[ik])
                                if i >= 2:
                                    A("dve", lambda e: e.tensor_reduce(out=bd[:, 0:1], in_=I[:, 0:cols], axis=AX.X, op=ALU.min), r=[ik], w=[bk])
                                    A("dve", lambda e: e.tensor_reduce(out=bd[:, 1:2], in_=I[:, 0:cols], axis=AX.X, op=ALU.max), r=[ik], w=[bk])
                                    A("dve", lambda e: e.tensor_tensor(out=bd[:, 2:3], in0=bd[:, 1:2], in1=bd[:, 0:1], op=ALU.subtract), r=[bk], w=[bk])
                                A("pool", lambda e: e.tensor_tensor(out=I[:, i * 128:(i + 1) * 128], in0=I[:, i * 128:(i + 1) * 128],
                                                                    in1=negm[:], op=ALU.add), r=[ik, "negm"], w=[ik])

                            def s2(i):
                                ti = i % 4
                                cols = (i + 1) * 128
                                I = Ib[i % 2]
                                ik = ("I", i % 2)
                                bd = bnd[i % 2]
                                bk = ("bnd", i % 2)
                                gm = gem[i % 2]
                                gk = ("gem", i % 2)
                                if i >= 2:
                                    for it in range(NIT):
                                        A("dve", lambda e, it=it: e.scalar_tensor_tensor(out=bd[:, 3:4], in0=bd[:, 2:3], scalar=float(2.0 ** -(it + 1)),
                                                                                        in1=bd[:, 0:1], op0=ALU.mult, op1=ALU.add), r=[bk], w=[bk])
                                        A("dve", lambda e: e.tensor_scalar(out=junk[:, 0:cols], in0=I[:, 0:cols], scalar1=bd[:, 3:4], scalar2=None,
                                                                           op0=ALU.is_ge, op1=ALU.add, accum_out=bd[:, 4:5]), r=[ik, bk], w=[bk, "junk"])
                                        A("dve", lambda e: e.tensor_single_scalar(out=gm[:, 0:1], in_=bd[:, 4:5], scalar=255.5, op=ALU.is_ge), r=[bk], w=[gk])
                                        A("dve", lambda e: e.copy_predicated(out=bd[:, 0:1], mask=gm[:, 0:1], data=bd[:, 3:4]), r=[bk, gk], w=[bk])
                                    A("dve", lambda e: e.tensor_scalar(out=mk[:, 0:cols], in0=I[:, 0:cols], scalar1=bd[:, 0:1], scalar2=None, op0=ALU.is_ge),
                                      r=[ik, bk], w=["mk"])
                                else:
                                    A("dve", lambda e: e.tensor_single_scalar(out=mk[:, 0:cols], in_=I[:, 0:cols], scalar=-1.0e29, op=ALU.is_ge), r=[ik], w=["mk"])
                                for j0 in range(0, i + 1, 8):
                                    nn = min(8, i + 1 - j0)
                                    for jj in range(nn):
                                        A("pe", lambda e, jj=jj, j0=j0: e.transpose(
                                            out=pTp[:, jj * 128:(jj + 1) * 128],
                                            in_=mk[:, (j0 + jj) * 128:(j0 + jj + 1) * 128], identity=ident),
                                          r=["mk", "mats"], w=["pT"])
                                    A("act", lambda e, nn=nn, j0=j0: e.activation(
                                        out=maskT[:, j0:j0 + nn, ti * 128:(ti + 1) * 128],
                                        in_=pTp[:, 0:nn * 128].rearrange("p (a b) -> p a b", a=nn), func=AF.Copy),
                                      r=["pT"], w=["maskT"])

                            s1(0)
                            for i in range(NTB):
                                if i + 1 < NTB:
                                    s1(i + 1)
                                s2(i)
                                if i % 4 == 3:
                                    c = i // 4
                                    for hp in range(4):
                                        attn_pairs_chunk(
                                            c, QA, hp, ("QA", hp, c), KA, 0, ("KA", 0, c),
                                            lambda j, e_: VA[:, j, :], ("VA", 0),
                                            lambda j, t0, n: maskT[:, j, t0:512], "maskT",
                                            merged[:, hp, c * 512:(c + 1) * 512], ("mg", hp, c), accrot.next(), srot, tiles)
                        P.barrier()

                if cfg["diff"]:
                    with ExitStack() as sa:
                        asb = mk_sb(sa)
                        QB = asb("QB", [128, 4, T], BF16)
                        KB = asb("KB", [128, 4, T], BF16)
                        VB = asb("VB", [128, NTB, 512], BF16)
                        fm = []
                        bufs = [(QB, n, "QB") for n in range(4)] + [(KB, n, "KB") for n in range(4)]
                        for ci, (buf, n, nm) in enumerate(bufs):
                            fm.append((ew_fm[10 + ci],
                                       lambda c, buf=buf, n=n, nm=nm: (buf[:, n, c * 512:(c + 1) * 512], (nm, n, c))))
                        tm = [(ew_vb[n], 128,
                               lambda blk, n=n: (VB[:, blk, n * 128:(n + 1) * 128], ("VB", blk, n)), None) for n in range(4)]
                        mixer_proj(gi, fm, tm)
                        with ExitStack() as s2:
                            f2sb = mk_sb(s2)
                            strip = f2sb("cstrip", [128, STRIP_W], BF16)
                            DMA(strip[:], strips_d[0], w=["strip"])
                            ebs = [f2sb(f"eb{i}", [128, 512], BF16) for i in range(3)]
                            pbs = [f2sb(f"pb{i}", [128, 512], BF16) for i in range(3)]
                            fa = [f2sb(f"fa{i}", [128, 512], F32) for i in range(4)]
                            ob = f2sb("ob", [128, 512], F32)
                            sq2 = f2sb("sq2", [128, 512], BF16)
                            erot, prot, mrot = Rot(range(3)), Rot(range(3)), Rot(["dve"])
                            srot = Rot([0, 1, 2])
                            for hd in range(4):
                                for c in range(NTC):
                                    last = 4 * c + 3
                                    for j in range(last + 1):
                                        t0 = max(0, j * 128 - c * 512)
                                        n = 512 - t0
                                        for e_ in range(2):
                                            lo, hi = e_ * 64, (e_ + 1) * 64
                                            b = srot.next()
                                            A("pe", lambda e, b=b, lo=lo, hi=hi, j=j, t0=t0, n=n, hd=hd, c=c: e.matmul(
                                                ps[b][:, 0:n], lhsT=KB[lo:hi, hd, j * 128:(j + 1) * 128],
                                                rhs=QB[lo:hi, hd, c * 512 + t0:(c + 1) * 512], start=True, stop=True),
                                              r=[("QB", hd, c), ("KB", hd, j // 4)], w=[("ps", b)])
                                            ei = erot.next()
                                            A("act", lambda e, b=b, ei=ei, n=n: e.activation(out=ebs[ei][:, 0:n], in_=ps[b][:, 0:n],
                                                                                             func=AF.Exp, scale=0.125),
                                              r=[("ps", b)], w=[("eb", ei)])
                                            if j >= 4 * c:
                                                off = c * 512 - j * 128 + 384 + t0
                                                pi = prot.next()
                                                A(mrot.next(), lambda e, ei=ei, pi=pi, n=n, off=off: e.tensor_tensor(
                                                    out=pbs[pi][:, 0:n], in0=ebs[ei][:, 0:n], in1=strip[:, off:off + n], op=ALU.mult),
                                                  r=[("eb", ei), "strip"], w=[("pb", pi)])
                                                src, skey = pbs[pi], ("pb", pi)
                                            else:
                                                src, skey = ebs[ei], ("eb", ei)
                                            nb, db = 3 + 2 * e_, 4 + 2 * e_
                                            A("pe", lambda e, nb=nb, t0=t0, n=n, j=j, hd=hd, src=src, last=last: e.matmul(
                                                ps[nb][:, t0:512], lhsT=VB[:, j, hd * 128:(hd + 1) * 128], rhs=src[:, 0:n],
                                                start=(j == 0), stop=(j == last)),
                                              r=[skey] + [("VB", j, hd)], w=[("ps", nb)])
                                            A("pe", lambda e, db=db, t0=t0, n=n, j=j, src=src, last=last: e.matmul(
                                                ps[db][:, t0:512], lhsT=ones, rhs=src[:, 0:n], start=(j == 0), stop=(j == last)),
                                              r=[skey, "mats"], w=[("ps", db)])
                                    A("dve", lambda e: e.reciprocal(out=fa[0][:], in_=ps[4][:]), r=[("ps", 4)], w=[("fa", 0)])
                                    A("dve", lambda e: e.tensor_tensor(out=fa[1][:], in0=ps[3][:], in1=fa[0][:], op=ALU.mult),
                                      r=[("ps", 3), ("fa", 0)], w=[("fa", 1)])
                                    A("dve", lambda e: e.reciprocal(out=fa[2][:], in_=ps[6][:]), r=[("ps", 6)], w=[("fa", 2)])
                                    A("dve", lambda e: e.tensor_tensor(out=fa[3][:], in0=ps[5][:], in1=fa[2][:], op=ALU.mult),
                                      r=[("ps", 5), ("fa", 2)], w=[("fa", 3)])
                                    A("dve", lambda e: e.scalar_tensor_tensor(out=ob[:], in0=fa[3][:], scalar=neglam, in1=fa[1][:],
                                                                              op0=ALU.mult, op1=ALU.add),
                                      r=[("fa", 3), ("fa", 1), "lsm"], w=["ob"])
                                    A("act", lambda e: e.activation(out=sq2[:], in_=ob[:], func=AF.Square, scale=float(128.0 ** -0.5)),
                                      r=["ob"], w=["sq2"])
                                    b = srot.next()
                                    A("pe", lambda e, b=b: e.matmul(ps[b][:], lhsT=ones, rhs=sq2[:], start=True, stop=True),
                                      r=["sq2", "mats"], w=[("ps", b)])
                                    A("act", lambda e, b=b: e.activation(out=fa[0][:], in_=ps[b][:], func=AF.Sqrt, bias=epsb[:, 1:2]),
                                      r=[("ps", b), "epsb"], w=[("fa", 0)])
                                    A("dve", lambda e: e.reciprocal(out=fa[2][:], in_=fa[0][:]), r=[("fa", 0)], w=[("fa", 2)])
                                    A("dve", lambda e, hd=hd, c=c: e.scalar_tensor_tensor(
                                        out=merged[:, 4 + hd, c * 512:(c + 1) * 512], in0=ob[:], scalar=gsub, in1=fa[2][:],
                                        op0=ALU.mult, op1=ALU.mult), r=["ob", ("fa", 2), "lsm"], w=[("mg", 4 + hd, c)])
                        P.barrier()
                apply_wout(even_w_out, merged)

        def odd_mixer(gi):
            with ExitStack() as sm:
                msb = mk_sb(sm)
                merged = msb("mergedo", [128, NKC, T], BF16)
                for g in range(2):
                    odd_group(g, gi, merged)
                apply_wout(odd_w_out, merged)

        def odd_group(g, gi, merged):
            if True:
                if True:
                    with ExitStack() as sa:
                        asb = mk_sb(sa)
                        QG = asb("QG", [128, 4, T], BF16)
                        KG = asb("KG", [128, 4, T], BF16)
                        VG = asb("VG", [128, NTB, 512], BF16)
                        fm = []
                        for n in range(4):
                            fm.append((odd_w_in[4 * g + n],
                                       lambda c, n=n: (QG[:, n, c * 512:(c + 1) * 512], ("QG", n, c))))
                        for n in range(4):
                            fm.append((odd_w_in[8 + 4 * g + n],
                                       lambda c, n=n: (KG[:, n, c * 512:(c + 1) * 512], ("KG", n, c))))
                        tm = [(odd_w_in[16 + 4 * g + n], 128,
                               lambda blk, n=n: (VG[:, blk, n * 128:(n + 1) * 128], ("VG", blk, n)), None) for n in range(4)]
                        if cfg["proj"]:
                            mixer_proj(gi, fm, tm)
                        if not cfg["attn"]:
                            for hp in range(4):
                                A("pool", lambda e, hp=hp: e.memset(merged[:, 4 * g + hp, :], 0.0), w=[("mg", 4 * g + hp, c) for c in range(NTC)])
                            return
                        with ExitStack() as s2:
                            f2sb = mk_sb(s2)
                            strip = f2sb("mstrip", [128, STRIP_W], BF16)
                            DMA(strip[:], strips_d[1], w=["strip"])
                            tiles = attn_tiles(f2sb, mengs=("dve",))
                            srot = Rot([0, 1, 2])
                            accrot = Rot([(3, 4), (5, 6)])
                            for hp in range(4):
                                for c in range(NTC):
                                    attn_pairs_chunk(
                                        c, QG, hp, ("QG", hp, c), KG, hp, ("KG", hp, c),
                                        lambda j, e_, hp=hp: VG[:, j, (2 * hp + e_) * 64:(2 * hp + e_ + 1) * 64], ("VG", 0, 0),
                                        lambda j, t0, n, c=c: strip[:, c * 512 - j * 128 + 384 + t0:c * 512 - j * 128 + 384 + t0 + n], "strip",
                                        merged[:, 4 * g + hp, c * 512:(c + 1) * 512], ("mg", 4 * g + hp, c), accrot.next(), srot, tiles)
                        P.barrier()

        for l in range(cfg["layers"]):
            if cfg["ffn"]:
                ffn(l, ffn_w["ffn_a_wg"], ffn_w["ffn_a_wu"], ffn_w["ffn_a_wd"], l * 4 + 0)
            if l % 2 == 0:
                if cfg["mix_even"]:
                    even_mixer(l * 4 + 1)
            else:
                if cfg["mix_odd"]:
                    odd_mixer(l * 4 + 1)
            if cfg["ffn"]:
                ffn(l, ffn_w["ffn_b_wg"], ffn_w["ffn_b_wu"], ffn_w["ffn_b_wd"], l * 4 + 2)
            if cfg["ple"]:
                ple(l, l * 4 + 3)
        final_norm()
        P.emit(st)
    nc._declared_inputs = declared
    return nc


def prep_inputs(inputs, b):
    f = np.float32
    g = lambda k: np.asarray(inputs[k], f)
    rope, strips, mats, negmask = _host_consts()
    gains = []
    for l in range(2):
        for nm in ("norm_ffn_a", "norm_mix", "norm_ffn_b", "norm_ple"):
            gains.append(_col(g(nm)[l]))
    gains.append(_col(g("final_norm")))
    gains = np.ascontiguousarray(np.concatenate(gains, axis=1))
    ewi = g("even_w_in")[0]
    qa, ka, va = ewi[:, 0:512], ewi[:, 512:576], ewi[:, 576:640]
    qi, ki, wi = ewi[:, 640:1152], ewi[:, 1152:1216], ewi[:, 1216:1224]
    qb, kb, vb = ewi[:, 1224:1736], ewi[:, 1736:2248], ewi[:, 2248:2760]
    ew_fm = np.ascontiguousarray(np.concatenate([qa, ka, ka, qi, ki, ki, qb, kb], axis=1))
    def tile_w(W, tc):
        R_, C_ = W.shape
        return np.ascontiguousarray(W.reshape(R_ // 128, 128, C_ // tc, tc).transpose(2, 1, 0, 3))

    m = {
        "xT": np.ascontiguousarray(g("x")[b].T),
        "pT": np.ascontiguousarray(np.transpose(g("p")[:, b], (0, 2, 1))),
        "gains": gains,
        "ple_gate": np.stack([tile_w(g("ple_gate")[l], 256) for l in range(2)], 0),
        "ple_proj": np.stack([tile_w(g("ple_proj")[l], 256) for l in range(2)], 0),
        "ew_fm": tile_w(ew_fm, 128), "ew_va": tile_w(np.ascontiguousarray(va), 64),
        "ew_wi": tile_w(np.ascontiguousarray(wi), 8), "ew_vb": tile_w(np.ascontiguousarray(vb), 128),
        "even_w_out": tile_w(g("even_w_out")[0], 256),
        "lamv": np.ascontiguousarray(np.stack([g("diff_lambda_q1")[0], g("diff_lambda_k1")[0],
                                               g("diff_lambda_q2")[0], g("diff_lambda_k2")[0]], 0)),
        "subln": np.ascontiguousarray(g("diff_subln")[0].reshape(128, 1)),
        "odd_w_in": tile_w(g("odd_w_in")[0], 128), "odd_w_out": tile_w(g("odd_w_out")[0], 256),
        "rope": rope, "strips": strips, "mats": mats, "negmask": negmask,
    }
    for nm in ("ffn_a_wg", "ffn_a_wu", "ffn_b_wg", "ffn_b_wu"):
        m[nm] = np.stack([tile_w(g(nm)[l], 256) for l in range(2)], 0)
    for nm in ("ffn_a_wd", "ffn_b_wd"):
        w = g(nm)
        m[nm] = np.stack([np.stack([np.stack([tile_w(w[l][hf * 1408:(hf + 1) * 1408, d2 * 256:(d2 + 1) * 256], 256)[0]
                                              for hf in range(2)], 0) for d2 in range(4)], 0) for l in range(2)], 0)
    return m


_NC_CACHE = {}


def kernel(**inputs):
    import time as _time
    _t0 = _time.time()
    key = tuple(sorted(CFG.items()))
    if key not in _NC_CACHE:
        _NC_CACHE[key] = build_program(CFG)
    nc = _NC_CACHE[key]
    print(f"[kernel] build {_time.time() - _t0:.1f}s", flush=True)
    n = 8
    shared = prep_inputs(inputs, 0)
    in_maps = []
    for b in range(n):
        m = dict(shared)
        m["xT"] = np.ascontiguousarray(np.asarray(inputs["x"], np.float32)[b].T)
        m["pT"] = np.ascontiguousarray(np.transpose(np.asarray(inputs["p"], np.float32)[:, b], (0, 2, 1)))
        in_maps.append({k: v for k, v in m.items() if k in nc._declared_inputs})
    print(f"[kernel] prep {_time.time() - _t0:.1f}s", flush=True)
    import os as _os
    _nd = int(_os.environ.get("DBG_CORES", "8"))
    if _nd != 8:
        res = run_bass_kernel_spmd(nc, in_maps[:_nd], core_ids=list(range(_nd)))
        res.results.extend([res.results[0]] * (8 - _nd))
    else:
        res = run_bass_kernel_spmd(nc, in_maps, core_ids=list(range(n)))
    print(f"[kernel] ran {_time.time() - _t0:.1f}s", flush=True)
    out = np.stack([np.asarray(res.results[b]["outT"], np.float32).T for b in range(n)], 0)
    return np.ascontiguousarray(out)
```

```python
import numpy as np
import ml_dtypes
import concourse.bass as bass
import concourse.mybir as mybir
from concourse.bass_utils import run_bass_kernel_spmd

F32 = mybir.dt.float32
BF16 = mybir.dt.bfloat16
ALU = mybir.AluOpType
AF = mybir.ActivationFunctionType
AX = mybir.AxisListType

T = 2048
D = 1024
DFF = 2816
NKC = D // 128
NFC = DFF // 128
NTB = T // 128
NTC = T // 512
PLE = 256
EVEN_IN = 2760
NEG = -1.0e30
NEG2 = -2.0e30


class Op:
    __slots__ = ("eng", "fn", "waits", "signal", "sem", "val", "is_dma", "idx", "is_bar")

    def __init__(self, eng, fn, is_dma=False):
        self.eng = eng
        self.fn = fn
        self.waits = []
        self.signal = False
        self.sem = None
        self.val = 0
        self.is_dma = is_dma
        self.idx = -1
        self.is_bar = False


ENGS = ("pe", "act", "dve", "pool", "sp")
SEM_LIM = 30000


class Prog:
    def __init__(self, nc):
        self.nc = nc
        self.ops = {e: [] for e in ENGS}
        self.last_w = {}
        self.readers = {}
        self.seen = {e: {s: -1 for s in ENGS} for e in ENGS}
        self.seen_dma = {e: set() for e in ENGS}
        self.pending_dma = []
        self.dma_list = []
        self.nops = 0

    NDMA = 24

    def add(self, eng, fn, reads=(), writes=(), dma=False):
        op = Op(eng, fn, is_dma=dma)
        deps = []
        if dma:
            k = len(self.dma_list)
            if k >= self.NDMA:
                deps.append((self.dma_list[k - self.NDMA], True))
            self.dma_list.append(op)
        for r in reads:
            w = self.last_w.get(r)
            if w is not None:
                deps.append((w, True))
            if r == "pT" or (isinstance(r, tuple) and r[0] == "ps"):
                for rd in self.readers.get(r, ()):
                    if rd.eng != eng:
                        deps.append((rd, True))
        for r in writes:
            w = self.last_w.get(r)
            if w is not None:
                deps.append((w, True))
            for rd in self.readers.get(r, ()):
                deps.append((rd, False))
        best = {}
        for d, is_raw in deps:
            if d.is_dma:
                self._dep(op, d, is_raw)
                continue
            if d.eng == eng and eng in ("pe", "sp"):
                continue
            cur = best.get(d.eng)
            if cur is None or d.idx > cur.idx:
                best[d.eng] = d
        for d in best.values():
            self._dep(op, d, True)
        op.idx = len(self.ops[eng])
        self.ops[eng].append(op)
        for r in reads:
            self.readers.setdefault(r, []).append(op)
        for r in writes:
            self.last_w[r] = op
            self.readers[r] = []
        if dma:
            self.pending_dma.append(op)
        self.nops += 1
        return op

    def _dep(self, op, d, is_raw):
        e = op.eng
        if d.is_dma:
            if d in self.seen_dma[e]:
                return
            self.seen_dma[e].add(d)
            d.signal = True
            op.waits.append(d)
            return
        if d.eng == e:
            if e == "pe" or e == "sp":
                return
        if self.seen[e][d.eng] >= d.idx:
            return
        self.seen[e][d.eng] = d.idx
        d.signal = True
        op.waits.append(d)

    def barrier(self):
        bar = Op("sp", None)
        bar.is_bar = True
        for e in ENGS:
            if e == "sp":
                continue
            if self.ops[e]:
                self._dep(bar, self.ops[e][-1], True)
        for d in self.pending_dma:
            self._dep(bar, d, True)
        self.pending_dma = []
        bar.idx = len(self.ops["sp"])
        self.ops["sp"].append(bar)
        bar.signal = True
        for e in ENGS:
            if e == "sp":
                continue
            w = Op(e, None)
            w.is_bar = True
            w.waits.append(bar)
            self.seen[e]["sp"] = bar.idx
            w.idx = len(self.ops[e])
            self.ops[e].append(w)
        self.last_w = {}
        self.readers = {}

    def emit(self, stack):
        nc = self.nc
        ndma = self.NDMA
        dsems = [stack.enter_context(nc.semaphore(f"s_dma_{i}")) for i in range(ndma)]
        dcnt = [0] * ndma
        for k, op in enumerate(self.dma_list):
            di = k % ndma
            if dcnt[di] + 16 > SEM_LIM:
                dsems[di] = stack.enter_context(nc.semaphore(f"s_dma_{di}_{k}"))
                dcnt[di] = 0
            dcnt[di] += 16
            op.sem = dsems[di]
            op.val = dcnt[di]
        eng_sems = {}
        for e in ENGS:
            cur = None
            cnt = 0
            for op in self.ops[e]:
                if not op.signal or op.is_dma:
                    continue
                if cur is None or cnt >= SEM_LIM:
                    cur = stack.enter_context(nc.semaphore(f"s_{e}_{len(eng_sems)}"))
                    eng_sems[(e, len(eng_sems))] = cur
                    cnt = 0
                cnt += 1
                op.sem = cur
                op.val = cnt
        block = stack.enter_context(nc.Block())

        def run(e):
            def body(eng):
                for op in self.ops[e]:
                    for d in op.waits:
                        eng.wait_ge(d.sem, d.val)
                    if op.fn is None:
                        if op.signal:
                            eng.sem_inc(op.sem, 1)
                        continue
                    ins = op.fn(eng)
                    if op.is_dma:
                        ins.then_inc(op.sem, 16)
                    elif op.signal:
                        ins.then_inc(op.sem, 1)
            return body

        block.tensor(run("pe"))
        block.scalar(run("act"))
        block.vector(run("dve"))
        block.gpsimd(run("pool"))
        block.sync(run("sp"))


STRIP_W = 2432


def _host_consts():
    f32 = np.float32
    inv = (1.0 / (f32(10000.0) ** (np.arange(0, 64, 2, dtype=f32) / f32(64)))).astype(f32)
    ang = (np.arange(T, dtype=f32)[:, None] * inv[None, :]).astype(f32)
    cos = np.cos(ang).astype(f32)
    sin = np.sin(ang).astype(f32)
    p = np.arange(128)
    d = p % 64
    j = d % 32
    C2 = cos[:, j].T.copy()
    S2 = (sin[:, j].T * np.where(d < 32, -1.0, 1.0)[:, None]).astype(f32)
    rope = np.ascontiguousarray(np.stack([C2, S2], 0)).astype(f32)
    x = np.arange(STRIP_W)[None, :] - 384 - p[:, None]
    causal = (x >= 0).astype(f32)
    mult = (((x >= 0) & (x <= 128)).astype(f32)
            + ((x >= 0) & (x % 4 == 0) & (x <= 512)).astype(f32)
            + ((x >= 0) & (x % 16 == 0) & (x <= 2048)).astype(f32))
    strips = np.stack([causal, mult], 0).astype(ml_dtypes.bfloat16)
    ident = np.eye(128, dtype=f32)
    perm = np.zeros((128, 128), f32)
    for m in range(128):
        perm[m ^ 32, m] = 1.0
    ones = np.ones((128, 128), f32)
    mats = np.stack([ident, perm, ones], 0).astype(ml_dtypes.bfloat16)
    tt = np.arange(128)
    negmask = np.where(tt[None, :] > tt[:, None], np.float32(NEG), np.float32(0.0)).astype(f32)
    return rope, strips, mats, negmask


def _col(v):
    v = np.asarray(v, np.float32)
    return np.ascontiguousarray(v.reshape(-1, 128).T)


CFG = {"layers": 2, "mix_even": True, "mix_odd": True, "dsa": True, "diff": True, "ffn": True, "ple": True, "proj": True, "attn": True}


class Rot:
    def __init__(self, items):
        self.items = list(items)
        self.i = 0

    def next(self):
        v = self.items[self.i % len(self.items)]
        self.i += 1
        return v


def build_program(cfg=None):
    from contextlib import ExitStack
    cfg = dict(CFG if cfg is None else cfg)
    nc = bass.Bass("TRN2", target_bir_lowering=False)

    declared = []

    def din(name, shape, dt=F32, need=True):
        if not need:
            return None
        declared.append(name)
        return nc.dram_tensor(name, list(shape), dt, kind="ExternalInput").ap()

    xT = din("xT", [D, T])
    pTd = din("pT", [2, PLE, T])
    gains_d = din("gains", [128, 72])
    ffn_w = {}
    for nm in ("ffn_a_wg", "ffn_a_wu", "ffn_b_wg", "ffn_b_wu"):
        ffn_w[nm] = din(nm, [2, 11, 128, 8, 256], need=cfg["ffn"])
    for nm in ("ffn_a_wd", "ffn_b_wd"):
        ffn_w[nm] = din(nm, [2, 4, 2, 128, 11, 256], need=cfg["ffn"])
    ple_gate = din("ple_gate", [2, 4, 128, 8, 256], need=cfg["ple"])
    ple_proj = din("ple_proj", [2, 4, 128, 2, 256], need=cfg["ple"])
    ew_fm = din("ew_fm", [18, 128, 8, 128])
    ew_va = din("ew_va", [1, 128, 8, 64])
    ew_wi = din("ew_wi", [1, 128, 8, 8])
    ew_vb = din("ew_vb", [4, 128, 8, 128])
    even_w_out = din("even_w_out", [4, 128, 8, 256])
    lamv = din("lamv", [4, 64])
    subln_d = din("subln", [128, 1])
    odd_w_in = din("odd_w_in", [24, 128, 8, 128])
    odd_w_out = din("odd_w_out", [4, 128, 8, 256])
    rope_d = din("rope", [2, 128, T])
    strips_d = din("strips", [2, 128, STRIP_W], BF16)
    mats_d = din("mats", [3, 128, 128], BF16)
    negm_d = din("negmask", [128, 128])
    outT = nc.dram_tensor("outT", [D, T], F32, kind="ExternalOutput").ap()

    uid = [0]

    with ExitStack() as st:
        def mk_sb(stack):
            def f(name, shape, dt):
                uid[0] += 1
                return stack.enter_context(nc.sbuf_tensor(f"{name}_{uid[0]}", list(shape), dt))
            return f

        sb = mk_sb(st)
        h = sb("h", [128, NKC, T], F32)
        gsb = sb("gsb", [128, 72], F32)
        mats = sb("mats", [128, 3, 128], BF16)
        negm = sb("negm", [128, 128], F32)
        epsb = sb("epsb", [128, 2], F32)
        lv = sb("lv", [128, 4, 64], F32)
        lsm = sb("lsm", [128, 8], F32)
        sqb = [sb(f"sqb{i}", [128, 512], BF16) for i in range(2)]
        sdt = sb("sdt", [128, 512], F32)
        rstd = sb("rstd", [128, 512], F32)
        ps = [st.enter_context(nc.psum_tensor(f"ps{i}", [128, 512], F32)) for i in range(7)]
        pTp = st.enter_context(nc.psum_tensor("pTp", [128, 1024], BF16))
        ident = mats[:, 0, :]
        perm = mats[:, 1, :]
        ones = mats[:, 2, :]

        P = Prog(nc)

        def A(eng, fn, r=(), w=()):
            return P.add(eng, fn, reads=r, writes=w)

        def DMA(out, in_, r=(), w=()):
            return P.add("sp", lambda e: e.dma_start(out=out, in_=in_), reads=r, writes=w, dma=True)

        for k in range(NKC):
            DMA(h[:, k, :], xT[k * 128:(k + 1) * 128, :], w=[("h", k, c) for c in range(NTC)])
        DMA(gsb[:], gains_d, w=["gsb"])
        DMA(mats[:], mats_d.rearrange("a p n -> p a n"), w=["mats"])
        DMA(negm[:], negm_d, w=["negm"])
        DMA(lv[:], lamv.partition_broadcast(128), w=["lv"])
        DMA(lsm[:, 7:8], subln_d, w=["lsm7"])
        A("pool", lambda e: e.memset(epsb[:, 0:1], 1e-6), w=["epsb"])
        A("pool", lambda e: e.memset(epsb[:, 1:2], 1e-5), w=["epsb"])

        sqrot = Rot([0, 1])
        cvt_rot = Rot(["act", "dve"])

        class WPool:
            def __init__(self, stack, elems, nst, nbf, tag):
                f = mk_sb(stack)
                self.stg = [f(f"wst{tag}{i}", [128, elems], F32) for i in range(nst)]
                self.bf = [f(f"wbf{tag}{i}", [128, elems], BF16) for i in range(nbf)]
                self.i = 0
                uid[0] += 1
                self.tag = f"{tag}{uid[0]}"

            def load(self, desc):
                src, a, b = desc
                i = self.i
                self.i += 1
                si, bi = i % len(self.stg), i % len(self.bf)
                sv = self.stg[si][:, 0:a * b].rearrange("p (a b) -> p a b", a=a)
                bv = self.bf[bi][:, 0:a * b].rearrange("p (a b) -> p a b", a=a)
                sk, bk = (self.tag, "st", si), (self.tag, "bf", bi)
                DMA(sv, src, w=[sk])
                self._cvt(bv, sv, sk, bk)
                return bv, bk

            def _cvt(self, dst, src, sk, dk):
                eng = cvt_rot.next()
                if eng == "act":
                    A("act", lambda e: e.activation(out=dst, in_=src, func=AF.Copy), r=[sk], w=[dk])
                else:
                    A(eng, lambda e: e.tensor_copy(out=dst, in_=src), r=[sk], w=[dk])

            def load_into(self, desc, dst, dkey):
                src, a, b = desc
                i = self.i
                self.i += 1
                si = i % len(self.stg)
                sv = self.stg[si][:, 0:a * b].rearrange("p (a b) -> p a b", a=a)
                sk = (self.tag, "st", si)
                DMA(sv, src, w=[sk])
                self._cvt(dst, sv, sk, dkey)

        def stream(wp, descs, depth=2):
            q = []
            n = len(descs)
            nxt = 0
            for i in range(n):
                while nxt < n and nxt <= i + depth:
                    q.append(wp.load(descs[nxt]))
                    nxt += 1
                yield q[i]

        def wdesc(w2d, r0, nr, c0, ncol):
            return (w2d[r0:r0 + nr, c0:c0 + ncol].rearrange("(a p) n -> p a n", p=128), nr // 128, ncol)

        def norm_chunk(gi, c, dst_fn, dkey_fn, srot, after=None, engs=("dve",)):
            b = srot.next()
            for k in range(NKC):
                si = sqrot.next()
                A("act", lambda e, k=k, si=si: e.activation(out=sqb[si][:], in_=h[:, k, c * 512:(c + 1) * 512],
                                                            func=AF.Square, scale=1.0 / 32.0),
                  r=[("h", k, c)], w=[("sqb", si)])
                A("pe", lambda e, k=k, si=si: e.matmul(ps[b][:], lhsT=ones, rhs=sqb[si][:], start=(k == 0), stop=(k == NKC - 1)),
                  r=[("sqb", si), "mats"], w=[("ps", b)])
            A("act", lambda e: e.activation(out=sdt[:], in_=ps[b][:], func=AF.Sqrt, bias=epsb[:, 0:1]),
              r=[("ps", b), "epsb"], w=["sdt"])
            A("dve", lambda e: e.reciprocal(out=rstd[:], in_=sdt[:]), r=["sdt"], w=["rstd"])
            for k in range(NKC):
                eng = engs[k % len(engs)]
                dst = dst_fn(k)
                A(eng, lambda e, k=k, dst=dst: e.scalar_tensor_tensor(out=dst, in0=h[:, k, c * 512:(c + 1) * 512],
                                                                      scalar=gsb[:, gi * 8 + k:gi * 8 + k + 1], in1=rstd[:],
                                                                      op0=ALU.mult, op1=ALU.mult),
                  r=[("h", k, c), "gsb", "rstd"], w=[dkey_fn(k)])
                if after is not None:
                    after(k)

        def ffn(l, wg, wu, wd, gi):
            for tp in range(2):
                ffn_pass(l, wg, wu, wd, gi, tp)
                P.barrier()

        def ffn_pass(l, wg, wu, wd, gi, tp):
            if True:
                with ExitStack() as s2:
                    f2sb = mk_sb(s2)
                    hn = f2sb("hn", [128, NKC, 1024], BF16)
                    act = f2sb("act", [128, NFC, 1024], BF16)
                    sg = [f2sb(f"sg{i}", [128, 512], F32) for i in range(2)]
                    wp = WPool(s2, 2816, 3, 4, "f")
                    srot = Rot(range(7))
                    sgrot = Rot([0, 1])
                    for sub in range(2):
                        norm_chunk(gi, tp * 2 + sub, lambda k, sub=sub: hn[:, k, sub * 512:(sub + 1) * 512],
                                   lambda k, sub=sub: ("hn", k, sub), srot)
                    descs = []
                    for f2 in range(NFC // 2):
                        descs.append((wg[l, f2], 8, 256))
                        descs.append((wu[l, f2], 8, 256))
                    it = stream(wp, descs)
                    for f2 in range(NFC // 2):
                        gt, gk = next(it)
                        ut, uk = next(it)
                        for fi in range(2):
                            f = 2 * f2 + fi
                            for sub in range(2):
                                bg = srot.next()
                                bu = srot.next()
                                for k in range(NKC):
                                    A("pe", lambda e, k=k, bg=bg, gt=gt, fi=fi, sub=sub: e.matmul(
                                        ps[bg][:], lhsT=gt[:, k, fi * 128:(fi + 1) * 128], rhs=hn[:, k, sub * 512:(sub + 1) * 512],
                                        start=(k == 0), stop=(k == NKC - 1)), r=[gk, ("hn", k, sub)], w=[("ps", bg)])
                                for k in range(NKC):
                                    A("pe", lambda e, k=k, bu=bu, ut=ut, fi=fi, sub=sub: e.matmul(
                                        ps[bu][:], lhsT=ut[:, k, fi * 128:(fi + 1) * 128], rhs=hn[:, k, sub * 512:(sub + 1) * 512],
                                        start=(k == 0), stop=(k == NKC - 1)), r=[uk, ("hn", k, sub)], w=[("ps", bu)])
                                si = sgrot.next()
                                A("act", lambda e, si=si, bg=bg: e.activation(out=sg[si][:], in_=ps[bg][:], func=AF.Silu),
                                  r=[("ps", bg)], w=[("sg", si)])
                                A("dve", lambda e, si=si, bu=bu, f=f, sub=sub: e.tensor_tensor(
                                    out=act[:, f, sub * 512:(sub + 1) * 512], in0=sg[si][:], in1=ps[bu][:], op=ALU.mult),
                                  r=[("sg", si), ("ps", bu)], w=[("act", f, sub)])
                    descs = []
                    for d2 in range(4):
                        for half in range(2):
                            descs.append((wd[l, d2, half], 11, 256))
                    it = stream(wp, descs)
                    for d2 in range(4):
                        tl0, tk0 = next(it)
                        tl1, tk1 = next(it)
                        for di in range(2):
                            d = 2 * d2 + di
                            for sub in range(2):
                                b = srot.next()
                                c = tp * 2 + sub
                                for fc in range(NFC):
                                    tl, tk = (tl0, tk0) if fc < 11 else (tl1, tk1)
                                    A("pe", lambda e, fc=fc, tl=tl, di=di, sub=sub, b=b: e.matmul(
                                        ps[b][:], lhsT=tl[:, fc % 11, di * 128:(di + 1) * 128], rhs=act[:, fc, sub * 512:(sub + 1) * 512],
                                        start=(fc == 0), stop=(fc == NFC - 1)), r=[tk, ("act", fc, sub)], w=[("ps", b)])
                                A("dve", lambda e, d=d, c=c, b=b: e.scalar_tensor_tensor(
                                    out=h[:, d, c * 512:(c + 1) * 512], in0=ps[b][:], scalar=0.5, in1=h[:, d, c * 512:(c + 1) * 512],
                                    op0=ALU.mult, op1=ALU.add), r=[("ps", b), ("h", d, c)], w=[("h", d, c)])

        def ple(l, gi):
            with ExitStack() as s2:
                f2sb = mk_sb(s2)
                hn = f2sb("hnp", [128, NKC, T], BF16)
                ptb = f2sb("ptb", [128, 2, T], BF16)
                sig = [f2sb(f"sig{i}", [128, 512], F32) for i in range(2)]
                tmp = [f2sb(f"ptmp{i}", [128, 512], F32) for i in range(2)]
                wp = WPool(s2, 2048, 3, 4, "p")
                srot = Rot(range(7))
                r2 = Rot([0, 1])
                for c in range(NTC):
                    norm_chunk(gi, c, lambda k, c=c: hn[:, k, c * 512:(c + 1) * 512], lambda k, c=c: ("hnp", k, c), srot)
                    wp.load_into((pTd[l][:, c * 512:(c + 1) * 512].rearrange("(a p) t -> p a t", p=128), 2, 512),
                                 ptb[:, :, c * 512:(c + 1) * 512], ("ptb", c))
                descs = []
                for d2 in range(4):
                    descs.append((ple_gate[l, d2], 8, 256))
                    descs.append((ple_proj[l, d2], 2, 256))
                it = stream(wp, descs)
                for d2 in range(4):
                    gt, gk = next(it)
                    pt, pk = next(it)
                    for di in range(2):
                        d = 2 * d2 + di
                        for c in range(NTC):
                            bg = srot.next()
                            bp = srot.next()
                            for k in range(NKC):
                                A("pe", lambda e, k=k, bg=bg, gt=gt, di=di, c=c: e.matmul(
                                    ps[bg][:], lhsT=gt[:, k, di * 128:(di + 1) * 128], rhs=hn[:, k, c * 512:(c + 1) * 512],
                                    start=(k == 0), stop=(k == NKC - 1)), r=[gk, ("hnp", k, c)], w=[("ps", bg)])
                            for k in range(2):
                                A("pe", lambda e, k=k, bp=bp, pt=pt, di=di, c=c: e.matmul(
                                    ps[bp][:], lhsT=pt[:, k, di * 128:(di + 1) * 128], rhs=ptb[:, k, c * 512:(c + 1) * 512],
                                    start=(k == 0), stop=(k == 1)), r=[pk, ("ptb", c)], w=[("ps", bp)])
                            si = r2.next()
                            A("act", lambda e, si=si, bg=bg: e.activation(out=sig[si][:], in_=ps[bg][:], func=AF.Sigmoid),
                              r=[("ps", bg)], w=[("sig", si)])
                            A("dve", lambda e, si=si, bp=bp: e.tensor_tensor(out=tmp[si][:], in0=sig[si][:], in1=ps[bp][:], op=ALU.mult),
                              r=[("sig", si), ("ps", bp)], w=[("ptmp", si)])
                            A("pool", lambda e, si=si, d=d, c=c: e.tensor_tensor(
                                out=h[:, d, c * 512:(c + 1) * 512], in0=h[:, d, c * 512:(c + 1) * 512], in1=tmp[si][:], op=ALU.add),
                              r=[("ptmp", si), ("h", d, c)], w=[("h", d, c)])
            P.barrier()

        def final_norm():
            with ExitStack() as s2:
                f2sb = mk_sb(s2)
                ot = [f2sb(f"ot{i}", [128, 512], F32) for i in range(4)]
                srot = Rot(range(7))
                for c in range(NTC):
                    def after(k, c=c):
                        i = (c * 8 + k) % 4
                        DMA(outT[k * 128:(k + 1) * 128, c * 512:(c + 1) * 512], ot[i][:], r=[("ot", i)])
                    norm_chunk(8, c, lambda k, c=c: ot[(c * 8 + k) % 4][:], lambda k, c=c: ("ot", (c * 8 + k) % 4), srot,
                               after=after, engs=("dve",))
            P.barrier()

        def mixer_proj(gi, fm_items, tm_items):
            import os as _os
            _nfm = int(_os.environ.get("PROJ_FM", "99"))
            _ntm = int(_os.environ.get("PROJ_TM", "99"))
            fm_items = fm_items[:_nfm]
            tm_items = tm_items[:_ntm]
            with ExitStack() as s2:
                f2sb = mk_sb(s2)
                hnc = f2sb("hnc", [128, NKC, 512], BF16)
                ropc = [f2sb(f"ropc{i}", [128, 2, 512], F32) for i in range(2)]
                xb = [f2sb(f"xb{i}", [128, 512], BF16) for i in range(2)]
                t1 = [f2sb(f"t1{i}", [128, 512], F32) for i in range(2)]
                t2 = [f2sb(f"t2{i}", [128, 512], F32) for i in range(2)]
                wp = WPool(s2, 1024, 2, 3, "m")
                srot = Rot(range(7))
                r2 = Rot([0, 1])
                for c in range(NTC):
                    norm_chunk(gi, c, lambda k: hnc[:, k, :], lambda k: ("hnc", k), srot)
                    rc = c % 2
                    DMA(ropc[rc][:], rope_d[:, :, c * 512:(c + 1) * 512].rearrange("a p t -> p a t"), w=[("ropc", rc)])
                    descs = [(w_ap, 8, 128) for (w_ap, _) in fm_items]
                    descs += [(w_ap, 8, n) for (w_ap, n, _, _) in tm_items]
                    it = stream(wp, descs)
                    for (_, dst_fn) in fm_items:
                        wt, wk = next(it)
                        b = srot.next()
                        for k in range(NKC):
                            A("pe", lambda e, k=k, b=b, wt=wt: e.matmul(ps[b][:], lhsT=wt[:, k, :], rhs=hnc[:, k, :],
                                                                        start=(k == 0), stop=(k == NKC - 1)),
                              r=[wk, ("hnc", k)], w=[("ps", b)])
                        i = r2.next()
                        A("act", lambda e, i=i, b=b: e.activation(out=xb[i][:], in_=ps[b][:], func=AF.Copy),
                          r=[("ps", b)], w=[("xb", i)])
                        A("dve", lambda e, i=i, b=b, rc=rc: e.tensor_tensor(out=t1[i][:], in0=ps[b][:], in1=ropc[rc][:, 0, :], op=ALU.mult),
                          r=[("ps", b), ("ropc", rc), ("xb", i)], w=[("t1", i)])
                        b2 = srot.next()
                        A("pe", lambda e, i=i, b2=b2: e.matmul(ps[b2][:], lhsT=perm, rhs=xb[i][:], start=True, stop=True),
                          r=[("xb", i), "mats"], w=[("ps", b2)])
                        A("dve", lambda e, i=i, b2=b2, rc=rc: e.tensor_tensor(out=t2[i][:], in0=ps[b2][:], in1=ropc[rc][:, 1, :], op=ALU.mult),
                          r=[("ps", b2), ("ropc", rc)], w=[("t2", i)])
                        dst, dkey = dst_fn(c)
                        A("pool", lambda e, i=i, dst=dst: e.tensor_tensor(out=dst, in0=t1[i][:], in1=t2[i][:], op=ALU.add),
                          r=[("t1", i), ("t2", i)], w=[dkey])
                    for (_, n, dst_fn, _) in tm_items:
                        wt, wk = next(it)
                        for tb in range(4):
                            b = srot.next()
                            for k in range(NKC):
                                A("pe", lambda e, k=k, b=b, wt=wt, tb=tb, n=n: e.matmul(
                                    ps[b][:, 0:n], lhsT=hnc[:, k, tb * 128:(tb + 1) * 128], rhs=wt[:, k, :],
                                    start=(k == 0), stop=(k == NKC - 1)), r=[wk, ("hnc", k)], w=[("ps", b)])
                            dst, dkey = dst_fn(c * 4 + tb)
                            A("act", lambda e, b=b, dst=dst, n=n: e.activation(out=dst, in_=ps[b][:, 0:n], func=AF.Copy),
                              r=[("ps", b)], w=[dkey])
            P.barrier()

        LOOK = 2

        def run_units(units, front, back):
            fr = []
            for idx, u in enumerate(units):
                fr.append(front(u))
                if idx >= LOOK:
                    back(units[idx - LOOK], fr[idx - LOOK])
            for idx in range(max(0, len(units) - LOOK), len(units)):
                back(units[idx], fr[idx])

        def attn_jobs(jobs, accrot, srot, tiles):
            ebs, pbs, rds, erot, prot, rrot, mengs = tiles
            units = []
            for job in jobs:
                c = job[0]
                accs = accrot.next()
                last = 4 * c + 3
                for j in range(last + 1):
                    for e_ in range(2):
                        units.append((job, accs, j, e_, j == last and e_ == 1))

            def front(u):
                (c, Q, qc, qkey, K, kc, kkey, vfn, vkey, mfn, mkey, dst, dkey), accs, j, e_, is_last = u
                t0 = max(0, j * 128 - c * 512)
                n = 512 - t0
                lo, hi = e_ * 64, (e_ + 1) * 64
                b = srot.next()
                A("pe", lambda e: e.matmul(
                    ps[b][:, 0:n], lhsT=K[lo:hi, kc, j * 128:(j + 1) * 128], rhs=Q[lo:hi, qc, c * 512 + t0:(c + 1) * 512],
                    start=True, stop=True), r=[qkey, kkey], w=[("ps", b)])
                ei = erot.next()
                A("act", lambda e: e.activation(out=ebs[ei][:, 0:n], in_=ps[b][:, 0:n], func=AF.Exp, scale=0.125),
                  r=[("ps", b)], w=[("eb", ei)])
                m = mfn(j, t0, n)
                if m is not None:
                    pi = prot.next()
                    A(mengs.next(), lambda e: e.tensor_tensor(out=pbs[pi][:, 0:n], in0=ebs[ei][:, 0:n], in1=m, op=ALU.mult),
                      r=[("eb", ei), mkey], w=[("pb", pi)])
                    return pbs[pi], ("pb", pi)
                return ebs[ei], ("eb", ei)

            def back(u, fr):
                (c, Q, qc, qkey, K, kc, kkey, vfn, vkey, mfn, mkey, dst, dkey), (nb, db), j, e_, is_last = u
                src, skey = fr
                last = 4 * c + 3
                t0 = max(0, j * 128 - c * 512)
                n = 512 - t0
                lo, hi = e_ * 64, (e_ + 1) * 64
                v = vfn(j, e_)
                A("pe", lambda e: e.matmul(
                    ps[nb][lo:hi, t0:512], lhsT=v, rhs=src[:, 0:n], start=(j == 0), stop=(j == last), tile_position=(0, lo)),
                  r=[skey, vkey], w=[("ps", nb)])
                A("pe", lambda e: e.matmul(
                    ps[db][lo:hi, t0:512], lhsT=ones[:, 0:64], rhs=src[:, 0:n], start=(j == 0), stop=(j == last), tile_position=(0, lo)),
                  r=[skey, "mats"], w=[("ps", db)])
                if is_last:
                    ri = rrot.next()
                    A("dve", lambda e: e.reciprocal(out=rds[ri][:], in_=ps[db][:]), r=[("ps", db)], w=[("rd", ri)])
                    A("dve", lambda e: e.tensor_tensor(out=dst, in0=ps[nb][:], in1=rds[ri][:], op=ALU.mult),
                      r=[("ps", nb), ("rd", ri)], w=[dkey])

            run_units(units, front, back)

        def attn_tiles(f2sb, mengs=("dve", "pool")):
            ebs = [f2sb(f"eb{i}", [128, 512], BF16) for i in range(4)]
            pbs = [f2sb(f"pb{i}", [128, 512], BF16) for i in range(4)]
            rds = [f2sb(f"rd{i}", [128, 512], F32) for i in range(2)]
            return (ebs, pbs, rds, Rot(range(4)), Rot(range(4)), Rot(range(2)), Rot(mengs))

        def apply_wout(w2d, merged):
            with ExitStack() as s2:
                wp = WPool(s2, 2048, 3, 4, "o")
                srot = Rot(range(7))
                descs = [(w2d[d2], 8, 256) for d2 in range(4)]
                it = stream(wp, descs)
                for d2 in range(4):
                    wt, wk = next(it)
                    for di in range(2):
                        d = 2 * d2 + di
                        for c in range(NTC):
                            b = srot.next()
                            for k in range(NKC):
                                A("pe", lambda e, k=k, b=b, wt=wt, di=di, c=c: e.matmul(
                                    ps[b][:], lhsT=wt[:, k, di * 128:(di + 1) * 128], rhs=merged[:, k, c * 512:(c + 1) * 512],
                                    start=(k == 0), stop=(k == NKC - 1)), r=[wk, ("mg", k, c)], w=[("ps", b)])
                            A("dve", lambda e, b=b, d=d, c=c: e.tensor_tensor(
                                out=h[:, d, c * 512:(c + 1) * 512], in0=ps[b][:], in1=h[:, d, c * 512:(c + 1) * 512], op=ALU.add),
                              r=[("ps", b), ("h", d, c)], w=[("h", d, c)])
            P.barrier()

        def even_mixer(gi):
            with ExitStack() as sm:
                msb = mk_sb(sm)
                merged = msb("merged", [128, NKC, T], BF16)
                A("dve", lambda e: e.tensor_tensor(out=lv[:, 0, :], in0=lv[:, 0, :], in1=lv[:, 1, :], op=ALU.mult), r=["lv"], w=["lv"])
                A("dve", lambda e: e.tensor_tensor(out=lv[:, 2, :], in0=lv[:, 2, :], in1=lv[:, 3, :], op=ALU.mult), r=["lv"], w=["lv"])
                A("dve", lambda e: e.reduce_sum(out=lsm[:, 0:1], in_=lv[:, 0, :], axis=AX.X), r=["lv"], w=["lsm"])
                A("dve", lambda e: e.reduce_sum(out=lsm[:, 1:2], in_=lv[:, 2, :], axis=AX.X), r=["lv"], w=["lsm"])
                A("act", lambda e: e.activation(out=lsm[:, 2:4], in_=lsm[:, 0:2], func=AF.Exp), r=["lsm"], w=["lsm"])
                A("dve", lambda e: e.tensor_tensor(out=lsm[:, 4:5], in0=lsm[:, 3:4], in1=lsm[:, 2:3], op=ALU.subtract), r=["lsm"], w=["lsm"])
                A("dve", lambda e: e.tensor_scalar(out=lsm[:, 5:6], in0=lsm[:, 4:5], scalar1=-0.2, scalar2=None, op0=ALU.add), r=["lsm"], w=["lsm"])
                A("dve", lambda e: e.tensor_scalar(out=lsm[:, 6:7], in0=lsm[:, 7:8], scalar1=0.8, scalar2=None, op0=ALU.mult), r=["lsm", "lsm7"], w=["lsm"])
                neglam = lsm[:, 5:6]
                gsub = lsm[:, 6:7]

                for kk in range(NKC):
                    if (kk < 4 and not cfg["dsa"]) or (kk >= 4 and not cfg["diff"]):
                        A("pool", lambda e, kk=kk: e.memset(merged[:, kk, :], 0.0), w=[("mg", kk, c) for c in range(NTC)])
                if cfg["dsa"]:
                    with ExitStack() as sa:
                        asb = mk_sb(sa)
                        QA = asb("QA", [128, 4, T], BF16)
                        KA = asb("KA", [128, 1, T], BF16)
                        QI = asb("QI", [128, 4, T], BF16)
                        KI = asb("KI", [128, 1, T], BF16)
                        VA = asb("VA", [128, NTB, 64], BF16)
                        WI = asb("WI", [128, NTB, 8], F32)
                        fm = []
                        bufs = [(QA, n, "QA") for n in range(4)] + [(KA, 0, "KA")] + [(QI, n, "QI") for n in range(4)] + [(KI, 0, "KI")]
                        for ci, (buf, n, nm) in enumerate(bufs):
                            fm.append((ew_fm[ci],
                                       lambda c, buf=buf, n=n, nm=nm: (buf[:, n, c * 512:(c + 1) * 512], (nm, n, c))))
                        tm = [(ew_va[0], 64, lambda blk: (VA[:, blk, :], ("VA", blk)), None),
                              (ew_wi[0], 8, lambda blk: (WI[:, blk, :], ("WI", blk)), None)]
                        mixer_proj(gi, fm, tm)
                        with ExitStack() as s2:
                            f2sb = mk_sb(s2)
                            Ib = [f2sb(f"Ib{i}", [128, T], F32) for i in range(2)]
                            rb = [f2sb(f"rb{i}", [128, 512], F32) for i in range(3)]
                            mk = f2sb("mk", [128, T], BF16)
                            junk = f2sb("junk", [128, T], BF16)
                            maskT = f2sb("maskT", [128, NTB, 512], BF16)
                            bnd = [f2sb(f"bnd{i}", [128, 8], F32) for i in range(2)]
                            gem = [f2sb(f"gem{i}", [128, 2], mybir.dt.uint32) for i in range(2)]
                            tiles = attn_tiles(f2sb, mengs=("dve",))
                            srot = Rot([0, 1, 2])
                            rrot = Rot(range(3))
                            accrot = Rot([(3, 4), (5, 6)])
                            NIT = 18

                            def s1(i):
                                c = i // 4
                                cols = (i + 1) * 128
                                I = Ib[i % 2]
                                ik = ("I", i % 2)
                                bd = bnd[i % 2]
                                bk = ("bnd", i % 2)
                                for sc in range((cols + 511) // 512):
                                    w_ = min(512, cols - sc * 512)
                                    for hd in range(8):
                                        lo, hi = (hd % 2) * 64, (hd % 2 + 1) * 64
                                        b = srot.next()
                                        A("pe", lambda e, b=b, lo=lo, hi=hi, hd=hd, sc=sc, w_=w_: e.matmul(
                                            ps[b][:, 0:w_], lhsT=QI[lo:hi, hd // 2, i * 128:(i + 1) * 128],
                                            rhs=KI[lo:hi, 0, sc * 512:sc * 512 + w_], start=True, stop=True),
                                          r=[("QI", hd // 2, c), ("KI", 0, sc)], w=[("ps", b)])
                                        ri = rrot.next()
                                        A("act", lambda e, b=b, ri=ri, w_=w_: e.activation(out=rb[ri][:, 0:w_], in_=ps[b][:, 0:w_], func=AF.Relu),
                                          r=[("ps", b)], w=[("rb", ri)])
                                        if hd == 0:
                                            A("dve", lambda e, ri=ri, sc=sc, w_=w_: e.tensor_scalar(
                                                out=I[:, sc * 512:sc * 512 + w_], in0=rb[ri][:, 0:w_], scalar1=WI[:, i, 0:1], scalar2=None, op0=ALU.mult),
                                              r=[("rb", ri), ("WI", i)], w=[ik])
                                        else:
                                            A("dve", lambda e, ri=ri, sc=sc, w_=w_, hd=hd: e.scalar_tensor_tensor(
                                                out=I[:, sc * 512:sc * 512 + w_], in0=rb[ri][:, 0:w_], scalar=WI[:, i, hd:hd + 1],
                                                in1=I[:, sc * 512:sc * 512 + w_], op0=ALU.mult, op1=ALU.add),
                                              r=[("rb", ri), ("WI", i), ik], w=[ik])
                                if i >= 2:
                                    A("dve", lambda e: e.tensor_reduce(out=bd[:, 0:1], in_=I[:, 0:cols], axis=AX.X, op=ALU.min), r=[ik], w=[bk])
                                    A("dve", lambda e: e.tensor_reduce(out=bd[:, 1:2], in_=I[:, 0:cols], axis=AX.X, op=ALU.max), r=[ik], w=[bk])
                                    A("dve", lambda e: e.tensor_tensor(out=bd[:, 2:3], in0=bd[:, 1:2], in1=bd[:, 0:1], op=ALU.subtract), r=[bk], w=[bk])
                                A("pool", lambda e: e.tensor_tensor(out=I[:, i * 128:(i + 1) * 128], in0=I[:, i * 128:(i + 1) * 128],
                                                                    in1=negm[:], op=ALU.add), r=[ik, "negm"], w=[ik])

                            def s2(i):
                                ti = i % 4
                                cols = (i + 1) * 128
                                I = Ib[i % 2]
                                ik = ("I", i % 2)
                                bd = bnd[i % 2]
                                bk = ("bnd", i % 2)
                                gm = gem[i % 2]
                                gk = ("gem", i % 2)
                                if i >= 2:
                                    for it in range(NIT):
                                        A("dve", lambda e, it=it: e.scalar_tensor_tensor(out=bd[:, 3:4], in0=bd[:, 2:3], scalar=float(2.0 ** -(it + 1)),
                                                                                        in1=bd[:, 0:1], op0=ALU.mult, op1=ALU.add), r=[bk], w=[bk])
                                        A("dve", lambda e: e.tensor_scalar(out=junk[:, 0:cols], in0=I[:, 0:cols], scalar1=bd[:, 3:4], scalar2=None,
                                                                           op0=ALU.is_ge, op1=ALU.add, accum_out=bd[:, 4:5]), r=[ik, bk], w=[bk, "junk"])
                                        A("dve", lambda e: e.tensor_single_scalar(out=gm[:, 0:1], in_=bd[:, 4:5], scalar=255.5, op=ALU.is_ge), r=[bk], w=[gk])
                                        A("dve", lambda e: e.copy_predicated(out=bd[:, 0:1], mask=gm[:, 0:1], data=bd[:, 3:4]), r=[bk, gk], w=[bk])
                                    A("dve", lambda e: e.tensor_scalar(out=mk[:, 0:cols], in0=I[:, 0:cols], scalar1=bd[:, 0:1], scalar2=None, op0=ALU.is_ge),
                                      r=[ik, bk], w=["mk"])
                                else:
                                    A("dve", lambda e: e.tensor_single_scalar(out=mk[:, 0:cols], in_=I[:, 0:cols], scalar=-1.0e29, op=ALU.is_ge), r=[ik], w=["mk"])
                                for j0 in range(0, i + 1, 8):
                                    nn = min(8, i + 1 - j0)
                                    for jj in range(nn):
                                        A("pe", lambda e, jj=jj, j0=j0: e.transpose(
                                            out=pTp[:, jj * 128:(jj + 1) * 128],
                                            in_=mk[:, (j0 + jj) * 128:(j0 + jj + 1) * 128], identity=ident),
                                          r=["mk", "mats"], w=["pT"])
                                    A("act", lambda e, nn=nn, j0=j0: e.activation(
                                        out=maskT[:, j0:j0 + nn, ti * 128:(ti + 1) * 128],
                                        in_=pTp[:, 0:nn * 128].rearrange("p (a b) -> p a b", a=nn), func=AF.Copy),
                                      r=["pT"], w=["maskT"])

                            s1(0)
                            for i in range(NTB):
                                if i + 1 < NTB:
                                    s1(i + 1)
                                s2(i)
                                if i % 4 == 3:
                                    c = i // 4
                                    attn_jobs([(c, QA, hp, ("QA", hp, c), KA, 0, ("KA", 0, c),
                                                lambda j, e_: VA[:, j, :], ("VA", 0),
                                                lambda j, t0, n: maskT[:, j, t0:512], "maskT",
                                                merged[:, hp, c * 512:(c + 1) * 512], ("mg", hp, c)) for hp in range(4)],
                                              accrot, srot, tiles)
                        P.barrier()

                if cfg["diff"]:
                    with ExitStack() as sa:
                        asb = mk_sb(sa)
                        QB = asb("QB", [128, 4, T], BF16)
                        KB = asb("KB", [128, 4, T], BF16)
                        VB = asb("VB", [128, NTB, 512], BF16)
                        fm = []
                        bufs = [(QB, n, "QB") for n in range(4)] + [(KB, n, "KB") for n in range(4)]
                        for ci, (buf, n, nm) in enumerate(bufs):
                            fm.append((ew_fm[10 + ci],
                                       lambda c, buf=buf, n=n, nm=nm: (buf[:, n, c * 512:(c + 1) * 512], (nm, n, c))))
                        tm = [(ew_vb[n], 128,
                               lambda blk, n=n: (VB[:, blk, n * 128:(n + 1) * 128], ("VB", blk, n)), None) for n in range(4)]
                        mixer_proj(gi, fm, tm)
                        with ExitStack() as s2:
                            f2sb = mk_sb(s2)
                            strip = f2sb("cstrip", [128, STRIP_W], BF16)
                            DMA(strip[:], strips_d[0], w=["strip"])
                            ebs = [f2sb(f"eb{i}", [128, 512], BF16) for i in range(3)]
                            pbs = [f2sb(f"pb{i}", [128, 512], BF16) for i in range(3)]
                            fa = [f2sb(f"fa{i}", [128, 512], F32) for i in range(4)]
                            ob = f2sb("ob", [128, 512], F32)
                            sq2 = f2sb("sq2", [128, 512], BF16)
                            ebs = ebs + [f2sb("eb3", [128, 512], BF16)]
                            pbs = pbs + [f2sb("pb3", [128, 512], BF16)]
                            erot, prot = Rot(range(4)), Rot(range(4))
                            srot = Rot([0, 1, 2])
                            units = []
                            for hd in range(4):
                                for c in range(NTC):
                                    last = 4 * c + 3
                                    for j in range(last + 1):
                                        for e_ in range(2):
                                            units.append((hd, c, j, e_, j == last and e_ == 1))

                            def dfront(u):
                                hd, c, j, e_, is_last = u
                                t0 = max(0, j * 128 - c * 512)
                                n = 512 - t0
                                lo, hi = e_ * 64, (e_ + 1) * 64
                                b = srot.next()
                                A("pe", lambda e: e.matmul(
                                    ps[b][:, 0:n], lhsT=KB[lo:hi, hd, j * 128:(j + 1) * 128],
                                    rhs=QB[lo:hi, hd, c * 512 + t0:(c + 1) * 512], start=True, stop=True),
                                  r=[("QB", hd, c), ("KB", hd, j // 4)], w=[("ps", b)])
                                ei = erot.next()
                                A("act", lambda e: e.activation(out=ebs[ei][:, 0:n], in_=ps[b][:, 0:n], func=AF.Exp, scale=0.125),
                                  r=[("ps", b)], w=[("eb", ei)])
                                if j >= 4 * c:
                                    off = c * 512 - j * 128 + 384 + t0
                                    pi = prot.next()
                                    A("dve", lambda e: e.tensor_tensor(
                                        out=pbs[pi][:, 0:n], in0=ebs[ei][:, 0:n], in1=strip[:, off:off + n], op=ALU.mult),
                                      r=[("eb", ei), "strip"], w=[("pb", pi)])
                                    return pbs[pi], ("pb", pi)
                                return ebs[ei], ("eb", ei)

                            def dback(u, fr):
                                hd, c, j, e_, is_last = u
                                src, skey = fr
                                last = 4 * c + 3
                                t0 = max(0, j * 128 - c * 512)
                                n = 512 - t0
                                nb, db = 3 + 2 * e_, 4 + 2 * e_
                                A("pe", lambda e: e.matmul(
                                    ps[nb][:, t0:512], lhsT=VB[:, j, hd * 128:(hd + 1) * 128], rhs=src[:, 0:n],
                                    start=(j == 0), stop=(j == last)),
                                  r=[skey, ("VB", j, hd)], w=[("ps", nb)])
                                A("pe", lambda e: e.matmul(
                                    ps[db][:, t0:512], lhsT=ones, rhs=src[:, 0:n], start=(j == 0), stop=(j == last)),
                                  r=[skey, "mats"], w=[("ps", db)])
                                if not is_last:
                                    return
                                A("dve", lambda e: e.reciprocal(out=fa[0][:], in_=ps[4][:]), r=[("ps", 4)], w=[("fa", 0)])
                                A("dve", lambda e: e.tensor_tensor(out=fa[1][:], in0=ps[3][:], in1=fa[0][:], op=ALU.mult),
                                  r=[("ps", 3), ("fa", 0)], w=[("fa", 1)])
                                A("dve", lambda e: e.reciprocal(out=fa[2][:], in_=ps[6][:]), r=[("ps", 6)], w=[("fa", 2)])
                                A("dve", lambda e: e.tensor_tensor(out=fa[3][:], in0=ps[5][:], in1=fa[2][:], op=ALU.mult),
                                  r=[("ps", 5), ("fa", 2)], w=[("fa", 3)])
                                A("dve", lambda e: e.scalar_tensor_tensor(out=ob[:], in0=fa[3][:], scalar=neglam, in1=fa[1][:],
                                                                          op0=ALU.mult, op1=ALU.add),
                                  r=[("fa", 3), ("fa", 1), "lsm"], w=["ob"])
                                A("act", lambda e: e.activation(out=sq2[:], in_=ob[:], func=AF.Square, scale=float(128.0 ** -0.5)),
                                  r=["ob"], w=["sq2"])
                                b = srot.next()
                                A("pe", lambda e: e.matmul(ps[b][:], lhsT=ones, rhs=sq2[:], start=True, stop=True),
                                  r=["sq2", "mats"], w=[("ps", b)])
                                A("act", lambda e: e.activation(out=fa[0][:], in_=ps[b][:], func=AF.Sqrt, bias=epsb[:, 1:2]),
                                  r=[("ps", b), "epsb"], w=[("fa", 0)])
                                A("dve", lambda e: e.reciprocal(out=fa[2][:], in_=fa[0][:]), r=[("fa", 0)], w=[("fa", 2)])
                                A("dve", lambda e: e.scalar_tensor_tensor(
                                    out=merged[:, 4 + hd, c * 512:(c + 1) * 512], in0=ob[:], scalar=gsub, in1=fa[2][:],
                                    op0=ALU.mult, op1=ALU.mult), r=["ob", ("fa", 2), "lsm"], w=[("mg", 4 + hd, c)])

                            run_units(units, dfront, dback)
                        P.barrier()
                apply_wout(even_w_out, merged)

        def odd_mixer(gi):
            with ExitStack() as sm:
                msb = mk_sb(sm)
                merged = msb("mergedo", [128, NKC, T], BF16)
                for g in range(2):
                    odd_group(g, gi, merged)
                apply_wout(odd_w_out, merged)

        def odd_group(g, gi, merged):
            if True:
                if True:
                    with ExitStack() as sa:
                        asb = mk_sb(sa)
                        QG = asb("QG", [128, 4, T], BF16)
                        KG = asb("KG", [128, 4, T], BF16)
                        VG = asb("VG", [128, NTB, 512], BF16)
                        fm = []
                        for n in range(4):
                            fm.append((odd_w_in[4 * g + n],
                                       lambda c, n=n: (QG[:, n, c * 512:(c + 1) * 512], ("QG", n, c))))
                        for n in range(4):
                            fm.append((odd_w_in[8 + 4 * g + n],
                                       lambda c, n=n: (KG[:, n, c * 512:(c + 1) * 512], ("KG", n, c))))
                        tm = [(odd_w_in[16 + 4 * g + n], 128,
                               lambda blk, n=n: (VG[:, blk, n * 128:(n + 1) * 128], ("VG", blk, n)), None) for n in range(4)]
                        if cfg["proj"]:
                            mixer_proj(gi, fm, tm)
                        if not cfg["attn"]:
                            for hp in range(4):
                                A("pool", lambda e, hp=hp: e.memset(merged[:, 4 * g + hp, :], 0.0), w=[("mg", 4 * g + hp, c) for c in range(NTC)])
                            return
                        with ExitStack() as s2:
                            f2sb = mk_sb(s2)
                            strip = f2sb("mstrip", [128, STRIP_W], BF16)
                            DMA(strip[:], strips_d[1], w=["strip"])
                            tiles = attn_tiles(f2sb, mengs=("dve",))
                            srot = Rot([0, 1, 2])
                            accrot = Rot([(3, 4), (5, 6)])
                            jobs = []
                            for hp in range(4):
                                for c in range(NTC):
                                    jobs.append((c, QG, hp, ("QG", hp, c), KG, hp, ("KG", hp, c),
                                                 lambda j, e_, hp=hp: VG[:, j, (2 * hp + e_) * 64:(2 * hp + e_ + 1) * 64], ("VG", 0, 0),
                                                 lambda j, t0, n, c=c: strip[:, c * 512 - j * 128 + 384 + t0:c * 512 - j * 128 + 384 + t0 + n], "strip",
                                                 merged[:, 4 * g + hp, c * 512:(c + 1) * 512], ("mg", 4 * g + hp, c)))
                            attn_jobs(jobs, accrot, srot, tiles)
                        P.barrier()

        for l in range(cfg["layers"]):
            if cfg["ffn"]:
                ffn(l, ffn_w["ffn_a_wg"], ffn_w["ffn_a_wu"], ffn_w["ffn_a_wd"], l * 4 + 0)
            if l % 2 == 0:
                if cfg["mix_even"]:
                    even_mixer(l * 4 + 1)
            else:
                if cfg["mix_odd"]:
                    odd_mixer(l * 4 + 1)
            if cfg["ffn"]:
                ffn(l, ffn_w["ffn_b_wg"], ffn_w["ffn_b_wu"], ffn_w["ffn_b_wd"], l * 4 + 2)
            if cfg["ple"]:
                ple(l, l * 4 + 3)
        final_norm()
        P.emit(st)
    nc._declared_inputs = declared
    return nc


def prep_inputs(inputs, b):
    f = np.float32
    g = lambda k: np.asarray(inputs[k], f)
    rope, strips, mats, negmask = _host_consts()
    gains = []
    for l in range(2):
        for nm in ("norm_ffn_a", "norm_mix", "norm_ffn_b", "norm_ple"):
            gains.append(_col(g(nm)[l]))
    gains.append(_col(g("final_norm")))
    gains = np.ascontiguousarray(np.concatenate(gains, axis=1))
    ewi = g("even_w_in")[0]
    qa, ka, va = ewi[:, 0:512], ewi[:, 512:576], ewi[:, 576:640]
    qi, ki, wi = ewi[:, 640:1152], ewi[:, 1152:1216], ewi[:, 1216:1224]
    qb, kb, vb = ewi[:, 1224:1736], ewi[:, 1736:2248], ewi[:, 2248:2760]
    ew_fm = np.ascontiguousarray(np.concatenate([qa, ka, ka, qi, ki, ki, qb, kb], axis=1))
    def tile_w(W, tc):
        R_, C_ = W.shape
        return np.ascontiguousarray(W.reshape(R_ // 128, 128, C_ // tc, tc).transpose(2, 1, 0, 3))

    m = {
        "xT": np.ascontiguousarray(g("x")[b].T),
        "pT": np.ascontiguousarray(np.transpose(g("p")[:, b], (0, 2, 1))),
        "gains": gains,
        "ple_gate": np.stack([tile_w(g("ple_gate")[l], 256) for l in range(2)], 0),
        "ple_proj": np.stack([tile_w(g("ple_proj")[l], 256) for l in range(2)], 0),
        "ew_fm": tile_w(ew_fm, 128), "ew_va": tile_w(np.ascontiguousarray(va), 64),
        "ew_wi": tile_w(np.ascontiguousarray(wi), 8), "ew_vb": tile_w(np.ascontiguousarray(vb), 128),
        "even_w_out": tile_w(g("even_w_out")[0], 256),
        "lamv": np.ascontiguousarray(np.stack([g("diff_lambda_q1")[0], g("diff_lambda_k1")[0],
                                               g("diff_lambda_q2")[0], g("diff_lambda_k2")[0]], 0)),
        "subln": np.ascontiguousarray(g("diff_subln")[0].reshape(128, 1)),
        "odd_w_in": tile_w(g("odd_w_in")[0], 128), "odd_w_out": tile_w(g("odd_w_out")[0], 256),
        "rope": rope, "strips": strips, "mats": mats, "negmask": negmask,
    }
    for nm in ("ffn_a_wg", "ffn_a_wu", "ffn_b_wg", "ffn_b_wu"):
        m[nm] = np.stack([tile_w(g(nm)[l], 256) for l in range(2)], 0)
    for nm in ("ffn_a_wd", "ffn_b_wd"):
        w = g(nm)
        m[nm] = np.stack([np.stack([np.stack([tile_w(w[l][hf * 1408:(hf + 1) * 1408, d2 * 256:(d2 + 1) * 256], 256)[0]
                                              for hf in range(2)], 0) for d2 in range(4)], 0) for l in range(2)], 0)
    return m


_NC_CACHE = {}


def kernel(**inputs):
    import time as _time
    _t0 = _time.time()
    key = tuple(sorted(CFG.items()))
    if key not in _NC_CACHE:
        _NC_CACHE[key] = build_program(CFG)
    nc = _NC_CACHE[key]
    print(f"[kernel] build {_time.time() - _t0:.1f}s", flush=True)
    n = 8
    shared = prep_inputs(inputs, 0)
    in_maps = []
    for b in range(n):
        m = dict(shared)
        m["xT"] = np.ascontiguousarray(np.asarray(inputs["x"], np.float32)[b].T)
        m["pT"] = np.ascontiguousarray(np.transpose(np.asarray(inputs["p"], np.float32)[:, b], (0, 2, 1)))
        in_maps.append({k: v for k, v in m.items() if k in nc._declared_inputs})
    print(f"[kernel] prep {_time.time() - _t0:.1f}s", flush=True)
    import os as _os
    _nd = int(_os.environ.get("DBG_CORES", "8"))
    if _nd != 8:
        res = run_bass_kernel_spmd(nc, in_maps[:_nd], core_ids=list(range(_nd)))
        res.results.extend([res.results[0]] * (8 - _nd))
    else:
        res = run_bass_kernel_spmd(nc, in_maps, core_ids=list(range(n)))
    print(f"[kernel] ran {_time.time() - _t0:.1f}s", flush=True)
    out = np.stack([np.asarray(res.results[b]["outT"], np.float32).T for b in range(n)], 0)
    return np.ascontiguousarray(out)
```

```python
import numpy as np
import ml_dtypes
import concourse.bass as bass
import concourse.mybir as mybir
from concourse.bass_utils import run_bass_kernel_spmd

F32 = mybir.dt.float32
BF16 = mybir.dt.bfloat16
ALU = mybir.AluOpType
AF = mybir.ActivationFunctionType
AX = mybir.AxisListType

T = 2048
D = 1024
DFF = 2816
NKC = D // 128
NFC = DFF // 128
NTB = T // 128
NTC = T // 512
PLE = 256
EVEN_IN = 2760
NEG = -1.0e30
NEG2 = -2.0e30


class Op:
    __slots__ = ("eng", "fn", "waits", "signal", "sem", "val", "is_dma", "idx", "is_bar")

    def __init__(self, eng, fn, is_dma=False):
        self.eng = eng
        self.fn = fn
        self.waits = []
        self.signal = False
        self.sem = None
        self.val = 0
        self.is_dma = is_dma
        self.idx = -1
        self.is_bar = False


ENGS = ("pe", "act", "dve", "pool", "sp")
SEM_LIM = 30000


class Prog:
    def __init__(self, nc):
        self.nc = nc
        self.ops = {e: [] for e in ENGS}
        self.last_w = {}
        self.readers = {}
        self.seen = {e: {s: -1 for s in ENGS} for e in ENGS}
        self.seen_dma = {e: set() for e in ENGS}
        self.pending_dma = []
        self.dma_list = []
        self.nops = 0

    NDMA = 24

    def add(self, eng, fn, reads=(), writes=(), dma=False):
        op = Op(eng, fn, is_dma=dma)
        deps = []
        if dma:
            k = len(self.dma_list)
            if k >= self.NDMA:
                deps.append((self.dma_list[k - self.NDMA], True))
            self.dma_list.append(op)
        for r in reads:
            w = self.last_w.get(r)
            if w is not None:
                deps.append((w, True))
            if r == "pT" or (isinstance(r, tuple) and r[0] == "ps"):
                for rd in self.readers.get(r, ()):
                    if rd.eng != eng:
                        deps.append((rd, True))
        for r in writes:
            w = self.last_w.get(r)
            if w is not None:
                deps.append((w, True))
            for rd in self.readers.get(r, ()):
                deps.append((rd, False))
        best = {}
        for d, is_raw in deps:
            if d.is_dma:
                self._dep(op, d, is_raw)
                continue
            if d.eng == eng and eng in ("pe", "sp"):
                continue
            cur = best.get(d.eng)
            if cur is None or d.idx > cur.idx:
                best[d.eng] = d
        for d in best.values():
            self._dep(op, d, True)
        op.idx = len(self.ops[eng])
        self.ops[eng].append(op)
        for r in reads:
            self.readers.setdefault(r, []).append(op)
        for r in writes:
            self.last_w[r] = op
            self.readers[r] = []
        if dma:
            self.pending_dma.append(op)
        self.nops += 1
        return op

    def _dep(self, op, d, is_raw):
        e = op.eng
        if d.is_dma:
            if d in self.seen_dma[e]:
                return
            self.seen_dma[e].add(d)
            d.signal = True
            op.waits.append(d)
            return
        if d.eng == e:
            if e == "pe" or e == "sp":
                return
        if self.seen[e][d.eng] >= d.idx:
            return
        self.seen[e][d.eng] = d.idx
        d.signal = True
        op.waits.append(d)

    def barrier(self):
        bar = Op("sp", None)
        bar.is_bar = True
        for e in ENGS:
            if e == "sp":
                continue
            if self.ops[e]:
                self._dep(bar, self.ops[e][-1], True)
        for d in self.pending_dma:
            self._dep(bar, d, True)
        self.pending_dma = []
        bar.idx = len(self.ops["sp"])
        self.ops["sp"].append(bar)
        bar.signal = True
        for e in ENGS:
            if e == "sp":
                continue
            w = Op(e, None)
            w.is_bar = True
            w.waits.append(bar)
            self.seen[e]["sp"] = bar.idx
            w.idx = len(self.ops[e])
            self.ops[e].append(w)
        self.last_w = {}
        self.readers = {}

    def emit(self, stack):
        nc = self.nc
        ndma = self.NDMA
        dsems = [stack.enter_context(nc.semaphore(f"s_dma_{i}")) for i in range(ndma)]
        dcnt = [0] * ndma
        for k, op in enumerate(self.dma_list):
            di = k % ndma
            if dcnt[di] + 16 > SEM_LIM:
                dsems[di] = stack.enter_context(nc.semaphore(f"s_dma_{di}_{k}"))
                dcnt[di] = 0
            dcnt[di] += 16
            op.sem = dsems[di]
            op.val = dcnt[di]
        eng_sems = {}
        for e in ENGS:
            cur = None
            cnt = 0
            for op in self.ops[e]:
                if not op.signal or op.is_dma:
                    continue
                if cur is None or cnt >= SEM_LIM:
                    cur = stack.enter_context(nc.semaphore(f"s_{e}_{len(eng_sems)}"))
                    eng_sems[(e, len(eng_sems))] = cur
                    cnt = 0
                cnt += 1
                op.sem = cur
                op.val = cnt
        block = stack.enter_context(nc.Block())

        def run(e):
            def body(eng):
                for op in self.ops[e]:
                    for d in op.waits:
                        eng.wait_ge(d.sem, d.val)
                    if op.fn is None:
                        if op.signal:
                            eng.sem_inc(op.sem, 1)
                        continue
                    ins = op.fn(eng)
                    if op.is_dma:
                        ins.then_inc(op.sem, 16)
                    elif op.signal:
                        ins.then_inc(op.sem, 1)
            return body

        block.tensor(run("pe"))
        block.scalar(run("act"))
        block.vector(run("dve"))
        block.gpsimd(run("pool"))
        block.sync(run("sp"))


STRIP_W = 2432


def _host_consts():
    f32 = np.float32
    inv = (1.0 / (f32(10000.0) ** (np.arange(0, 64, 2, dtype=f32) / f32(64)))).astype(f32)
    ang = (np.arange(T, dtype=f32)[:, None] * inv[None, :]).astype(f32)
    cos = np.cos(ang).astype(f32)
    sin = np.sin(ang).astype(f32)
    p = np.arange(128)
    d = p % 64
    j = d % 32
    C2 = cos[:, j].T.copy()
    S2 = (sin[:, j].T * np.where(d < 32, -1.0, 1.0)[:, None]).astype(f32)
    rope = np.ascontiguousarray(np.stack([C2, S2], 0)).astype(f32)
    x = np.arange(STRIP_W)[None, :] - 384 - p[:, None]
    causal = (x >= 0).astype(f32)
    mult = (((x >= 0) & (x <= 128)).astype(f32)
            + ((x >= 0) & (x % 4 == 0) & (x <= 512)).astype(f32)
            + ((x >= 0) & (x % 16 == 0) & (x <= 2048)).astype(f32))
    strips = np.stack([causal, mult], 0).astype(ml_dtypes.bfloat16)
    ident = np.eye(128, dtype=f32)
    perm = np.zeros((128, 128), f32)
    for m in range(128):
        perm[m ^ 32, m] = 1.0
    ones = np.ones((128, 128), f32)
    mats = np.stack([ident, perm, ones], 0).astype(ml_dtypes.bfloat16)
    tt = np.arange(128)
    negmask = np.where(tt[None, :] > tt[:, None], np.float32(NEG), np.float32(0.0)).astype(f32)
    return rope, strips, mats, negmask


def _col(v):
    v = np.asarray(v, np.float32)
    return np.ascontiguousarray(v.reshape(-1, 128).T)


CFG = {"layers": 2, "mix_even": True, "mix_odd": True, "dsa": True, "diff": True, "ffn": True, "ple": True, "proj": True, "attn": True}


class Rot:
    def __init__(self, items):
        self.items = list(items)
        self.i = 0

    def next(self):
        v = self.items[self.i % len(self.items)]
        self.i += 1
        return v


def build_program(cfg=None):
    from contextlib import ExitStack
    cfg = dict(CFG if cfg is None else cfg)
    nc = bass.Bass("TRN2", target_bir_lowering=False)

    declared = []

    def din(name, shape, dt=F32, need=True):
        if not need:
            return None
        declared.append(name)
        return nc.dram_tensor(name, list(shape), dt, kind="ExternalInput").ap()

    xT = din("xT", [D, T])
    pTd = din("pT", [2, PLE, T])
    gains_d = din("gains", [128, 72])
    ffn_w = {}
    for nm in ("ffn_a_wg", "ffn_a_wu", "ffn_b_wg", "ffn_b_wu"):
        ffn_w[nm] = din(nm, [2, 11, 128, 8, 256], need=cfg["ffn"])
    for nm in ("ffn_a_wd", "ffn_b_wd"):
        ffn_w[nm] = din(nm, [2, 4, 2, 128, 11, 256], need=cfg["ffn"])
    ple_gate = din("ple_gate", [2, 4, 128, 8, 256], need=cfg["ple"])
    ple_proj = din("ple_proj", [2, 4, 128, 2, 256], need=cfg["ple"])
    ew_fm = din("ew_fm", [18, 128, 8, 128])
    ew_va = din("ew_va", [1, 128, 8, 64])
    ew_wi = din("ew_wi", [1, 128, 8, 8])
    ew_vb = din("ew_vb", [4, 128, 8, 128])
    even_w_out = din("even_w_out", [4, 128, 8, 256])
    lamv = din("lamv", [4, 64])
    subln_d = din("subln", [128, 1])
    odd_w_in = din("odd_w_in", [24, 128, 8, 128])
    odd_w_out = din("odd_w_out", [4, 128, 8, 256])
    rope_d = din("rope", [2, 128, T])
    strips_d = din("strips", [2, 128, STRIP_W], BF16)
    mats_d = din("mats", [3, 128, 128], BF16)
    negm_d = din("negmask", [128, 128])
    outT = nc.dram_tensor("outT", [D, T], F32, kind="ExternalOutput").ap()

    uid = [0]

    with ExitStack() as st:
        def mk_sb(stack):
            def f(name, shape, dt):
                uid[0] += 1
                return stack.enter_context(nc.sbuf_tensor(f"{name}_{uid[0]}", list(shape), dt))
            return f

        sb = mk_sb(st)
        h = sb("h", [128, NKC, T], F32)
        gsb = sb("gsb", [128, 72], F32)
        mats = sb("mats", [128, 3, 128], BF16)
        negm = sb("negm", [128, 128], F32)
        epsb = sb("epsb", [128, 2], F32)
        lv = sb("lv", [128, 4, 64], F32)
        lsm = sb("lsm", [128, 8], F32)
        sqb = [sb(f"sqb{i}", [128, 512], BF16) for i in range(2)]
        sdt = sb("sdt", [128, 512], F32)
        rstd = sb("rstd", [128, 512], F32)
        ps = [st.enter_context(nc.psum_tensor(f"ps{i}", [128, 512], F32)) for i in range(7)]
        pTp = st.enter_context(nc.psum_tensor("pTp", [128, 1024], BF16))
        ident = mats[:, 0, :]
        perm = mats[:, 1, :]
        ones = mats[:, 2, :]

        P = Prog(nc)

        def A(eng, fn, r=(), w=()):
            return P.add(eng, fn, reads=r, writes=w)

        def DMA(out, in_, r=(), w=()):
            return P.add("sp", lambda e: e.dma_start(out=out, in_=in_), reads=r, writes=w, dma=True)

        for k in range(NKC):
            DMA(h[:, k, :], xT[k * 128:(k + 1) * 128, :], w=[("h", k, c) for c in range(NTC)])
        DMA(gsb[:], gains_d, w=["gsb"])
        DMA(mats[:], mats_d.rearrange("a p n -> p a n"), w=["mats"])
        DMA(negm[:], negm_d, w=["negm"])
        DMA(lv[:], lamv.partition_broadcast(128), w=["lv"])
        DMA(lsm[:, 7:8], subln_d, w=["lsm7"])
        A("pool", lambda e: e.memset(epsb[:, 0:1], 1e-6), w=["epsb"])
        A("pool", lambda e: e.memset(epsb[:, 1:2], 1e-5), w=["epsb"])

        sqrot = Rot([0, 1])
        cvt_rot = Rot(["act", "dve"])

        class WPool:
            def __init__(self, stack, elems, nst, nbf, tag):
                f = mk_sb(stack)
                self.stg = [f(f"wst{tag}{i}", [128, elems], F32) for i in range(nst)]
                self.bf = [f(f"wbf{tag}{i}", [128, elems], BF16) for i in range(nbf)]
                self.i = 0
                uid[0] += 1
                self.tag = f"{tag}{uid[0]}"

            def load(self, desc):
                src, a, b = desc
                i = self.i
                self.i += 1
                si, bi = i % len(self.stg), i % len(self.bf)
                sv = self.stg[si][:, 0:a * b].rearrange("p (a b) -> p a b", a=a)
                bv = self.bf[bi][:, 0:a * b].rearrange("p (a b) -> p a b", a=a)
                sk, bk = (self.tag, "st", si), (self.tag, "bf", bi)
                DMA(sv, src, w=[sk])
                self._cvt(bv, sv, sk, bk)
                return bv, bk

            def _cvt(self, dst, src, sk, dk):
                eng = cvt_rot.next()
                if eng == "act":
                    A("act", lambda e: e.activation(out=dst, in_=src, func=AF.Copy), r=[sk], w=[dk])
                else:
                    A(eng, lambda e: e.tensor_copy(out=dst, in_=src), r=[sk], w=[dk])

            def load_into(self, desc, dst, dkey):
                src, a, b = desc
                i = self.i
                self.i += 1
                si = i % len(self.stg)
                sv = self.stg[si][:, 0:a * b].rearrange("p (a b) -> p a b", a=a)
                sk = (self.tag, "st", si)
                DMA(sv, src, w=[sk])
                self._cvt(dst, sv, sk, dkey)

        def stream(wp, descs, depth=2):
            q = []
            n = len(descs)
            nxt = 0
            for i in range(n):
                while nxt < n and nxt <= i + depth:
                    q.append(wp.load(descs[nxt]))
                    nxt += 1
                yield q[i]

        def wdesc(w2d, r0, nr, c0, ncol):
            return (w2d[r0:r0 + nr, c0:c0 + ncol].rearrange("(a p) n -> p a n", p=128), nr // 128, ncol)

        def norm_chunk(gi, c, dst_fn, dkey_fn, srot, after=None, engs=("dve",)):
            b = srot.next()
            for k in range(NKC):
                si = sqrot.next()
                A("act", lambda e, k=k, si=si: e.activation(out=sqb[si][:], in_=h[:, k, c * 512:(c + 1) * 512],
                                                            func=AF.Square, scale=1.0 / 32.0),
                  r=[("h", k, c)], w=[("sqb", si)])
                A("pe", lambda e, k=k, si=si: e.matmul(ps[b][:], lhsT=ones, rhs=sqb[si][:], start=(k == 0), stop=(k == NKC - 1)),
                  r=[("sqb", si), "mats"], w=[("ps", b)])
            A("act", lambda e: e.activation(out=sdt[:], in_=ps[b][:], func=AF.Sqrt, bias=epsb[:, 0:1]),
              r=[("ps", b), "epsb"], w=["sdt"])
            A("dve", lambda e: e.reciprocal(out=rstd[:], in_=sdt[:]), r=["sdt"], w=["rstd"])
            for k in range(NKC):
                eng = engs[k % len(engs)]
                dst = dst_fn(k)
                A(eng, lambda e, k=k, dst=dst: e.scalar_tensor_tensor(out=dst, in0=h[:, k, c * 512:(c + 1) * 512],
                                                                      scalar=gsb[:, gi * 8 + k:gi * 8 + k + 1], in1=rstd[:],
                                                                      op0=ALU.mult, op1=ALU.mult),
                  r=[("h", k, c), "gsb", "rstd"], w=[dkey_fn(k)])
                if after is not None:
                    after(k)

        def ffn(l, wg, wu, wd, gi):
            for tp in range(2):
                ffn_pass(l, wg, wu, wd, gi, tp)
                P.barrier()

        def ffn_pass(l, wg, wu, wd, gi, tp):
            if True:
                with ExitStack() as s2:
                    f2sb = mk_sb(s2)
                    hn = f2sb("hn", [128, NKC, 1024], BF16)
                    act = f2sb("act", [128, NFC, 1024], BF16)
                    sg = [f2sb(f"sg{i}", [128, 512], F32) for i in range(2)]
                    wp = WPool(s2, 2816, 3, 4, "f")
                    srot = Rot(range(7))
                    sgrot = Rot([0, 1])
                    for sub in range(2):
                        norm_chunk(gi, tp * 2 + sub, lambda k, sub=sub: hn[:, k, sub * 512:(sub + 1) * 512],
                                   lambda k, sub=sub: ("hn", k, sub), srot)
                    descs = []
                    for f2 in range(NFC // 2):
                        descs.append((wg[l, f2], 8, 256))
                        descs.append((wu[l, f2], 8, 256))
                    it = stream(wp, descs)
                    for f2 in range(NFC // 2):
                        gt, gk = next(it)
                        ut, uk = next(it)
                        for fi in range(2):
                            f = 2 * f2 + fi
                            for sub in range(2):
                                bg = srot.next()
                                bu = srot.next()
                                for k in range(NKC):
                                    A("pe", lambda e, k=k, bg=bg, gt=gt, fi=fi, sub=sub: e.matmul(
                                        ps[bg][:], lhsT=gt[:, k, fi * 128:(fi + 1) * 128], rhs=hn[:, k, sub * 512:(sub + 1) * 512],
                                        start=(k == 0), stop=(k == NKC - 1)), r=[gk, ("hn", k, sub)], w=[("ps", bg)])
                                for k in range(NKC):
                                    A("pe", lambda e, k=k, bu=bu, ut=ut, fi=fi, sub=sub: e.matmul(
                                        ps[bu][:], lhsT=ut[:, k, fi * 128:(fi + 1) * 128], rhs=hn[:, k, sub * 512:(sub + 1) * 512],
                                        start=(k == 0), stop=(k == NKC - 1)), r=[uk, ("hn", k, sub)], w=[("ps", bu)])
                                si = sgrot.next()
                                A("act", lambda e, si=si, bg=bg: e.activation(out=sg[si][:], in_=ps[bg][:], func=AF.Silu),
                                  r=[("ps", bg)], w=[("sg", si)])
                                A("dve", lambda e, si=si, bu=bu, f=f, sub=sub: e.tensor_tensor(
                                    out=act[:, f, sub * 512:(sub + 1) * 512], in0=sg[si][:], in1=ps[bu][:], op=ALU.mult),
                                  r=[("sg", si), ("ps", bu)], w=[("act", f, sub)])
                    descs = []
                    for d2 in range(4):
                        for half in range(2):
                            descs.append((wd[l, d2, half], 11, 256))
                    it = stream(wp, descs)
                    for d2 in range(4):
                        tl0, tk0 = next(it)
                        tl1, tk1 = next(it)
                        for di in range(2):
                            d = 2 * d2 + di
                            for sub in range(2):
                                b = srot.next()
                                c = tp * 2 + sub
                                for fc in range(NFC):
                                    tl, tk = (tl0, tk0) if fc < 11 else (tl1, tk1)
                                    A("pe", lambda e, fc=fc, tl=tl, di=di, sub=sub, b=b: e.matmul(
                                        ps[b][:], lhsT=tl[:, fc % 11, di * 128:(di + 1) * 128], rhs=act[:, fc, sub * 512:(sub + 1) * 512],
                                        start=(fc == 0), stop=(fc == NFC - 1)), r=[tk, ("act", fc, sub)], w=[("ps", b)])
                                A("dve", lambda e, d=d, c=c, b=b: e.scalar_tensor_tensor(
                                    out=h[:, d, c * 512:(c + 1) * 512], in0=ps[b][:], scalar=0.5, in1=h[:, d, c * 512:(c + 1) * 512],
                                    op0=ALU.mult, op1=ALU.add), r=[("ps", b), ("h", d, c)], w=[("h", d, c)])

        def ple(l, gi):
            with ExitStack() as s2:
                f2sb = mk_sb(s2)
                hn = f2sb("hnp", [128, NKC, T], BF16)
                ptb = f2sb("ptb", [128, 2, T], BF16)
                sig = [f2sb(f"sig{i}", [128, 512], F32) for i in range(2)]
                tmp = [f2sb(f"ptmp{i}", [128, 512], F32) for i in range(2)]
                wp = WPool(s2, 2048, 3, 4, "p")
                srot = Rot(range(7))
                r2 = Rot([0, 1])
                for c in range(NTC):
                    norm_chunk(gi, c, lambda k, c=c: hn[:, k, c * 512:(c + 1) * 512], lambda k, c=c: ("hnp", k, c), srot)
                    wp.load_into((pTd[l][:, c * 512:(c + 1) * 512].rearrange("(a p) t -> p a t", p=128), 2, 512),
                                 ptb[:, :, c * 512:(c + 1) * 512], ("ptb", c))
                descs = []
                for d2 in range(4):
                    descs.append((ple_gate[l, d2], 8, 256))
                    descs.append((ple_proj[l, d2], 2, 256))
                it = stream(wp, descs)
                for d2 in range(4):
                    gt, gk = next(it)
                    pt, pk = next(it)
                    for di in range(2):
                        d = 2 * d2 + di
                        for c in range(NTC):
                            bg = srot.next()
                            bp = srot.next()
                            for k in range(NKC):
                                A("pe", lambda e, k=k, bg=bg, gt=gt, di=di, c=c: e.matmul(
                                    ps[bg][:], lhsT=gt[:, k, di * 128:(di + 1) * 128], rhs=hn[:, k, c * 512:(c + 1) * 512],
                                    start=(k == 0), stop=(k == NKC - 1)), r=[gk, ("hnp", k, c)], w=[("ps", bg)])
                            for k in range(2):
                                A("pe", lambda e, k=k, bp=bp, pt=pt, di=di, c=c: e.matmul(
                                    ps[bp][:], lhsT=pt[:, k, di * 128:(di + 1) * 128], rhs=ptb[:, k, c * 512:(c + 1) * 512],
                                    start=(k == 0), stop=(k == 1)), r=[pk, ("ptb", c)], w=[("ps", bp)])
                            si = r2.next()
                            A("act", lambda e, si=si, bg=bg: e.activation(out=sig[si][:], in_=ps[bg][:], func=AF.Sigmoid),
                              r=[("ps", bg)], w=[("sig", si)])
                            A("dve", lambda e, si=si, bp=bp: e.tensor_tensor(out=tmp[si][:], in0=sig[si][:], in1=ps[bp][:], op=ALU.mult),
                              r=[("sig", si), ("ps", bp)], w=[("ptmp", si)])
                            A("pool", lambda e, si=si, d=d, c=c: e.tensor_tensor(
                                out=h[:, d, c * 512:(c + 1) * 512], in0=h[:, d, c * 512:(c + 1) * 512], in1=tmp[si][:], op=ALU.add),
                              r=[("ptmp", si), ("h", d, c)], w=[("h", d, c)])
            P.barrier()

        def final_norm():
            with ExitStack() as s2:
                f2sb = mk_sb(s2)
                ot = [f2sb(f"ot{i}", [128, 512], F32) for i in range(4)]
                srot = Rot(range(7))
                for c in range(NTC):
                    def after(k, c=c):
                        i = (c * 8 + k) % 4
                        DMA(outT[k * 128:(k + 1) * 128, c * 512:(c + 1) * 512], ot[i][:], r=[("ot", i)])
                    norm_chunk(8, c, lambda k, c=c: ot[(c * 8 + k) % 4][:], lambda k, c=c: ("ot", (c * 8 + k) % 4), srot,
                               after=after, engs=("dve",))
            P.barrier()

        def mixer_proj(gi, fm_items, tm_items):
            import os as _os
            _nfm = int(_os.environ.get("PROJ_FM", "99"))
            _ntm = int(_os.environ.get("PROJ_TM", "99"))
            fm_items = fm_items[:_nfm]
            tm_items = tm_items[:_ntm]
            with ExitStack() as s2:
                f2sb = mk_sb(s2)
                hnc = f2sb("hnc", [128, NKC, 512], BF16)
                ropc = [f2sb(f"ropc{i}", [128, 2, 512], F32) for i in range(2)]
                xb = [f2sb(f"xb{i}", [128, 512], BF16) for i in range(2)]
                t1 = [f2sb(f"t1{i}", [128, 512], F32) for i in range(2)]
                t2 = [f2sb(f"t2{i}", [128, 512], F32) for i in range(2)]
                wp = WPool(s2, 1024, 2, 3, "m")
                srot = Rot(range(7))
                r2 = Rot([0, 1])
                for c in range(NTC):
                    norm_chunk(gi, c, lambda k: hnc[:, k, :], lambda k: ("hnc", k), srot)
                    rc = c % 2
                    DMA(ropc[rc][:], rope_d[:, :, c * 512:(c + 1) * 512].rearrange("a p t -> p a t"), w=[("ropc", rc)])
                    descs = [(w_ap, 8, 128) for (w_ap, _) in fm_items]
                    descs += [(w_ap, 8, n) for (w_ap, n, _, _) in tm_items]
                    it = stream(wp, descs)
                    for (_, dst_fn) in fm_items:
                        wt, wk = next(it)
                        b = srot.next()
                        for k in range(NKC):
                            A("pe", lambda e, k=k, b=b, wt=wt: e.matmul(ps[b][:], lhsT=wt[:, k, :], rhs=hnc[:, k, :],
                                                                        start=(k == 0), stop=(k == NKC - 1)),
                              r=[wk, ("hnc", k)], w=[("ps", b)])
                        i = r2.next()
                        A("act", lambda e, i=i, b=b: e.activation(out=xb[i][:], in_=ps[b][:], func=AF.Copy),
                          r=[("ps", b)], w=[("xb", i)])
                        A("dve", lambda e, i=i, b=b, rc=rc: e.tensor_tensor(out=t1[i][:], in0=ps[b][:], in1=ropc[rc][:, 0, :], op=ALU.mult),
                          r=[("ps", b), ("ropc", rc), ("xb", i)], w=[("t1", i)])
                        b2 = srot.next()
                        A("pe", lambda e, i=i, b2=b2: e.matmul(ps[b2][:], lhsT=perm, rhs=xb[i][:], start=True, stop=True),
                          r=[("xb", i), "mats"], w=[("ps", b2)])
                        A("dve", lambda e, i=i, b2=b2, rc=rc: e.tensor_tensor(out=t2[i][:], in0=ps[b2][:], in1=ropc[rc][:, 1, :], op=ALU.mult),
                          r=[("ps", b2), ("ropc", rc)], w=[("t2", i)])
                        dst, dkey = dst_fn(c)
                        A("pool", lambda e, i=i, dst=dst: e.tensor_tensor(out=dst, in0=t1[i][:], in1=t2[i][:], op=ALU.add),
                          r=[("t1", i), ("t2", i)], w=[dkey])
                    for (_, n, dst_fn, _) in tm_items:
                        wt, wk = next(it)
                        for tb in range(4):
                            b = srot.next()
                            for k in range(NKC):
                                A("pe", lambda e, k=k, b=b, wt=wt, tb=tb, n=n: e.matmul(
                                    ps[b][:, 0:n], lhsT=hnc[:, k, tb * 128:(tb + 1) * 128], rhs=wt[:, k, :],
                                    start=(k == 0), stop=(k == NKC - 1)), r=[wk, ("hnc", k)], w=[("ps", b)])
                            dst, dkey = dst_fn(c * 4 + tb)
                            if isinstance(dst, list):
                                for (dap, lo_, hi_) in dst:
                                    A("act", lambda e, b=b, dap=dap, lo_=lo_, hi_=hi_: e.activation(out=dap, in_=ps[b][:, lo_:hi_], func=AF.Copy),
                                      r=[("ps", b)], w=[dkey])
                            else:
                                A("act", lambda e, b=b, dst=dst, n=n: e.activation(out=dst, in_=ps[b][:, 0:n], func=AF.Copy),
                                  r=[("ps", b)], w=[dkey])
            P.barrier()

        LOOK = 2

        def run_units(units, front, back):
            fr = []
            for idx, u in enumerate(units):
                fr.append(front(u))
                if idx >= LOOK:
                    back(units[idx - LOOK], fr[idx - LOOK])
            for idx in range(max(0, len(units) - LOOK), len(units)):
                back(units[idx], fr[idx])

        def attn_jobs(jobs, accrot, srot, tiles):
            ebs, pbs, rds, erot, prot, rrot, mengs = tiles
            units = []
            for job in jobs:
                c = job[0]
                accs = accrot.next()
                last = 4 * c + 3
                for j in range(last + 1):
                    for e_ in range(2):
                        units.append((job, accs, j, e_, j == last and e_ == 1))

            def front(u):
                (c, Q, qc, qkey, K, kc, kkey, vfn, vkey, mfn, mkey, dst, dkey), accs, j, e_, is_last = u
                t0 = max(0, j * 128 - c * 512)
                n = 512 - t0
                lo, hi = e_ * 64, (e_ + 1) * 64
                b = srot.next()
                A("pe", lambda e: e.matmul(
                    ps[b][:, 0:n], lhsT=K[lo:hi, kc, j * 128:(j + 1) * 128], rhs=Q[lo:hi, qc, c * 512 + t0:(c + 1) * 512],
                    start=True, stop=True), r=[qkey, kkey], w=[("ps", b)])
                ei = erot.next()
                A("act", lambda e: e.activation(out=ebs[ei][:, 0:n], in_=ps[b][:, 0:n], func=AF.Exp, scale=0.125),
                  r=[("ps", b)], w=[("eb", ei)])
                m = mfn(j, t0, n)
                if m is not None:
                    pi = prot.next()
                    A(mengs.next(), lambda e: e.tensor_tensor(out=pbs[pi][:, 0:n], in0=ebs[ei][:, 0:n], in1=m, op=ALU.mult),
                      r=[("eb", ei), mkey], w=[("pb", pi)])
                    return pbs[pi], ("pb", pi)
                return ebs[ei], ("eb", ei)

            def back(u, fr):
                (c, Q, qc, qkey, K, kc, kkey, vfn, vkey, mfn, mkey, dst, dkey), (ba, bb), j, e_, is_last = u
                src, skey = fr
                last = 4 * c + 3
                t0 = max(0, j * 128 - c * 512)
                n = 512 - t0
                bk_ = ba if e_ == 0 else bb
                v = vfn(j, e_)
                A("pe", lambda e: e.matmul(ps[bk_][:, t0:512], lhsT=v, rhs=src[:, 0:n], start=(j == 0), stop=(j == last)),
                  r=[skey, vkey], w=[("ps", bk_)])
                if is_last:
                    di = rrot.next()
                    dsh = rds[di]
                    A("act", lambda e: e.activation(out=dsh[0:64, :], in_=ps[ba][64:128, :], func=AF.Copy), r=[("ps", ba)], w=[("rd", di)])
                    A("act", lambda e: e.activation(out=dsh[64:128, :], in_=ps[bb][0:64, :], func=AF.Copy), r=[("ps", bb)], w=[("rd", di)])
                    A("dve", lambda e: e.reciprocal(out=dsh[:], in_=dsh[:]), r=[("rd", di)], w=[("rd", di)])
                    A("dve", lambda e: e.tensor_tensor(out=dst[0:64, :], in0=ps[ba][0:64, :], in1=dsh[0:64, :], op=ALU.mult),
                      r=[("ps", ba), ("rd", di)], w=[dkey])
                    A("dve", lambda e: e.tensor_tensor(out=dst[64:128, :], in0=ps[bb][64:128, :], in1=dsh[64:128, :], op=ALU.mult),
                      r=[("ps", bb), ("rd", di)], w=[dkey])

            run_units(units, front, back)

        def attn_tiles(f2sb, mengs=("dve", "pool")):
            ebs = [f2sb(f"eb{i}", [128, 512], BF16) for i in range(4)]
            pbs = [f2sb(f"pb{i}", [128, 512], BF16) for i in range(4)]
            rds = [f2sb(f"rd{i}", [128, 512], F32) for i in range(2)]
            return (ebs, pbs, rds, Rot(range(4)), Rot(range(4)), Rot(range(2)), Rot(mengs))

        def apply_wout(w2d, merged):
            with ExitStack() as s2:
                wp = WPool(s2, 2048, 3, 4, "o")
                srot = Rot(range(7))
                descs = [(w2d[d2], 8, 256) for d2 in range(4)]
                it = stream(wp, descs)
                for d2 in range(4):
                    wt, wk = next(it)
                    for di in range(2):
                        d = 2 * d2 + di
                        for c in range(NTC):
                            b = srot.next()
                            for k in range(NKC):
                                A("pe", lambda e, k=k, b=b, wt=wt, di=di, c=c: e.matmul(
                                    ps[b][:], lhsT=wt[:, k, di * 128:(di + 1) * 128], rhs=merged[:, k, c * 512:(c + 1) * 512],
                                    start=(k == 0), stop=(k == NKC - 1)), r=[wk, ("mg", k, c)], w=[("ps", b)])
                            A("dve", lambda e, b=b, d=d, c=c: e.tensor_tensor(
                                out=h[:, d, c * 512:(c + 1) * 512], in0=ps[b][:], in1=h[:, d, c * 512:(c + 1) * 512], op=ALU.add),
                              r=[("ps", b), ("h", d, c)], w=[("h", d, c)])
            P.barrier()

        def even_mixer(gi):
            with ExitStack() as sm:
                msb = mk_sb(sm)
                merged = msb("merged", [128, NKC, T], BF16)
                A("dve", lambda e: e.tensor_tensor(out=lv[:, 0, :], in0=lv[:, 0, :], in1=lv[:, 1, :], op=ALU.mult), r=["lv"], w=["lv"])
                A("dve", lambda e: e.tensor_tensor(out=lv[:, 2, :], in0=lv[:, 2, :], in1=lv[:, 3, :], op=ALU.mult), r=["lv"], w=["lv"])
                A("dve", lambda e: e.reduce_sum(out=lsm[:, 0:1], in_=lv[:, 0, :], axis=AX.X), r=["lv"], w=["lsm"])
                A("dve", lambda e: e.reduce_sum(out=lsm[:, 1:2], in_=lv[:, 2, :], axis=AX.X), r=["lv"], w=["lsm"])
                A("act", lambda e: e.activation(out=lsm[:, 2:4], in_=lsm[:, 0:2], func=AF.Exp), r=["lsm"], w=["lsm"])
                A("dve", lambda e: e.tensor_tensor(out=lsm[:, 4:5], in0=lsm[:, 3:4], in1=lsm[:, 2:3], op=ALU.subtract), r=["lsm"], w=["lsm"])
                A("dve", lambda e: e.tensor_scalar(out=lsm[:, 5:6], in0=lsm[:, 4:5], scalar1=-0.2, scalar2=None, op0=ALU.add), r=["lsm"], w=["lsm"])
                A("dve", lambda e: e.tensor_scalar(out=lsm[:, 6:7], in0=lsm[:, 7:8], scalar1=0.8, scalar2=None, op0=ALU.mult), r=["lsm", "lsm7"], w=["lsm"])
                neglam = lsm[:, 5:6]
                gsub = lsm[:, 6:7]

                for kk in range(NKC):
                    if (kk < 4 and not cfg["dsa"]) or (kk >= 4 and not cfg["diff"]):
                        A("pool", lambda e, kk=kk: e.memset(merged[:, kk, :], 0.0), w=[("mg", kk, c) for c in range(NTC)])
                if cfg["dsa"]:
                    with ExitStack() as sa:
                        asb = mk_sb(sa)
                        QA = asb("QA", [128, 4, T], BF16)
                        KA = asb("KA", [128, 1, T], BF16)
                        QI = asb("QI", [128, 4, T], BF16)
                        KI = asb("KI", [128, 1, T], BF16)
                        VA = asb("VA", [128, NTB, 192], BF16)
                        A("pool", lambda e: e.memset(VA[:, :, 64:128], 1.0), w=[("VA1",)])
                        WI = asb("WI", [128, NTB, 8], F32)
                        fm = []
                        bufs = [(QA, n, "QA") for n in range(4)] + [(KA, 0, "KA")] + [(QI, n, "QI") for n in range(4)] + [(KI, 0, "KI")]
                        for ci, (buf, n, nm) in enumerate(bufs):
                            fm.append((ew_fm[ci],
                                       lambda c, buf=buf, n=n, nm=nm: (buf[:, n, c * 512:(c + 1) * 512], (nm, n, c))))
                        tm = [(ew_va[0], 64, lambda blk: ([(VA[:, blk, 0:64], 0, 64), (VA[:, blk, 128:192], 0, 64)], ("VA", blk)), None),
                              (ew_wi[0], 8, lambda blk: (WI[:, blk, :], ("WI", blk)), None)]
                        mixer_proj(gi, fm, tm)
                        with ExitStack() as s2:
                            f2sb = mk_sb(s2)
                            Ib = [f2sb(f"Ib{i}", [128, T], F32) for i in range(2)]
                            rb = [f2sb(f"rb{i}", [128, 512], F32) for i in range(2)]
                            mk = f2sb("mk", [128, T], BF16)
                            junk = mk
                            maskT = f2sb("maskT", [128, NTB, 512], BF16)
                            bnd = [f2sb(f"bnd{i}", [128, 8], F32) for i in range(2)]
                            gem = [f2sb(f"gem{i}", [128, 2], mybir.dt.uint32) for i in range(2)]
                            tiles = attn_tiles(f2sb, mengs=("dve",))
                            srot = Rot([0, 1, 2])
                            rrot = Rot(range(2))
                            accrot = Rot([(3, 4), (5, 6)])
                            NIT = 18

                            def s1(i):
                                c = i // 4
                                cols = (i + 1) * 128
                                I = Ib[i % 2]
                                ik = ("I", i % 2)
                                bd = bnd[i % 2]
                                bk = ("bnd", i % 2)
                                for sc in range((cols + 511) // 512):
                                    w_ = min(512, cols - sc * 512)
                                    for hd in range(8):
                                        lo, hi = (hd % 2) * 64, (hd % 2 + 1) * 64
                                        b = srot.next()
                                        A("pe", lambda e, b=b, lo=lo, hi=hi, hd=hd, sc=sc, w_=w_: e.matmul(
                                            ps[b][:, 0:w_], lhsT=QI[lo:hi, hd // 2, i * 128:(i + 1) * 128],
                                            rhs=KI[lo:hi, 0, sc * 512:sc * 512 + w_], start=True, stop=True),
                                          r=[("QI", hd // 2, c), ("KI", 0, sc)], w=[("ps", b)])
                                        ri = rrot.next()
                                        A("act", lambda e, b=b, ri=ri, w_=w_: e.activation(out=rb[ri][:, 0:w_], in_=ps[b][:, 0:w_], func=AF.Relu),
                                          r=[("ps", b)], w=[("rb", ri)])
                                        if hd == 0:
                                            A("dve", lambda e, ri=ri, sc=sc, w_=w_: e.tensor_scalar(
                                                out=I[:, sc * 512:sc * 512 + w_], in0=rb[ri][:, 0:w_], scalar1=WI[:, i, 0:1], scalar2=None, op0=ALU.mult),
                                              r=[("rb", ri), ("WI", i)], w=[ik])
                                        else:
                                            A("dve", lambda e, ri=ri, sc=sc, w_=w_, hd=hd: e.scalar_tensor_tensor(
                                                out=I[:, sc * 512:sc * 512 + w_], in0=rb[ri][:, 0:w_], scalar=WI[:, i, hd:hd + 1],
                                                in1=I[:, sc * 512:sc * 512 + w_], op0=ALU.mult, op1=ALU.add),
                                              r=[("rb", ri), ("WI", i), ik], w=[ik])
                                if i >= 2:
                                    A("dve", lambda e: e.tensor_reduce(out=bd[:, 0:1], in_=I[:, 0:cols], axis=AX.X, op=ALU.min), r=[ik], w=[bk])
                                    A("dve", lambda e: e.tensor_reduce(out=bd[:, 1:2], in_=I[:, 0:cols], axis=AX.X, op=ALU.max), r=[ik], w=[bk])
                                    A("dve", lambda e: e.tensor_tensor(out=bd[:, 2:3], in0=bd[:, 1:2], in1=bd[:, 0:1], op=ALU.subtract), r=[bk], w=[bk])
                                A("pool", lambda e: e.tensor_tensor(out=I[:, i * 128:(i + 1) * 128], in0=I[:, i * 128:(i + 1) * 128],
                                                                    in1=negm[:], op=ALU.add), r=[ik, "negm"], w=[ik])

                            def s2(i):
                                ti = i % 4
                                cols = (i + 1) * 128
                                I = Ib[i % 2]
                                ik = ("I", i % 2)
                                bd = bnd[i % 2]
                                bk = ("bnd", i % 2)
                                gm = gem[i % 2]
                                gk = ("gem", i % 2)
                                if i >= 2:
                                    for it in range(NIT):
                                        A("dve", lambda e, it=it: e.scalar_tensor_tensor(out=bd[:, 3:4], in0=bd[:, 2:3], scalar=float(2.0 ** -(it + 1)),
                                                                                        in1=bd[:, 0:1], op0=ALU.mult, op1=ALU.add), r=[bk], w=[bk])
                                        A("dve", lambda e: e.tensor_scalar(out=junk[:, 0:cols], in0=I[:, 0:cols], scalar1=bd[:, 3:4], scalar2=None,
                                                                           op0=ALU.is_ge, op1=ALU.add, accum_out=bd[:, 4:5]), r=[ik, bk], w=[bk, "mk"])
                                        A("dve", lambda e: e.tensor_single_scalar(out=gm[:, 0:1], in_=bd[:, 4:5], scalar=255.5, op=ALU.is_ge), r=[bk], w=[gk])
                                        A("dve", lambda e: e.copy_predicated(out=bd[:, 0:1], mask=gm[:, 0:1], data=bd[:, 3:4]), r=[bk, gk], w=[bk])
                                    A("dve", lambda e: e.tensor_scalar(out=mk[:, 0:cols], in0=I[:, 0:cols], scalar1=bd[:, 0:1], scalar2=None, op0=ALU.is_ge),
                                      r=[ik, bk], w=["mk"])
                                else:
                                    A("dve", lambda e: e.tensor_single_scalar(out=mk[:, 0:cols], in_=I[:, 0:cols], scalar=-1.0e29, op=ALU.is_ge), r=[ik], w=["mk"])
                                for j0 in range(0, i + 1, 8):
                                    nn = min(8, i + 1 - j0)
                                    for jj in range(nn):
                                        A("pe", lambda e, jj=jj, j0=j0: e.transpose(
                                            out=pTp[:, jj * 128:(jj + 1) * 128],
                                            in_=mk[:, (j0 + jj) * 128:(j0 + jj + 1) * 128], identity=ident),
                                          r=["mk", "mats"], w=["pT"])
                                    A("act", lambda e, nn=nn, j0=j0: e.activation(
                                        out=maskT[:, j0:j0 + nn, ti * 128:(ti + 1) * 128],
                                        in_=pTp[:, 0:nn * 128].rearrange("p (a b) -> p a b", a=nn), func=AF.Copy),
                                      r=["pT"], w=["maskT"])

                            s1(0)
                            for i in range(NTB):
                                if i + 1 < NTB:
                                    s1(i + 1)
                                s2(i)
                                if i % 4 == 3:
                                    c = i // 4
                                    attn_jobs([(c, QA, hp, ("QA", hp, c), KA, 0, ("KA", 0, c),
                                                lambda j, e_: VA[:, j, e_ * 64:e_ * 64 + 128], ("VA", 0),
                                                lambda j, t0, n: maskT[:, j, t0:512], "maskT",
                                                merged[:, hp, c * 512:(c + 1) * 512], ("mg", hp, c)) for hp in range(4)],
                                              accrot, srot, tiles)
                        P.barrier()

                if cfg["diff"]:
                    with ExitStack() as sa:
                        asb = mk_sb(sa)
                        QB = asb("QB", [128, 4, T], BF16)
                        KB = asb("KB", [128, 4, T], BF16)
                        VB = asb("VB", [128, NTB, 512], BF16)
                        fm = []
                        bufs = [(QB, n, "QB") for n in range(4)] + [(KB, n, "KB") for n in range(4)]
                        for ci, (buf, n, nm) in enumerate(bufs):
                            fm.append((ew_fm[10 + ci],
                                       lambda c, buf=buf, n=n, nm=nm: (buf[:, n, c * 512:(c + 1) * 512], (nm, n, c))))
                        tm = [(ew_vb[n], 128,
                               lambda blk, n=n: (VB[:, blk, n * 128:(n + 1) * 128], ("VB", blk, n)), None) for n in range(4)]
                        mixer_proj(gi, fm, tm)
                        with ExitStack() as s2:
                            f2sb = mk_sb(s2)
                            strip = f2sb("cstrip", [128, STRIP_W], BF16)
                            DMA(strip[:], strips_d[0], w=["strip"])
                            ebs = [f2sb(f"eb{i}", [128, 512], BF16) for i in range(3)]
                            pbs = [f2sb(f"pb{i}", [128, 512], BF16) for i in range(3)]
                            fa = [f2sb(f"fa{i}", [128, 512], F32) for i in range(4)]
                            ob = f2sb("ob", [128, 512], F32)
                            sq2 = f2sb("sq2", [128, 512], BF16)
                            ebs = ebs + [f2sb("eb3", [128, 512], BF16)]
                            pbs = pbs + [f2sb("pb3", [128, 512], BF16)]
                            erot, prot = Rot(range(4)), Rot(range(4))
                            srot = Rot([0, 1, 2])
                            units = []
                            for hd in range(4):
                                for c in range(NTC):
                                    last = 4 * c + 3
                                    for j in range(last + 1):
                                        for e_ in range(2):
                                            units.append((hd, c, j, e_, j == last and e_ == 1))

                            def dfront(u):
                                hd, c, j, e_, is_last = u
                                t0 = max(0, j * 128 - c * 512)
                                n = 512 - t0
                                lo, hi = e_ * 64, (e_ + 1) * 64
                                b = srot.next()
                                A("pe", lambda e: e.matmul(
                                    ps[b][:, 0:n], lhsT=KB[lo:hi, hd, j * 128:(j + 1) * 128],
                                    rhs=QB[lo:hi, hd, c * 512 + t0:(c + 1) * 512], start=True, stop=True),
                                  r=[("QB", hd, c), ("KB", hd, j // 4)], w=[("ps", b)])
                                ei = erot.next()
                                A("act", lambda e: e.activation(out=ebs[ei][:, 0:n], in_=ps[b][:, 0:n], func=AF.Exp, scale=0.125),
                                  r=[("ps", b)], w=[("eb", ei)])
                                if j >= 4 * c:
                                    off = c * 512 - j * 128 + 384 + t0
                                    pi = prot.next()
                                    A("dve", lambda e: e.tensor_tensor(
                                        out=pbs[pi][:, 0:n], in0=ebs[ei][:, 0:n], in1=strip[:, off:off + n], op=ALU.mult),
                                      r=[("eb", ei), "strip"], w=[("pb", pi)])
                                    return pbs[pi], ("pb", pi)
                                return ebs[ei], ("eb", ei)

                            def dback(u, fr):
                                hd, c, j, e_, is_last = u
                                src, skey = fr
                                last = 4 * c + 3
                                t0 = max(0, j * 128 - c * 512)
                                n = 512 - t0
                                nb, db = 3 + 2 * e_, 4 + 2 * e_
                                A("pe", lambda e: e.matmul(
                                    ps[nb][:, t0:512], lhsT=VB[:, j, hd * 128:(hd + 1) * 128], rhs=src[:, 0:n],
                                    start=(j == 0), stop=(j == last)),
                                  r=[skey, ("VB", j, hd)], w=[("ps", nb)])
                                A("pe", lambda e: e.matmul(
                                    ps[db][:, t0:512], lhsT=ones, rhs=src[:, 0:n], start=(j == 0), stop=(j == last)),
                                  r=[skey, "mats"], w=[("ps", db)])
                                if not is_last:
                                    return
                                A("dve", lambda e: e.reciprocal(out=fa[0][:], in_=ps[4][:]), r=[("ps", 4)], w=[("fa", 0)])
                                A("dve", lambda e: e.tensor_tensor(out=fa[1][:], in0=ps[3][:], in1=fa[0][:], op=ALU.mult),
                                  r=[("ps", 3), ("fa", 0)], w=[("fa", 1)])
                                A("dve", lambda e: e.reciprocal(out=fa[2][:], in_=ps[6][:]), r=[("ps", 6)], w=[("fa", 2)])
                                A("dve", lambda e: e.tensor_tensor(out=fa[3][:], in0=ps[5][:], in1=fa[2][:], op=ALU.mult),
                                  r=[("ps", 5), ("fa", 2)], w=[("fa", 3)])
                                A("dve", lambda e: e.scalar_tensor_tensor(out=ob[:], in0=fa[3][:], scalar=neglam, in1=fa[1][:],
                                                                          op0=ALU.mult, op1=ALU.add),
                                  r=[("fa", 3), ("fa", 1), "lsm"], w=["ob"])
                                A("act", lambda e: e.activation(out=sq2[:], in_=ob[:], func=AF.Square, scale=float(128.0 ** -0.5)),
                                  r=["ob"], w=["sq2"])
                                b = srot.next()
                                A("pe", lambda e: e.matmul(ps[b][:], lhsT=ones, rhs=sq2[:], start=True, stop=True),
                                  r=["sq2", "mats"], w=[("ps", b)])
                                A("act", lambda e: e.activation(out=fa[0][:], in_=ps[b][:], func=AF.Sqrt, bias=epsb[:, 1:2]),
                                  r=[("ps", b), "epsb"], w=[("fa", 0)])
                                A("dve", lambda e: e.reciprocal(out=fa[2][:], in_=fa[0][:]), r=[("fa", 0)], w=[("fa", 2)])
                                A("dve", lambda e: e.scalar_tensor_tensor(
                                    out=merged[:, 4 + hd, c * 512:(c + 1) * 512], in0=ob[:], scalar=gsub, in1=fa[2][:],
                                    op0=ALU.mult, op1=ALU.mult), r=["ob", ("fa", 2), "lsm"], w=[("mg", 4 + hd, c)])

                            run_units(units, dfront, dback)
                        P.barrier()
                apply_wout(even_w_out, merged)

        def odd_mixer(gi):
            with ExitStack() as sm:
                msb = mk_sb(sm)
                merged = msb("mergedo", [128, NKC, T], BF16)
                for g in range(2):
                    odd_group(g, gi, merged)
                apply_wout(odd_w_out, merged)

        def odd_group(g, gi, merged):
            if True:
                if True:
                    with ExitStack() as sa:
                        asb = mk_sb(sa)
                        QG = asb("QG", [128, 4, T], BF16)
                        KG = asb("KG", [128, 4, T], BF16)
                        VG = asb("VG", [128, NTB, 4, 192], BF16)
                        A("pool", lambda e: e.memset(VG[:, :, :, 64:128], 1.0), w=[("VG1",)])
                        fm = []
                        for n in range(4):
                            fm.append((odd_w_in[4 * g + n],
                                       lambda c, n=n: (QG[:, n, c * 512:(c + 1) * 512], ("QG", n, c))))
                        for n in range(4):
                            fm.append((odd_w_in[8 + 4 * g + n],
                                       lambda c, n=n: (KG[:, n, c * 512:(c + 1) * 512], ("KG", n, c))))
                        tm = [(odd_w_in[16 + 4 * g + n], 128,
                               lambda blk, n=n: ([(VG[:, blk, n, 0:64], 0, 64), (VG[:, blk, n, 128:192], 64, 128)], ("VG", blk, n)), None) for n in range(4)]
                        if cfg["proj"]:
                            mixer_proj(gi, fm, tm)
                        if not cfg["attn"]:
                            for hp in range(4):
                                A("pool", lambda e, hp=hp: e.memset(merged[:, 4 * g + hp, :], 0.0), w=[("mg", 4 * g + hp, c) for c in range(NTC)])
                            return
                        with ExitStack() as s2:
                            f2sb = mk_sb(s2)
                            strip = f2sb("mstrip", [128, STRIP_W], BF16)
                            DMA(strip[:], strips_d[1], w=["strip"])
                            tiles = attn_tiles(f2sb, mengs=("dve",))
                            srot = Rot([0, 1, 2])
                            accrot = Rot([(3, 4), (5, 6)])
                            jobs = []
                            for hp in range(4):
                                for c in range(NTC):
                                    jobs.append((c, QG, hp, ("QG", hp, c), KG, hp, ("KG", hp, c),
                                                 lambda j, e_, hp=hp: VG[:, j, hp, e_ * 64:e_ * 64 + 128], ("VG", 0, 0),
                                                 lambda j, t0, n, c=c: strip[:, c * 512 - j * 128 + 384 + t0:c * 512 - j * 128 + 384 + t0 + n], "strip",
                                                 merged[:, 4 * g + hp, c * 512:(c + 1) * 512], ("mg", 4 * g + hp, c)))
                            attn_jobs(jobs, accrot, srot, tiles)
                        P.barrier()

        for l in range(cfg["layers"]):
            if cfg["ffn"]:
                ffn(l, ffn_w["ffn_a_wg"], ffn_w["ffn_a_wu"], ffn_w["ffn_a_wd"], l * 4 + 0)
            if l % 2 == 0:
                if cfg["mix_even"]:
                    even_mixer(l * 4 + 1)
            else:
                if cfg["mix_odd"]:
                    odd_mixer(l * 4 + 1)
            if cfg["ffn"]:
                ffn(l, ffn_w["ffn_b_wg"], ffn_w["ffn_b_wu"], ffn_w["ffn_b_wd"], l * 4 + 2)
            if cfg["ple"]:
                ple(l, l * 4 + 3)
        final_norm()
        P.emit(st)
    nc._declared_inputs = declared
    return nc


def prep_inputs(inputs, b):
    f = np.float32
    g = lambda k: np.asarray(inputs[k], f)
    rope, strips, mats, negmask = _host_consts()
    gains = []
    for l in range(2):
        for nm in ("norm_ffn_a", "norm_mix", "norm_ffn_b", "norm_ple"):
            gains.append(_col(g(nm)[l]))
    gains.append(_col(g("final_norm")))
    gains = np.ascontiguousarray(np.concatenate(gains, axis=1))
    ewi = g("even_w_in")[0]
    qa, ka, va = ewi[:, 0:512], ewi[:, 512:576], ewi[:, 576:640]
    qi, ki, wi = ewi[:, 640:1152], ewi[:, 1152:1216], ewi[:, 1216:1224]
    qb, kb, vb = ewi[:, 1224:1736], ewi[:, 1736:2248], ewi[:, 2248:2760]
    ew_fm = np.ascontiguousarray(np.concatenate([qa, ka, ka, qi, ki, ki, qb, kb], axis=1))
    def tile_w(W, tc):
        R_, C_ = W.shape
        return np.ascontiguousarray(W.reshape(R_ // 128, 128, C_ // tc, tc).transpose(2, 1, 0, 3))

    m = {
        "xT": np.ascontiguousarray(g("x")[b].T),
        "pT": np.ascontiguousarray(np.transpose(g("p")[:, b], (0, 2, 1))),
        "gains": gains,
        "ple_gate": np.stack([tile_w(g("ple_gate")[l], 256) for l in range(2)], 0),
        "ple_proj": np.stack([tile_w(g("ple_proj")[l], 256) for l in range(2)], 0),
        "ew_fm": tile_w(ew_fm, 128), "ew_va": tile_w(np.ascontiguousarray(va), 64),
        "ew_wi": tile_w(np.ascontiguousarray(wi), 8), "ew_vb": tile_w(np.ascontiguousarray(vb), 128),
        "even_w_out": tile_w(g("even_w_out")[0], 256),
        "lamv": np.ascontiguousarray(np.stack([g("diff_lambda_q1")[0], g("diff_lambda_k1")[0],
                                               g("diff_lambda_q2")[0], g("diff_lambda_k2")[0]], 0)),
        "subln": np.ascontiguousarray(g("diff_subln")[0].reshape(128, 1)),
        "odd_w_in": tile_w(g("odd_w_in")[0], 128), "odd_w_out": tile_w(g("odd_w_out")[0], 256),
        "rope": rope, "strips": strips, "mats": mats, "negmask": negmask,
    }
    for nm in ("ffn_a_wg", "ffn_a_wu", "ffn_b_wg", "ffn_b_wu"):
        m[nm] = np.stack([tile_w(g(nm)[l], 256) for l in range(2)], 0)
    for nm in ("ffn_a_wd", "ffn_b_wd"):
        w = g(nm)
        m[nm] = np.stack([np.stack([np.stack([tile_w(w[l][hf * 1408:(hf + 1) * 1408, d2 * 256:(d2 + 1) * 256], 256)[0]
                                              for hf in range(2)], 0) for d2 in range(4)], 0) for l in range(2)], 0)
    return m


_NC_CACHE = {}


def kernel(**inputs):
    import time as _time
    _t0 = _time.time()
    key = tuple(sorted(CFG.items()))
    if key not in _NC_CACHE:
        _NC_CACHE[key] = build_program(CFG)
    nc = _NC_CACHE[key]
    print(f"[kernel] build {_time.time() - _t0:.1f}s", flush=True)
    n = 8
    shared = prep_inputs(inputs, 0)
    in_maps = []
    for b in range(n):
        m = dict(shared)
        m["xT"] = np.ascontiguousarray(np.asarray(inputs["x"], np.float32)[b].T)
        m["pT"] = np.ascontiguousarray(np.transpose(np.asarray(inputs["p"], np.float32)[:, b], (0, 2, 1)))
        in_maps.append({k: v for k, v in m.items() if k in nc._declared_inputs})
    print(f"[kernel] prep {_time.time() - _t0:.1f}s", flush=True)
    import os as _os
    _nd = int(_os.environ.get("DBG_CORES", "8"))
    if _nd != 8:
        res = run_bass_kernel_spmd(nc, in_maps[:_nd], core_ids=list(range(_nd)))
        res.results.extend([res.results[0]] * (8 - _nd))
    else:
        res = run_bass_kernel_spmd(nc, in_maps, core_ids=list(range(n)))
    print(f"[kernel] ran {_time.time() - _t0:.1f}s", flush=True)
    out = np.stack([np.asarray(res.results[b]["outT"], np.float32).T for b in range(n)], 0)
    return np.ascontiguousarray(out)
```

```python
import numpy as np
import ml_dtypes
import concourse.bass as bass
import concourse.mybir as mybir
from concourse.bass_utils import run_bass_kernel_spmd

F32 = mybir.dt.float32
BF16 = mybir.dt.bfloat16
ALU = mybir.AluOpType
AF = mybir.ActivationFunctionType
AX = mybir.AxisListType

T = 2048
D = 1024
DFF = 2816
NKC = D // 128
NFC = DFF // 128
NTB = T // 128
NTC = T // 512
PLE = 256
EVEN_IN = 2760
NEG = -1.0e30
NEG2 = -2.0e30


class Op:
    __slots__ = ("eng", "fn", "waits", "signal", "sem", "val", "is_dma", "idx", "is_bar")

    def __init__(self, eng, fn, is_dma=False):
        self.eng = eng
        self.fn = fn
        self.waits = []
        self.signal = False
        self.sem = None
        self.val = 0
        self.is_dma = is_dma
        self.idx = -1
        self.is_bar = False


ENGS = ("pe", "act", "dve", "pool", "sp")
SEM_LIM = 30000


class Prog:
    def __init__(self, nc):
        self.nc = nc
        self.ops = {e: [] for e in ENGS}
        self.last_w = {}
        self.readers = {}
        self.seen = {e: {s: -1 for s in ENGS} for e in ENGS}
        self.seen_dma = {e: set() for e in ENGS}
        self.pending_dma = []
        self.dma_list = []
        self.nops = 0

    NDMA = 24

    def add(self, eng, fn, reads=(), writes=(), dma=False):
        op = Op(eng, fn, is_dma=dma)
        deps = []
        if dma:
            k = len(self.dma_list)
            if k >= self.NDMA:
                deps.append((self.dma_list[k - self.NDMA], True))
            self.dma_list.append(op)
        for r in reads:
            w = self.last_w.get(r)
            if w is not None:
                deps.append((w, True))
            if r == "pT" or (isinstance(r, tuple) and r[0] == "ps"):
                for rd in self.readers.get(r, ()):
                    if rd.eng != eng:
                        deps.append((rd, True))
        for r in writes:
            w = self.last_w.get(r)
            if w is not None:
                deps.append((w, True))
            for rd in self.readers.get(r, ()):
                deps.append((rd, False))
        best = {}
        for d, is_raw in deps:
            if d.is_dma:
                self._dep(op, d, is_raw)
                continue
            if d.eng == eng and eng in ("pe", "sp"):
                continue
            cur = best.get(d.eng)
            if cur is None or d.idx > cur.idx:
                best[d.eng] = d
        for d in best.values():
            self._dep(op, d, True)
        op.idx = len(self.ops[eng])
        self.ops[eng].append(op)
        for r in reads:
            self.readers.setdefault(r, []).append(op)
        for r in writes:
            self.last_w[r] = op
            self.readers[r] = []
        if dma:
            self.pending_dma.append(op)
        self.nops += 1
        return op

    def _dep(self, op, d, is_raw):
        e = op.eng
        if d.is_dma:
            if d in self.seen_dma[e]:
                return
            self.seen_dma[e].add(d)
            d.signal = True
            op.waits.append(d)
            return
        if d.eng == e:
            if e == "pe" or e == "sp":
                return
        if self.seen[e][d.eng] >= d.idx:
            return
        self.seen[e][d.eng] = d.idx
        d.signal = True
        op.waits.append(d)

    def barrier(self):
        bar = Op("sp", None)
        bar.is_bar = True
        for e in ENGS:
            if e == "sp":
                continue
            if self.ops[e]:
                self._dep(bar, self.ops[e][-1], True)
        for d in self.pending_dma:
            self._dep(bar, d, True)
        self.pending_dma = []
        bar.idx = len(self.ops["sp"])
        self.ops["sp"].append(bar)
        bar.signal = True
        for e in ENGS:
            if e == "sp":
                continue
            w = Op(e, None)
            w.is_bar = True
            w.waits.append(bar)
            self.seen[e]["sp"] = bar.idx
            w.idx = len(self.ops[e])
            self.ops[e].append(w)
        self.last_w = {}
        self.readers = {}

    def emit(self, stack):
        nc = self.nc
        ndma = self.NDMA
        dsems = [stack.enter_context(nc.semaphore(f"s_dma_{i}")) for i in range(ndma)]
        dcnt = [0] * ndma
        for k, op in enumerate(self.dma_list):
            di = k % ndma
            if dcnt[di] + 16 > SEM_LIM:
                dsems[di] = stack.enter_context(nc.semaphore(f"s_dma_{di}_{k}"))
                dcnt[di] = 0
            dcnt[di] += 16
            op.sem = dsems[di]
            op.val = dcnt[di]
        eng_sems = {}
        for e in ENGS:
            cur = None
            cnt = 0
            for op in self.ops[e]:
                if not op.signal or op.is_dma:
                    continue
                if cur is None or cnt >= SEM_LIM:
                    cur = stack.enter_context(nc.semaphore(f"s_{e}_{len(eng_sems)}"))
                    eng_sems[(e, len(eng_sems))] = cur
                    cnt = 0
                cnt += 1
                op.sem = cur
                op.val = cnt
        block = stack.enter_context(nc.Block())

        def run(e):
            def body(eng):
                for op in self.ops[e]:
                    for d in op.waits:
                        eng.wait_ge(d.sem, d.val)
                    if op.fn is None:
                        if op.signal:
                            eng.sem_inc(op.sem, 1)
                        continue
                    ins = op.fn(eng)
                    if op.is_dma:
                        ins.then_inc(op.sem, 16)
                    elif op.signal:
                        ins.then_inc(op.sem, 1)
            return body

        block.tensor(run("pe"))
        block.scalar(run("act"))
        block.vector(run("dve"))
        block.gpsimd(run("pool"))
        block.sync(run("sp"))


STRIP_W = 2432


def _host_consts():
    f32 = np.float32
    inv = (1.0 / (f32(10000.0) ** (np.arange(0, 64, 2, dtype=f32) / f32(64)))).astype(f32)
    ang = (np.arange(T, dtype=f32)[:, None] * inv[None, :]).astype(f32)
    cos = np.cos(ang).astype(f32)
    sin = np.sin(ang).astype(f32)
    p = np.arange(128)
    d = p % 64
    j = d % 32
    C2 = cos[:, j].T.copy()
    S2 = (sin[:, j].T * np.where(d < 32, -1.0, 1.0)[:, None]).astype(f32)
    rope = np.ascontiguousarray(np.stack([C2, S2], 0)).astype(f32)
    x = np.arange(STRIP_W)[None, :] - 384 - p[:, None]
    causal = (x >= 0).astype(f32)
    mult = (((x >= 0) & (x <= 128)).astype(f32)
            + ((x >= 0) & (x % 4 == 0) & (x <= 512)).astype(f32)
            + ((x >= 0) & (x % 16 == 0) & (x <= 2048)).astype(f32))
    strips = np.stack([causal, mult], 0).astype(ml_dtypes.bfloat16)
    ident = np.eye(128, dtype=f32)
    perm = np.zeros((128, 128), f32)
    for m in range(128):
        perm[m ^ 32, m] = 1.0
    ones = np.ones((128, 128), f32)
    mats = np.stack([ident, perm, ones], 0).astype(ml_dtypes.bfloat16)
    tt = np.arange(128)
    negmask = np.where(tt[None, :] > tt[:, None], np.float32(NEG), np.float32(0.0)).astype(f32)
    return rope, strips, mats, negmask


def _col(v):
    v = np.asarray(v, np.float32)
    return np.ascontiguousarray(v.reshape(-1, 128).T)


CFG = {"layers": 2, "mix_even": True, "mix_odd": True, "dsa": True, "diff": True, "ffn": True, "ple": True, "proj": True, "attn": True}


class Rot:
    def __init__(self, items):
        self.items = list(items)
        self.i = 0

    def next(self):
        v = self.items[self.i % len(self.items)]
        self.i += 1
        return v


def build_program(cfg=None):
    from contextlib import ExitStack
    cfg = dict(CFG if cfg is None else cfg)
    nc = bass.Bass("TRN2", target_bir_lowering=False)

    declared = []

    def din(name, shape, dt=F32, need=True):
        if not need:
            return None
        declared.append(name)
        return nc.dram_tensor(name, list(shape), dt, kind="ExternalInput").ap()

    xT = din("xT", [D, T])
    pTd = din("pT", [2, PLE, T])
    gains_d = din("gains", [128, 72])
    ffn_w = {}
    for nm in ("ffn_a_wg", "ffn_a_wu", "ffn_b_wg", "ffn_b_wu"):
        ffn_w[nm] = din(nm, [2, 11, 128, 8, 256], need=cfg["ffn"])
    for nm in ("ffn_a_wd", "ffn_b_wd"):
        ffn_w[nm] = din(nm, [2, 4, 2, 128, 11, 256], need=cfg["ffn"])
    ple_gate = din("ple_gate", [2, 4, 128, 8, 256], need=cfg["ple"])
    ple_proj = din("ple_proj", [2, 4, 128, 2, 256], need=cfg["ple"])
    ew_fm = din("ew_fm", [18, 128, 8, 128])
    ew_va = din("ew_va", [1, 128, 8, 64])
    ew_wi = din("ew_wi", [1, 128, 8, 8])
    ew_vb = din("ew_vb", [4, 128, 8, 128])
    even_w_out = din("even_w_out", [4, 128, 8, 256])
    lamv = din("lamv", [4, 64])
    subln_d = din("subln", [128, 1])
    odd_w_in = din("odd_w_in", [24, 128, 8, 128])
    odd_w_out = din("odd_w_out", [4, 128, 8, 256])
    rope_d = din("rope", [2, 128, T])
    strips_d = din("strips", [2, 128, STRIP_W], BF16)
    mats_d = din("mats", [3, 128, 128], BF16)
    negm_d = din("negmask", [128, 128])
    outT = nc.dram_tensor("outT", [D, T], F32, kind="ExternalOutput").ap()

    uid = [0]

    with ExitStack() as st:
        def mk_sb(stack):
            def f(name, shape, dt):
                uid[0] += 1
                return stack.enter_context(nc.sbuf_tensor(f"{name}_{uid[0]}", list(shape), dt))
            return f

        sb = mk_sb(st)
        h = sb("h", [128, NKC, T], F32)
        gsb = sb("gsb", [128, 72], F32)
        mats = sb("mats", [128, 3, 128], BF16)
        negm = sb("negm", [128, 128], F32)
        epsb = sb("epsb", [128, 2], F32)
        lv = sb("lv", [128, 4, 64], F32)
        lsm = sb("lsm", [128, 8], F32)
        sqb = [sb(f"sqb{i}", [128, 512], BF16) for i in range(2)]
        sdt = sb("sdt", [128, 512], F32)
        rstd = sb("rstd", [128, 512], F32)
        ps = [st.enter_context(nc.psum_tensor(f"ps{i}", [128, 512], F32)) for i in range(7)]
        pTp = st.enter_context(nc.psum_tensor("pTp", [128, 1024], BF16))
        ident = mats[:, 0, :]
        perm = mats[:, 1, :]
        ones = mats[:, 2, :]

        P = Prog(nc)

        def A(eng, fn, r=(), w=()):
            return P.add(eng, fn, reads=r, writes=w)

        def DMA(out, in_, r=(), w=()):
            return P.add("sp", lambda e: e.dma_start(out=out, in_=in_), reads=r, writes=w, dma=True)

        for k in range(NKC):
            DMA(h[:, k, :], xT[k * 128:(k + 1) * 128, :], w=[("h", k, c) for c in range(NTC)])
        DMA(gsb[:], gains_d, w=["gsb"])
        DMA(mats[:], mats_d.rearrange("a p n -> p a n"), w=["mats"])
        DMA(negm[:], negm_d, w=["negm"])
        DMA(lv[:], lamv.partition_broadcast(128), w=["lv"])
        DMA(lsm[:, 7:8], subln_d, w=["lsm7"])
        A("pool", lambda e: e.memset(epsb[:, 0:1], 1e-6), w=["epsb"])
        A("pool", lambda e: e.memset(epsb[:, 1:2], 1e-5), w=["epsb"])

        sqrot = Rot([0, 1])
        cvt_rot = Rot(["act", "dve"])

        class WPool:
            def __init__(self, stack, elems, nst, nbf, tag):
                f = mk_sb(stack)
                self.stg = [f(f"wst{tag}{i}", [128, elems], F32) for i in range(nst)]
                self.bf = [f(f"wbf{tag}{i}", [128, elems], BF16) for i in range(nbf)]
                self.i = 0
                uid[0] += 1
                self.tag = f"{tag}{uid[0]}"

            def load(self, desc):
                src, a, b = desc
                i = self.i
                self.i += 1
                si, bi = i % len(self.stg), i % len(self.bf)
                sv = self.stg[si][:, 0:a * b].rearrange("p (a b) -> p a b", a=a)
                bv = self.bf[bi][:, 0:a * b].rearrange("p (a b) -> p a b", a=a)
                sk, bk = (self.tag, "st", si), (self.tag, "bf", bi)
                DMA(sv, src, w=[sk])
                self._cvt(bv, sv, sk, bk)
                return bv, bk

            def _cvt(self, dst, src, sk, dk):
                eng = cvt_rot.next()
                if eng == "act":
                    A("act", lambda e: e.activation(out=dst, in_=src, func=AF.Copy), r=[sk], w=[dk])
                else:
                    A(eng, lambda e: e.tensor_copy(out=dst, in_=src), r=[sk], w=[dk])

            def load_into(self, desc, dst, dkey):
                src, a, b = desc
                i = self.i
                self.i += 1
                si = i % len(self.stg)
                sv = self.stg[si][:, 0:a * b].rearrange("p (a b) -> p a b", a=a)
                sk = (self.tag, "st", si)
                DMA(sv, src, w=[sk])
                self._cvt(dst, sv, sk, dkey)

        def stream(wp, descs, depth=2):
            q = []
            n = len(descs)
            nxt = 0
            for i in range(n):
                while nxt < n and nxt <= i + depth:
                    q.append(wp.load(descs[nxt]))
                    nxt += 1
                yield q[i]

        def wdesc(w2d, r0, nr, c0, ncol):
            return (w2d[r0:r0 + nr, c0:c0 + ncol].rearrange("(a p) n -> p a n", p=128), nr // 128, ncol)

        def norm_chunk(gi, c, dst_fn, dkey_fn, srot, after=None, engs=("dve",)):
            b = srot.next()
            for k in range(NKC):
                si = sqrot.next()
                A("act", lambda e, k=k, si=si: e.activation(out=sqb[si][:], in_=h[:, k, c * 512:(c + 1) * 512],
                                                            func=AF.Square, scale=1.0 / 32.0),
                  r=[("h", k, c)], w=[("sqb", si)])
                A("pe", lambda e, k=k, si=si: e.matmul(ps[b][:], lhsT=ones, rhs=sqb[si][:], start=(k == 0), stop=(k == NKC - 1)),
                  r=[("sqb", si), "mats"], w=[("ps", b)])
            A("act", lambda e: e.activation(out=sdt[:], in_=ps[b][:], func=AF.Sqrt, bias=epsb[:, 0:1]),
              r=[("ps", b), "epsb"], w=["sdt"])
            A("dve", lambda e: e.reciprocal(out=rstd[:], in_=sdt[:]), r=["sdt"], w=["rstd"])
            for k in range(NKC):
                eng = engs[k % len(engs)]
                dst = dst_fn(k)
                A(eng, lambda e, k=k, dst=dst: e.scalar_tensor_tensor(out=dst, in0=h[:, k, c * 512:(c + 1) * 512],
                                                                      scalar=gsb[:, gi * 8 + k:gi * 8 + k + 1], in1=rstd[:],
                                                                      op0=ALU.mult, op1=ALU.mult),
                  r=[("h", k, c), "gsb", "rstd"], w=[dkey_fn(k)])
                if after is not None:
                    after(k)

        def ffn(l, wg, wu, wd, gi):
            for tp in range(2):
                ffn_pass(l, wg, wu, wd, gi, tp)
                P.barrier()

        def ffn_pass(l, wg, wu, wd, gi, tp):
            if True:
                with ExitStack() as s2:
                    f2sb = mk_sb(s2)
                    hn = f2sb("hn", [128, NKC, 1024], BF16)
                    act = f2sb("act", [128, NFC, 1024], BF16)
                    sg = [f2sb(f"sg{i}", [128, 512], F32) for i in range(2)]
                    wp = WPool(s2, 2816, 3, 4, "f")
                    srot = Rot(range(7))
                    sgrot = Rot([0, 1])
                    for sub in range(2):
                        norm_chunk(gi, tp * 2 + sub, lambda k, sub=sub: hn[:, k, sub * 512:(sub + 1) * 512],
                                   lambda k, sub=sub: ("hn", k, sub), srot)
                    descs = []
                    for f2 in range(NFC // 2):
                        descs.append((wg[l, f2], 8, 256))
                        descs.append((wu[l, f2], 8, 256))
                    it = stream(wp, descs)
                    for f2 in range(NFC // 2):
                        gt, gk = next(it)
                        ut, uk = next(it)
                        for fi in range(2):
                            f = 2 * f2 + fi
                            for sub in range(2):
                                bg = srot.next()
                                bu = srot.next()
                                for k in range(NKC):
                                    A("pe", lambda e, k=k, bg=bg, gt=gt, fi=fi, sub=sub: e.matmul(
                                        ps[bg][:], lhsT=gt[:, k, fi * 128:(fi + 1) * 128], rhs=hn[:, k, sub * 512:(sub + 1) * 512],
                                        start=(k == 0), stop=(k == NKC - 1)), r=[gk, ("hn", k, sub)], w=[("ps", bg)])
                                for k in range(NKC):
                                    A("pe", lambda e, k=k, bu=bu, ut=ut, fi=fi, sub=sub: e.matmul(
                                        ps[bu][:], lhsT=ut[:, k, fi * 128:(fi + 1) * 128], rhs=hn[:, k, sub * 512:(sub + 1) * 512],
                                        start=(k == 0), stop=(k == NKC - 1)), r=[uk, ("hn", k, sub)], w=[("ps", bu)])
                                si = sgrot.next()
                                A("act", lambda e, si=si, bg=bg: e.activation(out=sg[si][:], in_=ps[bg][:], func=AF.Silu),
                                  r=[("ps", bg)], w=[("sg", si)])
                                A("dve", lambda e, si=si, bu=bu, f=f, sub=sub: e.tensor_tensor(
                                    out=act[:, f, sub * 512:(sub + 1) * 512], in0=sg[si][:], in1=ps[bu][:], op=ALU.mult),
                                  r=[("sg", si), ("ps", bu)], w=[("act", f, sub)])
                    descs = []
                    for d2 in range(4):
                        for half in range(2):
                            descs.append((wd[l, d2, half], 11, 256))
                    it = stream(wp, descs)
                    for d2 in range(4):
                        tl0, tk0 = next(it)
                        tl1, tk1 = next(it)
                        for di in range(2):
                            d = 2 * d2 + di
                            for sub in range(2):
                                b = srot.next()
                                c = tp * 2 + sub
                                for fc in range(NFC):
                                    tl, tk = (tl0, tk0) if fc < 11 else (tl1, tk1)
                                    A("pe", lambda e, fc=fc, tl=tl, di=di, sub=sub, b=b: e.matmul(
                                        ps[b][:], lhsT=tl[:, fc % 11, di * 128:(di + 1) * 128], rhs=act[:, fc, sub * 512:(sub + 1) * 512],
                                        start=(fc == 0), stop=(fc == NFC - 1)), r=[tk, ("act", fc, sub)], w=[("ps", b)])
                                A("dve", lambda e, d=d, c=c, b=b: e.scalar_tensor_tensor(
                                    out=h[:, d, c * 512:(c + 1) * 512], in0=ps[b][:], scalar=0.5, in1=h[:, d, c * 512:(c + 1) * 512],
                                    op0=ALU.mult, op1=ALU.add), r=[("ps", b), ("h", d, c)], w=[("h", d, c)])

        def ple(l, gi):
            with ExitStack() as s2:
                f2sb = mk_sb(s2)
                hn = f2sb("hnp", [128, NKC, T], BF16)
                ptb = f2sb("ptb", [128, 2, T], BF16)
                sig = [f2sb(f"sig{i}", [128, 512], F32) for i in range(2)]
                tmp = [f2sb(f"ptmp{i}", [128, 512], F32) for i in range(2)]
                wp = WPool(s2, 2048, 3, 4, "p")
                srot = Rot(range(7))
                r2 = Rot([0, 1])
                for c in range(NTC):
                    norm_chunk(gi, c, lambda k, c=c: hn[:, k, c * 512:(c + 1) * 512], lambda k, c=c: ("hnp", k, c), srot)
                    wp.load_into((pTd[l][:, c * 512:(c + 1) * 512].rearrange("(a p) t -> p a t", p=128), 2, 512),
                                 ptb[:, :, c * 512:(c + 1) * 512], ("ptb", c))
                descs = []
                for d2 in range(4):
                    descs.append((ple_gate[l, d2], 8, 256))
                    descs.append((ple_proj[l, d2], 2, 256))
                it = stream(wp, descs)
                for d2 in range(4):
                    gt, gk = next(it)
                    pt, pk = next(it)
                    for di in range(2):
                        d = 2 * d2 + di
                        for c in range(NTC):
                            bg = srot.next()
                            bp = srot.next()
                            for k in range(NKC):
                                A("pe", lambda e, k=k, bg=bg, gt=gt, di=di, c=c: e.matmul(
                                    ps[bg][:], lhsT=gt[:, k, di * 128:(di + 1) * 128], rhs=hn[:, k, c * 512:(c + 1) * 512],
                                    start=(k == 0), stop=(k == NKC - 1)), r=[gk, ("hnp", k, c)], w=[("ps", bg)])
                            for k in range(2):
                                A("pe", lambda e, k=k, bp=bp, pt=pt, di=di, c=c: e.matmul(
                                    ps[bp][:], lhsT=pt[:, k, di * 128:(di + 1) * 128], rhs=ptb[:, k, c * 512:(c + 1) * 512],
                                    start=(k == 0), stop=(k == 1)), r=[pk, ("ptb", c)], w=[("ps", bp)])
                            si = r2.next()
                            A("act", lambda e, si=si, bg=bg: e.activation(out=sig[si][:], in_=ps[bg][:], func=AF.Sigmoid),
                              r=[("ps", bg)], w=[("sig", si)])
                            A("dve", lambda e, si=si, bp=bp: e.tensor_tensor(out=tmp[si][:], in0=sig[si][:], in1=ps[bp][:], op=ALU.mult),
                              r=[("sig", si), ("ps", bp)], w=[("ptmp", si)])
                            A("pool", lambda e, si=si, d=d, c=c: e.tensor_tensor(
                                out=h[:, d, c * 512:(c + 1) * 512], in0=h[:, d, c * 512:(c + 1) * 512], in1=tmp[si][:], op=ALU.add),
                              r=[("ptmp", si), ("h", d, c)], w=[("h", d, c)])
            P.barrier()

        def final_norm():
            with ExitStack() as s2:
                f2sb = mk_sb(s2)
                ot = [f2sb(f"ot{i}", [128, 512], F32) for i in range(4)]
                srot = Rot(range(7))
                for c in range(NTC):
                    def after(k, c=c):
                        i = (c * 8 + k) % 4
                        DMA(outT[k * 128:(k + 1) * 128, c * 512:(c + 1) * 512], ot[i][:], r=[("ot", i)])
                    norm_chunk(8, c, lambda k, c=c: ot[(c * 8 + k) % 4][:], lambda k, c=c: ("ot", (c * 8 + k) % 4), srot,
                               after=after, engs=("dve",))
            P.barrier()

        def mixer_proj(gi, fm_items, tm_items):
            import os as _os
            _nfm = int(_os.environ.get("PROJ_FM", "99"))
            _ntm = int(_os.environ.get("PROJ_TM", "99"))
            fm_items = fm_items[:_nfm]
            tm_items = tm_items[:_ntm]
            with ExitStack() as s2:
                f2sb = mk_sb(s2)
                hnc = f2sb("hnc", [128, NKC, 512], BF16)
                ropc = [f2sb(f"ropc{i}", [128, 2, 512], F32) for i in range(2)]
                xb = [f2sb(f"xb{i}", [128, 512], BF16) for i in range(2)]
                t1 = [f2sb(f"t1{i}", [128, 512], F32) for i in range(2)]
                t2 = [f2sb(f"t2{i}", [128, 512], F32) for i in range(2)]
                wp = WPool(s2, 1024, 2, 3, "m")
                srot = Rot(range(7))
                r2 = Rot([0, 1])
                for c in range(NTC):
                    norm_chunk(gi, c, lambda k: hnc[:, k, :], lambda k: ("hnc", k), srot)
                    rc = c % 2
                    DMA(ropc[rc][:], rope_d[:, :, c * 512:(c + 1) * 512].rearrange("a p t -> p a t"), w=[("ropc", rc)])
                    descs = [(w_ap, 8, 128) for (w_ap, _) in fm_items]
                    descs += [(w_ap, 8, n) for (w_ap, n, _, _) in tm_items]
                    it = stream(wp, descs)
                    for (_, dst_fn) in fm_items:
                        wt, wk = next(it)
                        b = srot.next()
                        for k in range(NKC):
                            A("pe", lambda e, k=k, b=b, wt=wt: e.matmul(ps[b][:], lhsT=wt[:, k, :], rhs=hnc[:, k, :],
                                                                        start=(k == 0), stop=(k == NKC - 1)),
                              r=[wk, ("hnc", k)], w=[("ps", b)])
                        i = r2.next()
                        A("act", lambda e, i=i, b=b: e.activation(out=xb[i][:], in_=ps[b][:], func=AF.Copy),
                          r=[("ps", b)], w=[("xb", i)])
                        A("dve", lambda e, i=i, b=b, rc=rc: e.tensor_tensor(out=t1[i][:], in0=ps[b][:], in1=ropc[rc][:, 0, :], op=ALU.mult),
                          r=[("ps", b), ("ropc", rc), ("xb", i)], w=[("t1", i)])
                        b2 = srot.next()
                        A("pe", lambda e, i=i, b2=b2: e.matmul(ps[b2][:], lhsT=perm, rhs=xb[i][:], start=True, stop=True),
                          r=[("xb", i), "mats"], w=[("ps", b2)])
                        A("dve", lambda e, i=i, b2=b2, rc=rc: e.tensor_tensor(out=t2[i][:], in0=ps[b2][:], in1=ropc[rc][:, 1, :], op=ALU.mult),
                          r=[("ps", b2), ("ropc", rc)], w=[("t2", i)])
                        dst, dkey = dst_fn(c)
                        A("pool", lambda e, i=i, dst=dst: e.tensor_tensor(out=dst, in0=t1[i][:], in1=t2[i][:], op=ALU.add),
                          r=[("t1", i), ("t2", i)], w=[dkey])
                    for (_, n, dst_fn, _) in tm_items:
                        wt, wk = next(it)
                        for tb in range(4):
                            b = srot.next()
                            for k in range(NKC):
                                A("pe", lambda e, k=k, b=b, wt=wt, tb=tb, n=n: e.matmul(
                                    ps[b][:, 0:n], lhsT=hnc[:, k, tb * 128:(tb + 1) * 128], rhs=wt[:, k, :],
                                    start=(k == 0), stop=(k == NKC - 1)), r=[wk, ("hnc", k)], w=[("ps", b)])
                            dst, dkey = dst_fn(c * 4 + tb)
                            if isinstance(dst, list):
                                for (dap, lo_, hi_) in dst:
                                    A("act", lambda e, b=b, dap=dap, lo_=lo_, hi_=hi_: e.activation(out=dap, in_=ps[b][:, lo_:hi_], func=AF.Copy),
                                      r=[("ps", b)], w=[dkey])
                            else:
                                A("act", lambda e, b=b, dst=dst, n=n: e.activation(out=dst, in_=ps[b][:, 0:n], func=AF.Copy),
                                  r=[("ps", b)], w=[dkey])
            P.barrier()

        LOOK = 2

        def run_units(units, front, back):
            fr = []
            for idx, u in enumerate(units):
                fr.append(front(u))
                if idx >= LOOK:
                    back(units[idx - LOOK], fr[idx - LOOK])
            for idx in range(max(0, len(units) - LOOK), len(units)):
                back(units[idx], fr[idx])

        def attn_jobs(jobs, accrot, srot, tiles, kz=None):
            ebs, pbs, rds, erot, prot, rrot, mengs = tiles
            units = []
            for ji, job in enumerate(jobs):
                c = job[0]
                accs = accrot.next()
                last = 4 * c + 3
                for j in range(last + 1):
                    for e_ in range(2):
                        units.append((job, accs, j, e_, j == last and e_ == 1, ji, j == 0 and e_ == 0))

            def front(u):
                (c, Q, qc, qkey, K, kc, kkey, vfn, vkey, mfn, mkey, dst, dkey), accs, j, e_, is_last, ji, is_first = u
                t0 = max(0, j * 128 - c * 512)
                n = 512 - t0
                lo, hi = e_ * 64, (e_ + 1) * 64
                b = srot.next()
                if kz is not None:
                    kzt, kzkey, pre = kz[ji]
                    if is_first and pre is not None:
                        pre()
                    A("pe", lambda e: e.matmul(
                        ps[b][:, 0:n], lhsT=kzt[:, e_, j * 128:(j + 1) * 128], rhs=Q[:, qc, c * 512 + t0:(c + 1) * 512],
                        start=True, stop=True), r=[qkey, kzkey], w=[("ps", b)])
                else:
                    A("pe", lambda e: e.matmul(
                        ps[b][:, 0:n], lhsT=K[lo:hi, kc, j * 128:(j + 1) * 128], rhs=Q[lo:hi, qc, c * 512 + t0:(c + 1) * 512],
                        start=True, stop=True), r=[qkey, kkey], w=[("ps", b)])
                ei = erot.next()
                A("act", lambda e: e.activation(out=ebs[ei][:, 0:n], in_=ps[b][:, 0:n], func=AF.Exp, scale=0.125),
                  r=[("ps", b)], w=[("eb", ei)])
                m = mfn(j, t0, n)
                if m is not None:
                    pi = prot.next()
                    A(mengs.next(), lambda e: e.tensor_tensor(out=pbs[pi][:, 0:n], in0=ebs[ei][:, 0:n], in1=m, op=ALU.mult),
                      r=[("eb", ei), mkey], w=[("pb", pi)])
                    return pbs[pi], ("pb", pi)
                return ebs[ei], ("eb", ei)

            def back(u, fr):
                (c, Q, qc, qkey, K, kc, kkey, vfn, vkey, mfn, mkey, dst, dkey), (ba, bb), j, e_, is_last, ji, is_first = u
                src, skey = fr
                last = 4 * c + 3
                t0 = max(0, j * 128 - c * 512)
                n = 512 - t0
                bk_ = ba if e_ == 0 else bb
                v = vfn(j, e_)
                A("pe", lambda e: e.matmul(ps[bk_][:, t0:512], lhsT=v, rhs=src[:, 0:n], start=(j == 0), stop=(j == last)),
                  r=[skey, vkey], w=[("ps", bk_)])
                if is_last:
                    di = rrot.next()
                    dsh = rds[di]
                    A("act", lambda e: e.activation(out=dsh[0:64, :], in_=ps[ba][64:128, :], func=AF.Copy), r=[("ps", ba)], w=[("rd", di)])
                    A("act", lambda e: e.activation(out=dsh[64:128, :], in_=ps[bb][0:64, :], func=AF.Copy), r=[("ps", bb)], w=[("rd", di)])
                    A("dve", lambda e: e.reciprocal(out=dsh[:], in_=dsh[:]), r=[("rd", di)], w=[("rd", di)])
                    A("dve", lambda e: e.tensor_tensor(out=dst[0:64, :], in0=ps[ba][0:64, :], in1=dsh[0:64, :], op=ALU.mult),
                      r=[("ps", ba), ("rd", di)], w=[dkey])
                    A("dve", lambda e: e.tensor_tensor(out=dst[64:128, :], in0=ps[bb][64:128, :], in1=dsh[64:128, :], op=ALU.mult),
                      r=[("ps", bb), ("rd", di)], w=[dkey])

            run_units(units, front, back)

        def make_kz(f2sb, nbuf=2):
            kzs = [f2sb(f"kz{i}", [128, 2, T], BF16) for i in range(nbuf)]
            for i in range(nbuf):
                A("pool", lambda e, i=i: e.memset(kzs[i][64:128, 0, :], 0.0), w=[("kz", i)])
                A("pool", lambda e, i=i: e.memset(kzs[i][0:64, 1, :], 0.0), w=[("kz", i)])

            def fill(i, K, kc):
                def pre():
                    A("pool", lambda e: e.tensor_copy(out=kzs[i][0:64, 0, :], in_=K[0:64, kc, :]), r=[], w=[("kz", i)])
                    A("dve", lambda e: e.tensor_copy(out=kzs[i][64:128, 1, :], in_=K[64:128, kc, :]), r=[], w=[("kz", i)])
                return pre
            return kzs, fill

        def attn_tiles(f2sb, mengs=("dve", "pool")):
            ebs = [f2sb(f"eb{i}", [128, 512], BF16) for i in range(4)]
            pbs = [f2sb(f"pb{i}", [128, 512], BF16) for i in range(4)]
            rds = [f2sb(f"rd{i}", [128, 512], F32) for i in range(2)]
            return (ebs, pbs, rds, Rot(range(4)), Rot(range(4)), Rot(range(2)), Rot(mengs))

        def apply_wout(w2d, merged):
            with ExitStack() as s2:
                wp = WPool(s2, 2048, 3, 4, "o")
                srot = Rot(range(7))
                descs = [(w2d[d2], 8, 256) for d2 in range(4)]
                it = stream(wp, descs)
                for d2 in range(4):
                    wt, wk = next(it)
                    for di in range(2):
                        d = 2 * d2 + di
                        for c in range(NTC):
                            b = srot.next()
                            for k in range(NKC):
                                A("pe", lambda e, k=k, b=b, wt=wt, di=di, c=c: e.matmul(
                                    ps[b][:], lhsT=wt[:, k, di * 128:(di + 1) * 128], rhs=merged[:, k, c * 512:(c + 1) * 512],
                                    start=(k == 0), stop=(k == NKC - 1)), r=[wk, ("mg", k, c)], w=[("ps", b)])
                            A("dve", lambda e, b=b, d=d, c=c: e.tensor_tensor(
                                out=h[:, d, c * 512:(c + 1) * 512], in0=ps[b][:], in1=h[:, d, c * 512:(c + 1) * 512], op=ALU.add),
                              r=[("ps", b), ("h", d, c)], w=[("h", d, c)])
            P.barrier()

        def even_mixer(gi):
            with ExitStack() as sm:
                msb = mk_sb(sm)
                merged = msb("merged", [128, NKC, T], BF16)
                A("dve", lambda e: e.tensor_tensor(out=lv[:, 0, :], in0=lv[:, 0, :], in1=lv[:, 1, :], op=ALU.mult), r=["lv"], w=["lv"])
                A("dve", lambda e: e.tensor_tensor(out=lv[:, 2, :], in0=lv[:, 2, :], in1=lv[:, 3, :], op=ALU.mult), r=["lv"], w=["lv"])
                A("dve", lambda e: e.reduce_sum(out=lsm[:, 0:1], in_=lv[:, 0, :], axis=AX.X), r=["lv"], w=["lsm"])
                A("dve", lambda e: e.reduce_sum(out=lsm[:, 1:2], in_=lv[:, 2, :], axis=AX.X), r=["lv"], w=["lsm"])
                A("act", lambda e: e.activation(out=lsm[:, 2:4], in_=lsm[:, 0:2], func=AF.Exp), r=["lsm"], w=["lsm"])
                A("dve", lambda e: e.tensor_tensor(out=lsm[:, 4:5], in0=lsm[:, 3:4], in1=lsm[:, 2:3], op=ALU.subtract), r=["lsm"], w=["lsm"])
                A("dve", lambda e: e.tensor_scalar(out=lsm[:, 5:6], in0=lsm[:, 4:5], scalar1=-0.2, scalar2=None, op0=ALU.add), r=["lsm"], w=["lsm"])
                A("dve", lambda e: e.tensor_scalar(out=lsm[:, 6:7], in0=lsm[:, 7:8], scalar1=0.8, scalar2=None, op0=ALU.mult), r=["lsm", "lsm7"], w=["lsm"])
                neglam = lsm[:, 5:6]
                gsub = lsm[:, 6:7]

                for kk in range(NKC):
                    if (kk < 4 and not cfg["dsa"]) or (kk >= 4 and not cfg["diff"]):
                        A("pool", lambda e, kk=kk: e.memset(merged[:, kk, :], 0.0), w=[("mg", kk, c) for c in range(NTC)])
                if cfg["dsa"]:
                    with ExitStack() as sa:
                        asb = mk_sb(sa)
                        QA = asb("QA", [128, 4, T], BF16)
                        KA = asb("KA", [128, 1, T], BF16)
                        QI = asb("QI", [128, 4, T], BF16)
                        KI = asb("KI", [128, 1, T], BF16)
                        VA = asb("VA", [128, NTB, 192], BF16)
                        A("pool", lambda e: e.memset(VA[:, :, 64:128], 1.0), w=[("VA1",)])
                        WI = asb("WI", [128, NTB, 8], F32)
                        fm = []
                        bufs = [(QA, n, "QA") for n in range(4)] + [(KA, 0, "KA")] + [(QI, n, "QI") for n in range(4)] + [(KI, 0, "KI")]
                        for ci, (buf, n, nm) in enumerate(bufs):
                            fm.append((ew_fm[ci],
                                       lambda c, buf=buf, n=n, nm=nm: (buf[:, n, c * 512:(c + 1) * 512], (nm, n, c))))
                        tm = [(ew_va[0], 64, lambda blk: ([(VA[:, blk, 0:64], 0, 64), (VA[:, blk, 128:192], 0, 64)], ("VA", blk)), None),
                              (ew_wi[0], 8, lambda blk: (WI[:, blk, :], ("WI", blk)), None)]
                        mixer_proj(gi, fm, tm)
                        with ExitStack() as s2:
                            f2sb = mk_sb(s2)
                            Ib = [f2sb(f"Ib{i}", [128, T], F32) for i in range(2)]
                            rb = [f2sb(f"rb{i}", [128, 512], F32) for i in range(2)]
                            mk = f2sb("mk", [128, T], BF16)
                            junk = mk
                            maskT = f2sb("maskT", [128, NTB, 512], BF16)
                            bnd = [f2sb(f"bnd{i}", [128, 8], F32) for i in range(2)]
                            gem = [f2sb(f"gem{i}", [128, 2], mybir.dt.uint32) for i in range(2)]
                            tiles = attn_tiles(f2sb, mengs=("dve",))
                            srot = Rot([0, 1, 2])
                            rrot = Rot(range(2))
                            accrot = Rot([(3, 4), (5, 6)])
                            NIT = 18

                            def s1(i):
                                c = i // 4
                                cols = (i + 1) * 128
                                I = Ib[i % 2]
                                ik = ("I", i % 2)
                                bd = bnd[i % 2]
                                bk = ("bnd", i % 2)
                                for sc in range((cols + 511) // 512):
                                    w_ = min(512, cols - sc * 512)
                                    for hd in range(8):
                                        lo, hi = (hd % 2) * 64, (hd % 2 + 1) * 64
                                        b = srot.next()
                                        A("pe", lambda e, b=b, lo=lo, hi=hi, hd=hd, sc=sc, w_=w_: e.matmul(
                                            ps[b][:, 0:w_], lhsT=QI[lo:hi, hd // 2, i * 128:(i + 1) * 128],
                                            rhs=KI[lo:hi, 0, sc * 512:sc * 512 + w_], start=True, stop=True),
                                          r=[("QI", hd // 2, c), ("KI", 0, sc)], w=[("ps", b)])
                                        ri = rrot.next()
                                        A("act", lambda e, b=b, ri=ri, w_=w_: e.activation(out=rb[ri][:, 0:w_], in_=ps[b][:, 0:w_], func=AF.Relu),
                                          r=[("ps", b)], w=[("rb", ri)])
                                        if hd == 0:
                                            A("dve", lambda e, ri=ri, sc=sc, w_=w_: e.tensor_scalar(
                                                out=I[:, sc * 512:sc * 512 + w_], in0=rb[ri][:, 0:w_], scalar1=WI[:, i, 0:1], scalar2=None, op0=ALU.mult),
                                              r=[("rb", ri), ("WI", i)], w=[ik])
                                        else:
                                            A("dve", lambda e, ri=ri, sc=sc, w_=w_, hd=hd: e.scalar_tensor_tensor(
                                                out=I[:, sc * 512:sc * 512 + w_], in0=rb[ri][:, 0:w_], scalar=WI[:, i, hd:hd + 1],
                                                in1=I[:, sc * 512:sc * 512 + w_], op0=ALU.mult, op1=ALU.add),
                                              r=[("rb", ri), ("WI", i), ik], w=[ik])
                                if i >= 2:
                                    A("dve", lambda e: e.tensor_reduce(out=bd[:, 0:1], in_=I[:, 0:cols], axis=AX.X, op=ALU.min), r=[ik], w=[bk])
                                    A("dve", lambda e: e.tensor_reduce(out=bd[:, 1:2], in_=I[:, 0:cols], axis=AX.X, op=ALU.max), r=[ik], w=[bk])
                                    A("dve", lambda e: e.tensor_tensor(out=bd[:, 2:3], in0=bd[:, 1:2], in1=bd[:, 0:1], op=ALU.subtract), r=[bk], w=[bk])
                                A("pool", lambda e: e.tensor_tensor(out=I[:, i * 128:(i + 1) * 128], in0=I[:, i * 128:(i + 1) * 128],
                                                                    in1=negm[:], op=ALU.add), r=[ik, "negm"], w=[ik])

                            def s2(i):
                                ti = i % 4
                                cols = (i + 1) * 128
                                I = Ib[i % 2]
                                ik = ("I", i % 2)
                                bd = bnd[i % 2]
                                bk = ("bnd", i % 2)
                                gm = gem[i % 2]
                                gk = ("gem", i % 2)
                                if i >= 2:
                                    for it in range(NIT):
                                        A("dve", lambda e, it=it: e.scalar_tensor_tensor(out=bd[:, 3:4], in0=bd[:, 2:3], scalar=float(2.0 ** -(it + 1)),
                                                                                        in1=bd[:, 0:1], op0=ALU.mult, op1=ALU.add), r=[bk], w=[bk])
                                        A("dve", lambda e: e.tensor_scalar(out=junk[:, 0:cols], in0=I[:, 0:cols], scalar1=bd[:, 3:4], scalar2=None,
                                                                           op0=ALU.is_ge, op1=ALU.add, accum_out=bd[:, 4:5]), r=[ik, bk], w=[bk, "mk"])
                                        A("dve", lambda e: e.tensor_single_scalar(out=gm[:, 0:1], in_=bd[:, 4:5], scalar=255.5, op=ALU.is_ge), r=[bk], w=[gk])
                                        A("dve", lambda e: e.copy_predicated(out=bd[:, 0:1], mask=gm[:, 0:1], data=bd[:, 3:4]), r=[bk, gk], w=[bk])
                                    A("dve", lambda e: e.tensor_scalar(out=mk[:, 0:cols], in0=I[:, 0:cols], scalar1=bd[:, 0:1], scalar2=None, op0=ALU.is_ge),
                                      r=[ik, bk], w=["mk"])
                                else:
                                    A("dve", lambda e: e.tensor_single_scalar(out=mk[:, 0:cols], in_=I[:, 0:cols], scalar=-1.0e29, op=ALU.is_ge), r=[ik], w=["mk"])
                                for j0 in range(0, i + 1, 8):
                                    nn = min(8, i + 1 - j0)
                                    for jj in range(nn):
                                        A("pe", lambda e, jj=jj, j0=j0: e.transpose(
                                            out=pTp[:, jj * 128:(jj + 1) * 128],
                                            in_=mk[:, (j0 + jj) * 128:(j0 + jj + 1) * 128], identity=ident),
                                          r=["mk", "mats"], w=["pT"])
                                    A("act", lambda e, nn=nn, j0=j0: e.activation(
                                        out=maskT[:, j0:j0 + nn, ti * 128:(ti + 1) * 128],
                                        in_=pTp[:, 0:nn * 128].rearrange("p (a b) -> p a b", a=nn), func=AF.Copy),
                                      r=["pT"], w=["maskT"])

                            s1(0)
                            for i in range(NTB):
                                if i + 1 < NTB:
                                    s1(i + 1)
                                s2(i)
                                if i % 4 == 3:
                                    c = i // 4
                                    attn_jobs([(c, QA, hp, ("QA", hp, c), KA, 0, ("KA", 0, c),
                                                lambda j, e_: VA[:, j, e_ * 64:e_ * 64 + 128], ("VA", 0),
                                                lambda j, t0, n: maskT[:, j, t0:512], "maskT",
                                                merged[:, hp, c * 512:(c + 1) * 512], ("mg", hp, c)) for hp in range(4)],
                                              accrot, srot, tiles)
                        P.barrier()

                if cfg["diff"]:
                    with ExitStack() as sa:
                        asb = mk_sb(sa)
                        QB = asb("QB", [128, 4, T], BF16)
                        KB = asb("KB", [128, 4, T], BF16)
                        VB = asb("VB", [128, NTB, 512], BF16)
                        fm = []
                        bufs = [(QB, n, "QB") for n in range(4)] + [(KB, n, "KB") for n in range(4)]
                        for ci, (buf, n, nm) in enumerate(bufs):
                            fm.append((ew_fm[10 + ci],
                                       lambda c, buf=buf, n=n, nm=nm: (buf[:, n, c * 512:(c + 1) * 512], (nm, n, c))))
                        tm = [(ew_vb[n], 128,
                               lambda blk, n=n: (VB[:, blk, n * 128:(n + 1) * 128], ("VB", blk, n)), None) for n in range(4)]
                        mixer_proj(gi, fm, tm)
                        with ExitStack() as s2:
                            f2sb = mk_sb(s2)
                            strip = f2sb("cstrip", [128, STRIP_W], BF16)
                            DMA(strip[:], strips_d[0], w=["strip"])
                            ebs = [f2sb(f"eb{i}", [128, 512], BF16) for i in range(3)]
                            pbs = [f2sb(f"pb{i}", [128, 512], BF16) for i in range(3)]
                            fa = [f2sb(f"fa{i}", [128, 512], F32) for i in range(4)]
                            ob = f2sb("ob", [128, 512], F32)
                            sq2 = f2sb("sq2", [128, 512], BF16)
                            ebs = ebs + [f2sb("eb3", [128, 512], BF16)]
                            pbs = pbs + [f2sb("pb3", [128, 512], BF16)]
                            erot, prot = Rot(range(4)), Rot(range(4))
                            srot = Rot([0, 1, 2])
                            dkzs, dfill = make_kz(f2sb)
                            units = []
                            for hd in range(4):
                                for c in range(NTC):
                                    last = 4 * c + 3
                                    for j in range(last + 1):
                                        for e_ in range(2):
                                            units.append((hd, c, j, e_, j == last and e_ == 1))

                            def dfront(u):
                                hd, c, j, e_, is_last = u
                                t0 = max(0, j * 128 - c * 512)
                                n = 512 - t0
                                lo, hi = e_ * 64, (e_ + 1) * 64
                                b = srot.next()
                                if c == 0 and j == 0 and e_ == 0:
                                    dfill(hd % 2, KB, hd)()
                                kzt = dkzs[hd % 2]
                                A("pe", lambda e: e.matmul(
                                    ps[b][:, 0:n], lhsT=kzt[:, e_, j * 128:(j + 1) * 128],
                                    rhs=QB[:, hd, c * 512 + t0:(c + 1) * 512], start=True, stop=True),
                                  r=[("QB", hd, c), ("kz", hd % 2)], w=[("ps", b)])
                                ei = erot.next()
                                A("act", lambda e: e.activation(out=ebs[ei][:, 0:n], in_=ps[b][:, 0:n], func=AF.Exp, scale=0.125),
                                  r=[("ps", b)], w=[("eb", ei)])
                                if j >= 4 * c:
                                    off = c * 512 - j * 128 + 384 + t0
                                    pi = prot.next()
                                    A("dve", lambda e: e.tensor_tensor(
                                        out=pbs[pi][:, 0:n], in0=ebs[ei][:, 0:n], in1=strip[:, off:off + n], op=ALU.mult),
                                      r=[("eb", ei), "strip"], w=[("pb", pi)])
                                    return pbs[pi], ("pb", pi)
                                return ebs[ei], ("eb", ei)

                            def dback(u, fr):
                                hd, c, j, e_, is_last = u
                                src, skey = fr
                                last = 4 * c + 3
                                t0 = max(0, j * 128 - c * 512)
                                n = 512 - t0
                                nb, db = 3 + 2 * e_, 4 + 2 * e_
                                A("pe", lambda e: e.matmul(
                                    ps[nb][:, t0:512], lhsT=VB[:, j, hd * 128:(hd + 1) * 128], rhs=src[:, 0:n],
                                    start=(j == 0), stop=(j == last)),
                                  r=[skey, ("VB", j, hd)], w=[("ps", nb)])
                                A("pe", lambda e: e.matmul(
                                    ps[db][:, t0:512], lhsT=ones, rhs=src[:, 0:n], start=(j == 0), stop=(j == last)),
                                  r=[skey, "mats"], w=[("ps", db)])
                                if not is_last:
                                    return
                                A("dve", lambda e: e.reciprocal(out=fa[0][:], in_=ps[4][:]), r=[("ps", 4)], w=[("fa", 0)])
                                A("dve", lambda e: e.tensor_tensor(out=fa[1][:], in0=ps[3][:], in1=fa[0][:], op=ALU.mult),
                                  r=[("ps", 3), ("fa", 0)], w=[("fa", 1)])
                                A("dve", lambda e: e.reciprocal(out=fa[2][:], in_=ps[6][:]), r=[("ps", 6)], w=[("fa", 2)])
                                A("dve", lambda e: e.tensor_tensor(out=fa[3][:], in0=ps[5][:], in1=fa[2][:], op=ALU.mult),
                                  r=[("ps", 5), ("fa", 2)], w=[("fa", 3)])
                                A("dve", lambda e: e.scalar_tensor_tensor(out=ob[:], in0=fa[3][:], scalar=neglam, in1=fa[1][:],
                                                                          op0=ALU.mult, op1=ALU.add),
                                  r=[("fa", 3), ("fa", 1), "lsm"], w=["ob"])
                                A("act", lambda e: e.activation(out=sq2[:], in_=ob[:], func=AF.Square, scale=float(128.0 ** -0.5)),
                                  r=["ob"], w=["sq2"])
                                b = srot.next()
                                A("pe", lambda e: e.matmul(ps[b][:], lhsT=ones, rhs=sq2[:], start=True, stop=True),
                                  r=["sq2", "mats"], w=[("ps", b)])
                                A("act", lambda e: e.activation(out=fa[0][:], in_=ps[b][:], func=AF.Sqrt, bias=epsb[:, 1:2]),
                                  r=[("ps", b), "epsb"], w=[("fa", 0)])
                                A("dve", lambda e: e.reciprocal(out=fa[2][:], in_=fa[0][:]), r=[("fa", 0)], w=[("fa", 2)])
                                A("dve", lambda e: e.scalar_tensor_tensor(
                                    out=merged[:, 4 + hd, c * 512:(c + 1) * 512], in0=ob[:], scalar=gsub, in1=fa[2][:],
                                    op0=ALU.mult, op1=ALU.mult), r=["ob", ("fa", 2), "lsm"], w=[("mg", 4 + hd, c)])

                            run_units(units, dfront, dback)
                        P.barrier()
                apply_wout(even_w_out, merged)

        def odd_mixer(gi):
            with ExitStack() as sm:
                msb = mk_sb(sm)
                merged = msb("mergedo", [128, NKC, T], BF16)
                for g in range(2):
                    odd_group(g, gi, merged)
                apply_wout(odd_w_out, merged)

        def odd_group(g, gi, merged):
            if True:
                if True:
                    with ExitStack() as sa:
                        asb = mk_sb(sa)
                        QG = asb("QG", [128, 4, T], BF16)
                        KG = asb("KG", [128, 4, T], BF16)
                        VG = asb("VG", [128, NTB, 4, 192], BF16)
                        A("pool", lambda e: e.memset(VG[:, :, :, 64:128], 1.0), w=[("VG1",)])
                        fm = []
                        for n in range(4):
                            fm.append((odd_w_in[4 * g + n],
                                       lambda c, n=n: (QG[:, n, c * 512:(c + 1) * 512], ("QG", n, c))))
                        for n in range(4):
                            fm.append((odd_w_in[8 + 4 * g + n],
                                       lambda c, n=n: (KG[:, n, c * 512:(c + 1) * 512], ("KG", n, c))))
                        tm = [(odd_w_in[16 + 4 * g + n], 128,
                               lambda blk, n=n: ([(VG[:, blk, n, 0:64], 0, 64), (VG[:, blk, n, 128:192], 64, 128)], ("VG", blk, n)), None) for n in range(4)]
                        if cfg["proj"]:
                            mixer_proj(gi, fm, tm)
                        if not cfg["attn"]:
                            for hp in range(4):
                                A("pool", lambda e, hp=hp: e.memset(merged[:, 4 * g + hp, :], 0.0), w=[("mg", 4 * g + hp, c) for c in range(NTC)])
                            return
                        with ExitStack() as s2:
                            f2sb = mk_sb(s2)
                            strip = f2sb("mstrip", [128, STRIP_W], BF16)
                            DMA(strip[:], strips_d[1], w=["strip"])
                            tiles = attn_tiles(f2sb, mengs=("dve",))
                            srot = Rot([0, 1, 2])
                            accrot = Rot([(3, 4), (5, 6)])
                            jobs = []
                            for hp in range(4):
                                for c in range(NTC):
                                    jobs.append((c, QG, hp, ("QG", hp, c), KG, hp, ("KG", hp, c),
                                                 lambda j, e_, hp=hp: VG[:, j, hp, e_ * 64:e_ * 64 + 128], ("VG", 0, 0),
                                                 lambda j, t0, n, c=c: strip[:, c * 512 - j * 128 + 384 + t0:c * 512 - j * 128 + 384 + t0 + n], "strip",
                                                 merged[:, 4 * g + hp, c * 512:(c + 1) * 512], ("mg", 4 * g + hp, c)))
                            kzs, fill = make_kz(f2sb)
                            kzd = {}
                            for ji in range(len(jobs)):
                                hp = ji // NTC
                                kzd[ji] = (kzs[hp % 2], ("kz", hp % 2), fill(hp % 2, KG, hp) if ji % NTC == 0 else None)
                            attn_jobs(jobs, accrot, srot, tiles, kz=kzd)
                        P.barrier()

        for l in range(cfg["layers"]):
            if cfg["ffn"]:
                ffn(l, ffn_w["ffn_a_wg"], ffn_w["ffn_a_wu"], ffn_w["ffn_a_wd"], l * 4 + 0)
            if l % 2 == 0:
                if cfg["mix_even"]:
                    even_mixer(l * 4 + 1)
            else:
                if cfg["mix_odd"]:
                    odd_mixer(l * 4 + 1)
            if cfg["ffn"]:
                ffn(l, ffn_w["ffn_b_wg"], ffn_w["ffn_b_wu"], ffn_w["ffn_b_wd"], l * 4 + 2)
            if cfg["ple"]:
                ple(l, l * 4 + 3)
        final_norm()
        P.emit(st)
    nc._declared_inputs = declared
    return nc


def prep_inputs(inputs, b):
    f = np.float32
    g = lambda k: np.asarray(inputs[k], f)
    rope, strips, mats, negmask = _host_consts()
    gains = []
    for l in range(2):
        for nm in ("norm_ffn_a", "norm_mix", "norm_ffn_b", "norm_ple"):
            gains.append(_col(g(nm)[l]))
    gains.append(_col(g("final_norm")))
    gains = np.ascontiguousarray(np.concatenate(gains, axis=1))
    ewi = g("even_w_in")[0]
    qa, ka, va = ewi[:, 0:512], ewi[:, 512:576], ewi[:, 576:640]
    qi, ki, wi = ewi[:, 640:1152], ewi[:, 1152:1216], ewi[:, 1216:1224]
    qb, kb, vb = ewi[:, 1224:1736], ewi[:, 1736:2248], ewi[:, 2248:2760]
    ew_fm = np.ascontiguousarray(np.concatenate([qa, ka, ka, qi, ki, ki, qb, kb], axis=1))
    def tile_w(W, tc):
        R_, C_ = W.shape
        return np.ascontiguousarray(W.reshape(R_ // 128, 128, C_ // tc, tc).transpose(2, 1, 0, 3))

    m = {
        "xT": np.ascontiguousarray(g("x")[b].T),
        "pT": np.ascontiguousarray(np.transpose(g("p")[:, b], (0, 2, 1))),
        "gains": gains,
        "ple_gate": np.stack([tile_w(g("ple_gate")[l], 256) for l in range(2)], 0),
        "ple_proj": np.stack([tile_w(g("ple_proj")[l], 256) for l in range(2)], 0),
        "ew_fm": tile_w(ew_fm, 128), "ew_va": tile_w(np.ascontiguousarray(va), 64),
        "ew_wi": tile_w(np.ascontiguousarray(wi), 8), "ew_vb": tile_w(np.ascontiguousarray(vb), 128),
        "even_w_out": tile_w(g("even_w_out")[0], 256),
        "lamv": np.ascontiguousarray(np.stack([g("diff_lambda_q1")[0], g("diff_lambda_k1")[0],
                                               g("diff_lambda_q2")[0], g("diff_lambda_k2")[0]], 0)),
        "subln": np.ascontiguousarray(g("diff_subln")[0].reshape(128, 1)),
        "odd_w_in": tile_w(g("odd_w_in")[0], 128), "odd_w_out": tile_w(g("odd_w_out")[0], 256),
        "rope": rope, "strips": strips, "mats": mats, "negmask": negmask,
    }
    for nm in ("ffn_a_wg", "ffn_a_wu", "ffn_b_wg", "ffn_b_wu"):
        m[nm] = np.stack([tile_w(g(nm)[l], 256) for l in range(2)], 0)
    for nm in ("ffn_a_wd", "ffn_b_wd"):
        w = g(nm)
        m[nm] = np.stack([np.stack([np.stack([tile_w(w[l][hf * 1408:(hf + 1) * 1408, d2 * 256:(d2 + 1) * 256], 256)[0]
                                              for hf in range(2)], 0) for d2 in range(4)], 0) for l in range(2)], 0)
    return m


_NC_CACHE = {}


def kernel(**inputs):
    import time as _time
    _t0 = _time.time()
    key = tuple(sorted(CFG.items()))
    if key not in _NC_CACHE:
        _NC_CACHE[key] = build_program(CFG)
    nc = _NC_CACHE[key]
    print(f"[kernel] build {_time.time() - _t0:.1f}s", flush=True)
    n = 8
    shared = prep_inputs(inputs, 0)
    in_maps = []
    for b in range(n):
        m = dict(shared)
        m["xT"] = np.ascontiguousarray(np.asarray(inputs["x"], np.float32)[b].T)
        m["pT"] = np.ascontiguousarray(np.transpose(np.asarray(inputs["p"], np.float32)[:, b], (0, 2, 1)))
        in_maps.append({k: v for k, v in m.items() if k in nc._declared_inputs})
    print(f"[kernel] prep {_time.time() - _t0:.1f}s", flush=True)
    import os as _os
    _nd = int(_os.environ.get("DBG_CORES", "8"))
    if _nd != 8:
        res = run_bass_kernel_spmd(nc, in_maps[:_nd], core_ids=list(range(_nd)))
        res.results.extend([res.results[0]] * (8 - _nd))
    else:
        res = run_bass_kernel_spmd(nc, in_maps, core_ids=list(range(n)))
    print(f"[kernel] ran {_time.time() - _t0:.1f}s", flush=True)
    out = np.stack([np.asarray(res.results[b]["outT"], np.float32).T for b in range(n)], 0)
    return np.ascontiguousarray(out)
```

```python
import numpy as np
import ml_dtypes
import concourse.bass as bass
import concourse.mybir as mybir
from concourse.bass_utils import run_bass_kernel_spmd

F32 = mybir.dt.float32
BF16 = mybir.dt.bfloat16
ALU = mybir.AluOpType
AF = mybir.ActivationFunctionType
AX = mybir.AxisListType

T = 2048
D = 1024
DFF = 2816
NKC = D // 128
NFC = DFF // 128
NTB = T // 128
NTC = T // 512
PLE = 256
EVEN_IN = 2760
NEG = -1.0e30
NEG2 = -2.0e30


class Op:
    __slots__ = ("eng", "fn", "waits", "signal", "sem", "val", "is_dma", "idx", "is_bar")

    def __init__(self, eng, fn, is_dma=False):
        self.eng = eng
        self.fn = fn
        self.waits = []
        self.signal = False
        self.sem = None
        self.val = 0
        self.is_dma = is_dma
        self.idx = -1
        self.is_bar = False


ENGS = ("pe", "act", "dve", "pool", "sp")
SEM_LIM = 30000


class Prog:
    def __init__(self, nc):
        self.nc = nc
        self.ops = {e: [] for e in ENGS}
        self.last_w = {}
        self.readers = {}
        self.seen = {e: {s: -1 for s in ENGS} for e in ENGS}
        self.seen_dma = {e: set() for e in ENGS}
        self.pending_dma = []
        self.dma_list = []
        self.nops = 0

    NDMA = 24

    def add(self, eng, fn, reads=(), writes=(), dma=False):
        op = Op(eng, fn, is_dma=dma)
        deps = []
        if dma:
            k = len(self.dma_list)
            if k >= self.NDMA:
                deps.append((self.dma_list[k - self.NDMA], True))
            self.dma_list.append(op)
        for r in reads:
            w = self.last_w.get(r)
            if w is not None:
                deps.append((w, True))
            if r == "pT" or (isinstance(r, tuple) and r[0] == "ps"):
                for rd in self.readers.get(r, ()):
                    if rd.eng != eng:
                        deps.append((rd, True))
        for r in writes:
            w = self.last_w.get(r)
            if w is not None:
                deps.append((w, True))
            for rd in self.readers.get(r, ()):
                deps.append((rd, False))
        best = {}
        for d, is_raw in deps:
            if d.is_dma:
                self._dep(op, d, is_raw)
                continue
            if d.eng == eng and eng in ("pe", "sp"):
                continue
            cur = best.get(d.eng)
            if cur is None or d.idx > cur.idx:
                best[d.eng] = d
        for d in best.values():
            self._dep(op, d, True)
        op.idx = len(self.ops[eng])
        self.ops[eng].append(op)
        for r in reads:
            self.readers.setdefault(r, []).append(op)
        for r in writes:
            self.last_w[r] = op
            self.readers[r] = []
        if dma:
            self.pending_dma.append(op)
        self.nops += 1
        return op

    def _dep(self, op, d, is_raw):
        e = op.eng
        if d.is_dma:
            if d in self.seen_dma[e]:
                return
            self.seen_dma[e].add(d)
            d.signal = True
            op.waits.append(d)
            return
        if d.eng == e:
            if e == "pe" or e == "sp":
                return
        if self.seen[e][d.eng] >= d.idx:
            return
        self.seen[e][d.eng] = d.idx
        d.signal = True
        op.waits.append(d)

    def barrier(self):
        bar = Op("sp", None)
        bar.is_bar = True
        for e in ENGS:
            if e == "sp":
                continue
            if self.ops[e]:
                self._dep(bar, self.ops[e][-1], True)
        for d in self.pending_dma:
            self._dep(bar, d, True)
        self.pending_dma = []
        bar.idx = len(self.ops["sp"])
        self.ops["sp"].append(bar)
        bar.signal = True
        for e in ENGS:
            if e == "sp":
                continue
            w = Op(e, None)
            w.is_bar = True
            w.waits.append(bar)
            self.seen[e]["sp"] = bar.idx
            w.idx = len(self.ops[e])
            self.ops[e].append(w)
        self.last_w = {}
        self.readers = {}

    def emit(self, stack):
        nc = self.nc
        ndma = self.NDMA
        dsems = [stack.enter_context(nc.semaphore(f"s_dma_{i}")) for i in range(ndma)]
        dcnt = [0] * ndma
        for k, op in enumerate(self.dma_list):
            di = k % ndma
            if dcnt[di] + 16 > SEM_LIM:
                dsems[di] = stack.enter_context(nc.semaphore(f"s_dma_{di}_{k}"))
                dcnt[di] = 0
            dcnt[di] += 16
            op.sem = dsems[di]
            op.val = dcnt[di]
        eng_sems = {}
        for e in ENGS:
            cur = None
            cnt = 0
            for op in self.ops[e]:
                if not op.signal or op.is_dma:
                    continue
                if cur is None or cnt >= SEM_LIM:
                    cur = stack.enter_context(nc.semaphore(f"s_{e}_{len(eng_sems)}"))
                    eng_sems[(e, len(eng_sems))] = cur
                    cnt = 0
                cnt += 1
                op.sem = cur
                op.val = cnt
        block = stack.enter_context(nc.Block())

        def run(e):
            def body(eng):
                for op in self.ops[e]:
                    for d in op.waits:
                        eng.wait_ge(d.sem, d.val)
                    if op.fn is None:
                        if op.signal:
                            eng.sem_inc(op.sem, 1)
                        continue
                    ins = op.fn(eng)
                    if op.is_dma:
                        ins.then_inc(op.sem, 16)
                    elif op.signal:
                        ins.then_inc(op.sem, 1)
            return body

        block.tensor(run("pe"))
        block.scalar(run("act"))
        block.vector(run("dve"))
        block.gpsimd(run("pool"))
        block.sync(run("sp"))


STRIP_W = 2432


def _host_consts():
    f32 = np.float32
    inv = (1.0 / (f32(10000.0) ** (np.arange(0, 64, 2, dtype=f32) / f32(64)))).astype(f32)
    ang = (np.arange(T, dtype=f32)[:, None] * inv[None, :]).astype(f32)
    cos = np.cos(ang).astype(f32)
    sin = np.sin(ang).astype(f32)
    p = np.arange(128)
    d = p % 64
    j = d % 32
    C2 = cos[:, j].T.copy()
    S2 = (sin[:, j].T * np.where(d < 32, -1.0, 1.0)[:, None]).astype(f32)
    rope = np.ascontiguousarray(np.stack([C2, S2], 0)).astype(f32)
    x = np.arange(STRIP_W)[None, :] - 384 - p[:, None]
    causal = (x >= 0).astype(f32)
    mult = (((x >= 0) & (x <= 128)).astype(f32)
            + ((x >= 0) & (x % 4 == 0) & (x <= 512)).astype(f32)
            + ((x >= 0) & (x % 16 == 0) & (x <= 2048)).astype(f32))
    strips = np.stack([causal, mult], 0).astype(ml_dtypes.bfloat16)
    ident = np.eye(128, dtype=f32)
    perm = np.zeros((128, 128), f32)
    for m in range(128):
        perm[m ^ 32, m] = 1.0
    ones = np.ones((128, 128), f32)
    mats = np.stack([ident, perm, ones], 0).astype(ml_dtypes.bfloat16)
    tt = np.arange(128)
    negmask = np.where(tt[None, :] > tt[:, None], np.float32(NEG), np.float32(0.0)).astype(f32)
    return rope, strips, mats, negmask


def _col(v):
    v = np.asarray(v, np.float32)
    return np.ascontiguousarray(v.reshape(-1, 128).T)


CFG = {"layers": 2, "mix_even": True, "mix_odd": True, "dsa": True, "diff": True, "ffn": True, "ple": True, "proj": True, "attn": True}


class Rot:
    def __init__(self, items):
        self.items = list(items)
        self.i = 0

    def next(self):
        v = self.items[self.i % len(self.items)]
        self.i += 1
        return v


def build_program(cfg=None):
    from contextlib import ExitStack
    cfg = dict(CFG if cfg is None else cfg)
    nc = bass.Bass("TRN2", target_bir_lowering=False)

    declared = []

    def din(name, shape, dt=F32, need=True):
        if not need:
            return None
        declared.append(name)
        return nc.dram_tensor(name, list(shape), dt, kind="ExternalInput").ap()

    xT = din("xT", [D, T])
    pTd = din("pT", [2, PLE, T])
    gains_d = din("gains", [128, 72])
    ffn_w = {}
    for nm in ("ffn_a_wg", "ffn_a_wu", "ffn_b_wg", "ffn_b_wu"):
        ffn_w[nm] = din(nm, [2, 11, 128, 8, 256], need=cfg["ffn"])
    for nm in ("ffn_a_wd", "ffn_b_wd"):
        ffn_w[nm] = din(nm, [2, 4, 2, 128, 11, 256], need=cfg["ffn"])
    ple_gate = din("ple_gate", [2, 4, 128, 8, 256], need=cfg["ple"])
    ple_proj = din("ple_proj", [2, 4, 128, 2, 256], need=cfg["ple"])
    ew_fm = din("ew_fm", [18, 128, 8, 128])
    ew_va = din("ew_va", [1, 128, 8, 64])
    ew_wi = din("ew_wi", [1, 128, 8, 8])
    ew_vb = din("ew_vb", [4, 128, 8, 128])
    even_w_out = din("even_w_out", [4, 128, 8, 256])
    lamv = din("lamv", [4, 64])
    subln_d = din("subln", [128, 1])
    odd_w_in = din("odd_w_in", [24, 128, 8, 128])
    odd_w_out = din("odd_w_out", [4, 128, 8, 256])
    rope_d = din("rope", [2, 128, T])
    strips_d = din("strips", [2, 128, STRIP_W], BF16)
    mats_d = din("mats", [3, 128, 128], BF16)
    negm_d = din("negmask", [128, 128])
    outT = nc.dram_tensor("outT", [D, T], F32, kind="ExternalOutput").ap()

    uid = [0]

    with ExitStack() as st:
        def mk_sb(stack):
            def f(name, shape, dt):
                uid[0] += 1
                return stack.enter_context(nc.sbuf_tensor(f"{name}_{uid[0]}", list(shape), dt))
            return f

        sb = mk_sb(st)
        h = sb("h", [128, NKC, T], F32)
        gsb = sb("gsb", [128, 72], F32)
        mats = sb("mats", [128, 3, 128], BF16)
        negm = sb("negm", [128, 128], F32)
        epsb = sb("epsb", [128, 2], F32)
        lv = sb("lv", [128, 4, 64], F32)
        lsm = sb("lsm", [128, 8], F32)
        sqb = [sb(f"sqb{i}", [128, 512], BF16) for i in range(2)]
        sdt = sb("sdt", [128, 512], F32)
        rstd = sb("rstd", [128, 512], F32)
        ps = [st.enter_context(nc.psum_tensor(f"ps{i}", [128, 512], F32)) for i in range(7)]
        pTp = st.enter_context(nc.psum_tensor("pTp", [128, 1024], BF16))
        ident = mats[:, 0, :]
        perm = mats[:, 1, :]
        ones = mats[:, 2, :]

        P = Prog(nc)

        def A(eng, fn, r=(), w=()):
            return P.add(eng, fn, reads=r, writes=w)

        def DMA(out, in_, r=(), w=()):
            return P.add("sp", lambda e: e.dma_start(out=out, in_=in_), reads=r, writes=w, dma=True)

        for k in range(NKC):
            DMA(h[:, k, :], xT[k * 128:(k + 1) * 128, :], w=[("h", k, c) for c in range(NTC)])
        DMA(gsb[:], gains_d, w=["gsb"])
        DMA(mats[:], mats_d.rearrange("a p n -> p a n"), w=["mats"])
        DMA(negm[:], negm_d, w=["negm"])
        DMA(lv[:], lamv.partition_broadcast(128), w=["lv"])
        DMA(lsm[:, 7:8], subln_d, w=["lsm7"])
        A("pool", lambda e: e.memset(epsb[:, 0:1], 1e-6), w=["epsb"])
        A("pool", lambda e: e.memset(epsb[:, 1:2], 1e-5), w=["epsb"])

        sqrot = Rot([0, 1])
        cvt_rot = Rot(["act", "dve"])

        class WPool:
            def __init__(self, stack, elems, nst, nbf, tag):
                f = mk_sb(stack)
                self.stg = [f(f"wst{tag}{i}", [128, elems], F32) for i in range(nst)]
                self.bf = [f(f"wbf{tag}{i}", [128, elems], BF16) for i in range(nbf)]
                self.i = 0
                uid[0] += 1
                self.tag = f"{tag}{uid[0]}"

            def load(self, desc):
                src, a, b = desc
                i = self.i
                self.i += 1
                si, bi = i % len(self.stg), i % len(self.bf)
                sv = self.stg[si][:, 0:a * b].rearrange("p (a b) -> p a b", a=a)
                bv = self.bf[bi][:, 0:a * b].rearrange("p (a b) -> p a b", a=a)
                sk, bk = (self.tag, "st", si), (self.tag, "bf", bi)
                DMA(sv, src, w=[sk])
                self._cvt(bv, sv, sk, bk)
                return bv, bk

            def _cvt(self, dst, src, sk, dk):
                eng = cvt_rot.next()
                if eng == "act":
                    A("act", lambda e: e.activation(out=dst, in_=src, func=AF.Copy), r=[sk], w=[dk])
                else:
                    A(eng, lambda e: e.tensor_copy(out=dst, in_=src), r=[sk], w=[dk])

            def load_into(self, desc, dst, dkey):
                src, a, b = desc
                i = self.i
                self.i += 1
                si = i % len(self.stg)
                sv = self.stg[si][:, 0:a * b].rearrange("p (a b) -> p a b", a=a)
                sk = (self.tag, "st", si)
                DMA(sv, src, w=[sk])
                self._cvt(dst, sv, sk, dkey)

        def stream(wp, descs, depth=2):
            q = []
            n = len(descs)
            nxt = 0
            for i in range(n):
                while nxt < n and nxt <= i + depth:
                    q.append(wp.load(descs[nxt]))
                    nxt += 1
                yield q[i]

        def wdesc(w2d, r0, nr, c0, ncol):
            return (w2d[r0:r0 + nr, c0:c0 + ncol].rearrange("(a p) n -> p a n", p=128), nr // 128, ncol)

        def norm_chunk(gi, c, dst_fn, dkey_fn, srot, after=None, engs=("dve",), exact=False):
            b = srot.next()
            for k in range(NKC):
                si = sqrot.next()
                A("act", lambda e, k=k, si=si: e.activation(out=sqb[si][:], in_=h[:, k, c * 512:(c + 1) * 512],
                                                            func=AF.Square, scale=1.0 / 32.0),
                  r=[("h", k, c)], w=[("sqb", si)])
                A("pe", lambda e, k=k, si=si: e.matmul(ps[b][:], lhsT=ones, rhs=sqb[si][:], start=(k == 0), stop=(k == NKC - 1)),
                  r=[("sqb", si), "mats"], w=[("ps", b)])
            if exact:
                A("act", lambda e: e.activation(out=sdt[:], in_=ps[b][:], func=AF.Sqrt, bias=epsb[:, 0:1]),
                  r=[("ps", b), "epsb"], w=["sdt"])
                A("dve", lambda e: e.reciprocal(out=rstd[:], in_=sdt[:]), r=["sdt"], w=["rstd"])
            else:
                A("act", lambda e: e.activation(out=sdt[:], in_=ps[b][:], func=AF.Ln, bias=epsb[:, 0:1]),
                  r=[("ps", b), "epsb"], w=["sdt"])
                A("act", lambda e: e.activation(out=rstd[:], in_=sdt[:], func=AF.Exp, scale=-0.5), r=["sdt"], w=["rstd"])
            for k in range(NKC):
                eng = engs[k % len(engs)]
                dst = dst_fn(k)
                A(eng, lambda e, k=k, dst=dst: e.scalar_tensor_tensor(out=dst, in0=h[:, k, c * 512:(c + 1) * 512],
                                                                      scalar=gsb[:, gi * 8 + k:gi * 8 + k + 1], in1=rstd[:],
                                                                      op0=ALU.mult, op1=ALU.mult),
                  r=[("h", k, c), "gsb", "rstd"], w=[dkey_fn(k)])
                if after is not None:
                    after(k)

        def ffn(l, wg, wu, wd, gi):
            for tp in range(2):
                ffn_pass(l, wg, wu, wd, gi, tp)
                P.barrier()

        def ffn_pass(l, wg, wu, wd, gi, tp):
            if True:
                with ExitStack() as s2:
                    f2sb = mk_sb(s2)
                    hn = f2sb("hn", [128, NKC, 1024], BF16)
                    act = f2sb("act", [128, NFC, 1024], BF16)
                    sg = [f2sb(f"sg{i}", [128, 512], F32) for i in range(2)]
                    wp = WPool(s2, 2816, 3, 4, "f")
                    srot = Rot(range(7))
                    sgrot = Rot([0, 1])
                    for sub in range(2):
                        norm_chunk(gi, tp * 2 + sub, lambda k, sub=sub: hn[:, k, sub * 512:(sub + 1) * 512],
                                   lambda k, sub=sub: ("hn", k, sub), srot)
                    descs = []
                    for f2 in range(NFC // 2):
                        descs.append((wg[l, f2], 8, 256))
                        descs.append((wu[l, f2], 8, 256))
                    it = stream(wp, descs)
                    for f2 in range(NFC // 2):
                        gt, gk = next(it)
                        ut, uk = next(it)
                        for fi in range(2):
                            f = 2 * f2 + fi
                            for sub in range(2):
                                bg = srot.next()
                                bu = srot.next()
                                for k in range(NKC):
                                    A("pe", lambda e, k=k, bg=bg, gt=gt, fi=fi, sub=sub: e.matmul(
                                        ps[bg][:], lhsT=gt[:, k, fi * 128:(fi + 1) * 128], rhs=hn[:, k, sub * 512:(sub + 1) * 512],
                                        start=(k == 0), stop=(k == NKC - 1)), r=[gk, ("hn", k, sub)], w=[("ps", bg)])
                                for k in range(NKC):
                                    A("pe", lambda e, k=k, bu=bu, ut=ut, fi=fi, sub=sub: e.matmul(
                                        ps[bu][:], lhsT=ut[:, k, fi * 128:(fi + 1) * 128], rhs=hn[:, k, sub * 512:(sub + 1) * 512],
                                        start=(k == 0), stop=(k == NKC - 1)), r=[uk, ("hn", k, sub)], w=[("ps", bu)])
                                si = sgrot.next()
                                A("act", lambda e, si=si, bg=bg: e.activation(out=sg[si][:], in_=ps[bg][:], func=AF.Silu),
                                  r=[("ps", bg)], w=[("sg", si)])
                                A("dve", lambda e, si=si, bu=bu, f=f, sub=sub: e.tensor_tensor(
                                    out=act[:, f, sub * 512:(sub + 1) * 512], in0=sg[si][:], in1=ps[bu][:], op=ALU.mult),
                                  r=[("sg", si), ("ps", bu)], w=[("act", f, sub)])
                    descs = []
                    for d2 in range(4):
                        for half in range(2):
                            descs.append((wd[l, d2, half], 11, 256))
                    it = stream(wp, descs)
                    for d2 in range(4):
                        tl0, tk0 = next(it)
                        tl1, tk1 = next(it)
                        for di in range(2):
                            d = 2 * d2 + di
                            for sub in range(2):
                                b = srot.next()
                                c = tp * 2 + sub
                                for fc in range(NFC):
                                    tl, tk = (tl0, tk0) if fc < 11 else (tl1, tk1)
                                    A("pe", lambda e, fc=fc, tl=tl, di=di, sub=sub, b=b: e.matmul(
                                        ps[b][:], lhsT=tl[:, fc % 11, di * 128:(di + 1) * 128], rhs=act[:, fc, sub * 512:(sub + 1) * 512],
                                        start=(fc == 0), stop=(fc == NFC - 1)), r=[tk, ("act", fc, sub)], w=[("ps", b)])
                                A("dve", lambda e, d=d, c=c, b=b: e.scalar_tensor_tensor(
                                    out=h[:, d, c * 512:(c + 1) * 512], in0=ps[b][:], scalar=0.5, in1=h[:, d, c * 512:(c + 1) * 512],
                                    op0=ALU.mult, op1=ALU.add), r=[("ps", b), ("h", d, c)], w=[("h", d, c)])

        def ple(l, gi):
            with ExitStack() as s2:
                f2sb = mk_sb(s2)
                hn = f2sb("hnp", [128, NKC, T], BF16)
                ptb = f2sb("ptb", [128, 2, T], BF16)
                sig = [f2sb(f"sig{i}", [128, 512], F32) for i in range(2)]
                tmp = [f2sb(f"ptmp{i}", [128, 512], F32) for i in range(2)]
                wp = WPool(s2, 2048, 3, 4, "p")
                srot = Rot(range(7))
                r2 = Rot([0, 1])
                for c in range(NTC):
                    norm_chunk(gi, c, lambda k, c=c: hn[:, k, c * 512:(c + 1) * 512], lambda k, c=c: ("hnp", k, c), srot)
                    wp.load_into((pTd[l][:, c * 512:(c + 1) * 512].rearrange("(a p) t -> p a t", p=128), 2, 512),
                                 ptb[:, :, c * 512:(c + 1) * 512], ("ptb", c))
                descs = []
                for d2 in range(4):
                    descs.append((ple_gate[l, d2], 8, 256))
                    descs.append((ple_proj[l, d2], 2, 256))
                it = stream(wp, descs)
                for d2 in range(4):
                    gt, gk = next(it)
                    pt, pk = next(it)
                    for di in range(2):
                        d = 2 * d2 + di
                        for c in range(NTC):
                            bg = srot.next()
                            bp = srot.next()
                            for k in range(NKC):
                                A("pe", lambda e, k=k, bg=bg, gt=gt, di=di, c=c: e.matmul(
                                    ps[bg][:], lhsT=gt[:, k, di * 128:(di + 1) * 128], rhs=hn[:, k, c * 512:(c + 1) * 512],
                                    start=(k == 0), stop=(k == NKC - 1)), r=[gk, ("hnp", k, c)], w=[("ps", bg)])
                            for k in range(2):
                                A("pe", lambda e, k=k, bp=bp, pt=pt, di=di, c=c: e.matmul(
                                    ps[bp][:], lhsT=pt[:, k, di * 128:(di + 1) * 128], rhs=ptb[:, k, c * 512:(c + 1) * 512],
                                    start=(k == 0), stop=(k == 1)), r=[pk, ("ptb", c)], w=[("ps", bp)])
                            si = r2.next()
                            A("act", lambda e, si=si, bg=bg: e.activation(out=sig[si][:], in_=ps[bg][:], func=AF.Sigmoid),
                              r=[("ps", bg)], w=[("sig", si)])
                            A("dve", lambda e, si=si, bp=bp: e.tensor_tensor(out=tmp[si][:], in0=sig[si][:], in1=ps[bp][:], op=ALU.mult),
                              r=[("sig", si), ("ps", bp)], w=[("ptmp", si)])
                            A("pool", lambda e, si=si, d=d, c=c: e.tensor_tensor(
                                out=h[:, d, c * 512:(c + 1) * 512], in0=h[:, d, c * 512:(c + 1) * 512], in1=tmp[si][:], op=ALU.add),
                              r=[("ptmp", si), ("h", d, c)], w=[("h", d, c)])
            P.barrier()

        def final_norm():
            with ExitStack() as s2:
                f2sb = mk_sb(s2)
                ot = [f2sb(f"ot{i}", [128, 512], F32) for i in range(4)]
                srot = Rot(range(7))
                for c in range(NTC):
                    def after(k, c=c):
                        i = (c * 8 + k) % 4
                        DMA(outT[k * 128:(k + 1) * 128, c * 512:(c + 1) * 512], ot[i][:], r=[("ot", i)])
                    norm_chunk(8, c, lambda k, c=c: ot[(c * 8 + k) % 4][:], lambda k, c=c: ("ot", (c * 8 + k) % 4), srot,
                               after=after, engs=("dve",), exact=True)
            P.barrier()

        def mixer_proj(gi, fm_items, tm_items):
            import os as _os
            _nfm = int(_os.environ.get("PROJ_FM", "99"))
            _ntm = int(_os.environ.get("PROJ_TM", "99"))
            fm_items = fm_items[:_nfm]
            tm_items = tm_items[:_ntm]
            with ExitStack() as s2:
                f2sb = mk_sb(s2)
                hnc = f2sb("hnc", [128, NKC, 512], BF16)
                ropc = [f2sb(f"ropc{i}", [128, 2, 512], F32) for i in range(2)]
                xb = [f2sb(f"xb{i}", [128, 512], BF16) for i in range(2)]
                t1 = [f2sb(f"t1{i}", [128, 512], F32) for i in range(2)]
                t2 = [f2sb(f"t2{i}", [128, 512], F32) for i in range(2)]
                wp = WPool(s2, 1024, 2, 3, "m")
                srot = Rot(range(7))
                r2 = Rot([0, 1])
                for c in range(NTC):
                    norm_chunk(gi, c, lambda k: hnc[:, k, :], lambda k: ("hnc", k), srot)
                    rc = c % 2
                    DMA(ropc[rc][:], rope_d[:, :, c * 512:(c + 1) * 512].rearrange("a p t -> p a t"), w=[("ropc", rc)])
                    descs = [(w_ap, 8, 128) for (w_ap, _) in fm_items]
                    descs += [(w_ap, 8, n) for (w_ap, n, _, _) in tm_items]
                    it = stream(wp, descs)
                    for (_, dst_fn) in fm_items:
                        wt, wk = next(it)
                        b = srot.next()
                        for k in range(NKC):
                            A("pe", lambda e, k=k, b=b, wt=wt: e.matmul(ps[b][:], lhsT=wt[:, k, :], rhs=hnc[:, k, :],
                                                                        start=(k == 0), stop=(k == NKC - 1)),
                              r=[wk, ("hnc", k)], w=[("ps", b)])
                        i = r2.next()
                        A("act", lambda e, i=i, b=b: e.activation(out=xb[i][:], in_=ps[b][:], func=AF.Copy),
                          r=[("ps", b)], w=[("xb", i)])
                        A("dve", lambda e, i=i, b=b, rc=rc: e.tensor_tensor(out=t1[i][:], in0=ps[b][:], in1=ropc[rc][:, 0, :], op=ALU.mult),
                          r=[("ps", b), ("ropc", rc), ("xb", i)], w=[("t1", i)])
                        b2 = srot.next()
                        A("pe", lambda e, i=i, b2=b2: e.matmul(ps[b2][:], lhsT=perm, rhs=xb[i][:], start=True, stop=True),
                          r=[("xb", i), "mats"], w=[("ps", b2)])
                        A("dve", lambda e, i=i, b2=b2, rc=rc: e.tensor_tensor(out=t2[i][:], in0=ps[b2][:], in1=ropc[rc][:, 1, :], op=ALU.mult),
                          r=[("ps", b2), ("ropc", rc)], w=[("t2", i)])
                        dst, dkey = dst_fn(c)
                        A("pool", lambda e, i=i, dst=dst: e.tensor_tensor(out=dst, in0=t1[i][:], in1=t2[i][:], op=ALU.add),
                          r=[("t1", i), ("t2", i)], w=[dkey])
                    for (_, n, dst_fn, _) in tm_items:
                        wt, wk = next(it)
                        for tb in range(4):
                            b = srot.next()
                            for k in range(NKC):
                                A("pe", lambda e, k=k, b=b, wt=wt, tb=tb, n=n: e.matmul(
                                    ps[b][:, 0:n], lhsT=hnc[:, k, tb * 128:(tb + 1) * 128], rhs=wt[:, k, :],
                                    start=(k == 0), stop=(k == NKC - 1)), r=[wk, ("hnc", k)], w=[("ps", b)])
                            dst, dkey = dst_fn(c * 4 + tb)
                            if isinstance(dst, list):
                                for (dap, lo_, hi_) in dst:
                                    A("act", lambda e, b=b, dap=dap, lo_=lo_, hi_=hi_: e.activation(out=dap, in_=ps[b][:, lo_:hi_], func=AF.Copy),
                                      r=[("ps", b)], w=[dkey])
                            else:
                                A("act", lambda e, b=b, dst=dst, n=n: e.activation(out=dst, in_=ps[b][:, 0:n], func=AF.Copy),
                                  r=[("ps", b)], w=[dkey])
            P.barrier()

        LOOK = 2

        def run_units(units, front, back):
            fr = []
            for idx, u in enumerate(units):
                fr.append(front(u))
                if idx >= LOOK:
                    back(units[idx - LOOK], fr[idx - LOOK])
            for idx in range(max(0, len(units) - LOOK), len(units)):
                back(units[idx], fr[idx])

        def attn_jobs(jobs, accrot, srot, tiles, kz=None):
            ebs, pbs, rds, erot, prot, rrot, mengs = tiles
            units = []
            for ji, job in enumerate(jobs):
                c = job[0]
                accs = accrot.next()
                last = 4 * c + 3
                for j in range(last + 1):
                    for e_ in range(2):
                        units.append((job, accs, j, e_, j == last and e_ == 1, ji, j == 0 and e_ == 0))

            def front(u):
                (c, Q, qc, qkey, K, kc, kkey, vfn, vkey, mfn, mkey, dst, dkey), accs, j, e_, is_last, ji, is_first = u
                t0 = max(0, j * 128 - c * 512)
                n = 512 - t0
                lo, hi = e_ * 64, (e_ + 1) * 64
                b = srot.next()
                if kz is not None:
                    kzt, kzkey, pre = kz[ji]
                    if is_first and pre is not None:
                        pre()
                    A("pe", lambda e: e.matmul(
                        ps[b][:, 0:n], lhsT=kzt[:, e_, j * 128:(j + 1) * 128], rhs=Q[:, qc, c * 512 + t0:(c + 1) * 512],
                        start=True, stop=True), r=[qkey, kzkey], w=[("ps", b)])
                else:
                    A("pe", lambda e: e.matmul(
                        ps[b][:, 0:n], lhsT=K[lo:hi, kc, j * 128:(j + 1) * 128], rhs=Q[lo:hi, qc, c * 512 + t0:(c + 1) * 512],
                        start=True, stop=True), r=[qkey, kkey], w=[("ps", b)])
                ei = erot.next()
                A("act", lambda e: e.activation(out=ebs[ei][:, 0:n], in_=ps[b][:, 0:n], func=AF.Exp, scale=0.125),
                  r=[("ps", b)], w=[("eb", ei)])
                m = mfn(j, t0, n)
                if m is not None:
                    pi = prot.next()
                    A(mengs.next(), lambda e: e.tensor_tensor(out=pbs[pi][:, 0:n], in0=ebs[ei][:, 0:n], in1=m, op=ALU.mult),
                      r=[("eb", ei), mkey], w=[("pb", pi)])
                    return pbs[pi], ("pb", pi)
                return ebs[ei], ("eb", ei)

            def back(u, fr):
                (c, Q, qc, qkey, K, kc, kkey, vfn, vkey, mfn, mkey, dst, dkey), (ba, bb), j, e_, is_last, ji, is_first = u
                src, skey = fr
                last = 4 * c + 3
                t0 = max(0, j * 128 - c * 512)
                n = 512 - t0
                bk_ = ba if e_ == 0 else bb
                v = vfn(j, e_)
                A("pe", lambda e: e.matmul(ps[bk_][:, t0:512], lhsT=v, rhs=src[:, 0:n], start=(j == 0), stop=(j == last)),
                  r=[skey, vkey], w=[("ps", bk_)])
                if is_last:
                    di = rrot.next()
                    dsh = rds[di]
                    A("act", lambda e: e.activation(out=dsh[0:64, :], in_=ps[ba][64:128, :], func=AF.Copy), r=[("ps", ba)], w=[("rd", di)])
                    A("act", lambda e: e.activation(out=dsh[64:128, :], in_=ps[bb][0:64, :], func=AF.Copy), r=[("ps", bb)], w=[("rd", di)])
                    A("act", lambda e: e.activation(out=dsh[:], in_=dsh[:], func=AF.Ln), r=[("rd", di)], w=[("rd", di)])
                    A("act", lambda e: e.activation(out=dsh[:], in_=dsh[:], func=AF.Exp, scale=-1.0), r=[("rd", di)], w=[("rd", di)])
                    A("dve", lambda e: e.tensor_tensor(out=dst[0:64, :], in0=ps[ba][0:64, :], in1=dsh[0:64, :], op=ALU.mult),
                      r=[("ps", ba), ("rd", di)], w=[dkey])
                    A("dve", lambda e: e.tensor_tensor(out=dst[64:128, :], in0=ps[bb][64:128, :], in1=dsh[64:128, :], op=ALU.mult),
                      r=[("ps", bb), ("rd", di)], w=[dkey])

            run_units(units, front, back)

        def make_kz(f2sb, nbuf=2):
            kzs = [f2sb(f"kz{i}", [128, 2, T], BF16) for i in range(nbuf)]
            for i in range(nbuf):
                A("pool", lambda e, i=i: e.memset(kzs[i][64:128, 0, :], 0.0), w=[("kz", i)])
                A("pool", lambda e, i=i: e.memset(kzs[i][0:64, 1, :], 0.0), w=[("kz", i)])

            def fill(i, K, kc):
                def pre():
                    A("pool", lambda e: e.tensor_copy(out=kzs[i][0:64, 0, :], in_=K[0:64, kc, :]), r=[], w=[("kz", i)])
                    A("dve", lambda e: e.tensor_copy(out=kzs[i][64:128, 1, :], in_=K[64:128, kc, :]), r=[], w=[("kz", i)])
                return pre
            return kzs, fill

        def attn_tiles(f2sb, mengs=("dve", "pool")):
            ebs = [f2sb(f"eb{i}", [128, 512], BF16) for i in range(4)]
            pbs = [f2sb(f"pb{i}", [128, 512], BF16) for i in range(4)]
            rds = [f2sb(f"rd{i}", [128, 512], F32) for i in range(2)]
            return (ebs, pbs, rds, Rot(range(4)), Rot(range(4)), Rot(range(2)), Rot(mengs))

        def apply_wout(w2d, merged):
            with ExitStack() as s2:
                wp = WPool(s2, 2048, 3, 4, "o")
                srot = Rot(range(7))
                descs = [(w2d[d2], 8, 256) for d2 in range(4)]
                it = stream(wp, descs)
                for d2 in range(4):
                    wt, wk = next(it)
                    for di in range(2):
                        d = 2 * d2 + di
                        for c in range(NTC):
                            b = srot.next()
                            for k in range(NKC):
                                A("pe", lambda e, k=k, b=b, wt=wt, di=di, c=c: e.matmul(
                                    ps[b][:], lhsT=wt[:, k, di * 128:(di + 1) * 128], rhs=merged[:, k, c * 512:(c + 1) * 512],
                                    start=(k == 0), stop=(k == NKC - 1)), r=[wk, ("mg", k, c)], w=[("ps", b)])
                            A("dve", lambda e, b=b, d=d, c=c: e.tensor_tensor(
                                out=h[:, d, c * 512:(c + 1) * 512], in0=ps[b][:], in1=h[:, d, c * 512:(c + 1) * 512], op=ALU.add),
                              r=[("ps", b), ("h", d, c)], w=[("h", d, c)])
            P.barrier()

        def even_mixer(gi):
            with ExitStack() as sm:
                msb = mk_sb(sm)
                merged = msb("merged", [128, NKC, T], BF16)
                A("dve", lambda e: e.tensor_tensor(out=lv[:, 0, :], in0=lv[:, 0, :], in1=lv[:, 1, :], op=ALU.mult), r=["lv"], w=["lv"])
                A("dve", lambda e: e.tensor_tensor(out=lv[:, 2, :], in0=lv[:, 2, :], in1=lv[:, 3, :], op=ALU.mult), r=["lv"], w=["lv"])
                A("dve", lambda e: e.reduce_sum(out=lsm[:, 0:1], in_=lv[:, 0, :], axis=AX.X), r=["lv"], w=["lsm"])
                A("dve", lambda e: e.reduce_sum(out=lsm[:, 1:2], in_=lv[:, 2, :], axis=AX.X), r=["lv"], w=["lsm"])
                A("act", lambda e: e.activation(out=lsm[:, 2:4], in_=lsm[:, 0:2], func=AF.Exp), r=["lsm"], w=["lsm"])
                A("dve", lambda e: e.tensor_tensor(out=lsm[:, 4:5], in0=lsm[:, 3:4], in1=lsm[:, 2:3], op=ALU.subtract), r=["lsm"], w=["lsm"])
                A("dve", lambda e: e.tensor_scalar(out=lsm[:, 5:6], in0=lsm[:, 4:5], scalar1=-0.2, scalar2=None, op0=ALU.add), r=["lsm"], w=["lsm"])
                A("dve", lambda e: e.tensor_scalar(out=lsm[:, 6:7], in0=lsm[:, 7:8], scalar1=0.8, scalar2=None, op0=ALU.mult), r=["lsm", "lsm7"], w=["lsm"])
                neglam = lsm[:, 5:6]
                gsub = lsm[:, 6:7]

                for kk in range(NKC):
                    if (kk < 4 and not cfg["dsa"]) or (kk >= 4 and not cfg["diff"]):
                        A("pool", lambda e, kk=kk: e.memset(merged[:, kk, :], 0.0), w=[("mg", kk, c) for c in range(NTC)])
                if cfg["dsa"]:
                    with ExitStack() as sa:
                        asb = mk_sb(sa)
                        QA = asb("QA", [128, 4, T], BF16)
                        KA = asb("KA", [128, 1, T], BF16)
                        QI = asb("QI", [128, 4, T], BF16)
                        KI = asb("KI", [128, 1, T], BF16)
                        VA = asb("VA", [128, NTB, 192], BF16)
                        A("pool", lambda e: e.memset(VA[:, :, 64:128], 1.0), w=[("VA1",)])
                        WI = asb("WI", [128, NTB, 8], F32)
                        fm = []
                        bufs = [(QA, n, "QA") for n in range(4)] + [(KA, 0, "KA")] + [(QI, n, "QI") for n in range(4)] + [(KI, 0, "KI")]
                        for ci, (buf, n, nm) in enumerate(bufs):
                            fm.append((ew_fm[ci],
                                       lambda c, buf=buf, n=n, nm=nm: (buf[:, n, c * 512:(c + 1) * 512], (nm, n, c))))
                        tm = [(ew_va[0], 64, lambda blk: ([(VA[:, blk, 0:64], 0, 64), (VA[:, blk, 128:192], 0, 64)], ("VA", blk)), None),
                              (ew_wi[0], 8, lambda blk: (WI[:, blk, :], ("WI", blk)), None)]
                        mixer_proj(gi, fm, tm)
                        with ExitStack() as s2:
                            f2sb = mk_sb(s2)
                            Ib = [f2sb(f"Ib{i}", [128, T], F32) for i in range(2)]
                            rb = [f2sb(f"rb{i}", [128, 512], F32) for i in range(2)]
                            mk = f2sb("mk", [128, T], BF16)
                            junk = mk
                            maskT = f2sb("maskT", [128, NTB, 512], BF16)
                            bnd = [f2sb(f"bnd{i}", [128, 8], F32) for i in range(2)]
                            gem = [f2sb(f"gem{i}", [128, 2], mybir.dt.uint32) for i in range(2)]
                            tiles = attn_tiles(f2sb, mengs=("dve",))
                            srot = Rot([0, 1, 2])
                            rrot = Rot(range(2))
                            accrot = Rot([(3, 4), (5, 6)])
                            NIT = 15

                            def s1(i):
                                c = i // 4
                                cols = (i + 1) * 128
                                I = Ib[i % 2]
                                ik = ("I", i % 2)
                                bd = bnd[i % 2]
                                bk = ("bnd", i % 2)
                                for sc in range((cols + 511) // 512):
                                    w_ = min(512, cols - sc * 512)
                                    for hd in range(8):
                                        lo, hi = (hd % 2) * 64, (hd % 2 + 1) * 64
                                        b = srot.next()
                                        A("pe", lambda e, b=b, lo=lo, hi=hi, hd=hd, sc=sc, w_=w_: e.matmul(
                                            ps[b][:, 0:w_], lhsT=QI[lo:hi, hd // 2, i * 128:(i + 1) * 128],
                                            rhs=KI[lo:hi, 0, sc * 512:sc * 512 + w_], start=True, stop=True),
                                          r=[("QI", hd // 2, c), ("KI", 0, sc)], w=[("ps", b)])
                                        ri = rrot.next()
                                        A("act", lambda e, b=b, ri=ri, w_=w_: e.activation(out=rb[ri][:, 0:w_], in_=ps[b][:, 0:w_], func=AF.Relu),
                                          r=[("ps", b)], w=[("rb", ri)])
                                        if hd == 0:
                                            A("dve", lambda e, ri=ri, sc=sc, w_=w_: e.tensor_scalar(
                                                out=I[:, sc * 512:sc * 512 + w_], in0=rb[ri][:, 0:w_], scalar1=WI[:, i, 0:1], scalar2=None, op0=ALU.mult),
                                              r=[("rb", ri), ("WI", i)], w=[ik])
                                        else:
                                            A("dve", lambda e, ri=ri, sc=sc, w_=w_, hd=hd: e.scalar_tensor_tensor(
                                                out=I[:, sc * 512:sc * 512 + w_], in0=rb[ri][:, 0:w_], scalar=WI[:, i, hd:hd + 1],
                                                in1=I[:, sc * 512:sc * 512 + w_], op0=ALU.mult, op1=ALU.add),
                                              r=[("rb", ri), ("WI", i), ik], w=[ik])
                                if i >= 2:
                                    A("dve", lambda e: e.tensor_reduce(out=bd[:, 0:1], in_=I[:, 0:cols], axis=AX.X, op=ALU.min), r=[ik], w=[bk])
                                    A("dve", lambda e: e.tensor_reduce(out=bd[:, 1:2], in_=I[:, 0:cols], axis=AX.X, op=ALU.max), r=[ik], w=[bk])
                                    A("dve", lambda e: e.tensor_tensor(out=bd[:, 2:3], in0=bd[:, 1:2], in1=bd[:, 0:1], op=ALU.subtract), r=[bk], w=[bk])
                                A("pool", lambda e: e.tensor_tensor(out=I[:, i * 128:(i + 1) * 128], in0=I[:, i * 128:(i + 1) * 128],
                                                                    in1=negm[:], op=ALU.add), r=[ik, "negm"], w=[ik])

                            def s2(i):
                                ti = i % 4
                                cols = (i + 1) * 128
                                I = Ib[i % 2]
                                ik = ("I", i % 2)
                                bd = bnd[i % 2]
                                bk = ("bnd", i % 2)
                                gm = gem[i % 2]
                                gk = ("gem", i % 2)
                                if i >= 2:
                                    for it in range(NIT):
                                        A("dve", lambda e, it=it: e.scalar_tensor_tensor(out=bd[:, 3:4], in0=bd[:, 2:3], scalar=float(2.0 ** -(it + 1)),
                                                                                        in1=bd[:, 0:1], op0=ALU.mult, op1=ALU.add), r=[bk], w=[bk])
                                        A("dve", lambda e: e.tensor_scalar(out=junk[:, 0:cols], in0=I[:, 0:cols], scalar1=bd[:, 3:4], scalar2=None,
                                                                           op0=ALU.is_ge, op1=ALU.add, accum_out=bd[:, 4:5]), r=[ik, bk], w=[bk, "mk"])
                                        A("dve", lambda e: e.tensor_single_scalar(out=gm[:, 0:1], in_=bd[:, 4:5], scalar=255.5, op=ALU.is_ge), r=[bk], w=[gk])
                                        A("dve", lambda e: e.copy_predicated(out=bd[:, 0:1], mask=gm[:, 0:1], data=bd[:, 3:4]), r=[bk, gk], w=[bk])
                                    A("dve", lambda e: e.tensor_scalar(out=mk[:, 0:cols], in0=I[:, 0:cols], scalar1=bd[:, 0:1], scalar2=None, op0=ALU.is_ge),
                                      r=[ik, bk], w=["mk"])
                                else:
                                    A("dve", lambda e: e.tensor_single_scalar(out=mk[:, 0:cols], in_=I[:, 0:cols], scalar=-1.0e29, op=ALU.is_ge), r=[ik], w=["mk"])
                                for j0 in range(0, i + 1, 8):
                                    nn = min(8, i + 1 - j0)
                                    for jj in range(nn):
                                        A("pe", lambda e, jj=jj, j0=j0: e.transpose(
                                            out=pTp[:, jj * 128:(jj + 1) * 128],
                                            in_=mk[:, (j0 + jj) * 128:(j0 + jj + 1) * 128], identity=ident),
                                          r=["mk", "mats"], w=["pT"])
                                    A("act", lambda e, nn=nn, j0=j0: e.activation(
                                        out=maskT[:, j0:j0 + nn, ti * 128:(ti + 1) * 128],
                                        in_=pTp[:, 0:nn * 128].rearrange("p (a b) -> p a b", a=nn), func=AF.Copy),
                                      r=["pT"], w=["maskT"])

                            s1(0)
                            for i in range(NTB):
                                if i + 1 < NTB:
                                    s1(i + 1)
                                s2(i)
                                if i % 4 == 3:
                                    c = i // 4
                                    attn_jobs([(c, QA, hp, ("QA", hp, c), KA, 0, ("KA", 0, c),
                                                lambda j, e_: VA[:, j, e_ * 64:e_ * 64 + 128], ("VA", 0),
                                                lambda j, t0, n: maskT[:, j, t0:512], "maskT",
                                                merged[:, hp, c * 512:(c + 1) * 512], ("mg", hp, c)) for hp in range(4)],
                                              accrot, srot, tiles)
                        P.barrier()

                if cfg["diff"]:
                    with ExitStack() as sa:
                        asb = mk_sb(sa)
                        QB = asb("QB", [128, 4, T], BF16)
                        KB = asb("KB", [128, 4, T], BF16)
                        VB = asb("VB", [128, NTB, 512], BF16)
                        fm = []
                        bufs = [(QB, n, "QB") for n in range(4)] + [(KB, n, "KB") for n in range(4)]
                        for ci, (buf, n, nm) in enumerate(bufs):
                            fm.append((ew_fm[10 + ci],
                                       lambda c, buf=buf, n=n, nm=nm: (buf[:, n, c * 512:(c + 1) * 512], (nm, n, c))))
                        tm = [(ew_vb[n], 128,
                               lambda blk, n=n: (VB[:, blk, n * 128:(n + 1) * 128], ("VB", blk, n)), None) for n in range(4)]
                        mixer_proj(gi, fm, tm)
                        with ExitStack() as s2:
                            f2sb = mk_sb(s2)
                            strip = f2sb("cstrip", [128, STRIP_W], BF16)
                            DMA(strip[:], strips_d[0], w=["strip"])
                            ebs = [f2sb(f"eb{i}", [128, 512], BF16) for i in range(3)]
                            pbs = [f2sb(f"pb{i}", [128, 512], BF16) for i in range(3)]
                            fa = [f2sb(f"fa{i}", [128, 512], F32) for i in range(4)]
                            ob = f2sb("ob", [128, 512], F32)
                            sq2 = f2sb("sq2", [128, 512], BF16)
                            ebs = ebs + [f2sb("eb3", [128, 512], BF16)]
                            pbs = pbs + [f2sb("pb3", [128, 512], BF16)]
                            erot, prot = Rot(range(4)), Rot(range(4))
                            srot = Rot([0, 1, 2])
                            dkzs, dfill = make_kz(f2sb)
                            units = []
                            for hd in range(4):
                                for c in range(NTC):
                                    last = 4 * c + 3
                                    for j in range(last + 1):
                                        for e_ in range(2):
                                            units.append((hd, c, j, e_, j == last and e_ == 1))

                            def dfront(u):
                                hd, c, j, e_, is_last = u
                                t0 = max(0, j * 128 - c * 512)
                                n = 512 - t0
                                lo, hi = e_ * 64, (e_ + 1) * 64
                                b = srot.next()
                                if c == 0 and j == 0 and e_ == 0:
                                    dfill(hd % 2, KB, hd)()
                                kzt = dkzs[hd % 2]
                                A("pe", lambda e: e.matmul(
                                    ps[b][:, 0:n], lhsT=kzt[:, e_, j * 128:(j + 1) * 128],
                                    rhs=QB[:, hd, c * 512 + t0:(c + 1) * 512], start=True, stop=True),
                                  r=[("QB", hd, c), ("kz", hd % 2)], w=[("ps", b)])
                                ei = erot.next()
                                A("act", lambda e: e.activation(out=ebs[ei][:, 0:n], in_=ps[b][:, 0:n], func=AF.Exp, scale=0.125),
                                  r=[("ps", b)], w=[("eb", ei)])
                                if j >= 4 * c:
                                    off = c * 512 - j * 128 + 384 + t0
                                    pi = prot.next()
                                    A("dve", lambda e: e.tensor_tensor(
                                        out=pbs[pi][:, 0:n], in0=ebs[ei][:, 0:n], in1=strip[:, off:off + n], op=ALU.mult),
                                      r=[("eb", ei), "strip"], w=[("pb", pi)])
                                    return pbs[pi], ("pb", pi)
                                return ebs[ei], ("eb", ei)

                            def dback(u, fr):
                                hd, c, j, e_, is_last = u
                                src, skey = fr
                                last = 4 * c + 3
                                t0 = max(0, j * 128 - c * 512)
                                n = 512 - t0
                                nb, db = 3 + 2 * e_, 4 + 2 * e_
                                A("pe", lambda e: e.matmul(
                                    ps[nb][:, t0:512], lhsT=VB[:, j, hd * 128:(hd + 1) * 128], rhs=src[:, 0:n],
                                    start=(j == 0), stop=(j == last)),
                                  r=[skey, ("VB", j, hd)], w=[("ps", nb)])
                                A("pe", lambda e: e.matmul(
                                    ps[db][:, t0:512], lhsT=ones, rhs=src[:, 0:n], start=(j == 0), stop=(j == last)),
                                  r=[skey, "mats"], w=[("ps", db)])
                                if not is_last:
                                    return
                                A("act", lambda e: e.activation(out=fa[0][:], in_=ps[4][:], func=AF.Ln), r=[("ps", 4)], w=[("fa", 0)])
                                A("act", lambda e: e.activation(out=fa[0][:], in_=fa[0][:], func=AF.Exp, scale=-1.0), r=[("fa", 0)], w=[("fa", 0)])
                                A("dve", lambda e: e.tensor_tensor(out=fa[1][:], in0=ps[3][:], in1=fa[0][:], op=ALU.mult),
                                  r=[("ps", 3), ("fa", 0)], w=[("fa", 1)])
                                A("act", lambda e: e.activation(out=fa[2][:], in_=ps[6][:], func=AF.Ln), r=[("ps", 6)], w=[("fa", 2)])
                                A("act", lambda e: e.activation(out=fa[2][:], in_=fa[2][:], func=AF.Exp, scale=-1.0), r=[("fa", 2)], w=[("fa", 2)])
                                A("dve", lambda e: e.tensor_tensor(out=fa[3][:], in0=ps[5][:], in1=fa[2][:], op=ALU.mult),
                                  r=[("ps", 5), ("fa", 2)], w=[("fa", 3)])
                                A("dve", lambda e: e.scalar_tensor_tensor(out=ob[:], in0=fa[3][:], scalar=neglam, in1=fa[1][:],
                                                                          op0=ALU.mult, op1=ALU.add),
                                  r=[("fa", 3), ("fa", 1), "lsm"], w=["ob"])
                                A("act", lambda e: e.activation(out=sq2[:], in_=ob[:], func=AF.Square, scale=float(128.0 ** -0.5)),
                                  r=["ob"], w=["sq2"])
                                b = srot.next()
                                A("pe", lambda e: e.matmul(ps[b][:], lhsT=ones, rhs=sq2[:], start=True, stop=True),
                                  r=["sq2", "mats"], w=[("ps", b)])
                                A("act", lambda e: e.activation(out=fa[0][:], in_=ps[b][:], func=AF.Ln, bias=epsb[:, 1:2]),
                                  r=[("ps", b), "epsb"], w=[("fa", 0)])
                                A("act", lambda e: e.activation(out=fa[2][:], in_=fa[0][:], func=AF.Exp, scale=-0.5), r=[("fa", 0)], w=[("fa", 2)])
                                A("dve", lambda e: e.scalar_tensor_tensor(
                                    out=merged[:, 4 + hd, c * 512:(c + 1) * 512], in0=ob[:], scalar=gsub, in1=fa[2][:],
                                    op0=ALU.mult, op1=ALU.mult), r=["ob", ("fa", 2), "lsm"], w=[("mg", 4 + hd, c)])

                            run_units(units, dfront, dback)
                        P.barrier()
                apply_wout(even_w_out, merged)

        def odd_mixer(gi):
            with ExitStack() as sm:
                msb = mk_sb(sm)
                merged = msb("mergedo", [128, NKC, T], BF16)
                for g in range(2):
                    odd_group(g, gi, merged)
                apply_wout(odd_w_out, merged)

        def odd_group(g, gi, merged):
            if True:
                if True:
                    with ExitStack() as sa:
                        asb = mk_sb(sa)
                        QG = asb("QG", [128, 4, T], BF16)
                        KG = asb("KG", [128, 4, T], BF16)
                        VG = asb("VG", [128, NTB, 4, 192], BF16)
                        A("pool", lambda e: e.memset(VG[:, :, :, 64:128], 1.0), w=[("VG1",)])
                        fm = []
                        for n in range(4):
                            fm.append((odd_w_in[4 * g + n],
                                       lambda c, n=n: (QG[:, n, c * 512:(c + 1) * 512], ("QG", n, c))))
                        for n in range(4):
                            fm.append((odd_w_in[8 + 4 * g + n],
                                       lambda c, n=n: (KG[:, n, c * 512:(c + 1) * 512], ("KG", n, c))))
                        tm = [(odd_w_in[16 + 4 * g + n], 128,
                               lambda blk, n=n: ([(VG[:, blk, n, 0:64], 0, 64), (VG[:, blk, n, 128:192], 64, 128)], ("VG", blk, n)), None) for n in range(4)]
                        if cfg["proj"]:
                            mixer_proj(gi, fm, tm)
                        if not cfg["attn"]:
                            for hp in range(4):
                                A("pool", lambda e, hp=hp: e.memset(merged[:, 4 * g + hp, :], 0.0), w=[("mg", 4 * g + hp, c) for c in range(NTC)])
                            return
                        with ExitStack() as s2:
                            f2sb = mk_sb(s2)
                            strip = f2sb("mstrip", [128, STRIP_W], BF16)
                            DMA(strip[:], strips_d[1], w=["strip"])
                            tiles = attn_tiles(f2sb, mengs=("dve",))
                            srot = Rot([0, 1, 2])
                            accrot = Rot([(3, 4), (5, 6)])
                            jobs = []
                            for hp in range(4):
                                for c in range(NTC):
                                    jobs.append((c, QG, hp, ("QG", hp, c), KG, hp, ("KG", hp, c),
                                                 lambda j, e_, hp=hp: VG[:, j, hp, e_ * 64:e_ * 64 + 128], ("VG", 0, 0),
                                                 lambda j, t0, n, c=c: strip[:, c * 512 - j * 128 + 384 + t0:c * 512 - j * 128 + 384 + t0 + n], "strip",
                                                 merged[:, 4 * g + hp, c * 512:(c + 1) * 512], ("mg", 4 * g + hp, c)))
                            kzs, fill = make_kz(f2sb)
                            kzd = {}
                            for ji in range(len(jobs)):
                                hp = ji // NTC
                                kzd[ji] = (kzs[hp % 2], ("kz", hp % 2), fill(hp % 2, KG, hp) if ji % NTC == 0 else None)
                            attn_jobs(jobs, accrot, srot, tiles, kz=kzd)
                        P.barrier()

        for l in range(cfg["layers"]):
            if cfg["ffn"]:
                ffn(l, ffn_w["ffn_a_wg"], ffn_w["ffn_a_wu"], ffn_w["ffn_a_wd"], l * 4 + 0)
            if l % 2 == 0:
                if cfg["mix_even"]:
                    even_mixer(l * 4 + 1)
            else:
                if cfg["mix_odd"]:
                    odd_mixer(l * 4 + 1)
            if cfg["ffn"]:
                ffn(l, ffn_w["ffn_b_wg"], ffn_w["ffn_b_wu"], ffn_w["ffn_b_wd"], l * 4 + 2)
            if cfg["ple"]:
                ple(l, l * 4 + 3)
        final_norm()
        P.emit(st)
    nc._declared_inputs = declared
    return nc


def prep_inputs(inputs, b):
    f = np.float32
    g = lambda k: np.asarray(inputs[k], f)
    rope, strips, mats, negmask = _host_consts()
    gains = []
    for l in range(2):
        for nm in ("norm_ffn_a", "norm_mix", "norm_ffn_b", "norm_ple"):
            gains.append(_col(g(nm)[l]))
    gains.append(_col(g("final_norm")))
    gains = np.ascontiguousarray(np.concatenate(gains, axis=1))
    ewi = g("even_w_in")[0]
    qa, ka, va = ewi[:, 0:512], ewi[:, 512:576], ewi[:, 576:640]
    qi, ki, wi = ewi[:, 640:1152], ewi[:, 1152:1216], ewi[:, 1216:1224]
    qb, kb, vb = ewi[:, 1224:1736], ewi[:, 1736:2248], ewi[:, 2248:2760]
    ew_fm = np.ascontiguousarray(np.concatenate([qa, ka, ka, qi, ki, ki, qb, kb], axis=1))
    def tile_w(W, tc):
        R_, C_ = W.shape
        return np.ascontiguousarray(W.reshape(R_ // 128, 128, C_ // tc, tc).transpose(2, 1, 0, 3))

    m = {
        "xT": np.ascontiguousarray(g("x")[b].T),
        "pT": np.ascontiguousarray(np.transpose(g("p")[:, b], (0, 2, 1))),
        "gains": gains,
        "ple_gate": np.stack([tile_w(g("ple_gate")[l], 256) for l in range(2)], 0),
        "ple_proj": np.stack([tile_w(g("ple_proj")[l], 256) for l in range(2)], 0),
        "ew_fm": tile_w(ew_fm, 128), "ew_va": tile_w(np.ascontiguousarray(va), 64),
        "ew_wi": tile_w(np.ascontiguousarray(wi), 8), "ew_vb": tile_w(np.ascontiguousarray(vb), 128),
        "even_w_out": tile_w(g("even_w_out")[0], 256),
        "lamv": np.ascontiguousarray(np.stack([g("diff_lambda_q1")[0], g("diff_lambda_k1")[0],
                                               g("diff_lambda_q2")[0], g("diff_lambda_k2")[0]], 0)),
        "subln": np.ascontiguousarray(g("diff_subln")[0].reshape(128, 1)),
        "odd_w_in": tile_w(g("odd_w_in")[0], 128), "odd_w_out": tile_w(g("odd_w_out")[0], 256),
        "rope": rope, "strips": strips, "mats": mats, "negmask": negmask,
    }
    for nm in ("ffn_a_wg", "ffn_a_wu", "ffn_b_wg", "ffn_b_wu"):
        m[nm] = np.stack([tile_w(g(nm)[l], 256) for l in range(2)], 0)
    for nm in ("ffn_a_wd", "ffn_b_wd"):
        w = g(nm)
        m[nm] = np.stack([np.stack([np.stack([tile_w(w[l][hf * 1408:(hf + 1) * 1408, d2 * 256:(d2 + 1) * 256], 256)[0]
                                              for hf in range(2)], 0) for d2 in range(4)], 0) for l in range(2)], 0)
    return m


_NC_CACHE = {}


def kernel(**inputs):
    import time as _time
    _t0 = _time.time()
    key = tuple(sorted(CFG.items()))
    if key not in _NC_CACHE:
        _NC_CACHE[key] = build_program(CFG)
    nc = _NC_CACHE[key]
    print(f"[kernel] build {_time.time() - _t0:.1f}s", flush=True)
    n = 8
    shared = prep_inputs(inputs, 0)
    in_maps = []
    for b in range(n):
        m = dict(shared)
        m["xT"] = np.ascontiguousarray(np.asarray(inputs["x"], np.float32)[b].T)
        m["pT"] = np.ascontiguousarray(np.transpose(np.asarray(inputs["p"], np.float32)[:, b], (0, 2, 1)))
        in_maps.append({k: v for k, v in m.items() if k in nc._declared_inputs})
    print(f"[kernel] prep {_time.time() - _t0:.1f}s", flush=True)
    import os as _os
    _nd = int(_os.environ.get("DBG_CORES", "8"))
    if _nd != 8:
        res = run_bass_kernel_spmd(nc, in_maps[:_nd], core_ids=list(range(_nd)))
        res.results.extend([res.results[0]] * (8 - _nd))
    else:
        res = run_bass_kernel_spmd(nc, in_maps, core_ids=list(range(n)))
    print(f"[kernel] ran {_time.time() - _t0:.1f}s", flush=True)
    out = np.stack([np.asarray(res.results[b]["outT"], np.float32).T for b in range(n)], 0)
    return np.ascontiguousarray(out)
```

```python
import numpy as np
import ml_dtypes
import concourse.bass as bass
import concourse.mybir as mybir
from concourse.bass_utils import run_bass_kernel_spmd

F32 = mybir.dt.float32
BF16 = mybir.dt.bfloat16
ALU = mybir.AluOpType
AF = mybir.ActivationFunctionType
AX = mybir.AxisListType

T = 2048
D = 1024
DFF = 2816
NKC = D // 128
NFC = DFF // 128
NTB = T // 128
NTC = T // 512
PLE = 256
EVEN_IN = 2760
NEG = -1.0e30
NEG2 = -2.0e30


class Op:
    __slots__ = ("eng", "fn", "waits", "signal", "sem", "val", "is_dma", "idx", "is_bar")

    def __init__(self, eng, fn, is_dma=False):
        self.eng = eng
        self.fn = fn
        self.waits = []
        self.signal = False
        self.sem = None
        self.val = 0
        self.is_dma = is_dma
        self.idx = -1
        self.is_bar = False


ENGS = ("pe", "act", "dve", "pool", "sp")
SEM_LIM = 30000


class Prog:
    def __init__(self, nc):
        self.nc = nc
        self.ops = {e: [] for e in ENGS}
        self.last_w = {}
        self.readers = {}
        self.seen = {e: {s: -1 for s in ENGS} for e in ENGS}
        self.seen_dma = {e: set() for e in ENGS}
        self.pending_dma = []
        self.dma_list = []
        self.nops = 0

    NDMA = 24

    def add(self, eng, fn, reads=(), writes=(), dma=False):
        op = Op(eng, fn, is_dma=dma)
        deps = []
        if dma:
            k = len(self.dma_list)
            if k >= self.NDMA:
                deps.append((self.dma_list[k - self.NDMA], True))
            self.dma_list.append(op)
        for r in reads:
            w = self.last_w.get(r)
            if w is not None:
                deps.append((w, True))
            if r == "pT" or (isinstance(r, tuple) and r[0] == "ps"):
                for rd in self.readers.get(r, ()):
                    if rd.eng != eng:
                        deps.append((rd, True))
        for r in writes:
            w = self.last_w.get(r)
            if w is not None:
                deps.append((w, True))
            for rd in self.readers.get(r, ()):
                deps.append((rd, False))
        best = {}
        for d, is_raw in deps:
            if d.is_dma:
                self._dep(op, d, is_raw)
                continue
            if d.eng == eng and eng in ("pe", "sp"):
                continue
            cur = best.get(d.eng)
            if cur is None or d.idx > cur.idx:
                best[d.eng] = d
        for d in best.values():
            self._dep(op, d, True)
        op.idx = len(self.ops[eng])
        self.ops[eng].append(op)
        for r in reads:
            self.readers.setdefault(r, []).append(op)
        for r in writes:
            self.last_w[r] = op
            self.readers[r] = []
        if dma:
            self.pending_dma.append(op)
        self.nops += 1
        return op

    def _dep(self, op, d, is_raw):
        e = op.eng
        if d.is_dma:
            if d in self.seen_dma[e]:
                return
            self.seen_dma[e].add(d)
            d.signal = True
            op.waits.append(d)
            return
        if d.eng == e:
            if e == "pe" or e == "sp":
                return
        if self.seen[e][d.eng] >= d.idx:
            return
        self.seen[e][d.eng] = d.idx
        d.signal = True
        op.waits.append(d)

    def barrier(self):
        bar = Op("sp", None)
        bar.is_bar = True
        for e in ENGS:
            if e == "sp":
                continue
            if self.ops[e]:
                self._dep(bar, self.ops[e][-1], True)
        for d in self.pending_dma:
            self._dep(bar, d, True)
        self.pending_dma = []
        bar.idx = len(self.ops["sp"])
        self.ops["sp"].append(bar)
        bar.signal = True
        for e in ENGS:
            if e == "sp":
                continue
            w = Op(e, None)
            w.is_bar = True
            w.waits.append(bar)
            self.seen[e]["sp"] = bar.idx
            w.idx = len(self.ops[e])
            self.ops[e].append(w)
        self.last_w = {}
        self.readers = {}

    def emit(self, stack):
        nc = self.nc
        ndma = self.NDMA
        dsems = [stack.enter_context(nc.semaphore(f"s_dma_{i}")) for i in range(ndma)]
        dcnt = [0] * ndma
        for k, op in enumerate(self.dma_list):
            di = k % ndma
            if dcnt[di] + 16 > SEM_LIM:
                dsems[di] = stack.enter_context(nc.semaphore(f"s_dma_{di}_{k}"))
                dcnt[di] = 0
            dcnt[di] += 16
            op.sem = dsems[di]
            op.val = dcnt[di]
        eng_sems = {}
        for e in ENGS:
            cur = None
            cnt = 0
            for op in self.ops[e]:
                if not op.signal or op.is_dma:
                    continue
                if cur is None or cnt >= SEM_LIM:
                    cur = stack.enter_context(nc.semaphore(f"s_{e}_{len(eng_sems)}"))
                    eng_sems[(e, len(eng_sems))] = cur
                    cnt = 0
                cnt += 1
                op.sem = cur
                op.val = cnt
        block = stack.enter_context(nc.Block())

        def run(e):
            def body(eng):
                for op in self.ops[e]:
                    for d in op.waits:
                        eng.wait_ge(d.sem, d.val)
                    if op.fn is None:
                        if op.signal:
                            eng.sem_inc(op.sem, 1)
                        continue
                    ins = op.fn(eng)
                    if op.is_dma:
                        ins.then_inc(op.sem, 16)
                    elif op.signal:
                        ins.then_inc(op.sem, 1)
            return body

        block.tensor(run("pe"))
        block.scalar(run("act"))
        block.vector(run("dve"))
        block.gpsimd(run("pool"))
        block.sync(run("sp"))


STRIP_W = 2432


def _host_consts():
    f32 = np.float32
    inv = (1.0 / (f32(10000.0) ** (np.arange(0, 64, 2, dtype=f32) / f32(64)))).astype(f32)
    ang = (np.arange(T, dtype=f32)[:, None] * inv[None, :]).astype(f32)
    cos = np.cos(ang).astype(f32)
    sin = np.sin(ang).astype(f32)
    p = np.arange(128)
    d = p % 64
    j = d % 32
    C2 = cos[:, j].T.copy()
    S2 = (sin[:, j].T * np.where(d < 32, -1.0, 1.0)[:, None]).astype(f32)
    rope = np.ascontiguousarray(np.stack([C2, S2], 0)).astype(f32)
    x = np.arange(STRIP_W)[None, :] - 384 - p[:, None]
    causal = (x >= 0).astype(f32)
    mult = (((x >= 0) & (x <= 128)).astype(f32)
            + ((x >= 0) & (x % 4 == 0) & (x <= 512)).astype(f32)
            + ((x >= 0) & (x % 16 == 0) & (x <= 2048)).astype(f32))
    strips = np.stack([causal, mult], 0).astype(ml_dtypes.bfloat16)
    ident = np.eye(128, dtype=f32)
    perm = np.zeros((128, 128), f32)
    for m in range(128):
        perm[m ^ 32, m] = 1.0
    ones = np.ones((128, 128), f32)
    mats = np.stack([ident, perm, ones], 0).astype(ml_dtypes.bfloat16)
    tt = np.arange(128)
    negmask = np.where(tt[None, :] > tt[:, None], np.float32(NEG), np.float32(0.0)).astype(f32)
    return rope, strips, mats, negmask


def _col(v):
    v = np.asarray(v, np.float32)
    return np.ascontiguousarray(v.reshape(-1, 128).T)


CFG = {"layers": 2, "mix_even": True, "mix_odd": True, "dsa": True, "diff": True, "ffn": True, "ple": True, "proj": True, "attn": True}


class Rot:
    def __init__(self, items):
        self.items = list(items)
        self.i = 0

    def next(self):
        v = self.items[self.i % len(self.items)]
        self.i += 1
        return v


def build_program(cfg=None):
    from contextlib import ExitStack
    cfg = dict(CFG if cfg is None else cfg)
    nc = bass.Bass("TRN2", target_bir_lowering=False)

    declared = []

    def din(name, shape, dt=F32, need=True):
        if not need:
            return None
        declared.append(name)
        return nc.dram_tensor(name, list(shape), dt, kind="ExternalInput").ap()

    xT = din("xT", [D, T])
    pTd = din("pT", [2, PLE, T])
    gains_d = din("gains", [128, 72])
    ffn_w = {}
    for nm in ("ffn_a_wg", "ffn_a_wu", "ffn_b_wg", "ffn_b_wu"):
        ffn_w[nm] = din(nm, [2, 11, 128, 8, 256], need=cfg["ffn"])
    for nm in ("ffn_a_wd", "ffn_b_wd"):
        ffn_w[nm] = din(nm, [2, 4, 2, 128, 11, 256], need=cfg["ffn"])
    ple_gate = din("ple_gate", [2, 4, 128, 8, 256], need=cfg["ple"])
    ple_proj = din("ple_proj", [2, 4, 128, 2, 256], need=cfg["ple"])
    ew_fm = din("ew_fm", [18, 128, 8, 128])
    ew_va = din("ew_va", [1, 128, 8, 64])
    ew_wi = din("ew_wi", [1, 128, 8, 8])
    ew_vb = din("ew_vb", [4, 128, 8, 128])
    even_w_out = din("even_w_out", [4, 128, 8, 256])
    lamv = din("lamv", [4, 64])
    subln_d = din("subln", [128, 1])
    odd_w_in = din("odd_w_in", [24, 128, 8, 128])
    odd_w_out = din("odd_w_out", [4, 128, 8, 256])
    rope_d = din("rope", [2, 128, T])
    strips_d = din("strips", [2, 128, STRIP_W], BF16)
    mats_d = din("mats", [3, 128, 128], BF16)
    negm_d = din("negmask", [128, 128])
    outT = nc.dram_tensor("outT", [D, T], F32, kind="ExternalOutput").ap()

    uid = [0]

    with ExitStack() as st:
        def mk_sb(stack):
            def f(name, shape, dt):
                uid[0] += 1
                return stack.enter_context(nc.sbuf_tensor(f"{name}_{uid[0]}", list(shape), dt))
            return f

        sb = mk_sb(st)
        h = sb("h", [128, NKC, T], F32)
        gsb = sb("gsb", [128, 72], F32)
        mats = sb("mats", [128, 3, 128], BF16)
        negm = sb("negm", [128, 128], F32)
        epsb = sb("epsb", [128, 2], F32)
        lv = sb("lv", [128, 4, 64], F32)
        lsm = sb("lsm", [128, 8], F32)
        sqb = [sb(f"sqb{i}", [128, 512], BF16) for i in range(2)]
        sdt = sb("sdt", [128, 512], F32)
        rstd = sb("rstd", [128, 512], F32)
        ps = [st.enter_context(nc.psum_tensor(f"ps{i}", [128, 512], F32)) for i in range(7)]
        pTp = st.enter_context(nc.psum_tensor("pTp", [128, 1024], BF16))
        ident = mats[:, 0, :]
        perm = mats[:, 1, :]
        ones = mats[:, 2, :]

        P = Prog(nc)

        def A(eng, fn, r=(), w=()):
            return P.add(eng, fn, reads=r, writes=w)

        def DMA(out, in_, r=(), w=()):
            return P.add("sp", lambda e: e.dma_start(out=out, in_=in_), reads=r, writes=w, dma=True)

        for c in range(NTC):
            for k in range(NKC):
                DMA(h[:, k, c * 512:(c + 1) * 512], xT[k * 128:(k + 1) * 128, c * 512:(c + 1) * 512], w=[("h", k, c)])
        DMA(gsb[:], gains_d, w=["gsb"])
        DMA(mats[:], mats_d.rearrange("a p n -> p a n"), w=["mats"])
        DMA(negm[:], negm_d, w=["negm"])
        DMA(lv[:], lamv.partition_broadcast(128), w=["lv"])
        DMA(lsm[:, 7:8], subln_d, w=["lsm7"])
        A("pool", lambda e: e.memset(epsb[:, 0:1], 1e-6), w=["epsb"])
        A("pool", lambda e: e.memset(epsb[:, 1:2], 1e-5), w=["epsb"])

        sqrot = Rot([0, 1])
        cvt_rot = Rot(["act", "dve"])

        class WPool:
            def __init__(self, stack, elems, nst, nbf, tag):
                f = mk_sb(stack)
                self.stg = [f(f"wst{tag}{i}", [128, elems], F32) for i in range(nst)]
                self.bf = [f(f"wbf{tag}{i}", [128, elems], BF16) for i in range(nbf)]
                self.i = 0
                uid[0] += 1
                self.tag = f"{tag}{uid[0]}"

            def load(self, desc):
                src, a, b = desc
                i = self.i
                self.i += 1
                si, bi = i % len(self.stg), i % len(self.bf)
                sv = self.stg[si][:, 0:a * b].rearrange("p (a b) -> p a b", a=a)
                bv = self.bf[bi][:, 0:a * b].rearrange("p (a b) -> p a b", a=a)
                sk, bk = (self.tag, "st", si), (self.tag, "bf", bi)
                DMA(sv, src, w=[sk])
                self._cvt(bv, sv, sk, bk)
                return bv, bk

            def _cvt(self, dst, src, sk, dk):
                eng = cvt_rot.next()
                if eng == "act":
                    A("act", lambda e: e.activation(out=dst, in_=src, func=AF.Copy), r=[sk], w=[dk])
                else:
                    A(eng, lambda e: e.tensor_copy(out=dst, in_=src), r=[sk], w=[dk])

            def load_into(self, desc, dst, dkey):
                src, a, b = desc
                i = self.i
                self.i += 1
                si = i % len(self.stg)
                sv = self.stg[si][:, 0:a * b].rearrange("p (a b) -> p a b", a=a)
                sk = (self.tag, "st", si)
                DMA(sv, src, w=[sk])
                self._cvt(dst, sv, sk, dkey)

        def stream(wp, descs, depth=2):
            q = []
            n = len(descs)
            nxt = 0
            for i in range(n):
                while nxt < n and nxt <= i + depth:
                    q.append(wp.load(descs[nxt]))
                    nxt += 1
                yield q[i]

        def wdesc(w2d, r0, nr, c0, ncol):
            return (w2d[r0:r0 + nr, c0:c0 + ncol].rearrange("(a p) n -> p a n", p=128), nr // 128, ncol)

        def norm_chunk(gi, c, dst_fn, dkey_fn, srot, after=None, engs=("dve",), exact=False):
            b = srot.next()
            for k in range(NKC):
                si = sqrot.next()
                A("act", lambda e, k=k, si=si: e.activation(out=sqb[si][:], in_=h[:, k, c * 512:(c + 1) * 512],
                                                            func=AF.Square, scale=1.0 / 32.0),
                  r=[("h", k, c)], w=[("sqb", si)])
                A("pe", lambda e, k=k, si=si: e.matmul(ps[b][:], lhsT=ones, rhs=sqb[si][:], start=(k == 0), stop=(k == NKC - 1)),
                  r=[("sqb", si), "mats"], w=[("ps", b)])
            if exact:
                A("act", lambda e: e.activation(out=sdt[:], in_=ps[b][:], func=AF.Sqrt, bias=epsb[:, 0:1]),
                  r=[("ps", b), "epsb"], w=["sdt"])
                A("dve", lambda e: e.reciprocal(out=rstd[:], in_=sdt[:]), r=["sdt"], w=["rstd"])
            else:
                A("act", lambda e: e.activation(out=sdt[:], in_=ps[b][:], func=AF.Ln, bias=epsb[:, 0:1]),
                  r=[("ps", b), "epsb"], w=["sdt"])
                A("act", lambda e: e.activation(out=rstd[:], in_=sdt[:], func=AF.Exp, scale=-0.5), r=["sdt"], w=["rstd"])
            for k in range(NKC):
                eng = engs[k % len(engs)]
                dst = dst_fn(k)
                A(eng, lambda e, k=k, dst=dst: e.scalar_tensor_tensor(out=dst, in0=h[:, k, c * 512:(c + 1) * 512],
                                                                      scalar=gsb[:, gi * 8 + k:gi * 8 + k + 1], in1=rstd[:],
                                                                      op0=ALU.mult, op1=ALU.mult),
                  r=[("h", k, c), "gsb", "rstd"], w=[dkey_fn(k)])
                if after is not None:
                    after(k)

        def ffn(l, wg, wu, wd, gi):
            for tp in range(2):
                ffn_pass(l, wg, wu, wd, gi, tp)
                P.barrier()

        def ffn_pass(l, wg, wu, wd, gi, tp):
            if True:
                with ExitStack() as s2:
                    f2sb = mk_sb(s2)
                    hn = f2sb("hn", [128, NKC, 1024], BF16)
                    act = f2sb("act", [128, NFC, 1024], BF16)
                    sg = [f2sb(f"sg{i}", [128, 512], F32) for i in range(2)]
                    wp = WPool(s2, 2816, 3, 4, "f")
                    srot = Rot(range(7))
                    sgrot = Rot([0, 1])
                    for sub in range(2):
                        norm_chunk(gi, tp * 2 + sub, lambda k, sub=sub: hn[:, k, sub * 512:(sub + 1) * 512],
                                   lambda k, sub=sub: ("hn", k, sub), srot)
                    descs = []
                    for f2 in range(NFC // 2):
                        descs.append((wg[l, f2], 8, 256))
                        descs.append((wu[l, f2], 8, 256))
                    it = stream(wp, descs)
                    for f2 in range(NFC // 2):
                        gt, gk = next(it)
                        ut, uk = next(it)
                        for fi in range(2):
                            f = 2 * f2 + fi
                            for sub in range(2):
                                bg = srot.next()
                                bu = srot.next()
                                for k in range(NKC):
                                    A("pe", lambda e, k=k, bg=bg, gt=gt, fi=fi, sub=sub: e.matmul(
                                        ps[bg][:], lhsT=gt[:, k, fi * 128:(fi + 1) * 128], rhs=hn[:, k, sub * 512:(sub + 1) * 512],
                                        start=(k == 0), stop=(k == NKC - 1)), r=[gk, ("hn", k, sub)], w=[("ps", bg)])
                                for k in range(NKC):
                                    A("pe", lambda e, k=k, bu=bu, ut=ut, fi=fi, sub=sub: e.matmul(
                                        ps[bu][:], lhsT=ut[:, k, fi * 128:(fi + 1) * 128], rhs=hn[:, k, sub * 512:(sub + 1) * 512],
                                        start=(k == 0), stop=(k == NKC - 1)), r=[uk, ("hn", k, sub)], w=[("ps", bu)])
                                si = sgrot.next()
                                A("act", lambda e, si=si, bg=bg: e.activation(out=sg[si][:], in_=ps[bg][:], func=AF.Silu),
                                  r=[("ps", bg)], w=[("sg", si)])
                                A("dve", lambda e, si=si, bu=bu, f=f, sub=sub: e.tensor_tensor(
                                    out=act[:, f, sub * 512:(sub + 1) * 512], in0=sg[si][:], in1=ps[bu][:], op=ALU.mult),
                                  r=[("sg", si), ("ps", bu)], w=[("act", f, sub)])
                    descs = []
                    for d2 in range(4):
                        for half in range(2):
                            descs.append((wd[l, d2, half], 11, 256))
                    it = stream(wp, descs)
                    for d2 in range(4):
                        tl0, tk0 = next(it)
                        tl1, tk1 = next(it)
                        for di in range(2):
                            d = 2 * d2 + di
                            for sub in range(2):
                                b = srot.next()
                                c = tp * 2 + sub
                                for fc in range(NFC):
                                    tl, tk = (tl0, tk0) if fc < 11 else (tl1, tk1)
                                    A("pe", lambda e, fc=fc, tl=tl, di=di, sub=sub, b=b: e.matmul(
                                        ps[b][:], lhsT=tl[:, fc % 11, di * 128:(di + 1) * 128], rhs=act[:, fc, sub * 512:(sub + 1) * 512],
                                        start=(fc == 0), stop=(fc == NFC - 1)), r=[tk, ("act", fc, sub)], w=[("ps", b)])
                                A("dve", lambda e, d=d, c=c, b=b: e.scalar_tensor_tensor(
                                    out=h[:, d, c * 512:(c + 1) * 512], in0=ps[b][:], scalar=0.5, in1=h[:, d, c * 512:(c + 1) * 512],
                                    op0=ALU.mult, op1=ALU.add), r=[("ps", b), ("h", d, c)], w=[("h", d, c)])

        def ple(l, gi):
            with ExitStack() as s2:
                f2sb = mk_sb(s2)
                hn = f2sb("hnp", [128, NKC, T], BF16)
                ptb = f2sb("ptb", [128, 2, T], BF16)
                sig = [f2sb(f"sig{i}", [128, 512], F32) for i in range(2)]
                tmp = [f2sb(f"ptmp{i}", [128, 512], F32) for i in range(2)]
                wp = WPool(s2, 2048, 3, 4, "p")
                srot = Rot(range(7))
                r2 = Rot([0, 1])
                for c in range(NTC):
                    norm_chunk(gi, c, lambda k, c=c: hn[:, k, c * 512:(c + 1) * 512], lambda k, c=c: ("hnp", k, c), srot)
                    wp.load_into((pTd[l][:, c * 512:(c + 1) * 512].rearrange("(a p) t -> p a t", p=128), 2, 512),
                                 ptb[:, :, c * 512:(c + 1) * 512], ("ptb", c))
                descs = []
                for d2 in range(4):
                    descs.append((ple_gate[l, d2], 8, 256))
                    descs.append((ple_proj[l, d2], 2, 256))
                it = stream(wp, descs)
                for d2 in range(4):
                    gt, gk = next(it)
                    pt, pk = next(it)
                    for di in range(2):
                        d = 2 * d2 + di
                        for c in range(NTC):
                            bg = srot.next()
                            bp = srot.next()
                            for k in range(NKC):
                                A("pe", lambda e, k=k, bg=bg, gt=gt, di=di, c=c: e.matmul(
                                    ps[bg][:], lhsT=gt[:, k, di * 128:(di + 1) * 128], rhs=hn[:, k, c * 512:(c + 1) * 512],
                                    start=(k == 0), stop=(k == NKC - 1)), r=[gk, ("hnp", k, c)], w=[("ps", bg)])
                            for k in range(2):
                                A("pe", lambda e, k=k, bp=bp, pt=pt, di=di, c=c: e.matmul(
                                    ps[bp][:], lhsT=pt[:, k, di * 128:(di + 1) * 128], rhs=ptb[:, k, c * 512:(c + 1) * 512],
                                    start=(k == 0), stop=(k == 1)), r=[pk, ("ptb", c)], w=[("ps", bp)])
                            si = r2.next()
                            A("act", lambda e, si=si, bg=bg: e.activation(out=sig[si][:], in_=ps[bg][:], func=AF.Sigmoid),
                              r=[("ps", bg)], w=[("sig", si)])
                            A("dve", lambda e, si=si, bp=bp: e.tensor_tensor(out=tmp[si][:], in0=sig[si][:], in1=ps[bp][:], op=ALU.mult),
                              r=[("sig", si), ("ps", bp)], w=[("ptmp", si)])
                            A("pool", lambda e, si=si, d=d, c=c: e.tensor_tensor(
                                out=h[:, d, c * 512:(c + 1) * 512], in0=h[:, d, c * 512:(c + 1) * 512], in1=tmp[si][:], op=ALU.add),
                              r=[("ptmp", si), ("h", d, c)], w=[("h", d, c)])
            P.barrier()

        def final_norm():
            with ExitStack() as s2:
                f2sb = mk_sb(s2)
                ot = [f2sb(f"ot{i}", [128, 512], F32) for i in range(4)]
                srot = Rot(range(7))
                for c in range(NTC):
                    def after(k, c=c):
                        i = (c * 8 + k) % 4
                        DMA(outT[k * 128:(k + 1) * 128, c * 512:(c + 1) * 512], ot[i][:], r=[("ot", i)])
                    norm_chunk(8, c, lambda k, c=c: ot[(c * 8 + k) % 4][:], lambda k, c=c: ("ot", (c * 8 + k) % 4), srot,
                               after=after, engs=("dve",), exact=True)
            P.barrier()

        def mixer_proj(gi, fm_items, tm_items):
            import os as _os
            _nfm = int(_os.environ.get("PROJ_FM", "99"))
            _ntm = int(_os.environ.get("PROJ_TM", "99"))
            fm_items = fm_items[:_nfm]
            tm_items = tm_items[:_ntm]
            with ExitStack() as s2:
                f2sb = mk_sb(s2)
                hnc = f2sb("hnc", [128, NKC, 512], BF16)
                ropc = [f2sb(f"ropc{i}", [128, 2, 512], F32) for i in range(2)]
                xb = [f2sb(f"xb{i}", [128, 512], BF16) for i in range(2)]
                t1 = [f2sb(f"t1{i}", [128, 512], F32) for i in range(2)]
                t2 = [f2sb(f"t2{i}", [128, 512], F32) for i in range(2)]
                wp = WPool(s2, 1024, 2, 3, "m")
                srot = Rot(range(7))
                r2 = Rot([0, 1])
                for c in range(NTC):
                    norm_chunk(gi, c, lambda k: hnc[:, k, :], lambda k: ("hnc", k), srot)
                    rc = c % 2
                    DMA(ropc[rc][:], rope_d[:, :, c * 512:(c + 1) * 512].rearrange("a p t -> p a t"), w=[("ropc", rc)])
                    descs = [(w_ap, 8, 128) for (w_ap, _) in fm_items]
                    descs += [(w_ap, 8, n) for (w_ap, n, _, _) in tm_items]
                    it = stream(wp, descs)
                    for (_, dst_fn) in fm_items:
                        wt, wk = next(it)
                        b = srot.next()
                        for k in range(NKC):
                            A("pe", lambda e, k=k, b=b, wt=wt: e.matmul(ps[b][:], lhsT=wt[:, k, :], rhs=hnc[:, k, :],
                                                                        start=(k == 0), stop=(k == NKC - 1)),
                              r=[wk, ("hnc", k)], w=[("ps", b)])
                        i = r2.next()
                        A("act", lambda e, i=i, b=b: e.activation(out=xb[i][:], in_=ps[b][:], func=AF.Copy),
                          r=[("ps", b)], w=[("xb", i)])
                        A("dve", lambda e, i=i, b=b, rc=rc: e.tensor_tensor(out=t1[i][:], in0=ps[b][:], in1=ropc[rc][:, 0, :], op=ALU.mult),
                          r=[("ps", b), ("ropc", rc), ("xb", i)], w=[("t1", i)])
                        b2 = srot.next()
                        A("pe", lambda e, i=i, b2=b2: e.matmul(ps[b2][:], lhsT=perm, rhs=xb[i][:], start=True, stop=True),
                          r=[("xb", i), "mats"], w=[("ps", b2)])
                        A("dve", lambda e, i=i, b2=b2, rc=rc: e.tensor_tensor(out=t2[i][:], in0=ps[b2][:], in1=ropc[rc][:, 1, :], op=ALU.mult),
                          r=[("ps", b2), ("ropc", rc)], w=[("t2", i)])
                        dst, dkey = dst_fn(c)
                        A("pool", lambda e, i=i, dst=dst: e.tensor_tensor(out=dst, in0=t1[i][:], in1=t2[i][:], op=ALU.add),
                          r=[("t1", i), ("t2", i)], w=[dkey])
                    for (_, n, dst_fn, _) in tm_items:
                        wt, wk = next(it)
                        for tb in range(4):
                            b = srot.next()
                            for k in range(NKC):
                                A("pe", lambda e, k=k, b=b, wt=wt, tb=tb, n=n: e.matmul(
                                    ps[b][:, 0:n], lhsT=hnc[:, k, tb * 128:(tb + 1) * 128], rhs=wt[:, k, :],
                                    start=(k == 0), stop=(k == NKC - 1)), r=[wk, ("hnc", k)], w=[("ps", b)])
                            dst, dkey = dst_fn(c * 4 + tb)
                            if isinstance(dst, list):
                                for (dap, lo_, hi_) in dst:
                                    A("act", lambda e, b=b, dap=dap, lo_=lo_, hi_=hi_: e.activation(out=dap, in_=ps[b][:, lo_:hi_], func=AF.Copy),
                                      r=[("ps", b)], w=[dkey])
                            else:
                                A("act", lambda e, b=b, dst=dst, n=n: e.activation(out=dst, in_=ps[b][:, 0:n], func=AF.Copy),
                                  r=[("ps", b)], w=[dkey])
            P.barrier()

        LOOK = 2

        def run_units(units, front, back):
            fr = []
            for idx, u in enumerate(units):
                fr.append(front(u))
                if idx >= LOOK:
                    back(units[idx - LOOK], fr[idx - LOOK])
            for idx in range(max(0, len(units) - LOOK), len(units)):
                back(units[idx], fr[idx])

        def attn_jobs(jobs, accrot, srot, tiles, kz=None):
            ebs, pbs, rds, erot, prot, rrot, mengs = tiles
            units = []
            for ji, job in enumerate(jobs):
                c = job[0]
                accs = accrot.next()
                last = 4 * c + 3
                for j in range(last + 1):
                    for e_ in range(2):
                        units.append((job, accs, j, e_, j == last and e_ == 1, ji, j == 0 and e_ == 0))

            def front(u):
                (c, Q, qc, qkey, K, kc, kkey, vfn, vkey, mfn, mkey, dst, dkey), accs, j, e_, is_last, ji, is_first = u
                t0 = max(0, j * 128 - c * 512)
                n = 512 - t0
                lo, hi = e_ * 64, (e_ + 1) * 64
                b = srot.next()
                if kz is not None:
                    kzt, kzkey, pre = kz[ji]
                    if is_first and pre is not None:
                        pre()
                    A("pe", lambda e: e.matmul(
                        ps[b][:, 0:n], lhsT=kzt[:, e_, j * 128:(j + 1) * 128], rhs=Q[:, qc, c * 512 + t0:(c + 1) * 512],
                        start=True, stop=True), r=[qkey, kzkey], w=[("ps", b)])
                else:
                    A("pe", lambda e: e.matmul(
                        ps[b][:, 0:n], lhsT=K[lo:hi, kc, j * 128:(j + 1) * 128], rhs=Q[lo:hi, qc, c * 512 + t0:(c + 1) * 512],
                        start=True, stop=True), r=[qkey, kkey], w=[("ps", b)])
                ei = erot.next()
                A("act", lambda e: e.activation(out=ebs[ei][:, 0:n], in_=ps[b][:, 0:n], func=AF.Exp, scale=0.125),
                  r=[("ps", b)], w=[("eb", ei)])
                m = mfn(j, t0, n)
                if m is not None:
                    pi = prot.next()
                    A(mengs.next(), lambda e: e.tensor_tensor(out=pbs[pi][:, 0:n], in0=ebs[ei][:, 0:n], in1=m, op=ALU.mult),
                      r=[("eb", ei), mkey], w=[("pb", pi)])
                    return pbs[pi], ("pb", pi)
                return ebs[ei], ("eb", ei)

            def back(u, fr):
                (c, Q, qc, qkey, K, kc, kkey, vfn, vkey, mfn, mkey, dst, dkey), (ba, bb), j, e_, is_last, ji, is_first = u
                src, skey = fr
                last = 4 * c + 3
                t0 = max(0, j * 128 - c * 512)
                n = 512 - t0
                bk_ = ba if e_ == 0 else bb
                v = vfn(j, e_)
                A("pe", lambda e: e.matmul(ps[bk_][:, t0:512], lhsT=v, rhs=src[:, 0:n], start=(j == 0), stop=(j == last)),
                  r=[skey, vkey], w=[("ps", bk_)])
                if is_last:
                    di = rrot.next()
                    dsh = rds[di]
                    A("dve", lambda e: e.tensor_copy(out=dsh[0:64, :], in_=ps[ba][64:128, :]), r=[("ps", ba)], w=[("rd", di)])
                    A("dve", lambda e: e.tensor_copy(out=dsh[64:128, :], in_=ps[bb][0:64, :]), r=[("ps", bb)], w=[("rd", di)])
                    A("act", lambda e: e.activation(out=dsh[:], in_=dsh[:], func=AF.Ln), r=[("rd", di)], w=[("rd", di)])
                    A("act", lambda e: e.activation(out=dsh[:], in_=dsh[:], func=AF.Exp, scale=-1.0), r=[("rd", di)], w=[("rd", di)])
                    A("dve", lambda e: e.tensor_tensor(out=dst[0:64, :], in0=ps[ba][0:64, :], in1=dsh[0:64, :], op=ALU.mult),
                      r=[("ps", ba), ("rd", di)], w=[dkey])
                    A("dve", lambda e: e.tensor_tensor(out=dst[64:128, :], in0=ps[bb][64:128, :], in1=dsh[64:128, :], op=ALU.mult),
                      r=[("ps", bb), ("rd", di)], w=[dkey])

            run_units(units, front, back)

        def make_kz(f2sb, nbuf=2):
            kzs = [f2sb(f"kz{i}", [128, 2, T], BF16) for i in range(nbuf)]
            for i in range(nbuf):
                A("pool", lambda e, i=i: e.memset(kzs[i][64:128, 0, :], 0.0), w=[("kz", i)])
                A("pool", lambda e, i=i: e.memset(kzs[i][0:64, 1, :], 0.0), w=[("kz", i)])

            def fill(i, K, kc):
                def pre():
                    A("pool", lambda e: e.tensor_copy(out=kzs[i][0:64, 0, :], in_=K[0:64, kc, :]), r=[], w=[("kz", i)])
                    A("dve", lambda e: e.tensor_copy(out=kzs[i][64:128, 1, :], in_=K[64:128, kc, :]), r=[], w=[("kz", i)])
                return pre
            return kzs, fill

        def attn_tiles(f2sb, mengs=("dve", "pool")):
            ebs = [f2sb(f"eb{i}", [128, 512], BF16) for i in range(4)]
            pbs = [f2sb(f"pb{i}", [128, 512], BF16) for i in range(4)]
            rds = [f2sb(f"rd{i}", [128, 512], F32) for i in range(2)]
            return (ebs, pbs, rds, Rot(range(4)), Rot(range(4)), Rot(range(2)), Rot(mengs))

        def apply_wout(w2d, merged):
            with ExitStack() as s2:
                wp = WPool(s2, 2048, 3, 4, "o")
                srot = Rot(range(7))
                descs = [(w2d[d2], 8, 256) for d2 in range(4)]
                it = stream(wp, descs)
                for d2 in range(4):
                    wt, wk = next(it)
                    for di in range(2):
                        d = 2 * d2 + di
                        for c in range(NTC):
                            b = srot.next()
                            for k in range(NKC):
                                A("pe", lambda e, k=k, b=b, wt=wt, di=di, c=c: e.matmul(
                                    ps[b][:], lhsT=wt[:, k, di * 128:(di + 1) * 128], rhs=merged[:, k, c * 512:(c + 1) * 512],
                                    start=(k == 0), stop=(k == NKC - 1)), r=[wk, ("mg", k, c)], w=[("ps", b)])
                            A("dve", lambda e, b=b, d=d, c=c: e.tensor_tensor(
                                out=h[:, d, c * 512:(c + 1) * 512], in0=ps[b][:], in1=h[:, d, c * 512:(c + 1) * 512], op=ALU.add),
                              r=[("ps", b), ("h", d, c)], w=[("h", d, c)])
            P.barrier()

        def even_mixer(gi):
            with ExitStack() as sm:
                msb = mk_sb(sm)
                merged = msb("merged", [128, NKC, T], BF16)
                A("dve", lambda e: e.tensor_tensor(out=lv[:, 0, :], in0=lv[:, 0, :], in1=lv[:, 1, :], op=ALU.mult), r=["lv"], w=["lv"])
                A("dve", lambda e: e.tensor_tensor(out=lv[:, 2, :], in0=lv[:, 2, :], in1=lv[:, 3, :], op=ALU.mult), r=["lv"], w=["lv"])
                A("dve", lambda e: e.reduce_sum(out=lsm[:, 0:1], in_=lv[:, 0, :], axis=AX.X), r=["lv"], w=["lsm"])
                A("dve", lambda e: e.reduce_sum(out=lsm[:, 1:2], in_=lv[:, 2, :], axis=AX.X), r=["lv"], w=["lsm"])
                A("act", lambda e: e.activation(out=lsm[:, 2:4], in_=lsm[:, 0:2], func=AF.Exp), r=["lsm"], w=["lsm"])
                A("dve", lambda e: e.tensor_tensor(out=lsm[:, 4:5], in0=lsm[:, 3:4], in1=lsm[:, 2:3], op=ALU.subtract), r=["lsm"], w=["lsm"])
                A("dve", lambda e: e.tensor_scalar(out=lsm[:, 5:6], in0=lsm[:, 4:5], scalar1=-0.2, scalar2=None, op0=ALU.add), r=["lsm"], w=["lsm"])
                A("dve", lambda e: e.tensor_scalar(out=lsm[:, 6:7], in0=lsm[:, 7:8], scalar1=0.8, scalar2=None, op0=ALU.mult), r=["lsm", "lsm7"], w=["lsm"])
                neglam = lsm[:, 5:6]
                gsub = lsm[:, 6:7]

                for kk in range(NKC):
                    if (kk < 4 and not cfg["dsa"]) or (kk >= 4 and not cfg["diff"]):
                        A("pool", lambda e, kk=kk: e.memset(merged[:, kk, :], 0.0), w=[("mg", kk, c) for c in range(NTC)])
                if cfg["dsa"]:
                    with ExitStack() as sa:
                        asb = mk_sb(sa)
                        QA = asb("QA", [128, 4, T], BF16)
                        KA = asb("KA", [128, 1, T], BF16)
                        QI = asb("QI", [128, 4, T], BF16)
                        KI = asb("KI", [128, 1, T], BF16)
                        VA = asb("VA", [128, NTB, 192], BF16)
                        A("pool", lambda e: e.memset(VA[:, :, 64:128], 1.0), w=[("VA1",)])
                        WI = asb("WI", [128, NTB, 8], F32)
                        fm = []
                        bufs = [(QA, n, "QA") for n in range(4)] + [(KA, 0, "KA")] + [(QI, n, "QI") for n in range(4)] + [(KI, 0, "KI")]
                        for ci, (buf, n, nm) in enumerate(bufs):
                            fm.append((ew_fm[ci],
                                       lambda c, buf=buf, n=n, nm=nm: (buf[:, n, c * 512:(c + 1) * 512], (nm, n, c))))
                        tm = [(ew_va[0], 64, lambda blk: ([(VA[:, blk, 0:64], 0, 64), (VA[:, blk, 128:192], 0, 64)], ("VA", blk)), None),
                              (ew_wi[0], 8, lambda blk: (WI[:, blk, :], ("WI", blk)), None)]
                        mixer_proj(gi, fm, tm)
                        with ExitStack() as s2:
                            f2sb = mk_sb(s2)
                            Ib = [f2sb(f"Ib{i}", [128, T], F32) for i in range(2)]
                            rb = [f2sb(f"rb{i}", [128, 512], F32) for i in range(2)]
                            mk = f2sb("mk", [128, T], BF16)
                            junk = mk
                            maskT = f2sb("maskT", [128, NTB, 512], BF16)
                            bnd = [f2sb(f"bnd{i}", [128, 8], F32) for i in range(2)]
                            gem = [f2sb(f"gem{i}", [128, 2], mybir.dt.uint32) for i in range(2)]
                            tiles = attn_tiles(f2sb, mengs=("dve",))
                            srot = Rot([0, 1, 2])
                            rrot = Rot(range(2))
                            accrot = Rot([(3, 4), (5, 6)])
                            NIT = 15

                            def s1(i):
                                c = i // 4
                                cols = (i + 1) * 128
                                I = Ib[i % 2]
                                ik = ("I", i % 2)
                                bd = bnd[i % 2]
                                bk = ("bnd", i % 2)
                                for sc in range((cols + 511) // 512):
                                    w_ = min(512, cols - sc * 512)
                                    for hd in range(8):
                                        lo, hi = (hd % 2) * 64, (hd % 2 + 1) * 64
                                        b = srot.next()
                                        A("pe", lambda e, b=b, lo=lo, hi=hi, hd=hd, sc=sc, w_=w_: e.matmul(
                                            ps[b][:, 0:w_], lhsT=QI[lo:hi, hd // 2, i * 128:(i + 1) * 128],
                                            rhs=KI[lo:hi, 0, sc * 512:sc * 512 + w_], start=True, stop=True),
                                          r=[("QI", hd // 2, c), ("KI", 0, sc)], w=[("ps", b)])
                                        ri = rrot.next()
                                        A("act", lambda e, b=b, ri=ri, w_=w_: e.activation(out=rb[ri][:, 0:w_], in_=ps[b][:, 0:w_], func=AF.Relu),
                                          r=[("ps", b)], w=[("rb", ri)])
                                        if hd == 0:
                                            A("dve", lambda e, ri=ri, sc=sc, w_=w_: e.tensor_scalar(
                                                out=I[:, sc * 512:sc * 512 + w_], in0=rb[ri][:, 0:w_], scalar1=WI[:, i, 0:1], scalar2=None, op0=ALU.mult),
                                              r=[("rb", ri), ("WI", i)], w=[ik])
                                        else:
                                            A("dve", lambda e, ri=ri, sc=sc, w_=w_, hd=hd: e.scalar_tensor_tensor(
                                                out=I[:, sc * 512:sc * 512 + w_], in0=rb[ri][:, 0:w_], scalar=WI[:, i, hd:hd + 1],
                                                in1=I[:, sc * 512:sc * 512 + w_], op0=ALU.mult, op1=ALU.add),
                                              r=[("rb", ri), ("WI", i), ik], w=[ik])
                                if i >= 2:
                                    A("dve", lambda e: e.tensor_reduce(out=bd[:, 0:1], in_=I[:, 0:cols], axis=AX.X, op=ALU.min), r=[ik], w=[bk])
                                    A("dve", lambda e: e.tensor_reduce(out=bd[:, 1:2], in_=I[:, 0:cols], axis=AX.X, op=ALU.max), r=[ik], w=[bk])
                                    A("dve", lambda e: e.tensor_tensor(out=bd[:, 2:3], in0=bd[:, 1:2], in1=bd[:, 0:1], op=ALU.subtract), r=[bk], w=[bk])
                                A("pool", lambda e: e.tensor_tensor(out=I[:, i * 128:(i + 1) * 128], in0=I[:, i * 128:(i + 1) * 128],
                                                                    in1=negm[:], op=ALU.add), r=[ik, "negm"], w=[ik])

                            def s2(i):
                                ti = i % 4
                                cols = (i + 1) * 128
                                I = Ib[i % 2]
                                ik = ("I", i % 2)
                                bd = bnd[i % 2]
                                bk = ("bnd", i % 2)
                                gm = gem[i % 2]
                                gk = ("gem", i % 2)
                                if i >= 2:
                                    for it in range(NIT):
                                        A("dve", lambda e, it=it: e.scalar_tensor_tensor(out=bd[:, 3:4], in0=bd[:, 2:3], scalar=float(2.0 ** -(it + 1)),
                                                                                        in1=bd[:, 0:1], op0=ALU.mult, op1=ALU.add), r=[bk], w=[bk])
                                        A("dve", lambda e: e.tensor_scalar(out=junk[:, 0:cols], in0=I[:, 0:cols], scalar1=bd[:, 3:4], scalar2=None,
                                                                           op0=ALU.is_ge, op1=ALU.add, accum_out=bd[:, 4:5]), r=[ik, bk], w=[bk, "mk"])
                                        A("dve", lambda e: e.tensor_single_scalar(out=gm[:, 0:1], in_=bd[:, 4:5], scalar=255.5, op=ALU.is_ge), r=[bk], w=[gk])
                                        A("dve", lambda e: e.copy_predicated(out=bd[:, 0:1], mask=gm[:, 0:1], data=bd[:, 3:4]), r=[bk, gk], w=[bk])
                                    A("dve", lambda e: e.tensor_scalar(out=mk[:, 0:cols], in0=I[:, 0:cols], scalar1=bd[:, 0:1], scalar2=None, op0=ALU.is_ge),
                                      r=[ik, bk], w=["mk"])
                                else:
                                    A("dve", lambda e: e.tensor_single_scalar(out=mk[:, 0:cols], in_=I[:, 0:cols], scalar=-1.0e29, op=ALU.is_ge), r=[ik], w=["mk"])
                                for j0 in range(0, i + 1, 8):
                                    nn = min(8, i + 1 - j0)
                                    for jj in range(nn):
                                        A("pe", lambda e, jj=jj, j0=j0: e.transpose(
                                            out=pTp[:, jj * 128:(jj + 1) * 128],
                                            in_=mk[:, (j0 + jj) * 128:(j0 + jj + 1) * 128], identity=ident),
                                          r=["mk", "mats"], w=["pT"])
                                    A("act", lambda e, nn=nn, j0=j0: e.activation(
                                        out=maskT[:, j0:j0 + nn, ti * 128:(ti + 1) * 128],
                                        in_=pTp[:, 0:nn * 128].rearrange("p (a b) -> p a b", a=nn), func=AF.Copy),
                                      r=["pT"], w=["maskT"])

                            s1(0)
                            for i in range(NTB):
                                if i + 1 < NTB:
                                    s1(i + 1)
                                s2(i)
                                if i % 4 == 3:
                                    c = i // 4
                                    attn_jobs([(c, QA, hp, ("QA", hp, c), KA, 0, ("KA", 0, c),
                                                lambda j, e_: VA[:, j, e_ * 64:e_ * 64 + 128], ("VA", 0),
                                                lambda j, t0, n: maskT[:, j, t0:512], "maskT",
                                                merged[:, hp, c * 512:(c + 1) * 512], ("mg", hp, c)) for hp in range(4)],
                                              accrot, srot, tiles)
                        P.barrier()

                if cfg["diff"]:
                    with ExitStack() as sa:
                        asb = mk_sb(sa)
                        QB = asb("QB", [128, 4, T], BF16)
                        KB = asb("KB", [128, 4, T], BF16)
                        VB = asb("VB", [128, NTB, 512], BF16)
                        fm = []
                        bufs = [(QB, n, "QB") for n in range(4)] + [(KB, n, "KB") for n in range(4)]
                        for ci, (buf, n, nm) in enumerate(bufs):
                            fm.append((ew_fm[10 + ci],
                                       lambda c, buf=buf, n=n, nm=nm: (buf[:, n, c * 512:(c + 1) * 512], (nm, n, c))))
                        tm = [(ew_vb[n], 128,
                               lambda blk, n=n: (VB[:, blk, n * 128:(n + 1) * 128], ("VB", blk, n)), None) for n in range(4)]
                        mixer_proj(gi, fm, tm)
                        with ExitStack() as s2:
                            f2sb = mk_sb(s2)
                            strip = f2sb("cstrip", [128, STRIP_W], BF16)
                            DMA(strip[:], strips_d[0], w=["strip"])
                            ebs = [f2sb(f"eb{i}", [128, 512], BF16) for i in range(3)]
                            pbs = [f2sb(f"pb{i}", [128, 512], BF16) for i in range(3)]
                            fa = [f2sb(f"fa{i}", [128, 512], F32) for i in range(4)]
                            ob = f2sb("ob", [128, 512], F32)
                            sq2 = f2sb("sq2", [128, 512], BF16)
                            ebs = ebs + [f2sb("eb3", [128, 512], BF16)]
                            pbs = pbs + [f2sb("pb3", [128, 512], BF16)]
                            erot, prot = Rot(range(4)), Rot(range(4))
                            srot = Rot([0, 1, 2])
                            dkzs, dfill = make_kz(f2sb)
                            units = []
                            for hd in range(4):
                                for c in range(NTC):
                                    last = 4 * c + 3
                                    for j in range(last + 1):
                                        for e_ in range(2):
                                            units.append((hd, c, j, e_, j == last and e_ == 1))

                            def dfront(u):
                                hd, c, j, e_, is_last = u
                                t0 = max(0, j * 128 - c * 512)
                                n = 512 - t0
                                lo, hi = e_ * 64, (e_ + 1) * 64
                                b = srot.next()
                                if c == 0 and j == 0 and e_ == 0:
                                    dfill(hd % 2, KB, hd)()
                                kzt = dkzs[hd % 2]
                                A("pe", lambda e: e.matmul(
                                    ps[b][:, 0:n], lhsT=kzt[:, e_, j * 128:(j + 1) * 128],
                                    rhs=QB[:, hd, c * 512 + t0:(c + 1) * 512], start=True, stop=True),
                                  r=[("QB", hd, c), ("kz", hd % 2)], w=[("ps", b)])
                                ei = erot.next()
                                A("act", lambda e: e.activation(out=ebs[ei][:, 0:n], in_=ps[b][:, 0:n], func=AF.Exp, scale=0.125),
                                  r=[("ps", b)], w=[("eb", ei)])
                                if j >= 4 * c:
                                    off = c * 512 - j * 128 + 384 + t0
                                    pi = prot.next()
                                    A("dve", lambda e: e.tensor_tensor(
                                        out=pbs[pi][:, 0:n], in0=ebs[ei][:, 0:n], in1=strip[:, off:off + n], op=ALU.mult),
                                      r=[("eb", ei), "strip"], w=[("pb", pi)])
                                    return pbs[pi], ("pb", pi)
                                return ebs[ei], ("eb", ei)

                            def dback(u, fr):
                                hd, c, j, e_, is_last = u
                                src, skey = fr
                                last = 4 * c + 3
                                t0 = max(0, j * 128 - c * 512)
                                n = 512 - t0
                                nb, db = 3 + 2 * e_, 4 + 2 * e_
                                A("pe", lambda e: e.matmul(
                                    ps[nb][:, t0:512], lhsT=VB[:, j, hd * 128:(hd + 1) * 128], rhs=src[:, 0:n],
                                    start=(j == 0), stop=(j == last)),
                                  r=[skey, ("VB", j, hd)], w=[("ps", nb)])
                                A("pe", lambda e: e.matmul(
                                    ps[db][:, t0:512], lhsT=ones, rhs=src[:, 0:n], start=(j == 0), stop=(j == last)),
                                  r=[skey, "mats"], w=[("ps", db)])
                                if not is_last:
                                    return
                                A("act", lambda e: e.activation(out=fa[0][:], in_=ps[4][:], func=AF.Ln), r=[("ps", 4)], w=[("fa", 0)])
                                A("act", lambda e: e.activation(out=fa[0][:], in_=fa[0][:], func=AF.Exp, scale=-1.0), r=[("fa", 0)], w=[("fa", 0)])
                                A("dve", lambda e: e.tensor_tensor(out=fa[1][:], in0=ps[3][:], in1=fa[0][:], op=ALU.mult),
                                  r=[("ps", 3), ("fa", 0)], w=[("fa", 1)])
                                A("act", lambda e: e.activation(out=fa[2][:], in_=ps[6][:], func=AF.Ln), r=[("ps", 6)], w=[("fa", 2)])
                                A("act", lambda e: e.activation(out=fa[2][:], in_=fa[2][:], func=AF.Exp, scale=-1.0), r=[("fa", 2)], w=[("fa", 2)])
                                A("dve", lambda e: e.tensor_tensor(out=fa[3][:], in0=ps[5][:], in1=fa[2][:], op=ALU.mult),
                                  r=[("ps", 5), ("fa", 2)], w=[("fa", 3)])
                                A("dve", lambda e: e.scalar_tensor_tensor(out=ob[:], in0=fa[3][:], scalar=neglam, in1=fa[1][:],
                                                                          op0=ALU.mult, op1=ALU.add),
                                  r=[("fa", 3), ("fa", 1), "lsm"], w=["ob"])
                                A("act", lambda e: e.activation(out=sq2[:], in_=ob[:], func=AF.Square, scale=float(128.0 ** -0.5)),
                                  r=["ob"], w=["sq2"])
                                b = srot.next()
                                A("pe", lambda e: e.matmul(ps[b][:], lhsT=ones, rhs=sq2[:], start=True, stop=True),
                                  r=["sq2", "mats"], w=[("ps", b)])
                                A("act", lambda e: e.activation(out=fa[0][:], in_=ps[b][:], func=AF.Ln, bias=epsb[:, 1:2]),
                                  r=[("ps", b), "epsb"], w=[("fa", 0)])
                                A("act", lambda e: e.activation(out=fa[2][:], in_=fa[0][:], func=AF.Exp, scale=-0.5), r=[("fa", 0)], w=[("fa", 2)])
                                A("dve", lambda e: e.scalar_tensor_tensor(
                                    out=merged[:, 4 + hd, c * 512:(c + 1) * 512], in0=ob[:], scalar=gsub, in1=fa[2][:],
                                    op0=ALU.mult, op1=ALU.mult), r=["ob", ("fa", 2), "lsm"], w=[("mg", 4 + hd, c)])

                            run_units(units, dfront, dback)
                        P.barrier()
                apply_wout(even_w_out, merged)

        def odd_mixer(gi):
            with ExitStack() as sm:
                msb = mk_sb(sm)
                merged = msb("mergedo", [128, NKC, T], BF16)
                for g in range(2):
                    odd_group(g, gi, merged)
                apply_wout(odd_w_out, merged)

        def odd_group(g, gi, merged):
            if True:
                if True:
                    with ExitStack() as sa:
                        asb = mk_sb(sa)
                        QG = asb("QG", [128, 4, T], BF16)
                        KG = asb("KG", [128, 4, T], BF16)
                        VG = asb("VG", [128, NTB, 4, 192], BF16)
                        A("pool", lambda e: e.memset(VG[:, :, :, 64:128], 1.0), w=[("VG1",)])
                        fm = []
                        for n in range(4):
                            fm.append((odd_w_in[4 * g + n],
                                       lambda c, n=n: (QG[:, n, c * 512:(c + 1) * 512], ("QG", n, c))))
                        for n in range(4):
                            fm.append((odd_w_in[8 + 4 * g + n],
                                       lambda c, n=n: (KG[:, n, c * 512:(c + 1) * 512], ("KG", n, c))))
                        tm = [(odd_w_in[16 + 4 * g + n], 128,
                               lambda blk, n=n: ([(VG[:, blk, n, 0:64], 0, 64), (VG[:, blk, n, 128:192], 64, 128)], ("VG", blk, n)), None) for n in range(4)]
                        if cfg["proj"]:
                            mixer_proj(gi, fm, tm)
                        if not cfg["attn"]:
                            for hp in range(4):
                                A("pool", lambda e, hp=hp: e.memset(merged[:, 4 * g + hp, :], 0.0), w=[("mg", 4 * g + hp, c) for c in range(NTC)])
                            return
                        with ExitStack() as s2:
                            f2sb = mk_sb(s2)
                            strip = f2sb("mstrip", [128, STRIP_W], BF16)
                            DMA(strip[:], strips_d[1], w=["strip"])
                            tiles = attn_tiles(f2sb, mengs=("dve",))
                            srot = Rot([0, 1, 2])
                            accrot = Rot([(3, 4), (5, 6)])
                            jobs = []
                            for hp in range(4):
                                for c in range(NTC):
                                    jobs.append((c, QG, hp, ("QG", hp, c), KG, hp, ("KG", hp, c),
                                                 lambda j, e_, hp=hp: VG[:, j, hp, e_ * 64:e_ * 64 + 128], ("VG", 0, 0),
                                                 lambda j, t0, n, c=c: strip[:, c * 512 - j * 128 + 384 + t0:c * 512 - j * 128 + 384 + t0 + n], "strip",
                                                 merged[:, 4 * g + hp, c * 512:(c + 1) * 512], ("mg", 4 * g + hp, c)))
                            kzs, fill = make_kz(f2sb)
                            kzd = {}
                            for ji in range(len(jobs)):
                                hp = ji // NTC
                                kzd[ji] = (kzs[hp % 2], ("kz", hp % 2), fill(hp % 2, KG, hp) if ji % NTC == 0 else None)
                            attn_jobs(jobs, accrot, srot, tiles, kz=kzd)
                        P.barrier()

        for l in range(cfg["layers"]):
            if cfg["ffn"]:
                ffn(l, ffn_w["ffn_a_wg"], ffn_w["ffn_a_wu"], ffn_w["ffn_a_wd"], l * 4 + 0)
            if l % 2 == 0:
                if cfg["mix_even"]:
                    even_mixer(l * 4 + 1)
            else:
                if cfg["mix_odd"]:
                    odd_mixer(l * 4 + 1)
            if cfg["ffn"]:
                ffn(l, ffn_w["ffn_b_wg"], ffn_w["ffn_b_wu"], ffn_w["ffn_b_wd"], l * 4 + 2)
            if cfg["ple"]:
                ple(l, l * 4 + 3)
        final_norm()
        P.emit(st)
    nc._declared_inputs = declared
    return nc


def prep_inputs(inputs, b):
    f = np.float32
    g = lambda k: np.asarray(inputs[k], f)
    rope, strips, mats, negmask = _host_consts()
    gains = []
    for l in range(2):
        for nm in ("norm_ffn_a", "norm_mix", "norm_ffn_b", "norm_ple"):
            gains.append(_col(g(nm)[l]))
    gains.append(_col(g("final_norm")))
    gains = np.ascontiguousarray(np.concatenate(gains, axis=1))
    ewi = g("even_w_in")[0]
    qa, ka, va = ewi[:, 0:512], ewi[:, 512:576], ewi[:, 576:640]
    qi, ki, wi = ewi[:, 640:1152], ewi[:, 1152:1216], ewi[:, 1216:1224]
    qb, kb, vb = ewi[:, 1224:1736], ewi[:, 1736:2248], ewi[:, 2248:2760]
    ew_fm = np.ascontiguousarray(np.concatenate([qa, ka, ka, qi, ki, ki, qb, kb], axis=1))
    def tile_w(W, tc):
        R_, C_ = W.shape
        return np.ascontiguousarray(W.reshape(R_ // 128, 128, C_ // tc, tc).transpose(2, 1, 0, 3))

    m = {
        "xT": np.ascontiguousarray(g("x")[b].T),
        "pT": np.ascontiguousarray(np.transpose(g("p")[:, b], (0, 2, 1))),
        "gains": gains,
        "ple_gate": np.stack([tile_w(g("ple_gate")[l], 256) for l in range(2)], 0),
        "ple_proj": np.stack([tile_w(g("ple_proj")[l], 256) for l in range(2)], 0),
        "ew_fm": tile_w(ew_fm, 128), "ew_va": tile_w(np.ascontiguousarray(va), 64),
        "ew_wi": tile_w(np.ascontiguousarray(wi), 8), "ew_vb": tile_w(np.ascontiguousarray(vb), 128),
        "even_w_out": tile_w(g("even_w_out")[0], 256),
        "lamv": np.ascontiguousarray(np.stack([g("diff_lambda_q1")[0], g("diff_lambda_k1")[0],
                                               g("diff_lambda_q2")[0], g("diff_lambda_k2")[0]], 0)),
        "subln": np.ascontiguousarray(g("diff_subln")[0].reshape(128, 1)),
        "odd_w_in": tile_w(g("odd_w_in")[0], 128), "odd_w_out": tile_w(g("odd_w_out")[0], 256),
        "rope": rope, "strips": strips, "mats": mats, "negmask": negmask,
    }
    for nm in ("ffn_a_wg", "ffn_a_wu", "ffn_b_wg", "ffn_b_wu"):
        m[nm] = np.stack([tile_w(g(nm)[l], 256) for l in range(2)], 0)
    for nm in ("ffn_a_wd", "ffn_b_wd"):
        w = g(nm)
        m[nm] = np.stack([np.stack([np.stack([tile_w(w[l][hf * 1408:(hf + 1) * 1408, d2 * 256:(d2 + 1) * 256], 256)[0]
                                              for hf in range(2)], 0) for d2 in range(4)], 0) for l in range(2)], 0)
    return m


_NC_CACHE = {}


def kernel(**inputs):
    import time as _time
    _t0 = _time.time()
    key = tuple(sorted(CFG.items()))
    if key not in _NC_CACHE:
        _NC_CACHE[key] = build_program(CFG)
    nc = _NC_CACHE[key]
    print(f"[kernel] build {_time.time() - _t0:.1f}s", flush=True)
    n = 8
    shared = prep_inputs(inputs, 0)
    in_maps = []
    for b in range(n):
        m = dict(shared)
        m["xT"] = np.ascontiguousarray(np.asarray(inputs["x"], np.float32)[b].T)
        m["pT"] = np.ascontiguousarray(np.transpose(np.asarray(inputs["p"], np.float32)[:, b], (0, 2, 1)))
        in_maps.append({k: v for k, v in m.items() if k in nc._declared_inputs})
    print(f"[kernel] prep {_time.time() - _t0:.1f}s", flush=True)
    import os as _os
    _nd = int(_os.environ.get("DBG_CORES", "8"))
    if _nd != 8:
        res = run_bass_kernel_spmd(nc, in_maps[:_nd], core_ids=list(range(_nd)))
        res.results.extend([res.results[0]] * (8 - _nd))
    else:
        res = run_bass_kernel_spmd(nc, in_maps, core_ids=list(range(n)))
    print(f"[kernel] ran {_time.time() - _t0:.1f}s", flush=True)
    out = np.stack([np.asarray(res.results[b]["outT"], np.float32).T for b in range(n)], 0)
    return np.ascontiguousarray(out)
```

```python
import numpy as np
import ml_dtypes
import concourse.bass as bass
import concourse.mybir as mybir
from concourse.bass_utils import run_bass_kernel_spmd

F32 = mybir.dt.float32
BF16 = mybir.dt.bfloat16
ALU = mybir.AluOpType
AF = mybir.ActivationFunctionType
AX = mybir.AxisListType

T = 2048
D = 1024
DFF = 2816
NKC = D // 128
NFC = DFF // 128
NTB = T // 128
NTC = T // 512
PLE = 256
EVEN_IN = 2760
NEG = -1.0e30
NEG2 = -2.0e30


class Op:
    __slots__ = ("eng", "fn", "waits", "signal", "sem", "val", "is_dma", "idx", "is_bar")

    def __init__(self, eng, fn, is_dma=False):
        self.eng = eng
        self.fn = fn
        self.waits = []
        self.signal = False
        self.sem = None
        self.val = 0
        self.is_dma = is_dma
        self.idx = -1
        self.is_bar = False


ENGS = ("pe", "act", "dve", "pool", "sp")
SEM_LIM = 30000


class Prog:
    def __init__(self, nc):
        self.nc = nc
        self.ops = {e: [] for e in ENGS}
        self.last_w = {}
        self.readers = {}
        self.seen = {e: {s: -1 for s in ENGS} for e in ENGS}
        self.seen_dma = {e: set() for e in ENGS}
        self.pending_dma = []
        self.dma_list = []
        self.nops = 0

    NDMA = 24

    def add(self, eng, fn, reads=(), writes=(), dma=False):
        op = Op(eng, fn, is_dma=dma)
        deps = []
        if dma:
            k = len(self.dma_list)
            if k >= self.NDMA:
                deps.append((self.dma_list[k - self.NDMA], True))
            self.dma_list.append(op)
        for r in reads:
            w = self.last_w.get(r)
            if w is not None:
                deps.append((w, True))
            if r == "pT" or (isinstance(r, tuple) and r[0] == "ps"):
                for rd in self.readers.get(r, ()):
                    if rd.eng != eng:
                        deps.append((rd, True))
        for r in writes:
            w = self.last_w.get(r)
            if w is not None:
                deps.append((w, True))
            for rd in self.readers.get(r, ()):
                deps.append((rd, False))
        best = {}
        for d, is_raw in deps:
            if d.is_dma:
                self._dep(op, d, is_raw)
                continue
            if d.eng == eng and eng in ("pe", "sp"):
                continue
            cur = best.get(d.eng)
            if cur is None or d.idx > cur.idx:
                best[d.eng] = d
        for d in best.values():
            self._dep(op, d, True)
        op.idx = len(self.ops[eng])
        self.ops[eng].append(op)
        for r in reads:
            self.readers.setdefault(r, []).append(op)
        for r in writes:
            self.last_w[r] = op
            self.readers[r] = []
        if dma:
            self.pending_dma.append(op)
        self.nops += 1
        return op

    def _dep(self, op, d, is_raw):
        e = op.eng
        if d.is_dma:
            if d in self.seen_dma[e]:
                return
            self.seen_dma[e].add(d)
            d.signal = True
            op.waits.append(d)
            return
        if d.eng == e:
            if e == "pe" or e == "sp":
                return
        if self.seen[e][d.eng] >= d.idx:
            return
        self.seen[e][d.eng] = d.idx
        d.signal = True
        op.waits.append(d)

    def barrier(self):
        bar = Op("sp", None)
        bar.is_bar = True
        for e in ENGS:
            if e == "sp":
                continue
            if self.ops[e]:
                self._dep(bar, self.ops[e][-1], True)
        for d in self.pending_dma:
            self._dep(bar, d, True)
        self.pending_dma = []
        bar.idx = len(self.ops["sp"])
        self.ops["sp"].append(bar)
        bar.signal = True
        for e in ENGS:
            if e == "sp":
                continue
            w = Op(e, None)
            w.is_bar = True
            w.waits.append(bar)
            self.seen[e]["sp"] = bar.idx
            w.idx = len(self.ops[e])
            self.ops[e].append(w)
        self.last_w = {}
        self.readers = {}

    def emit(self, stack):
        nc = self.nc
        ndma = self.NDMA
        dsems = [stack.enter_context(nc.semaphore(f"s_dma_{i}")) for i in range(ndma)]
        dcnt = [0] * ndma
        for k, op in enumerate(self.dma_list):
            di = k % ndma
            if dcnt[di] + 16 > SEM_LIM:
                dsems[di] = stack.enter_context(nc.semaphore(f"s_dma_{di}_{k}"))
                dcnt[di] = 0
            dcnt[di] += 16
            op.sem = dsems[di]
            op.val = dcnt[di]
        eng_sems = {}
        for e in ENGS:
            cur = None
            cnt = 0
            for op in self.ops[e]:
                if not op.signal or op.is_dma:
                    continue
                if cur is None or cnt >= SEM_LIM:
                    cur = stack.enter_context(nc.semaphore(f"s_{e}_{len(eng_sems)}"))
                    eng_sems[(e, len(eng_sems))] = cur
                    cnt = 0
                cnt += 1
                op.sem = cur
                op.val = cnt
        block = stack.enter_context(nc.Block())

        def run(e):
            def body(eng):
                for op in self.ops[e]:
                    for d in op.waits:
                        eng.wait_ge(d.sem, d.val)
                    if op.fn is None:
                        if op.signal:
                            eng.sem_inc(op.sem, 1)
                        continue
                    ins = op.fn(eng)
                    if op.is_dma:
                        ins.then_inc(op.sem, 16)
                    elif op.signal:
                        ins.then_inc(op.sem, 1)
            return body

        block.tensor(run("pe"))
        block.scalar(run("act"))
        block.vector(run("dve"))
        block.gpsimd(run("pool"))
        block.sync(run("sp"))


STRIP_W = 2432


def _host_consts():
    f32 = np.float32
    inv = (1.0 / (f32(10000.0) ** (np.arange(0, 64, 2, dtype=f32) / f32(64)))).astype(f32)
    ang = (np.arange(T, dtype=f32)[:, None] * inv[None, :]).astype(f32)
    cos = np.cos(ang).astype(f32)
    sin = np.sin(ang).astype(f32)
    p = np.arange(128)
    d = p % 64
    j = d % 32
    C2 = cos[:, j].T.copy()
    S2 = (sin[:, j].T * np.where(d < 32, -1.0, 1.0)[:, None]).astype(f32)
    rope = np.ascontiguousarray(np.stack([C2, S2], 0)).astype(f32)
    x = np.arange(STRIP_W)[None, :] - 384 - p[:, None]
    causal = (x >= 0).astype(f32)
    mult = (((x >= 0) & (x <= 128)).astype(f32)
            + ((x >= 0) & (x % 4 == 0) & (x <= 512)).astype(f32)
            + ((x >= 0) & (x % 16 == 0) & (x <= 2048)).astype(f32))
    strips = np.stack([causal, mult], 0).astype(ml_dtypes.bfloat16)
    ident = np.eye(128, dtype=f32)
    perm = np.zeros((128, 128), f32)
    for m in range(128):
        perm[m ^ 32, m] = 1.0
    ones = np.ones((128, 128), f32)
    mats = np.stack([ident, perm, ones], 0).astype(ml_dtypes.bfloat16)
    tt = np.arange(128)
    negmask = np.where(tt[None, :] > tt[:, None], np.float32(NEG), np.float32(0.0)).astype(f32)
    return rope, strips, mats, negmask


def _col(v):
    v = np.asarray(v, np.float32)
    return np.ascontiguousarray(v.reshape(-1, 128).T)


CFG = {"layers": 2, "mix_even": True, "mix_odd": True, "dsa": True, "diff": True, "ffn": True, "ple": True, "proj": True, "attn": True}


class Rot:
    def __init__(self, items):
        self.items = list(items)
        self.i = 0

    def next(self):
        v = self.items[self.i % len(self.items)]
        self.i += 1
        return v


def build_program(cfg=None):
    from contextlib import ExitStack
    cfg = dict(CFG if cfg is None else cfg)
    nc = bass.Bass("TRN2", target_bir_lowering=False)

    declared = []

    def din(name, shape, dt=F32, need=True):
        if not need:
            return None
        declared.append(name)
        return nc.dram_tensor(name, list(shape), dt, kind="ExternalInput").ap()

    xT = din("xT", [D, T])
    pTd = din("pT", [2, PLE, T])
    gains_d = din("gains", [128, 72])
    ffn_w = {}
    for nm in ("ffn_a_wg", "ffn_a_wu", "ffn_b_wg", "ffn_b_wu"):
        ffn_w[nm] = din(nm, [2, 11, 128, 8, 256], need=cfg["ffn"])
    for nm in ("ffn_a_wd", "ffn_b_wd"):
        ffn_w[nm] = din(nm, [2, 4, 2, 128, 11, 256], need=cfg["ffn"])
    ple_gate = din("ple_gate", [2, 4, 128, 8, 256], need=cfg["ple"])
    ple_proj = din("ple_proj", [2, 4, 128, 2, 256], need=cfg["ple"])
    ew_fm = din("ew_fm", [18, 128, 8, 128])
    ew_va = din("ew_va", [1, 128, 8, 64])
    ew_wi = din("ew_wi", [1, 128, 8, 8])
    ew_vb = din("ew_vb", [4, 128, 8, 128])
    even_w_out = din("even_w_out", [4, 128, 8, 256])
    lamv = din("lamv", [4, 64])
    subln_d = din("subln", [128, 1])
    odd_w_in = din("odd_w_in", [24, 128, 8, 128])
    odd_w_out = din("odd_w_out", [4, 128, 8, 256])
    rope_d = din("rope", [2, 128, T])
    strips_d = din("strips", [2, 128, STRIP_W], BF16)
    mats_d = din("mats", [3, 128, 128], BF16)
    negm_d = din("negmask", [128, 128])
    outT = nc.dram_tensor("outT", [D, T], F32, kind="ExternalOutput").ap()

    uid = [0]

    with ExitStack() as st:
        def mk_sb(stack):
            def f(name, shape, dt):
                uid[0] += 1
                return stack.enter_context(nc.sbuf_tensor(f"{name}_{uid[0]}", list(shape), dt))
            return f

        sb = mk_sb(st)
        h = sb("h", [128, NKC, T], F32)
        gsb = sb("gsb", [128, 72], F32)
        mats = sb("mats", [128, 3, 128], BF16)
        negm = sb("negm", [128, 128], F32)
        epsb = sb("epsb", [128, 2], F32)
        lv = sb("lv", [128, 4, 64], F32)
        lsm = sb("lsm", [128, 8], F32)
        sqb = [sb(f"sqb{i}", [128, 512], BF16) for i in range(2)]
        sdt = sb("sdt", [128, 512], F32)
        rstd = sb("rstd", [128, 512], F32)
        ps = [st.enter_context(nc.psum_tensor(f"ps{i}", [128, 512], F32)) for i in range(7)]
        pTp = st.enter_context(nc.psum_tensor("pTp", [128, 1024], BF16))
        ident = mats[:, 0, :]
        perm = mats[:, 1, :]
        ones = mats[:, 2, :]

        P = Prog(nc)

        def A(eng, fn, r=(), w=()):
            return P.add(eng, fn, reads=r, writes=w)

        def DMA(out, in_, r=(), w=()):
            return P.add("sp", lambda e: e.dma_start(out=out, in_=in_), reads=r, writes=w, dma=True)

        for c in range(NTC):
            for k in range(NKC):
                DMA(h[:, k, c * 512:(c + 1) * 512], xT[k * 128:(k + 1) * 128, c * 512:(c + 1) * 512], w=[("h", k, c)])
        DMA(gsb[:], gains_d, w=["gsb"])
        DMA(mats[:], mats_d.rearrange("a p n -> p a n"), w=["mats"])
        DMA(negm[:], negm_d, w=["negm"])
        DMA(lv[:], lamv.partition_broadcast(128), w=["lv"])
        DMA(lsm[:, 7:8], subln_d, w=["lsm7"])
        A("pool", lambda e: e.memset(epsb[:, 0:1], 1e-6), w=["epsb"])
        A("pool", lambda e: e.memset(epsb[:, 1:2], 1e-5), w=["epsb"])

        sqrot = Rot([0, 1])
        cvt_rot = Rot(["act", "dve"])

        class WPool:
            def __init__(self, stack, elems, nst, nbf, tag):
                f = mk_sb(stack)
                self.stg = [f(f"wst{tag}{i}", [128, elems], F32) for i in range(nst)]
                self.bf = [f(f"wbf{tag}{i}", [128, elems], BF16) for i in range(nbf)]
                self.i = 0
                uid[0] += 1
                self.tag = f"{tag}{uid[0]}"

            def load(self, desc):
                src, a, b = desc
                i = self.i
                self.i += 1
                si, bi = i % len(self.stg), i % len(self.bf)
                sv = self.stg[si][:, 0:a * b].rearrange("p (a b) -> p a b", a=a)
                bv = self.bf[bi][:, 0:a * b].rearrange("p (a b) -> p a b", a=a)
                sk, bk = (self.tag, "st", si), (self.tag, "bf", bi)
                DMA(sv, src, w=[sk])
                self._cvt(bv, sv, sk, bk)
                return bv, bk

            def _cvt(self, dst, src, sk, dk):
                eng = cvt_rot.next()
                if eng == "act":
                    A("act", lambda e: e.activation(out=dst, in_=src, func=AF.Copy), r=[sk], w=[dk])
                else:
                    A(eng, lambda e: e.tensor_copy(out=dst, in_=src), r=[sk], w=[dk])

            def load_into(self, desc, dst, dkey):
                src, a, b = desc
                i = self.i
                self.i += 1
                si = i % len(self.stg)
                sv = self.stg[si][:, 0:a * b].rearrange("p (a b) -> p a b", a=a)
                sk = (self.tag, "st", si)
                DMA(sv, src, w=[sk])
                self._cvt(dst, sv, sk, dkey)

        def stream(wp, descs, depth=2):
            q = []
            n = len(descs)
            nxt = 0
            for i in range(n):
                while nxt < n and nxt <= i + depth:
                    q.append(wp.load(descs[nxt]))
                    nxt += 1
                yield q[i]

        def wdesc(w2d, r0, nr, c0, ncol):
            return (w2d[r0:r0 + nr, c0:c0 + ncol].rearrange("(a p) n -> p a n", p=128), nr // 128, ncol)

        def norm_chunk(gi, c, dst_fn, dkey_fn, srot, after=None, engs=("dve",), exact=False):
            b = srot.next()
            for k in range(NKC):
                si = sqrot.next()
                A("act", lambda e, k=k, si=si: e.activation(out=sqb[si][:], in_=h[:, k, c * 512:(c + 1) * 512],
                                                            func=AF.Square, scale=1.0 / 32.0),
                  r=[("h", k, c)], w=[("sqb", si)])
                A("pe", lambda e, k=k, si=si: e.matmul(ps[b][:], lhsT=ones, rhs=sqb[si][:], start=(k == 0), stop=(k == NKC - 1)),
                  r=[("sqb", si), "mats"], w=[("ps", b)])
            if exact:
                A("act", lambda e: e.activation(out=sdt[:], in_=ps[b][:], func=AF.Sqrt, bias=epsb[:, 0:1]),
                  r=[("ps", b), "epsb"], w=["sdt"])
                A("dve", lambda e: e.reciprocal(out=rstd[:], in_=sdt[:]), r=["sdt"], w=["rstd"])
            else:
                A("act", lambda e: e.activation(out=sdt[:], in_=ps[b][:], func=AF.Ln, bias=epsb[:, 0:1]),
                  r=[("ps", b), "epsb"], w=["sdt"])
                A("act", lambda e: e.activation(out=rstd[:], in_=sdt[:], func=AF.Exp, scale=-0.5), r=["sdt"], w=["rstd"])
            for k in range(NKC):
                eng = engs[k % len(engs)]
                dst = dst_fn(k)
                A(eng, lambda e, k=k, dst=dst: e.scalar_tensor_tensor(out=dst, in0=h[:, k, c * 512:(c + 1) * 512],
                                                                      scalar=gsb[:, gi * 8 + k:gi * 8 + k + 1], in1=rstd[:],
                                                                      op0=ALU.mult, op1=ALU.mult),
                  r=[("h", k, c), "gsb", "rstd"], w=[dkey_fn(k)])
                if after is not None:
                    after(k)

        def ffn(l, wg, wu, wd, gi):
            for tp in range(2):
                ffn_pass(l, wg, wu, wd, gi, tp)
                P.barrier()

        def ffn_pass(l, wg, wu, wd, gi, tp):
            if True:
                with ExitStack() as s2:
                    f2sb = mk_sb(s2)
                    hn = f2sb("hn", [128, NKC, 1024], BF16)
                    act = f2sb("act", [128, NFC, 1024], BF16)
                    sg = [f2sb(f"sg{i}", [128, 512], F32) for i in range(2)]
                    wp = WPool(s2, 2816, 3, 4, "f")
                    srot = Rot(range(7))
                    sgrot = Rot([0, 1])
                    for sub in range(2):
                        norm_chunk(gi, tp * 2 + sub, lambda k, sub=sub: hn[:, k, sub * 512:(sub + 1) * 512],
                                   lambda k, sub=sub: ("hn", k, sub), srot)
                    descs = []
                    for f2 in range(NFC // 2):
                        descs.append((wg[l, f2], 8, 256))
                        descs.append((wu[l, f2], 8, 256))
                    it = stream(wp, descs)
                    for f2 in range(NFC // 2):
                        gt, gk = next(it)
                        ut, uk = next(it)
                        for fi in range(2):
                            f = 2 * f2 + fi
                            for sub in range(2):
                                bg = srot.next()
                                bu = srot.next()
                                for k in range(NKC):
                                    A("pe", lambda e, k=k, bg=bg, gt=gt, fi=fi, sub=sub: e.matmul(
                                        ps[bg][:], lhsT=gt[:, k, fi * 128:(fi + 1) * 128], rhs=hn[:, k, sub * 512:(sub + 1) * 512],
                                        start=(k == 0), stop=(k == NKC - 1)), r=[gk, ("hn", k, sub)], w=[("ps", bg)])
                                for k in range(NKC):
                                    A("pe", lambda e, k=k, bu=bu, ut=ut, fi=fi, sub=sub: e.matmul(
                                        ps[bu][:], lhsT=ut[:, k, fi * 128:(fi + 1) * 128], rhs=hn[:, k, sub * 512:(sub + 1) * 512],
                                        start=(k == 0), stop=(k == NKC - 1)), r=[uk, ("hn", k, sub)], w=[("ps", bu)])
                                si = sgrot.next()
                                A("act", lambda e, si=si, bg=bg: e.activation(out=sg[si][:], in_=ps[bg][:], func=AF.Silu),
                                  r=[("ps", bg)], w=[("sg", si)])
                                A("dve", lambda e, si=si, bu=bu, f=f, sub=sub: e.tensor_tensor(
                                    out=act[:, f, sub * 512:(sub + 1) * 512], in0=sg[si][:], in1=ps[bu][:], op=ALU.mult),
                                  r=[("sg", si), ("ps", bu)], w=[("act", f, sub)])
                    descs = []
                    for d2 in range(4):
                        for half in range(2):
                            descs.append((wd[l, d2, half], 11, 256))
                    it = stream(wp, descs)
                    for d2 in range(4):
                        tl0, tk0 = next(it)
                        tl1, tk1 = next(it)
                        for di in range(2):
                            d = 2 * d2 + di
                            for sub in range(2):
                                b = srot.next()
                                c = tp * 2 + sub
                                for fc in range(NFC):
                                    tl, tk = (tl0, tk0) if fc < 11 else (tl1, tk1)
                                    A("pe", lambda e, fc=fc, tl=tl, di=di, sub=sub, b=b: e.matmul(
                                        ps[b][:], lhsT=tl[:, fc % 11, di * 128:(di + 1) * 128], rhs=act[:, fc, sub * 512:(sub + 1) * 512],
                                        start=(fc == 0), stop=(fc == NFC - 1)), r=[tk, ("act", fc, sub)], w=[("ps", b)])
                                A("dve", lambda e, d=d, c=c, b=b: e.scalar_tensor_tensor(
                                    out=h[:, d, c * 512:(c + 1) * 512], in0=ps[b][:], scalar=0.5, in1=h[:, d, c * 512:(c + 1) * 512],
                                    op0=ALU.mult, op1=ALU.add), r=[("ps", b), ("h", d, c)], w=[("h", d, c)])

        def ple(l, gi):
            with ExitStack() as s2:
                f2sb = mk_sb(s2)
                hn = f2sb("hnp", [128, NKC, T], BF16)
                ptb = f2sb("ptb", [128, 2, T], BF16)
                sig = [f2sb(f"sig{i}", [128, 512], F32) for i in range(2)]
                tmp = [f2sb(f"ptmp{i}", [128, 512], F32) for i in range(2)]
                wp = WPool(s2, 2048, 3, 4, "p")
                srot = Rot(range(7))
                r2 = Rot([0, 1])
                for c in range(NTC):
                    norm_chunk(gi, c, lambda k, c=c: hn[:, k, c * 512:(c + 1) * 512], lambda k, c=c: ("hnp", k, c), srot)
                    wp.load_into((pTd[l][:, c * 512:(c + 1) * 512].rearrange("(a p) t -> p a t", p=128), 2, 512),
                                 ptb[:, :, c * 512:(c + 1) * 512], ("ptb", c))
                descs = []
                for d2 in range(4):
                    descs.append((ple_gate[l, d2], 8, 256))
                    descs.append((ple_proj[l, d2], 2, 256))
                it = stream(wp, descs)
                for d2 in range(4):
                    gt, gk = next(it)
                    pt, pk = next(it)
                    for di in range(2):
                        d = 2 * d2 + di
                        for c in range(NTC):
                            bg = srot.next()
                            bp = srot.next()
                            for k in range(NKC):
                                A("pe", lambda e, k=k, bg=bg, gt=gt, di=di, c=c: e.matmul(
                                    ps[bg][:], lhsT=gt[:, k, di * 128:(di + 1) * 128], rhs=hn[:, k, c * 512:(c + 1) * 512],
                                    start=(k == 0), stop=(k == NKC - 1)), r=[gk, ("hnp", k, c)], w=[("ps", bg)])
                            for k in range(2):
                                A("pe", lambda e, k=k, bp=bp, pt=pt, di=di, c=c: e.matmul(
                                    ps[bp][:], lhsT=pt[:, k, di * 128:(di + 1) * 128], rhs=ptb[:, k, c * 512:(c + 1) * 512],
                                    start=(k == 0), stop=(k == 1)), r=[pk, ("ptb", c)], w=[("ps", bp)])
                            si = r2.next()
                            A("act", lambda e, si=si, bg=bg: e.activation(out=sig[si][:], in_=ps[bg][:], func=AF.Sigmoid),
                              r=[("ps", bg)], w=[("sig", si)])
                            A("dve", lambda e, si=si, bp=bp: e.tensor_tensor(out=tmp[si][:], in0=sig[si][:], in1=ps[bp][:], op=ALU.mult),
                              r=[("sig", si), ("ps", bp)], w=[("ptmp", si)])
                            A("pool", lambda e, si=si, d=d, c=c: e.tensor_tensor(
                                out=h[:, d, c * 512:(c + 1) * 512], in0=h[:, d, c * 512:(c + 1) * 512], in1=tmp[si][:], op=ALU.add),
                              r=[("ptmp", si), ("h", d, c)], w=[("h", d, c)])
            P.barrier()

        def final_norm():
            with ExitStack() as s2:
                f2sb = mk_sb(s2)
                ot = [f2sb(f"ot{i}", [128, 512], F32) for i in range(4)]
                srot = Rot(range(7))
                for c in range(NTC):
                    def after(k, c=c):
                        i = (c * 8 + k) % 4
                        DMA(outT[k * 128:(k + 1) * 128, c * 512:(c + 1) * 512], ot[i][:], r=[("ot", i)])
                    norm_chunk(8, c, lambda k, c=c: ot[(c * 8 + k) % 4][:], lambda k, c=c: ("ot", (c * 8 + k) % 4), srot,
                               after=after, engs=("dve",), exact=True)
            P.barrier()

        def mixer_proj(gi, fm_items, tm_items):
            import os as _os
            _nfm = int(_os.environ.get("PROJ_FM", "99"))
            _ntm = int(_os.environ.get("PROJ_TM", "99"))
            fm_items = fm_items[:_nfm]
            tm_items = tm_items[:_ntm]
            with ExitStack() as s2:
                f2sb = mk_sb(s2)
                hnc = f2sb("hnc", [128, NKC, 512], BF16)
                ropc = [f2sb(f"ropc{i}", [128, 2, 512], F32) for i in range(2)]
                xb = [f2sb(f"xb{i}", [128, 512], BF16) for i in range(2)]
                t1 = [f2sb(f"t1{i}", [128, 512], F32) for i in range(2)]
                t2 = [f2sb(f"t2{i}", [128, 512], F32) for i in range(2)]
                wp = WPool(s2, 1024, 2, 3, "m")
                srot = Rot(range(7))
                r2 = Rot([0, 1])
                for c in range(NTC):
                    norm_chunk(gi, c, lambda k: hnc[:, k, :], lambda k: ("hnc", k), srot)
                    rc = c % 2
                    DMA(ropc[rc][:], rope_d[:, :, c * 512:(c + 1) * 512].rearrange("a p t -> p a t"), w=[("ropc", rc)])
                    descs = [(w_ap, 8, 128) for (w_ap, _) in fm_items]
                    descs += [(w_ap, 8, n) for (w_ap, n, _, _) in tm_items]
                    it = stream(wp, descs)
                    def finish(i, dst_fn):
                        b2 = srot.next()
                        A("pe", lambda e, i=i, b2=b2: e.matmul(ps[b2][:], lhsT=perm, rhs=xb[i][:], start=True, stop=True),
                          r=[("xb", i), "mats"], w=[("ps", b2)])
                        A("dve", lambda e, i=i, b2=b2, rc=rc: e.tensor_tensor(out=t2[i][:], in0=ps[b2][:], in1=ropc[rc][:, 1, :], op=ALU.mult),
                          r=[("ps", b2), ("ropc", rc)], w=[("t2", i)])
                        dst, dkey = dst_fn(c)
                        A("pool", lambda e, i=i, dst=dst: e.tensor_tensor(out=dst, in0=t1[i][:], in1=t2[i][:], op=ALU.add),
                          r=[("t1", i), ("t2", i)], w=[dkey])

                    pend = None
                    for (_, dst_fn) in fm_items:
                        wt, wk = next(it)
                        b = srot.next()
                        for k in range(NKC):
                            A("pe", lambda e, k=k, b=b, wt=wt: e.matmul(ps[b][:], lhsT=wt[:, k, :], rhs=hnc[:, k, :],
                                                                        start=(k == 0), stop=(k == NKC - 1)),
                              r=[wk, ("hnc", k)], w=[("ps", b)])
                        i = r2.next()
                        A("act", lambda e, i=i, b=b: e.activation(out=xb[i][:], in_=ps[b][:], func=AF.Copy),
                          r=[("ps", b)], w=[("xb", i)])
                        A("dve", lambda e, i=i, b=b, rc=rc: e.tensor_tensor(out=t1[i][:], in0=ps[b][:], in1=ropc[rc][:, 0, :], op=ALU.mult),
                          r=[("ps", b), ("ropc", rc), ("xb", i)], w=[("t1", i)])
                        if pend is not None:
                            finish(*pend)
                        pend = (i, dst_fn)
                    if pend is not None:
                        finish(*pend)
                    for (_, n, dst_fn, _) in tm_items:
                        wt, wk = next(it)
                        for tb in range(4):
                            b = srot.next()
                            for k in range(NKC):
                                A("pe", lambda e, k=k, b=b, wt=wt, tb=tb, n=n: e.matmul(
                                    ps[b][:, 0:n], lhsT=hnc[:, k, tb * 128:(tb + 1) * 128], rhs=wt[:, k, :],
                                    start=(k == 0), stop=(k == NKC - 1)), r=[wk, ("hnc", k)], w=[("ps", b)])
                            dst, dkey = dst_fn(c * 4 + tb)
                            if isinstance(dst, list):
                                for (dap, lo_, hi_) in dst:
                                    A("act", lambda e, b=b, dap=dap, lo_=lo_, hi_=hi_: e.activation(out=dap, in_=ps[b][:, lo_:hi_], func=AF.Copy),
                                      r=[("ps", b)], w=[dkey])
                            else:
                                A("act", lambda e, b=b, dst=dst, n=n: e.activation(out=dst, in_=ps[b][:, 0:n], func=AF.Copy),
                                  r=[("ps", b)], w=[dkey])
            P.barrier()

        LOOK = 2

        def run_units(units, front, back):
            fr = []
            for idx, u in enumerate(units):
                fr.append(front(u))
                if idx >= LOOK:
                    back(units[idx - LOOK], fr[idx - LOOK])
            for idx in range(max(0, len(units) - LOOK), len(units)):
                back(units[idx], fr[idx])

        def attn_jobs(jobs, accrot, srot, tiles, kz=None):
            ebs, pbs, rds, erot, prot, rrot, mengs = tiles
            units = []
            for ji, job in enumerate(jobs):
                c = job[0]
                accs = accrot.next()
                last = 4 * c + 3
                for j in range(last + 1):
                    for e_ in range(2):
                        units.append((job, accs, j, e_, j == last and e_ == 1, ji, j == 0 and e_ == 0))

            def front(u):
                (c, Q, qc, qkey, K, kc, kkey, vfn, vkey, mfn, mkey, dst, dkey), accs, j, e_, is_last, ji, is_first = u
                t0 = max(0, j * 128 - c * 512)
                n = 512 - t0
                lo, hi = e_ * 64, (e_ + 1) * 64
                b = srot.next()
                if kz is not None:
                    kzt, kzkey, pre = kz[ji]
                    if is_first and pre is not None:
                        pre()
                    A("pe", lambda e: e.matmul(
                        ps[b][:, 0:n], lhsT=kzt[:, e_, j * 128:(j + 1) * 128], rhs=Q[:, qc, c * 512 + t0:(c + 1) * 512],
                        start=True, stop=True), r=[qkey, kzkey], w=[("ps", b)])
                else:
                    A("pe", lambda e: e.matmul(
                        ps[b][:, 0:n], lhsT=K[lo:hi, kc, j * 128:(j + 1) * 128], rhs=Q[lo:hi, qc, c * 512 + t0:(c + 1) * 512],
                        start=True, stop=True), r=[qkey, kkey], w=[("ps", b)])
                ei = erot.next()
                A("act", lambda e: e.activation(out=ebs[ei][:, 0:n], in_=ps[b][:, 0:n], func=AF.Exp, scale=0.125),
                  r=[("ps", b)], w=[("eb", ei)])
                m = mfn(j, t0, n)
                if m is not None:
                    pi = prot.next()
                    A(mengs.next(), lambda e: e.tensor_tensor(out=pbs[pi][:, 0:n], in0=ebs[ei][:, 0:n], in1=m, op=ALU.mult),
                      r=[("eb", ei), mkey], w=[("pb", pi)])
                    return pbs[pi], ("pb", pi)
                return ebs[ei], ("eb", ei)

            def back(u, fr):
                (c, Q, qc, qkey, K, kc, kkey, vfn, vkey, mfn, mkey, dst, dkey), (ba, bb), j, e_, is_last, ji, is_first = u
                src, skey = fr
                last = 4 * c + 3
                t0 = max(0, j * 128 - c * 512)
                n = 512 - t0
                bk_ = ba if e_ == 0 else bb
                v = vfn(j, e_)
                A("pe", lambda e: e.matmul(ps[bk_][:, t0:512], lhsT=v, rhs=src[:, 0:n], start=(j == 0), stop=(j == last)),
                  r=[skey, vkey], w=[("ps", bk_)])
                if is_last:
                    di = rrot.next()
                    dsh = rds[di]
                    A("dve", lambda e: e.tensor_copy(out=dsh[0:64, :], in_=ps[ba][64:128, :]), r=[("ps", ba)], w=[("rd", di)])
                    A("dve", lambda e: e.tensor_copy(out=dsh[64:128, :], in_=ps[bb][0:64, :]), r=[("ps", bb)], w=[("rd", di)])
                    A("act", lambda e: e.activation(out=dsh[:], in_=dsh[:], func=AF.Ln), r=[("rd", di)], w=[("rd", di)])
                    A("act", lambda e: e.activation(out=dsh[:], in_=dsh[:], func=AF.Exp, scale=-1.0), r=[("rd", di)], w=[("rd", di)])
                    A("dve", lambda e: e.tensor_tensor(out=dst[0:64, :], in0=ps[ba][0:64, :], in1=dsh[0:64, :], op=ALU.mult),
                      r=[("ps", ba), ("rd", di)], w=[dkey])
                    A("dve", lambda e: e.tensor_tensor(out=dst[64:128, :], in0=ps[bb][64:128, :], in1=dsh[64:128, :], op=ALU.mult),
                      r=[("ps", bb), ("rd", di)], w=[dkey])

            run_units(units, front, back)

        def make_kz(f2sb, nbuf=2):
            kzs = [f2sb(f"kz{i}", [128, 2, T], BF16) for i in range(nbuf)]
            for i in range(nbuf):
                A("pool", lambda e, i=i: e.memset(kzs[i][64:128, 0, :], 0.0), w=[("kz", i)])
                A("pool", lambda e, i=i: e.memset(kzs[i][0:64, 1, :], 0.0), w=[("kz", i)])

            def fill(i, K, kc):
                def pre():
                    A("pool", lambda e: e.tensor_copy(out=kzs[i][0:64, 0, :], in_=K[0:64, kc, :]), r=[], w=[("kz", i)])
                    A("dve", lambda e: e.tensor_copy(out=kzs[i][64:128, 1, :], in_=K[64:128, kc, :]), r=[], w=[("kz", i)])
                return pre
            return kzs, fill

        def attn_tiles(f2sb, mengs=("dve", "pool")):
            ebs = [f2sb(f"eb{i}", [128, 512], BF16) for i in range(4)]
            pbs = [f2sb(f"pb{i}", [128, 512], BF16) for i in range(4)]
            rds = [f2sb(f"rd{i}", [128, 512], F32) for i in range(2)]
            return (ebs, pbs, rds, Rot(range(4)), Rot(range(4)), Rot(range(2)), Rot(mengs))

        def apply_wout(w2d, merged):
            with ExitStack() as s2:
                wp = WPool(s2, 2048, 3, 4, "o")
                srot = Rot(range(7))
                descs = [(w2d[d2], 8, 256) for d2 in range(4)]
                it = stream(wp, descs)
                for d2 in range(4):
                    wt, wk = next(it)
                    for di in range(2):
                        d = 2 * d2 + di
                        for c in range(NTC):
                            b = srot.next()
                            for k in range(NKC):
                                A("pe", lambda e, k=k, b=b, wt=wt, di=di, c=c: e.matmul(
                                    ps[b][:], lhsT=wt[:, k, di * 128:(di + 1) * 128], rhs=merged[:, k, c * 512:(c + 1) * 512],
                                    start=(k == 0), stop=(k == NKC - 1)), r=[wk, ("mg", k, c)], w=[("ps", b)])
                            A("dve", lambda e, b=b, d=d, c=c: e.tensor_tensor(
                                out=h[:, d, c * 512:(c + 1) * 512], in0=ps[b][:], in1=h[:, d, c * 512:(c + 1) * 512], op=ALU.add),
                              r=[("ps", b), ("h", d, c)], w=[("h", d, c)])
            P.barrier()

        def even_mixer(gi):
            with ExitStack() as sm:
                msb = mk_sb(sm)
                merged = msb("merged", [128, NKC, T], BF16)
                A("dve", lambda e: e.tensor_tensor(out=lv[:, 0, :], in0=lv[:, 0, :], in1=lv[:, 1, :], op=ALU.mult), r=["lv"], w=["lv"])
                A("dve", lambda e: e.tensor_tensor(out=lv[:, 2, :], in0=lv[:, 2, :], in1=lv[:, 3, :], op=ALU.mult), r=["lv"], w=["lv"])
                A("dve", lambda e: e.reduce_sum(out=lsm[:, 0:1], in_=lv[:, 0, :], axis=AX.X), r=["lv"], w=["lsm"])
                A("dve", lambda e: e.reduce_sum(out=lsm[:, 1:2], in_=lv[:, 2, :], axis=AX.X), r=["lv"], w=["lsm"])
                A("act", lambda e: e.activation(out=lsm[:, 2:4], in_=lsm[:, 0:2], func=AF.Exp), r=["lsm"], w=["lsm"])
                A("dve", lambda e: e.tensor_tensor(out=lsm[:, 4:5], in0=lsm[:, 3:4], in1=lsm[:, 2:3], op=ALU.subtract), r=["lsm"], w=["lsm"])
                A("dve", lambda e: e.tensor_scalar(out=lsm[:, 5:6], in0=lsm[:, 4:5], scalar1=-0.2, scalar2=None, op0=ALU.add), r=["lsm"], w=["lsm"])
                A("dve", lambda e: e.tensor_scalar(out=lsm[:, 6:7], in0=lsm[:, 7:8], scalar1=0.8, scalar2=None, op0=ALU.mult), r=["lsm", "lsm7"], w=["lsm"])
                neglam = lsm[:, 5:6]
                gsub = lsm[:, 6:7]

                for kk in range(NKC):
                    if (kk < 4 and not cfg["dsa"]) or (kk >= 4 and not cfg["diff"]):
                        A("pool", lambda e, kk=kk: e.memset(merged[:, kk, :], 0.0), w=[("mg", kk, c) for c in range(NTC)])
                if cfg["dsa"]:
                    with ExitStack() as sa:
                        asb = mk_sb(sa)
                        QA = asb("QA", [128, 4, T], BF16)
                        KA = asb("KA", [128, 1, T], BF16)
                        QI = asb("QI", [128, 4, T], BF16)
                        KI = asb("KI", [128, 1, T], BF16)
                        VA = asb("VA", [128, NTB, 192], BF16)
                        A("pool", lambda e: e.memset(VA[:, :, 64:128], 1.0), w=[("VA1",)])
                        WI = asb("WI", [128, NTB, 8], F32)
                        fm = []
                        bufs = [(QA, n, "QA") for n in range(4)] + [(KA, 0, "KA")] + [(QI, n, "QI") for n in range(4)] + [(KI, 0, "KI")]
                        for ci, (buf, n, nm) in enumerate(bufs):
                            fm.append((ew_fm[ci],
                                       lambda c, buf=buf, n=n, nm=nm: (buf[:, n, c * 512:(c + 1) * 512], (nm, n, c))))
                        tm = [(ew_va[0], 64, lambda blk: ([(VA[:, blk, 0:64], 0, 64), (VA[:, blk, 128:192], 0, 64)], ("VA", blk)), None),
                              (ew_wi[0], 8, lambda blk: (WI[:, blk, :], ("WI", blk)), None)]
                        mixer_proj(gi, fm, tm)
                        with ExitStack() as s2:
                            f2sb = mk_sb(s2)
                            Ib = [f2sb(f"Ib{i}", [128, T], F32) for i in range(2)]
                            rb = [f2sb(f"rb{i}", [128, 512], F32) for i in range(2)]
                            mk = f2sb("mk", [128, T], BF16)
                            junk = mk
                            maskT = f2sb("maskT", [128, NTB, 512], BF16)
                            bnd = [f2sb(f"bnd{i}", [128, 8], F32) for i in range(2)]
                            gem = [f2sb(f"gem{i}", [128, 2], mybir.dt.uint32) for i in range(2)]
                            tiles = attn_tiles(f2sb, mengs=("dve",))
                            srot = Rot([0, 1, 2])
                            rrot = Rot(range(2))
                            accrot = Rot([(3, 4), (5, 6)])
                            NIT = 15

                            def s1(i):
                                c = i // 4
                                cols = (i + 1) * 128
                                I = Ib[i % 2]
                                ik = ("I", i % 2)
                                bd = bnd[i % 2]
                                bk = ("bnd", i % 2)
                                for sc in range((cols + 511) // 512):
                                    w_ = min(512, cols - sc * 512)
                                    for hd in range(8):
                                        lo, hi = (hd % 2) * 64, (hd % 2 + 1) * 64
                                        b = srot.next()
                                        A("pe", lambda e, b=b, lo=lo, hi=hi, hd=hd, sc=sc, w_=w_: e.matmul(
                                            ps[b][:, 0:w_], lhsT=QI[lo:hi, hd // 2, i * 128:(i + 1) * 128],
                                            rhs=KI[lo:hi, 0, sc * 512:sc * 512 + w_], start=True, stop=True),
                                          r=[("QI", hd // 2, c), ("KI", 0, sc)], w=[("ps", b)])
                                        ri = rrot.next()
                                        A("act", lambda e, b=b, ri=ri, w_=w_: e.activation(out=rb[ri][:, 0:w_], in_=ps[b][:, 0:w_], func=AF.Relu),
                                          r=[("ps", b)], w=[("rb", ri)])
                                        if hd == 0:
                                            A("dve", lambda e, ri=ri, sc=sc, w_=w_: e.tensor_scalar(
                                                out=I[:, sc * 512:sc * 512 + w_], in0=rb[ri][:, 0:w_], scalar1=WI[:, i, 0:1], scalar2=None, op0=ALU.mult),
                                              r=[("rb", ri), ("WI", i)], w=[ik])
                                        else:
                                            A("dve", lambda e, ri=ri, sc=sc, w_=w_, hd=hd: e.scalar_tensor_tensor(
                                                out=I[:, sc * 512:sc * 512 + w_], in0=rb[ri][:, 0:w_], scalar=WI[:, i, hd:hd + 1],
                                                in1=I[:, sc * 512:sc * 512 + w_], op0=ALU.mult, op1=ALU.add),
                                              r=[("rb", ri), ("WI", i), ik], w=[ik])
                                if i >= 2:
                                    A("dve", lambda e: e.tensor_reduce(out=bd[:, 0:1], in_=I[:, 0:cols], axis=AX.X, op=ALU.min), r=[ik], w=[bk])
                                    A("dve", lambda e: e.tensor_reduce(out=bd[:, 1:2], in_=I[:, 0:cols], axis=AX.X, op=ALU.max), r=[ik], w=[bk])
                                    A("dve", lambda e: e.tensor_tensor(out=bd[:, 2:3], in0=bd[:, 1:2], in1=bd[:, 0:1], op=ALU.subtract), r=[bk], w=[bk])
                                A("pool", lambda e: e.tensor_tensor(out=I[:, i * 128:(i + 1) * 128], in0=I[:, i * 128:(i + 1) * 128],
                                                                    in1=negm[:], op=ALU.add), r=[ik, "negm"], w=[ik])

                            def s2(i):
                                ti = i % 4
                                cols = (i + 1) * 128
                                I = Ib[i % 2]
                                ik = ("I", i % 2)
                                bd = bnd[i % 2]
                                bk = ("bnd", i % 2)
                                gm = gem[i % 2]
                                gk = ("gem", i % 2)
                                if i >= 2:
                                    for it in range(NIT):
                                        A("dve", lambda e, it=it: e.scalar_tensor_tensor(out=bd[:, 3:4], in0=bd[:, 2:3], scalar=float(2.0 ** -(it + 1)),
                                                                                        in1=bd[:, 0:1], op0=ALU.mult, op1=ALU.add), r=[bk], w=[bk])
                                        A("dve", lambda e: e.tensor_scalar(out=junk[:, 0:cols], in0=I[:, 0:cols], scalar1=bd[:, 3:4], scalar2=None,
                                                                           op0=ALU.is_ge, op1=ALU.add, accum_out=bd[:, 4:5]), r=[ik, bk], w=[bk, "mk"])
                                        A("dve", lambda e: e.tensor_single_scalar(out=gm[:, 0:1], in_=bd[:, 4:5], scalar=255.5, op=ALU.is_ge), r=[bk], w=[gk])
                                        A("dve", lambda e: e.copy_predicated(out=bd[:, 0:1], mask=gm[:, 0:1], data=bd[:, 3:4]), r=[bk, gk], w=[bk])
                                    A("dve", lambda e: e.tensor_scalar(out=mk[:, 0:cols], in0=I[:, 0:cols], scalar1=bd[:, 0:1], scalar2=None, op0=ALU.is_ge),
                                      r=[ik, bk], w=["mk"])
                                else:
                                    A("dve", lambda e: e.tensor_single_scalar(out=mk[:, 0:cols], in_=I[:, 0:cols], scalar=-1.0e29, op=ALU.is_ge), r=[ik], w=["mk"])
                                for j0 in range(0, i + 1, 8):
                                    nn = min(8, i + 1 - j0)
                                    for jj in range(nn):
                                        A("pe", lambda e, jj=jj, j0=j0: e.transpose(
                                            out=pTp[:, jj * 128:(jj + 1) * 128],
                                            in_=mk[:, (j0 + jj) * 128:(j0 + jj + 1) * 128], identity=ident),
                                          r=["mk", "mats"], w=["pT"])
                                    A("act", lambda e, nn=nn, j0=j0: e.activation(
                                        out=maskT[:, j0:j0 + nn, ti * 128:(ti + 1) * 128],
                                        in_=pTp[:, 0:nn * 128].rearrange("p (a b) -> p a b", a=nn), func=AF.Copy),
                                      r=["pT"], w=["maskT"])

                            s1(0)
                            for i in range(NTB):
                                if i + 1 < NTB:
                                    s1(i + 1)
                                s2(i)
                                if i % 4 == 3:
                                    c = i // 4
                                    attn_jobs([(c, QA, hp, ("QA", hp, c), KA, 0, ("KA", 0, c),
                                                lambda j, e_: VA[:, j, e_ * 64:e_ * 64 + 128], ("VA", 0),
                                                lambda j, t0, n: maskT[:, j, t0:512], "maskT",
                                                merged[:, hp, c * 512:(c + 1) * 512], ("mg", hp, c)) for hp in range(4)],
                                              accrot, srot, tiles)
                        P.barrier()

                if cfg["diff"]:
                    with ExitStack() as sa:
                        asb = mk_sb(sa)
                        QB = asb("QB", [128, 4, T], BF16)
                        KB = asb("KB", [128, 4, T], BF16)
                        VB = asb("VB", [128, NTB, 512], BF16)
                        fm = []
                        bufs = [(QB, n, "QB") for n in range(4)] + [(KB, n, "KB") for n in range(4)]
                        for ci, (buf, n, nm) in enumerate(bufs):
                            fm.append((ew_fm[10 + ci],
                                       lambda c, buf=buf, n=n, nm=nm: (buf[:, n, c * 512:(c + 1) * 512], (nm, n, c))))
                        tm = [(ew_vb[n], 128,
                               lambda blk, n=n: (VB[:, blk, n * 128:(n + 1) * 128], ("VB", blk, n)), None) for n in range(4)]
                        mixer_proj(gi, fm, tm)
                        with ExitStack() as s2:
                            f2sb = mk_sb(s2)
                            strip = f2sb("cstrip", [128, STRIP_W], BF16)
                            DMA(strip[:], strips_d[0], w=["strip"])
                            ebs = [f2sb(f"eb{i}", [128, 512], BF16) for i in range(3)]
                            pbs = [f2sb(f"pb{i}", [128, 512], BF16) for i in range(3)]
                            fa = [f2sb(f"fa{i}", [128, 512], F32) for i in range(4)]
                            ob = f2sb("ob", [128, 512], F32)
                            sq2 = f2sb("sq2", [128, 512], BF16)
                            ebs = ebs + [f2sb("eb3", [128, 512], BF16)]
                            pbs = pbs + [f2sb("pb3", [128, 512], BF16)]
                            erot, prot = Rot(range(4)), Rot(range(4))
                            srot = Rot([0, 1, 2])
                            dkzs, dfill = make_kz(f2sb)
                            units = []
                            for hd in range(4):
                                for c in range(NTC):
                                    last = 4 * c + 3
                                    for j in range(last + 1):
                                        for e_ in range(2):
                                            units.append((hd, c, j, e_, j == last and e_ == 1))

                            def dfront(u):
                                hd, c, j, e_, is_last = u
                                t0 = max(0, j * 128 - c * 512)
                                n = 512 - t0
                                lo, hi = e_ * 64, (e_ + 1) * 64
                                b = srot.next()
                                if c == 0 and j == 0 and e_ == 0:
                                    dfill(hd % 2, KB, hd)()
                                kzt = dkzs[hd % 2]
                                A("pe", lambda e: e.matmul(
                                    ps[b][:, 0:n], lhsT=kzt[:, e_, j * 128:(j + 1) * 128],
                                    rhs=QB[:, hd, c * 512 + t0:(c + 1) * 512], start=True, stop=True),
                                  r=[("QB", hd, c), ("kz", hd % 2)], w=[("ps", b)])
                                ei = erot.next()
                                A("act", lambda e: e.activation(out=ebs[ei][:, 0:n], in_=ps[b][:, 0:n], func=AF.Exp, scale=0.125),
                                  r=[("ps", b)], w=[("eb", ei)])
                                if j >= 4 * c:
                                    off = c * 512 - j * 128 + 384 + t0
                                    pi = prot.next()
                                    A("dve", lambda e: e.tensor_tensor(
                                        out=pbs[pi][:, 0:n], in0=ebs[ei][:, 0:n], in1=strip[:, off:off + n], op=ALU.mult),
                                      r=[("eb", ei), "strip"], w=[("pb", pi)])
                                    return pbs[pi], ("pb", pi)
                                return ebs[ei], ("eb", ei)

                            def dback(u, fr):
                                hd, c, j, e_, is_last = u
                                src, skey = fr
                                last = 4 * c + 3
                                t0 = max(0, j * 128 - c * 512)
                                n = 512 - t0
                                nb, db = 3 + 2 * e_, 4 + 2 * e_
                                A("pe", lambda e: e.matmul(
                                    ps[nb][:, t0:512], lhsT=VB[:, j, hd * 128:(hd + 1) * 128], rhs=src[:, 0:n],
                                    start=(j == 0), stop=(j == last)),
                                  r=[skey, ("VB", j, hd)], w=[("ps", nb)])
                                A("pe", lambda e: e.matmul(
                                    ps[db][:, t0:512], lhsT=ones, rhs=src[:, 0:n], start=(j == 0), stop=(j == last)),
                                  r=[skey, "mats"], w=[("ps", db)])
                                if not is_last:
                                    return
                                A("act", lambda e: e.activation(out=fa[0][:], in_=ps[4][:], func=AF.Ln), r=[("ps", 4)], w=[("fa", 0)])
                                A("act", lambda e: e.activation(out=fa[0][:], in_=fa[0][:], func=AF.Exp, scale=-1.0), r=[("fa", 0)], w=[("fa", 0)])
                                A("dve", lambda e: e.tensor_tensor(out=fa[1][:], in0=ps[3][:], in1=fa[0][:], op=ALU.mult),
                                  r=[("ps", 3), ("fa", 0)], w=[("fa", 1)])
                                A("act", lambda e: e.activation(out=fa[2][:], in_=ps[6][:], func=AF.Ln), r=[("ps", 6)], w=[("fa", 2)])
                                A("act", lambda e: e.activation(out=fa[2][:], in_=fa[2][:], func=AF.Exp, scale=-1.0), r=[("fa", 2)], w=[("fa", 2)])
                                A("dve", lambda e: e.tensor_tensor(out=fa[3][:], in0=ps[5][:], in1=fa[2][:], op=ALU.mult),
                                  r=[("ps", 5), ("fa", 2)], w=[("fa", 3)])
                                A("dve", lambda e: e.scalar_tensor_tensor(out=ob[:], in0=fa[3][:], scalar=neglam, in1=fa[1][:],
                                                                          op0=ALU.mult, op1=ALU.add),
                                  r=[("fa", 3), ("fa", 1), "lsm"], w=["ob"])
                                A("act", lambda e: e.activation(out=sq2[:], in_=ob[:], func=AF.Square, scale=float(128.0 ** -0.5)),
                                  r=["ob"], w=["sq2"])
                                b = srot.next()
                                A("pe", lambda e: e.matmul(ps[b][:], lhsT=ones, rhs=sq2[:], start=True, stop=True),
                                  r=["sq2", "mats"], w=[("ps", b)])
                                A("act", lambda e: e.activation(out=fa[0][:], in_=ps[b][:], func=AF.Ln, bias=epsb[:, 1:2]),
                                  r=[("ps", b), "epsb"], w=[("fa", 0)])
                                A("act", lambda e: e.activation(out=fa[2][:], in_=fa[0][:], func=AF.Exp, scale=-0.5), r=[("fa", 0)], w=[("fa", 2)])
                                A("dve", lambda e: e.scalar_tensor_tensor(
                                    out=merged[:, 4 + hd, c * 512:(c + 1) * 512], in0=ob[:], scalar=gsub, in1=fa[2][:],
                                    op0=ALU.mult, op1=ALU.mult), r=["ob", ("fa", 2), "lsm"], w=[("mg", 4 + hd, c)])

                            run_units(units, dfront, dback)
                        P.barrier()
                apply_wout(even_w_out, merged)

        def odd_mixer(gi):
            with ExitStack() as sm:
                msb = mk_sb(sm)
                merged = msb("mergedo", [128, NKC, T], BF16)
                for g in range(2):
                    odd_group(g, gi, merged)
                apply_wout(odd_w_out, merged)

        def odd_group(g, gi, merged):
            if True:
                if True:
                    with ExitStack() as sa:
                        asb = mk_sb(sa)
                        QG = asb("QG", [128, 4, T], BF16)
                        KG = asb("KG", [128, 4, T], BF16)
                        VG = asb("VG", [128, NTB, 4, 192], BF16)
                        A("pool", lambda e: e.memset(VG[:, :, :, 64:128], 1.0), w=[("VG1",)])
                        fm = []
                        for n in range(4):
                            fm.append((odd_w_in[4 * g + n],
                                       lambda c, n=n: (QG[:, n, c * 512:(c + 1) * 512], ("QG", n, c))))
                        for n in range(4):
                            fm.append((odd_w_in[8 + 4 * g + n],
                                       lambda c, n=n: (KG[:, n, c * 512:(c + 1) * 512], ("KG", n, c))))
                        tm = [(odd_w_in[16 + 4 * g + n], 128,
                               lambda blk, n=n: ([(VG[:, blk, n, 0:64], 0, 64), (VG[:, blk, n, 128:192], 64, 128)], ("VG", blk, n)), None) for n in range(4)]
                        if cfg["proj"]:
                            mixer_proj(gi, fm, tm)
                        if not cfg["attn"]:
                            for hp in range(4):
                                A("pool", lambda e, hp=hp: e.memset(merged[:, 4 * g + hp, :], 0.0), w=[("mg", 4 * g + hp, c) for c in range(NTC)])
                            return
                        with ExitStack() as s2:
                            f2sb = mk_sb(s2)
                            strip = f2sb("mstrip", [128, STRIP_W], BF16)
                            DMA(strip[:], strips_d[1], w=["strip"])
                            tiles = attn_tiles(f2sb, mengs=("dve",))
                            srot = Rot([0, 1, 2])
                            accrot = Rot([(3, 4), (5, 6)])
                            jobs = []
                            for hp in range(4):
                                for c in range(NTC):
                                    jobs.append((c, QG, hp, ("QG", hp, c), KG, hp, ("KG", hp, c),
                                                 lambda j, e_, hp=hp: VG[:, j, hp, e_ * 64:e_ * 64 + 128], ("VG", 0, 0),
                                                 lambda j, t0, n, c=c: strip[:, c * 512 - j * 128 + 384 + t0:c * 512 - j * 128 + 384 + t0 + n], "strip",
                                                 merged[:, 4 * g + hp, c * 512:(c + 1) * 512], ("mg", 4 * g + hp, c)))
                            kzs, fill = make_kz(f2sb)
                            kzd = {}
                            for ji in range(len(jobs)):
                                hp = ji // NTC
                                kzd[ji] = (kzs[hp % 2], ("kz", hp % 2), fill(hp % 2, KG, hp) if ji % NTC == 0 else None)
                            attn_jobs(jobs, accrot, srot, tiles, kz=kzd)
                        P.barrier()

        for l in range(cfg["layers"]):
            if cfg["ffn"]:
                ffn(l, ffn_w["ffn_a_wg"], ffn_w["ffn_a_wu"], ffn_w["ffn_a_wd"], l * 4 + 0)
            if l % 2 == 0:
                if cfg["mix_even"]:
                    even_mixer(l * 4 + 1)
            else:
                if cfg["mix_odd"]:
                    odd_mixer(l * 4 + 1)
            if cfg["ffn"]:
                ffn(l, ffn_w["ffn_b_wg"], ffn_w["ffn_b_wu"], ffn_w["ffn_b_wd"], l * 4 + 2)
            if cfg["ple"]:
                ple(l, l * 4 + 3)
        final_norm()
        P.emit(st)
    nc._declared_inputs = declared
    return nc


def prep_inputs(inputs, b):
    f = np.float32
    g = lambda k: np.asarray(inputs[k], f)
    rope, strips, mats, negmask = _host_consts()
    gains = []
    for l in range(2):
        for nm in ("norm_ffn_a", "norm_mix", "norm_ffn_b", "norm_ple"):
            gains.append(_col(g(nm)[l]))
    gains.append(_col(g("final_norm")))
    gains = np.ascontiguousarray(np.concatenate(gains, axis=1))
    ewi = g("even_w_in")[0]
    qa, ka, va = ewi[:, 0:512], ewi[:, 512:576], ewi[:, 576:640]
    qi, ki, wi = ewi[:, 640:1152], ewi[:, 1152:1216], ewi[:, 1216:1224]
    qb, kb, vb = ewi[:, 1224:1736], ewi[:, 1736:2248], ewi[:, 2248:2760]
    ew_fm = np.ascontiguousarray(np.concatenate([qa, ka, ka, qi, ki, ki, qb, kb], axis=1))
    def tile_w(W, tc):
        R_, C_ = W.shape
        return np.ascontiguousarray(W.reshape(R_ // 128, 128, C_ // tc, tc).transpose(2, 1, 0, 3))

    m = {
        "xT": np.ascontiguousarray(g("x")[b].T),
        "pT": np.ascontiguousarray(np.transpose(g("p")[:, b], (0, 2, 1))),
        "gains": gains,
        "ple_gate": np.stack([tile_w(g("ple_gate")[l], 256) for l in range(2)], 0),
        "ple_proj": np.stack([tile_w(g("ple_proj")[l], 256) for l in range(2)], 0),
        "ew_fm": tile_w(ew_fm, 128), "ew_va": tile_w(np.ascontiguousarray(va), 64),
        "ew_wi": tile_w(np.ascontiguousarray(wi), 8), "ew_vb": tile_w(np.ascontiguousarray(vb), 128),
        "even_w_out": tile_w(g("even_w_out")[0], 256),
        "lamv": np.ascontiguousarray(np.stack([g("diff_lambda_q1")[0], g("diff_lambda_k1")[0],
                                               g("diff_lambda_q2")[0], g("diff_lambda_k2")[0]], 0)),
        "subln": np.ascontiguousarray(g("diff_subln")[0].reshape(128, 1)),
        "odd_w_in": tile_w(g("odd_w_in")[0], 128), "odd_w_out": tile_w(g("odd_w_out")[0], 256),
        "rope": rope, "strips": strips, "mats": mats, "negmask": negmask,
    }
    for nm in ("ffn_a_wg", "ffn_a_wu", "ffn_b_wg", "ffn_b_wu"):
        m[nm] = np.stack([tile_w(g(nm)[l], 256) for l in range(2)], 0)
    for nm in ("ffn_a_wd", "ffn_b_wd"):
        w = g(nm)
        m[nm] = np.stack([np.stack([np.stack([tile_w(w[l][hf * 1408:(hf + 1) * 1408, d2 * 256:(d2 + 1) * 256], 256)[0]
                                              for hf in range(2)], 0) for d2 in range(4)], 0) for l in range(2)], 0)
    return m


_NC_CACHE = {}


def kernel(**inputs):
    import time as _time
    _t0 = _time.time()
    key = tuple(sorted(CFG.items()))
    if key not in _NC_CACHE:
        _NC_CACHE[key] = build_program(CFG)
    nc = _NC_CACHE[key]
    print(f"[kernel] build {_time.time() - _t0:.1f}s", flush=True)
    n = 8
    shared = prep_inputs(inputs, 0)
    in_maps = []
    for b in range(n):
        m = dict(shared)
        m["xT"] = np.ascontiguousarray(np.asarray(inputs["x"], np.float32)[b].T)
        m["pT"] = np.ascontiguousarray(np.transpose(np.asarray(inputs["p"], np.float32)[:, b], (0, 2, 1)))
        in_maps.append({k: v for k, v in m.items() if k in nc._declared_inputs})
    print(f"[kernel] prep {_time.time() - _t0:.1f}s", flush=True)
    import os as _os
    _nd = int(_os.environ.get("DBG_CORES", "8"))
    if _nd != 8:
        res = run_bass_kernel_spmd(nc, in_maps[:_nd], core_ids=list(range(_nd)))
        res.results.extend([res.results[0]] * (8 - _nd))
    else:
        res = run_bass_kernel_spmd(nc, in_maps, core_ids=list(range(n)))
    print(f"[kernel] ran {_time.time() - _t0:.1f}s", flush=True)
    out = np.stack([np.asarray(res.results[b]["outT"], np.float32).T for b in range(n)], 0)
    return np.ascontiguousarray(out)
```

```python
import numpy as np
import ml_dtypes
import concourse.bass as bass
import concourse.mybir as mybir
from concourse.bass_utils import run_bass_kernel_spmd

F32 = mybir.dt.float32
BF16 = mybir.dt.bfloat16
ALU = mybir.AluOpType
AF = mybir.ActivationFunctionType
AX = mybir.AxisListType

T = 2048
D = 1024
DFF = 2816
NKC = D // 128
NFC = DFF // 128
NTB = T // 128
NTC = T // 512
PLE = 256
EVEN_IN = 2760
NEG = -1.0e30
NEG2 = -2.0e30


class Op:
    __slots__ = ("eng", "fn", "waits", "signal", "sem", "val", "is_dma", "idx", "is_bar")

    def __init__(self, eng, fn, is_dma=False):
        self.eng = eng
        self.fn = fn
        self.waits = []
        self.signal = False
        self.sem = None
        self.val = 0
        self.is_dma = is_dma
        self.idx = -1
        self.is_bar = False


ENGS = ("pe", "act", "dve", "pool", "sp")
SEM_LIM = 30000


class Prog:
    def __init__(self, nc):
        self.nc = nc
        self.ops = {e: [] for e in ENGS}
        self.last_w = {}
        self.readers = {}
        self.seen = {e: {s: -1 for s in ENGS} for e in ENGS}
        self.seen_dma = {e: set() for e in ENGS}
        self.pending_dma = []
        self.dma_list = []
        self.nops = 0

    NDMA = 24

    def add(self, eng, fn, reads=(), writes=(), dma=False):
        op = Op(eng, fn, is_dma=dma)
        deps = []
        if dma:
            k = len(self.dma_list)
            if k >= self.NDMA:
                deps.append((self.dma_list[k - self.NDMA], True))
            self.dma_list.append(op)
        for r in reads:
            w = self.last_w.get(r)
            if w is not None:
                deps.append((w, True))
            if r == "pT" or (isinstance(r, tuple) and r[0] == "ps"):
                for rd in self.readers.get(r, ()):
                    if rd.eng != eng:
                        deps.append((rd, True))
        for r in writes:
            w = self.last_w.get(r)
            if w is not None:
                deps.append((w, True))
            for rd in self.readers.get(r, ()):
                deps.append((rd, False))
        best = {}
        for d, is_raw in deps:
            if d.is_dma:
                self._dep(op, d, is_raw)
                continue
            if d.eng == eng and eng in ("pe", "sp"):
                continue
            cur = best.get(d.eng)
            if cur is None or d.idx > cur.idx:
                best[d.eng] = d
        for d in best.values():
            self._dep(op, d, True)
        op.idx = len(self.ops[eng])
        self.ops[eng].append(op)
        for r in reads:
            self.readers.setdefault(r, []).append(op)
        for r in writes:
            self.last_w[r] = op
            self.readers[r] = []
        if dma:
            self.pending_dma.append(op)
        self.nops += 1
        return op

    def _dep(self, op, d, is_raw):
        e = op.eng
        if d.is_dma:
            if d in self.seen_dma[e]:
                return
            self.seen_dma[e].add(d)
            d.signal = True
            op.waits.append(d)
            return
        if d.eng == e:
            if e == "pe" or e == "sp":
                return
        if self.seen[e][d.eng] >= d.idx:
            return
        self.seen[e][d.eng] = d.idx
        d.signal = True
        op.waits.append(d)

    def barrier(self):
        bar = Op("sp", None)
        bar.is_bar = True
        for e in ENGS:
            if e == "sp":
                continue
            if self.ops[e]:
                self._dep(bar, self.ops[e][-1], True)
        for d in self.pending_dma:
            self._dep(bar, d, True)
        self.pending_dma = []
        bar.idx = len(self.ops["sp"])
        self.ops["sp"].append(bar)
        bar.signal = True
        for e in ENGS:
            if e == "sp":
                continue
            w = Op(e, None)
            w.is_bar = True
            w.waits.append(bar)
            self.seen[e]["sp"] = bar.idx
            w.idx = len(self.ops[e])
            self.ops[e].append(w)
        self.last_w = {}
        self.readers = {}

    def emit(self, stack):
        nc = self.nc
        ndma = self.NDMA
        dsems = [stack.enter_context(nc.semaphore(f"s_dma_{i}")) for i in range(ndma)]
        dcnt = [0] * ndma
        for k, op in enumerate(self.dma_list):
            di = k % ndma
            if dcnt[di] + 16 > SEM_LIM:
                dsems[di] = stack.enter_context(nc.semaphore(f"s_dma_{di}_{k}"))
                dcnt[di] = 0
            dcnt[di] += 16
            op.sem = dsems[di]
            op.val = dcnt[di]
        eng_sems = {}
        for e in ENGS:
            cur = None
            cnt = 0
            for op in self.ops[e]:
                if not op.signal or op.is_dma:
                    continue
                if cur is None or cnt >= SEM_LIM:
                    cur = stack.enter_context(nc.semaphore(f"s_{e}_{len(eng_sems)}"))
                    eng_sems[(e, len(eng_sems))] = cur
                    cnt = 0
                cnt += 1
                op.sem = cur
                op.val = cnt
        block = stack.enter_context(nc.Block())

        def run(e):
            def body(eng):
                for op in self.ops[e]:
                    for d in op.waits:
                        eng.wait_ge(d.sem, d.val)
                    if op.fn is None:
                        if op.signal:
                            eng.sem_inc(op.sem, 1)
                        continue
                    ins = op.fn(eng)
                    if op.is_dma:
                        ins.then_inc(op.sem, 16)
                    elif op.signal:
                        ins.then_inc(op.sem, 1)
            return body

        block.tensor(run("pe"))
        block.scalar(run("act"))
        block.vector(run("dve"))
        block.gpsimd(run("pool"))
        block.sync(run("sp"))


STRIP_W = 2432


def _host_consts():
    f32 = np.float32
    inv = (1.0 / (f32(10000.0) ** (np.arange(0, 64, 2, dtype=f32) / f32(64)))).astype(f32)
    ang = (np.arange(T, dtype=f32)[:, None] * inv[None, :]).astype(f32)
    cos = np.cos(ang).astype(f32)
    sin = np.sin(ang).astype(f32)
    p = np.arange(128)
    d = p % 64
    j = d % 32
    C2 = cos[:, j].T.copy()
    S2 = (sin[:, j].T * np.where(d < 32, -1.0, 1.0)[:, None]).astype(f32)
    rope = np.ascontiguousarray(np.stack([C2, S2], 0)).astype(f32)
    x = np.arange(STRIP_W)[None, :] - 384 - p[:, None]
    causal = (x >= 0).astype(f32)
    mult = (((x >= 0) & (x <= 128)).astype(f32)
            + ((x >= 0) & (x % 4 == 0) & (x <= 512)).astype(f32)
            + ((x >= 0) & (x % 16 == 0) & (x <= 2048)).astype(f32))
    strips = np.stack([causal, mult], 0).astype(ml_dtypes.bfloat16)
    ident = np.eye(128, dtype=f32)
    perm = np.zeros((128, 128), f32)
    for m in range(128):
        perm[m ^ 32, m] = 1.0
    ones = np.ones((128, 128), f32)
    mats = np.stack([ident, perm, ones], 0).astype(ml_dtypes.bfloat16)
    tt = np.arange(128)
    negmask = np.where(tt[None, :] > tt[:, None], np.float32(NEG), np.float32(0.0)).astype(f32)
    return rope, strips, mats, negmask


def _col(v):
    v = np.asarray(v, np.float32)
    return np.ascontiguousarray(v.reshape(-1, 128).T)


CFG = {"layers": 2, "mix_even": True, "mix_odd": True, "dsa": True, "diff": True, "ffn": True, "ple": True, "proj": True, "attn": True}


class Rot:
    def __init__(self, items):
        self.items = list(items)
        self.i = 0

    def next(self):
        v = self.items[self.i % len(self.items)]
        self.i += 1
        return v


def build_program(cfg=None):
    from contextlib import ExitStack
    cfg = dict(CFG if cfg is None else cfg)
    nc = bass.Bass("TRN2", target_bir_lowering=False)

    declared = []

    def din(name, shape, dt=F32, need=True):
        if not need:
            return None
        declared.append(name)
        return nc.dram_tensor(name, list(shape), dt, kind="ExternalInput").ap()

    xT = din("xT", [D, T])
    pTd = din("pT", [2, PLE, T])
    gains_d = din("gains", [128, 72])
    ffn_w = {}
    for nm in ("ffn_a_wg", "ffn_a_wu", "ffn_b_wg", "ffn_b_wu"):
        ffn_w[nm] = din(nm, [2, 11, 128, 8, 256], need=cfg["ffn"])
    for nm in ("ffn_a_wd", "ffn_b_wd"):
        ffn_w[nm] = din(nm, [2, 4, 2, 128, 11, 256], need=cfg["ffn"])
    ple_gate = din("ple_gate", [2, 4, 128, 8, 256], need=cfg["ple"])
    ple_proj = din("ple_proj", [2, 4, 128, 2, 256], need=cfg["ple"])
    ew_fm = din("ew_fm", [18, 128, 8, 128])
    ew_va = din("ew_va", [1, 128, 8, 64])
    ew_wi = din("ew_wi", [1, 128, 8, 8])
    ew_vb = din("ew_vb", [4, 128, 8, 128])
    even_w_out = din("even_w_out", [4, 128, 8, 256])
    lamv = din("lamv", [4, 64])
    subln_d = din("subln", [128, 1])
    odd_w_in = din("odd_w_in", [24, 128, 8, 128])
    odd_w_out = din("odd_w_out", [4, 128, 8, 256])
    rope_d = din("rope", [2, 128, T])
    strips_d = din("strips", [2, 128, STRIP_W], BF16)
    mats_d = din("mats", [3, 128, 128], BF16)
    negm_d = din("negmask", [128, 128])
    outT = nc.dram_tensor("outT", [D, T], F32, kind="ExternalOutput").ap()

    uid = [0]

    with ExitStack() as st:
        def mk_sb(stack):
            def f(name, shape, dt):
                uid[0] += 1
                return stack.enter_context(nc.sbuf_tensor(f"{name}_{uid[0]}", list(shape), dt))
            return f

        sb = mk_sb(st)
        h = sb("h", [128, NKC, T], F32)
        gsb = sb("gsb", [128, 72], F32)
        mats = sb("mats", [128, 3, 128], BF16)
        negm = sb("negm", [128, 128], F32)
        epsb = sb("epsb", [128, 2], F32)
        lv = sb("lv", [128, 4, 64], F32)
        lsm = sb("lsm", [128, 8], F32)
        sqb = [sb(f"sqb{i}", [128, 512], BF16) for i in range(2)]
        sdt = sb("sdt", [128, 512], F32)
        rstd = sb("rstd", [128, 512], F32)
        ps = [st.enter_context(nc.psum_tensor(f"ps{i}", [128, 512], F32)) for i in range(8)]
        pTp = ps[7].bitcast(BF16)
        ident = mats[:, 0, :]
        perm = mats[:, 1, :]
        ones = mats[:, 2, :]

        P = Prog(nc)

        def A(eng, fn, r=(), w=()):
            return P.add(eng, fn, reads=r, writes=w)

        def DMA(out, in_, r=(), w=()):
            return P.add("sp", lambda e: e.dma_start(out=out, in_=in_), reads=r, writes=w, dma=True)

        for c in range(NTC):
            for k in range(NKC):
                DMA(h[:, k, c * 512:(c + 1) * 512], xT[k * 128:(k + 1) * 128, c * 512:(c + 1) * 512], w=[("h", k, c)])
        DMA(gsb[:], gains_d, w=["gsb"])
        DMA(mats[:], mats_d.rearrange("a p n -> p a n"), w=["mats"])
        DMA(negm[:], negm_d, w=["negm"])
        DMA(lv[:], lamv.partition_broadcast(128), w=["lv"])
        DMA(lsm[:, 7:8], subln_d, w=["lsm7"])
        A("pool", lambda e: e.memset(epsb[:, 0:1], 1e-6), w=["epsb"])
        A("pool", lambda e: e.memset(epsb[:, 1:2], 1e-5), w=["epsb"])

        sqrot = Rot([0, 1])
        cvt_rot = Rot(["act", "dve"])

        class WPool:
            def __init__(self, stack, elems, nst, nbf, tag):
                f = mk_sb(stack)
                self.stg = [f(f"wst{tag}{i}", [128, elems], F32) for i in range(nst)]
                self.bf = [f(f"wbf{tag}{i}", [128, elems], BF16) for i in range(nbf)]
                self.i = 0
                uid[0] += 1
                self.tag = f"{tag}{uid[0]}"

            def load(self, desc):
                src, a, b = desc
                i = self.i
                self.i += 1
                si, bi = i % len(self.stg), i % len(self.bf)
                sv = self.stg[si][:, 0:a * b].rearrange("p (a b) -> p a b", a=a)
                bv = self.bf[bi][:, 0:a * b].rearrange("p (a b) -> p a b", a=a)
                sk, bk = (self.tag, "st", si), (self.tag, "bf", bi)
                DMA(sv, src, w=[sk])
                self._cvt(bv, sv, sk, bk)
                return bv, bk

            def _cvt(self, dst, src, sk, dk):
                eng = cvt_rot.next()
                if eng == "act":
                    A("act", lambda e: e.activation(out=dst, in_=src, func=AF.Copy), r=[sk], w=[dk])
                else:
                    A(eng, lambda e: e.tensor_copy(out=dst, in_=src), r=[sk], w=[dk])

            def load_into(self, desc, dst, dkey):
                src, a, b = desc
                i = self.i
                self.i += 1
                si = i % len(self.stg)
                sv = self.stg[si][:, 0:a * b].rearrange("p (a b) -> p a b", a=a)
                sk = (self.tag, "st", si)
                DMA(sv, src, w=[sk])
                self._cvt(dst, sv, sk, dkey)

        def stream(wp, descs, depth=2):
            q = []
            n = len(descs)
            nxt = 0
            for i in range(n):
                while nxt < n and nxt <= i + depth:
                    q.append(wp.load(descs[nxt]))
                    nxt += 1
                yield q[i]

        def wdesc(w2d, r0, nr, c0, ncol):
            return (w2d[r0:r0 + nr, c0:c0 + ncol].rearrange("(a p) n -> p a n", p=128), nr // 128, ncol)

        def norm_chunk(gi, c, dst_fn, dkey_fn, srot, after=None, engs=("dve",), exact=False):
            b = srot.next()
            for k in range(NKC):
                si = sqrot.next()
                A("act", lambda e, k=k, si=si: e.activation(out=sqb[si][:], in_=h[:, k, c * 512:(c + 1) * 512],
                                                            func=AF.Square, scale=1.0 / 32.0),
                  r=[("h", k, c)], w=[("sqb", si)])
                A("pe", lambda e, k=k, si=si: e.matmul(ps[b][:], lhsT=ones, rhs=sqb[si][:], start=(k == 0), stop=(k == NKC - 1)),
                  r=[("sqb", si), "mats"], w=[("ps", b)])
            if exact:
                A("act", lambda e: e.activation(out=sdt[:], in_=ps[b][:], func=AF.Sqrt, bias=epsb[:, 0:1]),
                  r=[("ps", b), "epsb"], w=["sdt"])
                A("dve", lambda e: e.reciprocal(out=rstd[:], in_=sdt[:]), r=["sdt"], w=["rstd"])
            else:
                A("act", lambda e: e.activation(out=sdt[:], in_=ps[b][:], func=AF.Ln, bias=epsb[:, 0:1]),
                  r=[("ps", b), "epsb"], w=["sdt"])
                A("act", lambda e: e.activation(out=rstd[:], in_=sdt[:], func=AF.Exp, scale=-0.5), r=["sdt"], w=["rstd"])
            for k in range(NKC):
                eng = engs[k % len(engs)]
                dst = dst_fn(k)
                A(eng, lambda e, k=k, dst=dst: e.scalar_tensor_tensor(out=dst, in0=h[:, k, c * 512:(c + 1) * 512],
                                                                      scalar=gsb[:, gi * 8 + k:gi * 8 + k + 1], in1=rstd[:],
                                                                      op0=ALU.mult, op1=ALU.mult),
                  r=[("h", k, c), "gsb", "rstd"], w=[dkey_fn(k)])
                if after is not None:
                    after(k)

        def ffn(l, wg, wu, wd, gi):
            for tp in range(2):
                ffn_pass(l, wg, wu, wd, gi, tp)
                P.barrier()

        def ffn_pass(l, wg, wu, wd, gi, tp):
            if True:
                with ExitStack() as s2:
                    f2sb = mk_sb(s2)
                    hn = f2sb("hn", [128, NKC, 1024], BF16)
                    act = f2sb("act", [128, NFC, 1024], BF16)
                    sg = [f2sb(f"sg{i}", [128, 512], F32) for i in range(2)]
                    wp = WPool(s2, 2816, 3, 4, "f")
                    srot = Rot(range(7))
                    sgrot = Rot([0, 1])
                    for sub in range(2):
                        norm_chunk(gi, tp * 2 + sub, lambda k, sub=sub: hn[:, k, sub * 512:(sub + 1) * 512],
                                   lambda k, sub=sub: ("hn", k, sub), srot)
                    descs = []
                    for f2 in range(NFC // 2):
                        descs.append((wg[l, f2], 8, 256))
                        descs.append((wu[l, f2], 8, 256))
                    it = stream(wp, descs)
                    for f2 in range(NFC // 2):
                        gt, gk = next(it)
                        ut, uk = next(it)
                        for fi in range(2):
                            f = 2 * f2 + fi
                            for sub in range(2):
                                bg = srot.next()
                                bu = srot.next()
                                for k in range(NKC):
                                    A("pe", lambda e, k=k, bg=bg, gt=gt, fi=fi, sub=sub: e.matmul(
                                        ps[bg][:], lhsT=gt[:, k, fi * 128:(fi + 1) * 128], rhs=hn[:, k, sub * 512:(sub + 1) * 512],
                                        start=(k == 0), stop=(k == NKC - 1)), r=[gk, ("hn", k, sub)], w=[("ps", bg)])
                                for k in range(NKC):
                                    A("pe", lambda e, k=k, bu=bu, ut=ut, fi=fi, sub=sub: e.matmul(
                                        ps[bu][:], lhsT=ut[:, k, fi * 128:(fi + 1) * 128], rhs=hn[:, k, sub * 512:(sub + 1) * 512],
                                        start=(k == 0), stop=(k == NKC - 1)), r=[uk, ("hn", k, sub)], w=[("ps", bu)])
                                si = sgrot.next()
                                A("act", lambda e, si=si, bg=bg: e.activation(out=sg[si][:], in_=ps[bg][:], func=AF.Silu),
                                  r=[("ps", bg)], w=[("sg", si)])
                                A("dve", lambda e, si=si, bu=bu, f=f, sub=sub: e.tensor_tensor(
                                    out=act[:, f, sub * 512:(sub + 1) * 512], in0=sg[si][:], in1=ps[bu][:], op=ALU.mult),
                                  r=[("sg", si), ("ps", bu)], w=[("act", f, sub)])
                    descs = []
                    for d2 in range(4):
                        for half in range(2):
                            descs.append((wd[l, d2, half], 11, 256))
                    it = stream(wp, descs)
                    for d2 in range(4):
                        tl0, tk0 = next(it)
                        tl1, tk1 = next(it)
                        for di in range(2):
                            d = 2 * d2 + di
                            for sub in range(2):
                                b = srot.next()
                                c = tp * 2 + sub
                                for fc in range(NFC):
                                    tl, tk = (tl0, tk0) if fc < 11 else (tl1, tk1)
                                    A("pe", lambda e, fc=fc, tl=tl, di=di, sub=sub, b=b: e.matmul(
                                        ps[b][:], lhsT=tl[:, fc % 11, di * 128:(di + 1) * 128], rhs=act[:, fc, sub * 512:(sub + 1) * 512],
                                        start=(fc == 0), stop=(fc == NFC - 1)), r=[tk, ("act", fc, sub)], w=[("ps", b)])
                                A("dve", lambda e, d=d, c=c, b=b: e.scalar_tensor_tensor(
                                    out=h[:, d, c * 512:(c + 1) * 512], in0=ps[b][:], scalar=0.5, in1=h[:, d, c * 512:(c + 1) * 512],
                                    op0=ALU.mult, op1=ALU.add), r=[("ps", b), ("h", d, c)], w=[("h", d, c)])

        def ple(l, gi):
            with ExitStack() as s2:
                f2sb = mk_sb(s2)
                hn = f2sb("hnp", [128, NKC, T], BF16)
                ptb = f2sb("ptb", [128, 2, T], BF16)
                sig = [f2sb(f"sig{i}", [128, 512], F32) for i in range(2)]
                tmp = [f2sb(f"ptmp{i}", [128, 512], F32) for i in range(2)]
                wp = WPool(s2, 2048, 3, 4, "p")
                srot = Rot(range(7))
                r2 = Rot([0, 1])
                for c in range(NTC):
                    norm_chunk(gi, c, lambda k, c=c: hn[:, k, c * 512:(c + 1) * 512], lambda k, c=c: ("hnp", k, c), srot)
                    wp.load_into((pTd[l][:, c * 512:(c + 1) * 512].rearrange("(a p) t -> p a t", p=128), 2, 512),
                                 ptb[:, :, c * 512:(c + 1) * 512], ("ptb", c))
                descs = []
                for d2 in range(4):
                    descs.append((ple_gate[l, d2], 8, 256))
                    descs.append((ple_proj[l, d2], 2, 256))
                it = stream(wp, descs)
                for d2 in range(4):
                    gt, gk = next(it)
                    pt, pk = next(it)
                    for di in range(2):
                        d = 2 * d2 + di
                        for c in range(NTC):
                            bg = srot.next()
                            bp = srot.next()
                            for k in range(NKC):
                                A("pe", lambda e, k=k, bg=bg, gt=gt, di=di, c=c: e.matmul(
                                    ps[bg][:], lhsT=gt[:, k, di * 128:(di + 1) * 128], rhs=hn[:, k, c * 512:(c + 1) * 512],
                                    start=(k == 0), stop=(k == NKC - 1)), r=[gk, ("hnp", k, c)], w=[("ps", bg)])
                            for k in range(2):
                                A("pe", lambda e, k=k, bp=bp, pt=pt, di=di, c=c: e.matmul(
                                    ps[bp][:], lhsT=pt[:, k, di * 128:(di + 1) * 128], rhs=ptb[:, k, c * 512:(c + 1) * 512],
                                    start=(k == 0), stop=(k == 1)), r=[pk, ("ptb", c)], w=[("ps", bp)])
                            si = r2.next()
                            A("act", lambda e, si=si, bg=bg: e.activation(out=sig[si][:], in_=ps[bg][:], func=AF.Sigmoid),
                              r=[("ps", bg)], w=[("sig", si)])
                            A("dve", lambda e, si=si, bp=bp: e.tensor_tensor(out=tmp[si][:], in0=sig[si][:], in1=ps[bp][:], op=ALU.mult),
                              r=[("sig", si), ("ps", bp)], w=[("ptmp", si)])
                            A("pool", lambda e, si=si, d=d, c=c: e.tensor_tensor(
                                out=h[:, d, c * 512:(c + 1) * 512], in0=h[:, d, c * 512:(c + 1) * 512], in1=tmp[si][:], op=ALU.add),
                              r=[("ptmp", si), ("h", d, c)], w=[("h", d, c)])
            P.barrier()

        def final_norm():
            with ExitStack() as s2:
                f2sb = mk_sb(s2)
                ot = [f2sb(f"ot{i}", [128, 512], F32) for i in range(4)]
                srot = Rot(range(7))
                for c in range(NTC):
                    def after(k, c=c):
                        i = (c * 8 + k) % 4
                        DMA(outT[k * 128:(k + 1) * 128, c * 512:(c + 1) * 512], ot[i][:], r=[("ot", i)])
                    norm_chunk(8, c, lambda k, c=c: ot[(c * 8 + k) % 4][:], lambda k, c=c: ("ot", (c * 8 + k) % 4), srot,
                               after=after, engs=("dve",), exact=True)
            P.barrier()

        def mixer_proj(gi, fm_items, tm_items):
            import os as _os
            _nfm = int(_os.environ.get("PROJ_FM", "99"))
            _ntm = int(_os.environ.get("PROJ_TM", "99"))
            fm_items = fm_items[:_nfm]
            tm_items = tm_items[:_ntm]
            with ExitStack() as s2:
                f2sb = mk_sb(s2)
                hnc = f2sb("hnc", [128, NKC, 512], BF16)
                ropc = [f2sb(f"ropc{i}", [128, 2, 512], F32) for i in range(2)]
                xb = [f2sb(f"xb{i}", [128, 512], BF16) for i in range(2)]
                t1 = [f2sb(f"t1{i}", [128, 512], F32) for i in range(2)]
                t2 = [f2sb(f"t2{i}", [128, 512], F32) for i in range(2)]
                wp = WPool(s2, 1024, 2, 3, "m")
                srot = Rot(range(7))
                r2 = Rot([0, 1])
                for c in range(NTC):
                    norm_chunk(gi, c, lambda k: hnc[:, k, :], lambda k: ("hnc", k), srot)
                    rc = c % 2
                    DMA(ropc[rc][:], rope_d[:, :, c * 512:(c + 1) * 512].rearrange("a p t -> p a t"), w=[("ropc", rc)])
                    descs = [(w_ap, 8, 128) for (w_ap, _) in fm_items]
                    descs += [(w_ap, 8, n) for (w_ap, n, _, _) in tm_items]
                    it = stream(wp, descs)
                    def finish(i, dst_fn):
                        b2 = srot.next()
                        A("pe", lambda e, i=i, b2=b2: e.matmul(ps[b2][:], lhsT=perm, rhs=xb[i][:], start=True, stop=True),
                          r=[("xb", i), "mats"], w=[("ps", b2)])
                        A("dve", lambda e, i=i, b2=b2, rc=rc: e.tensor_tensor(out=t2[i][:], in0=ps[b2][:], in1=ropc[rc][:, 1, :], op=ALU.mult),
                          r=[("ps", b2), ("ropc", rc)], w=[("t2", i)])
                        dst, dkey = dst_fn(c)
                        A("pool", lambda e, i=i, dst=dst: e.tensor_tensor(out=dst, in0=t1[i][:], in1=t2[i][:], op=ALU.add),
                          r=[("t1", i), ("t2", i)], w=[dkey])

                    pend = None
                    for (_, dst_fn) in fm_items:
                        wt, wk = next(it)
                        b = srot.next()
                        for k in range(NKC):
                            A("pe", lambda e, k=k, b=b, wt=wt: e.matmul(ps[b][:], lhsT=wt[:, k, :], rhs=hnc[:, k, :],
                                                                        start=(k == 0), stop=(k == NKC - 1)),
                              r=[wk, ("hnc", k)], w=[("ps", b)])
                        i = r2.next()
                        A("act", lambda e, i=i, b=b: e.activation(out=xb[i][:], in_=ps[b][:], func=AF.Copy),
                          r=[("ps", b)], w=[("xb", i)])
                        A("dve", lambda e, i=i, b=b, rc=rc: e.tensor_tensor(out=t1[i][:], in0=ps[b][:], in1=ropc[rc][:, 0, :], op=ALU.mult),
                          r=[("ps", b), ("ropc", rc), ("xb", i)], w=[("t1", i)])
                        if pend is not None:
                            finish(*pend)
                        pend = (i, dst_fn)
                    if pend is not None:
                        finish(*pend)
                    for (_, n, dst_fn, _) in tm_items:
                        wt, wk = next(it)
                        for tb in range(4):
                            b = srot.next()
                            for k in range(NKC):
                                A("pe", lambda e, k=k, b=b, wt=wt, tb=tb, n=n: e.matmul(
                                    ps[b][:, 0:n], lhsT=hnc[:, k, tb * 128:(tb + 1) * 128], rhs=wt[:, k, :],
                                    start=(k == 0), stop=(k == NKC - 1)), r=[wk, ("hnc", k)], w=[("ps", b)])
                            dst, dkey = dst_fn(c * 4 + tb)
                            if isinstance(dst, list):
                                for (dap, lo_, hi_) in dst:
                                    A("act", lambda e, b=b, dap=dap, lo_=lo_, hi_=hi_: e.activation(out=dap, in_=ps[b][:, lo_:hi_], func=AF.Copy),
                                      r=[("ps", b)], w=[dkey])
                            else:
                                A("act", lambda e, b=b, dst=dst, n=n: e.activation(out=dst, in_=ps[b][:, 0:n], func=AF.Copy),
                                  r=[("ps", b)], w=[dkey])
            P.barrier()

        LOOK = 2

        def run_units(units, front, back, look=LOOK):
            fr = []
            for idx, u in enumerate(units):
                fr.append(front(u))
                if idx >= look:
                    back(units[idx - look], fr[idx - look])
            for idx in range(max(0, len(units) - look), len(units)):
                back(units[idx], fr[idx])

        def attn_jobs(jobs, accrot, srot, tiles, kz=None, look=LOOK):
            ebs, pbs, rds, erot, prot, rrot, mengs = tiles
            units = []
            for ji, job in enumerate(jobs):
                c = job[0]
                accs = accrot.next()
                last = 4 * c + 3
                for j in range(last + 1):
                    for e_ in range(2):
                        units.append((job, accs, j, e_, j == last and e_ == 1, ji, j == 0 and e_ == 0))

            def front(u):
                (c, Q, qc, qkey, K, kc, kkey, vfn, vkey, mfn, mkey, dst, dkey), accs, j, e_, is_last, ji, is_first = u
                t0 = max(0, j * 128 - c * 512)
                n = 512 - t0
                lo, hi = e_ * 64, (e_ + 1) * 64
                b = srot.next()
                if kz is not None:
                    kzt, kzkey, pre = kz[ji]
                    if is_first and pre is not None:
                        pre()
                    A("pe", lambda e: e.matmul(
                        ps[b][:, 0:n], lhsT=kzt[:, e_, j * 128:(j + 1) * 128], rhs=Q[:, qc, c * 512 + t0:(c + 1) * 512],
                        start=True, stop=True), r=[qkey, kzkey], w=[("ps", b)])
                else:
                    A("pe", lambda e: e.matmul(
                        ps[b][:, 0:n], lhsT=K[lo:hi, kc, j * 128:(j + 1) * 128], rhs=Q[lo:hi, qc, c * 512 + t0:(c + 1) * 512],
                        start=True, stop=True), r=[qkey, kkey], w=[("ps", b)])
                ei = erot.next()
                A("act", lambda e: e.activation(out=ebs[ei][:, 0:n], in_=ps[b][:, 0:n], func=AF.Exp, scale=0.125),
                  r=[("ps", b)], w=[("eb", ei)])
                m = mfn(j, t0, n)
                if m is not None:
                    pi = prot.next()
                    A(mengs.next(), lambda e: e.tensor_tensor(out=pbs[pi][:, 0:n], in0=ebs[ei][:, 0:n], in1=m, op=ALU.mult),
                      r=[("eb", ei), mkey], w=[("pb", pi)])
                    return pbs[pi], ("pb", pi)
                return ebs[ei], ("eb", ei)

            def back(u, fr):
                (c, Q, qc, qkey, K, kc, kkey, vfn, vkey, mfn, mkey, dst, dkey), (ba, bb), j, e_, is_last, ji, is_first = u
                src, skey = fr
                last = 4 * c + 3
                t0 = max(0, j * 128 - c * 512)
                n = 512 - t0
                bk_ = ba if e_ == 0 else bb
                v = vfn(j, e_)
                A("pe", lambda e: e.matmul(ps[bk_][:, t0:512], lhsT=v, rhs=src[:, 0:n], start=(j == 0), stop=(j == last)),
                  r=[skey, vkey], w=[("ps", bk_)])
                if is_last:
                    di = rrot.next()
                    dsh = rds[di]
                    A("dve", lambda e: e.tensor_copy(out=dsh[0:64, :], in_=ps[ba][64:128, :]), r=[("ps", ba)], w=[("rd", di)])
                    A("dve", lambda e: e.tensor_copy(out=dsh[64:128, :], in_=ps[bb][0:64, :]), r=[("ps", bb)], w=[("rd", di)])
                    A("act", lambda e: e.activation(out=dsh[:], in_=dsh[:], func=AF.Ln), r=[("rd", di)], w=[("rd", di)])
                    A("act", lambda e: e.activation(out=dsh[:], in_=dsh[:], func=AF.Exp, scale=-1.0), r=[("rd", di)], w=[("rd", di)])
                    A("dve", lambda e: e.tensor_tensor(out=dst[0:64, :], in0=ps[ba][0:64, :], in1=dsh[0:64, :], op=ALU.mult),
                      r=[("ps", ba), ("rd", di)], w=[dkey])
                    A("dve", lambda e: e.tensor_tensor(out=dst[64:128, :], in0=ps[bb][64:128, :], in1=dsh[64:128, :], op=ALU.mult),
                      r=[("ps", bb), ("rd", di)], w=[dkey])

            run_units(units, front, back, look)

        def make_kz(f2sb, nbuf=2):
            kzs = [f2sb(f"kz{i}", [128, 2, T], BF16) for i in range(nbuf)]
            for i in range(nbuf):
                A("pool", lambda e, i=i: e.memset(kzs[i][64:128, 0, :], 0.0), w=[("kz", i)])
                A("pool", lambda e, i=i: e.memset(kzs[i][0:64, 1, :], 0.0), w=[("kz", i)])

            def fill(i, K, kc):
                def pre():
                    A("pool", lambda e: e.tensor_copy(out=kzs[i][0:64, 0, :], in_=K[0:64, kc, :]), r=[], w=[("kz", i)])
                    A("dve", lambda e: e.tensor_copy(out=kzs[i][64:128, 1, :], in_=K[64:128, kc, :]), r=[], w=[("kz", i)])
                return pre
            return kzs, fill

        def attn_tiles(f2sb, mengs=("dve", "pool"), nslot=4):
            ebs = [f2sb(f"eb{i}", [128, 512], BF16) for i in range(nslot)]
            pbs = [f2sb(f"pb{i}", [128, 512], BF16) for i in range(nslot)]
            rds = [f2sb(f"rd{i}", [128, 512], F32) for i in range(2)]
            return (ebs, pbs, rds, Rot(range(nslot)), Rot(range(nslot)), Rot(range(2)), Rot(mengs))

        def apply_wout(w2d, merged):
            with ExitStack() as s2:
                wp = WPool(s2, 2048, 3, 4, "o")
                srot = Rot(range(7))
                descs = [(w2d[d2], 8, 256) for d2 in range(4)]
                it = stream(wp, descs)
                for d2 in range(4):
                    wt, wk = next(it)
                    for di in range(2):
                        d = 2 * d2 + di
                        for c in range(NTC):
                            b = srot.next()
                            for k in range(NKC):
                                A("pe", lambda e, k=k, b=b, wt=wt, di=di, c=c: e.matmul(
                                    ps[b][:], lhsT=wt[:, k, di * 128:(di + 1) * 128], rhs=merged[:, k, c * 512:(c + 1) * 512],
                                    start=(k == 0), stop=(k == NKC - 1)), r=[wk, ("mg", k, c)], w=[("ps", b)])
                            A("dve", lambda e, b=b, d=d, c=c: e.tensor_tensor(
                                out=h[:, d, c * 512:(c + 1) * 512], in0=ps[b][:], in1=h[:, d, c * 512:(c + 1) * 512], op=ALU.add),
                              r=[("ps", b), ("h", d, c)], w=[("h", d, c)])
            P.barrier()

        def even_mixer(gi):
            with ExitStack() as sm:
                msb = mk_sb(sm)
                merged = msb("merged", [128, NKC, T], BF16)
                A("dve", lambda e: e.tensor_tensor(out=lv[:, 0, :], in0=lv[:, 0, :], in1=lv[:, 1, :], op=ALU.mult), r=["lv"], w=["lv"])
                A("dve", lambda e: e.tensor_tensor(out=lv[:, 2, :], in0=lv[:, 2, :], in1=lv[:, 3, :], op=ALU.mult), r=["lv"], w=["lv"])
                A("dve", lambda e: e.reduce_sum(out=lsm[:, 0:1], in_=lv[:, 0, :], axis=AX.X), r=["lv"], w=["lsm"])
                A("dve", lambda e: e.reduce_sum(out=lsm[:, 1:2], in_=lv[:, 2, :], axis=AX.X), r=["lv"], w=["lsm"])
                A("act", lambda e: e.activation(out=lsm[:, 2:4], in_=lsm[:, 0:2], func=AF.Exp), r=["lsm"], w=["lsm"])
                A("dve", lambda e: e.tensor_tensor(out=lsm[:, 4:5], in0=lsm[:, 3:4], in1=lsm[:, 2:3], op=ALU.subtract), r=["lsm"], w=["lsm"])
                A("dve", lambda e: e.tensor_scalar(out=lsm[:, 5:6], in0=lsm[:, 4:5], scalar1=-0.2, scalar2=None, op0=ALU.add), r=["lsm"], w=["lsm"])
                A("dve", lambda e: e.tensor_scalar(out=lsm[:, 6:7], in0=lsm[:, 7:8], scalar1=0.8, scalar2=None, op0=ALU.mult), r=["lsm", "lsm7"], w=["lsm"])
                neglam = lsm[:, 5:6]
                gsub = lsm[:, 6:7]

                for kk in range(NKC):
                    if (kk < 4 and not cfg["dsa"]) or (kk >= 4 and not cfg["diff"]):
                        A("pool", lambda e, kk=kk: e.memset(merged[:, kk, :], 0.0), w=[("mg", kk, c) for c in range(NTC)])
                if cfg["dsa"]:
                    with ExitStack() as sa:
                        asb = mk_sb(sa)
                        QA = asb("QA", [128, 4, T], BF16)
                        KA = asb("KA", [128, 1, T], BF16)
                        QI = asb("QI", [128, 4, T], BF16)
                        KI = asb("KI", [128, 1, T], BF16)
                        VA = asb("VA", [128, NTB, 192], BF16)
                        A("pool", lambda e: e.memset(VA[:, :, 64:128], 1.0), w=[("VA1",)])
                        WI = asb("WI", [128, NTB, 8], F32)
                        fm = []
                        bufs = [(QA, n, "QA") for n in range(4)] + [(KA, 0, "KA")] + [(QI, n, "QI") for n in range(4)] + [(KI, 0, "KI")]
                        for ci, (buf, n, nm) in enumerate(bufs):
                            fm.append((ew_fm[ci],
                                       lambda c, buf=buf, n=n, nm=nm: (buf[:, n, c * 512:(c + 1) * 512], (nm, n, c))))
                        tm = [(ew_va[0], 64, lambda blk: ([(VA[:, blk, 0:64], 0, 64), (VA[:, blk, 128:192], 0, 64)], ("VA", blk)), None),
                              (ew_wi[0], 8, lambda blk: (WI[:, blk, :], ("WI", blk)), None)]
                        mixer_proj(gi, fm, tm)
                        with ExitStack() as s2:
                            f2sb = mk_sb(s2)
                            Ib = [f2sb(f"Ib{i}", [128, T], F32) for i in range(2)]
                            rb = [f2sb(f"rb{i}", [128, 512], F32) for i in range(2)]
                            mk = f2sb("mk", [128, T], BF16)
                            junk = mk
                            maskT = f2sb("maskT", [128, NTB, 512], BF16)
                            bnd = [f2sb(f"bnd{i}", [128, 8], F32) for i in range(2)]
                            gem = [f2sb(f"gem{i}", [128, 2], mybir.dt.uint32) for i in range(2)]
                            tiles = attn_tiles(f2sb, mengs=("dve",))
                            srot = Rot([0, 1, 2])
                            rrot = Rot(range(2))
                            accrot = Rot([(3, 4), (5, 6)])
                            NIT = 15

                            def s1(i):
                                c = i // 4
                                cols = (i + 1) * 128
                                I = Ib[i % 2]
                                ik = ("I", i % 2)
                                bd = bnd[i % 2]
                                bk = ("bnd", i % 2)
                                for sc in range((cols + 511) // 512):
                                    w_ = min(512, cols - sc * 512)
                                    for hd in range(8):
                                        lo, hi = (hd % 2) * 64, (hd % 2 + 1) * 64
                                        b = srot.next()
                                        A("pe", lambda e, b=b, lo=lo, hi=hi, hd=hd, sc=sc, w_=w_: e.matmul(
                                            ps[b][:, 0:w_], lhsT=QI[lo:hi, hd // 2, i * 128:(i + 1) * 128],
                                            rhs=KI[lo:hi, 0, sc * 512:sc * 512 + w_], start=True, stop=True),
                                          r=[("QI", hd // 2, c), ("KI", 0, sc)], w=[("ps", b)])
                                        ri = rrot.next()
                                        A("act", lambda e, b=b, ri=ri, w_=w_: e.activation(out=rb[ri][:, 0:w_], in_=ps[b][:, 0:w_], func=AF.Relu),
                                          r=[("ps", b)], w=[("rb", ri)])
                                        if hd == 0:
                                            A("dve", lambda e, ri=ri, sc=sc, w_=w_: e.tensor_scalar(
                                                out=I[:, sc * 512:sc * 512 + w_], in0=rb[ri][:, 0:w_], scalar1=WI[:, i, 0:1], scalar2=None, op0=ALU.mult),
                                              r=[("rb", ri), ("WI", i)], w=[ik])
                                        else:
                                            A("dve", lambda e, ri=ri, sc=sc, w_=w_, hd=hd: e.scalar_tensor_tensor(
                                                out=I[:, sc * 512:sc * 512 + w_], in0=rb[ri][:, 0:w_], scalar=WI[:, i, hd:hd + 1],
                                                in1=I[:, sc * 512:sc * 512 + w_], op0=ALU.mult, op1=ALU.add),
                                              r=[("rb", ri), ("WI", i), ik], w=[ik])
                                if i >= 2:
                                    A("dve", lambda e: e.tensor_reduce(out=bd[:, 0:1], in_=I[:, 0:cols], axis=AX.X, op=ALU.min), r=[ik], w=[bk])
                                    A("dve", lambda e: e.tensor_reduce(out=bd[:, 1:2], in_=I[:, 0:cols], axis=AX.X, op=ALU.max), r=[ik], w=[bk])
                                    A("dve", lambda e: e.tensor_tensor(out=bd[:, 2:3], in0=bd[:, 1:2], in1=bd[:, 0:1], op=ALU.subtract), r=[bk], w=[bk])
                                A("pool", lambda e: e.tensor_tensor(out=I[:, i * 128:(i + 1) * 128], in0=I[:, i * 128:(i + 1) * 128],
                                                                    in1=negm[:], op=ALU.add), r=[ik, "negm"], w=[ik])

                            def s2(i):
                                ti = i % 4
                                cols = (i + 1) * 128
                                I = Ib[i % 2]
                                ik = ("I", i % 2)
                                bd = bnd[i % 2]
                                bk = ("bnd", i % 2)
                                gm = gem[i % 2]
                                gk = ("gem", i % 2)
                                if i >= 2:
                                    for it in range(NIT):
                                        A("dve", lambda e, it=it: e.scalar_tensor_tensor(out=bd[:, 3:4], in0=bd[:, 2:3], scalar=float(2.0 ** -(it + 1)),
                                                                                        in1=bd[:, 0:1], op0=ALU.mult, op1=ALU.add), r=[bk], w=[bk])
                                        A("dve", lambda e: e.tensor_scalar(out=junk[:, 0:cols], in0=I[:, 0:cols], scalar1=bd[:, 3:4], scalar2=None,
                                                                           op0=ALU.is_ge, op1=ALU.add, accum_out=bd[:, 4:5]), r=[ik, bk], w=[bk, "mk"])
                                        A("dve", lambda e: e.tensor_single_scalar(out=gm[:, 0:1], in_=bd[:, 4:5], scalar=255.5, op=ALU.is_ge), r=[bk], w=[gk])
                                        A("dve", lambda e: e.copy_predicated(out=bd[:, 0:1], mask=gm[:, 0:1], data=bd[:, 3:4]), r=[bk, gk], w=[bk])
                                    A("dve", lambda e: e.tensor_scalar(out=mk[:, 0:cols], in0=I[:, 0:cols], scalar1=bd[:, 0:1], scalar2=None, op0=ALU.is_ge),
                                      r=[ik, bk], w=["mk"])
                                else:
                                    A("dve", lambda e: e.tensor_single_scalar(out=mk[:, 0:cols], in_=I[:, 0:cols], scalar=-1.0e29, op=ALU.is_ge), r=[ik], w=["mk"])
                                for j0 in range(0, i + 1, 8):
                                    nn = min(8, i + 1 - j0)
                                    for jj in range(nn):
                                        A("pe", lambda e, jj=jj, j0=j0: e.transpose(
                                            out=pTp[:, jj * 128:(jj + 1) * 128],
                                            in_=mk[:, (j0 + jj) * 128:(j0 + jj + 1) * 128], identity=ident),
                                          r=["mk", "mats"], w=["pT"])
                                    A("act", lambda e, nn=nn, j0=j0: e.activation(
                                        out=maskT[:, j0:j0 + nn, ti * 128:(ti + 1) * 128],
                                        in_=pTp[:, 0:nn * 128].rearrange("p (a b) -> p a b", a=nn), func=AF.Copy),
                                      r=["pT"], w=["maskT"])

                            s1(0)
                            for i in range(NTB):
                                if i + 1 < NTB:
                                    s1(i + 1)
                                s2(i)
                                if i % 4 == 3:
                                    c = i // 4
                                    attn_jobs([(c, QA, hp, ("QA", hp, c), KA, 0, ("KA", 0, c),
                                                lambda j, e_: VA[:, j, e_ * 64:e_ * 64 + 128], ("VA", 0),
                                                lambda j, t0, n: maskT[:, j, t0:512], "maskT",
                                                merged[:, hp, c * 512:(c + 1) * 512], ("mg", hp, c)) for hp in range(4)],
                                              accrot, srot, tiles)
                        P.barrier()

                if cfg["diff"]:
                    with ExitStack() as sa:
                        asb = mk_sb(sa)
                        QB = asb("QB", [128, 4, T], BF16)
                        KB = asb("KB", [128, 4, T], BF16)
                        VB = asb("VB", [128, NTB, 512], BF16)
                        fm = []
                        bufs = [(QB, n, "QB") for n in range(4)] + [(KB, n, "KB") for n in range(4)]
                        for ci, (buf, n, nm) in enumerate(bufs):
                            fm.append((ew_fm[10 + ci],
                                       lambda c, buf=buf, n=n, nm=nm: (buf[:, n, c * 512:(c + 1) * 512], (nm, n, c))))
                        tm = [(ew_vb[n], 128,
                               lambda blk, n=n: (VB[:, blk, n * 128:(n + 1) * 128], ("VB", blk, n)), None) for n in range(4)]
                        mixer_proj(gi, fm, tm)
                        with ExitStack() as s2:
                            f2sb = mk_sb(s2)
                            strip = f2sb("cstrip", [128, STRIP_W], BF16)
                            DMA(strip[:], strips_d[0], w=["strip"])
                            ebs = [f2sb(f"eb{i}", [128, 512], BF16) for i in range(3)]
                            pbs = [f2sb(f"pb{i}", [128, 512], BF16) for i in range(3)]
                            fa = [f2sb(f"fa{i}", [128, 512], F32) for i in range(4)]
                            ob = f2sb("ob", [128, 512], F32)
                            sq2 = f2sb("sq2", [128, 512], BF16)
                            ebs = ebs + [f2sb("eb3", [128, 512], BF16), f2sb("eb4", [128, 512], BF16)]
                            pbs = pbs + [f2sb("pb3", [128, 512], BF16), f2sb("pb4", [128, 512], BF16)]
                            erot, prot = Rot(range(5)), Rot(range(5))
                            srot = Rot([0, 1, 2, 7])
                            dkzs, dfill = make_kz(f2sb)
                            units = []
                            for hd in range(4):
                                for c in range(NTC):
                                    last = 4 * c + 3
                                    for j in range(last + 1):
                                        for e_ in range(2):
                                            units.append((hd, c, j, e_, j == last and e_ == 1))

                            def dfront(u):
                                hd, c, j, e_, is_last = u
                                t0 = max(0, j * 128 - c * 512)
                                n = 512 - t0
                                lo, hi = e_ * 64, (e_ + 1) * 64
                                b = srot.next()
                                if c == 0 and j == 0 and e_ == 0:
                                    dfill(hd % 2, KB, hd)()
                                kzt = dkzs[hd % 2]
                                A("pe", lambda e: e.matmul(
                                    ps[b][:, 0:n], lhsT=kzt[:, e_, j * 128:(j + 1) * 128],
                                    rhs=QB[:, hd, c * 512 + t0:(c + 1) * 512], start=True, stop=True),
                                  r=[("QB", hd, c), ("kz", hd % 2)], w=[("ps", b)])
                                ei = erot.next()
                                A("act", lambda e: e.activation(out=ebs[ei][:, 0:n], in_=ps[b][:, 0:n], func=AF.Exp, scale=0.125),
                                  r=[("ps", b)], w=[("eb", ei)])
                                if j >= 4 * c:
                                    off = c * 512 - j * 128 + 384 + t0
                                    pi = prot.next()
                                    A("dve", lambda e: e.tensor_tensor(
                                        out=pbs[pi][:, 0:n], in0=ebs[ei][:, 0:n], in1=strip[:, off:off + n], op=ALU.mult),
                                      r=[("eb", ei), "strip"], w=[("pb", pi)])
                                    return pbs[pi], ("pb", pi)
                                return ebs[ei], ("eb", ei)

                            def dback(u, fr):
                                hd, c, j, e_, is_last = u
                                src, skey = fr
                                last = 4 * c + 3
                                t0 = max(0, j * 128 - c * 512)
                                n = 512 - t0
                                nb, db = 3 + 2 * e_, 4 + 2 * e_
                                A("pe", lambda e: e.matmul(
                                    ps[nb][:, t0:512], lhsT=VB[:, j, hd * 128:(hd + 1) * 128], rhs=src[:, 0:n],
                                    start=(j == 0), stop=(j == last)),
                                  r=[skey, ("VB", j, hd)], w=[("ps", nb)])
                                A("pe", lambda e: e.matmul(
                                    ps[db][:, t0:512], lhsT=ones, rhs=src[:, 0:n], start=(j == 0), stop=(j == last)),
                                  r=[skey, "mats"], w=[("ps", db)])
                                if not is_last:
                                    return
                                A("act", lambda e: e.activation(out=fa[0][:], in_=ps[4][:], func=AF.Ln), r=[("ps", 4)], w=[("fa", 0)])
                                A("act", lambda e: e.activation(out=fa[0][:], in_=fa[0][:], func=AF.Exp, scale=-1.0), r=[("fa", 0)], w=[("fa", 0)])
                                A("dve", lambda e: e.tensor_tensor(out=fa[1][:], in0=ps[3][:], in1=fa[0][:], op=ALU.mult),
                                  r=[("ps", 3), ("fa", 0)], w=[("fa", 1)])
                                A("act", lambda e: e.activation(out=fa[2][:], in_=ps[6][:], func=AF.Ln), r=[("ps", 6)], w=[("fa", 2)])
                                A("act", lambda e: e.activation(out=fa[2][:], in_=fa[2][:], func=AF.Exp, scale=-1.0), r=[("fa", 2)], w=[("fa", 2)])
                                A("dve", lambda e: e.tensor_tensor(out=fa[3][:], in0=ps[5][:], in1=fa[2][:], op=ALU.mult),
                                  r=[("ps", 5), ("fa", 2)], w=[("fa", 3)])
                                A("dve", lambda e: e.scalar_tensor_tensor(out=ob[:], in0=fa[3][:], scalar=neglam, in1=fa[1][:],
                                                                          op0=ALU.mult, op1=ALU.add),
                                  r=[("fa", 3), ("fa", 1), "lsm"], w=["ob"])
                                A("act", lambda e: e.activation(out=sq2[:], in_=ob[:], func=AF.Square, scale=float(128.0 ** -0.5)),
                                  r=["ob"], w=["sq2"])
                                b = srot.next()
                                A("pe", lambda e: e.matmul(ps[b][:], lhsT=ones, rhs=sq2[:], start=True, stop=True),
                                  r=["sq2", "mats"], w=[("ps", b)])
                                A("act", lambda e: e.activation(out=fa[0][:], in_=ps[b][:], func=AF.Ln, bias=epsb[:, 1:2]),
                                  r=[("ps", b), "epsb"], w=[("fa", 0)])
                                A("act", lambda e: e.activation(out=fa[2][:], in_=fa[0][:], func=AF.Exp, scale=-0.5), r=[("fa", 0)], w=[("fa", 2)])
                                A("dve", lambda e: e.scalar_tensor_tensor(
                                    out=merged[:, 4 + hd, c * 512:(c + 1) * 512], in0=ob[:], scalar=gsub, in1=fa[2][:],
                                    op0=ALU.mult, op1=ALU.mult), r=["ob", ("fa", 2), "lsm"], w=[("mg", 4 + hd, c)])

                            run_units(units, dfront, dback, 3)
                        P.barrier()
                apply_wout(even_w_out, merged)

        def odd_mixer(gi):
            with ExitStack() as sm:
                msb = mk_sb(sm)
                merged = msb("mergedo", [128, NKC, T], BF16)
                for g in range(2):
                    odd_group(g, gi, merged)
                apply_wout(odd_w_out, merged)

        def odd_group(g, gi, merged):
            if True:
                if True:
                    with ExitStack() as sa:
                        asb = mk_sb(sa)
                        QG = asb("QG", [128, 4, T], BF16)
                        KG = asb("KG", [128, 4, T], BF16)
                        VG = asb("VG", [128, NTB, 4, 192], BF16)
                        A("pool", lambda e: e.memset(VG[:, :, :, 64:128], 1.0), w=[("VG1",)])
                        fm = []
                        for n in range(4):
                            fm.append((odd_w_in[4 * g + n],
                                       lambda c, n=n: (QG[:, n, c * 512:(c + 1) * 512], ("QG", n, c))))
                        for n in range(4):
                            fm.append((odd_w_in[8 + 4 * g + n],
                                       lambda c, n=n: (KG[:, n, c * 512:(c + 1) * 512], ("KG", n, c))))
                        tm = [(odd_w_in[16 + 4 * g + n], 128,
                               lambda blk, n=n: ([(VG[:, blk, n, 0:64], 0, 64), (VG[:, blk, n, 128:192], 64, 128)], ("VG", blk, n)), None) for n in range(4)]
                        if cfg["proj"]:
                            mixer_proj(gi, fm, tm)
                        if not cfg["attn"]:
                            for hp in range(4):
                                A("pool", lambda e, hp=hp: e.memset(merged[:, 4 * g + hp, :], 0.0), w=[("mg", 4 * g + hp, c) for c in range(NTC)])
                            return
                        with ExitStack() as s2:
                            f2sb = mk_sb(s2)
                            strip = f2sb("mstrip", [128, STRIP_W], BF16)
                            DMA(strip[:], strips_d[1], w=["strip"])
                            tiles = attn_tiles(f2sb, mengs=("dve",), nslot=5)
                            srot = Rot([0, 1, 2, 7])
                            accrot = Rot([(3, 4), (5, 6)])
                            jobs = []
                            for hp in range(4):
                                for c in range(NTC):
                                    jobs.append((c, QG, hp, ("QG", hp, c), KG, hp, ("KG", hp, c),
                                                 lambda j, e_, hp=hp: VG[:, j, hp, e_ * 64:e_ * 64 + 128], ("VG", 0, 0),
                                                 lambda j, t0, n, c=c: strip[:, c * 512 - j * 128 + 384 + t0:c * 512 - j * 128 + 384 + t0 + n], "strip",
                                                 merged[:, 4 * g + hp, c * 512:(c + 1) * 512], ("mg", 4 * g + hp, c)))
                            kzs, fill = make_kz(f2sb)
                            kzd = {}
                            for ji in range(len(jobs)):
                                hp = ji // NTC
                                kzd[ji] = (kzs[hp % 2], ("kz", hp % 2), fill(hp % 2, KG, hp) if ji % NTC == 0 else None)
                            attn_jobs(jobs, accrot, srot, tiles, kz=kzd, look=3)
                        P.barrier()

        for l in range(cfg["layers"]):
            if cfg["ffn"]:
                ffn(l, ffn_w["ffn_a_wg"], ffn_w["ffn_a_wu"], ffn_w["ffn_a_wd"], l * 4 + 0)
            if l % 2 == 0:
                if cfg["mix_even"]:
                    even_mixer(l * 4 + 1)
            else:
                if cfg["mix_odd"]:
                    odd_mixer(l * 4 + 1)
            if cfg["ffn"]:
                ffn(l, ffn_w["ffn_b_wg"], ffn_w["ffn_b_wu"], ffn_w["ffn_b_wd"], l * 4 + 2)
            if cfg["ple"]:
                ple(l, l * 4 + 3)
        final_norm()
        P.emit(st)
    nc._declared_inputs = declared
    return nc


def prep_inputs(inputs, b):
    f = np.float32
    g = lambda k: np.asarray(inputs[k], f)
    rope, strips, mats, negmask = _host_consts()
    gains = []
    for l in range(2):
        for nm in ("norm_ffn_a", "norm_mix", "norm_ffn_b", "norm_ple"):
            gains.append(_col(g(nm)[l]))
    gains.append(_col(g("final_norm")))
    gains = np.ascontiguousarray(np.concatenate(gains, axis=1))
    ewi = g("even_w_in")[0]
    qa, ka, va = ewi[:, 0:512], ewi[:, 512:576], ewi[:, 576:640]
    qi, ki, wi = ewi[:, 640:1152], ewi[:, 1152:1216], ewi[:, 1216:1224]
    qb, kb, vb = ewi[:, 1224:1736], ewi[:, 1736:2248], ewi[:, 2248:2760]
    ew_fm = np.ascontiguousarray(np.concatenate([qa, ka, ka, qi, ki, ki, qb, kb], axis=1))
    def tile_w(W, tc):
        R_, C_ = W.shape
        return np.ascontiguousarray(W.reshape(R_ // 128, 128, C_ // tc, tc).transpose(2, 1, 0, 3))

    m = {
        "xT": np.ascontiguousarray(g("x")[b].T),
        "pT": np.ascontiguousarray(np.transpose(g("p")[:, b], (0, 2, 1))),
        "gains": gains,
        "ple_gate": np.stack([tile_w(g("ple_gate")[l], 256) for l in range(2)], 0),
        "ple_proj": np.stack([tile_w(g("ple_proj")[l], 256) for l in range(2)], 0),
        "ew_fm": tile_w(ew_fm, 128), "ew_va": tile_w(np.ascontiguousarray(va), 64),
        "ew_wi": tile_w(np.ascontiguousarray(wi), 8), "ew_vb": tile_w(np.ascontiguousarray(vb), 128),
        "even_w_out": tile_w(g("even_w_out")[0], 256),
        "lamv": np.ascontiguousarray(np.stack([g("diff_lambda_q1")[0], g("diff_lambda_k1")[0],
                                               g("diff_lambda_q2")[0], g("diff_lambda_k2")[0]], 0)),
        "subln": np.ascontiguousarray(g("diff_subln")[0].reshape(128, 1)),
        "odd_w_in": tile_w(g("odd_w_in")[0], 128), "odd_w_out": tile_w(g("odd_w_out")[0], 256),
        "rope": rope, "strips": strips, "mats": mats, "negmask": negmask,
    }
    for nm in ("ffn_a_wg", "ffn_a_wu", "ffn_b_wg", "ffn_b_wu"):
        m[nm] = np.stack([tile_w(g(nm)[l], 256) for l in range(2)], 0)
    for nm in ("ffn_a_wd", "ffn_b_wd"):
        w = g(nm)
        m[nm] = np.stack([np.stack([np.stack([tile_w(w[l][hf * 1408:(hf + 1) * 1408, d2 * 256:(d2 + 1) * 256], 256)[0]
                                              for hf in range(2)], 0) for d2 in range(4)], 0) for l in range(2)], 0)
    return m


_NC_CACHE = {}


def kernel(**inputs):
    import time as _time
    _t0 = _time.time()
    key = tuple(sorted(CFG.items()))
    if key not in _NC_CACHE:
        _NC_CACHE[key] = build_program(CFG)
    nc = _NC_CACHE[key]
    print(f"[kernel] build {_time.time() - _t0:.1f}s", flush=True)
    n = 8
    shared = prep_inputs(inputs, 0)
    in_maps = []
    for b in range(n):
        m = dict(shared)
        m["xT"] = np.ascontiguousarray(np.asarray(inputs["x"], np.float32)[b].T)
        m["pT"] = np.ascontiguousarray(np.transpose(np.asarray(inputs["p"], np.float32)[:, b], (0, 2, 1)))
        in_maps.append({k: v for k, v in m.items() if k in nc._declared_inputs})
    print(f"[kernel] prep {_time.time() - _t0:.1f}s", flush=True)
    import os as _os
    _nd = int(_os.environ.get("DBG_CORES", "8"))
    if _nd != 8:
        res = run_bass_kernel_spmd(nc, in_maps[:_nd], core_ids=list(range(_nd)))
        res.results.extend([res.results[0]] * (8 - _nd))
    else:
        res = run_bass_kernel_spmd(nc, in_maps, core_ids=list(range(n)))
    print(f"[kernel] ran {_time.time() - _t0:.1f}s", flush=True)
    out = np.stack([np.asarray(res.results[b]["outT"], np.float32).T for b in range(n)], 0)
    return np.ascontiguousarray(out)
```
